# Optimizing a Trainium2 kernel written in Bass

```python
import jax, jax.numpy as jnp
from jax import lax
import numpy as np

D_MODEL = 1024
BATCH = 16
SEQ = 256
DEPTH = 2
DEC_BATCH = 8
DEC_SEQ = 1024
PAST_LEN = 512

GRID_W = 64
N_MIXERS = 2
N_POOL_LAYERS = (DEPTH + 1) // 2
N_ATTN_LAYERS = DEPTH // 2
POOL_SIZES = (2, 4, 8, 16)
N_POOL_GROUPS = 4
POOL_GROUP_DIM = D_MODEL // N_POOL_GROUPS
N_HEADS = 16
HEAD_DIM = D_MODEL // N_HEADS
WIN_H = 8
WIN_W = 16
REL_H = 2 * WIN_H - 1
REL_W = 2 * WIN_W - 1
NEG_INF = -1e30
N_EXPERTS = 16
N_EXPERT_GROUPS = 4
EXPERTS_PER_GROUP = N_EXPERTS // N_EXPERT_GROUPS
TOPK_GROUP = 1
TOP_K = 2
D_FF_EXPERT = 512
ALPHA = (2.0 * DEPTH) ** 0.25
BETA = (8.0 * DEPTH) ** -0.25
LN_EPS = 1e-5

kernel_name = "hybrid_pool_natten_moe_diffusion_step"


def layer_norm(x, g, b):
    xf = x.astype(jnp.float32)
    mu = jnp.mean(xf, axis=-1, keepdims=True)
    var = jnp.mean(jnp.square(xf - mu), axis=-1, keepdims=True)
    return ((xf - mu) * lax.rsqrt(var + LN_EPS) * g.astype(jnp.float32) + b.astype(jnp.float32)).astype(x.dtype)


def ada_modulation(cond, w, b):
    m = jax.nn.silu(cond) @ w + b
    return jnp.split(m[:, None, :], 6, axis=-1)


def modulate(x, shift, scale):
    return x * (1.0 + scale) + shift


def post_norm(x, out, g, b):
    return layer_norm(ALPHA * x + out, g, b)


def centred_pool_minus_self(x, w):
    n = x.shape[1]
    cs = jnp.concatenate([jnp.zeros_like(x[:, :1]), jnp.cumsum(x, axis=1)], axis=1)
    t = jnp.arange(n)
    lo = jnp.clip(t - w // 2, 0, n)
    hi = jnp.clip(t - w // 2 + w, 0, n)
    cnt = (hi - lo).astype(x.dtype)
    return (cs[:, hi] - cs[:, lo]) / cnt[None, :, None] - x


def pool_mixer(h, w_groups, scale):
    b, n, _ = h.shape
    hg = h.astype(jnp.float32).reshape(b, n, N_POOL_GROUPS, POOL_GROUP_DIM)
    pooled = jnp.stack([centred_pool_minus_self(hg[:, :, g], POOL_SIZES[g]) for g in range(N_POOL_GROUPS)], axis=2)
    out = jnp.einsum('bngc,gcd->bngd', pooled.astype(h.dtype), w_groups).reshape(b, n, D_MODEL)
    return out * scale


def project_qkv(h, w_qkv):
    b, n, _ = h.shape
    qkv = (h @ w_qkv).reshape(b, n, 3, N_HEADS, HEAD_DIM)
    return qkv[:, :, 0], qkv[:, :, 1], qkv[:, :, 2]


def context_attention(q, k, v):
    b, _, n, _ = q.shape
    s = jnp.einsum('bhqd,bhkd->bhqk', q, k).astype(jnp.float32) * (HEAD_DIM ** -0.5)
    p = jax.nn.softmax(s, axis=-1).astype(v.dtype)
    return jnp.einsum('bhqk,bhkd->bqhd', p, v).reshape(b, n, D_MODEL)


def latent_neighbourhood_attention(q, k, v, ctx_k, ctx_v, rel_bias):
    b, n, _, _ = q.shape
    rows = n // GRID_W
    wh = min(WIN_H, rows)
    r = jnp.arange(rows)
    row_start = jnp.clip(r - wh // 2, 0, rows - wh)
    row_idx = row_start[:, None] + jnp.arange(wh)[None, :]
    qg = q.reshape(b, rows, GRID_W, N_HEADS, HEAD_DIM)
    kg = k.reshape(b, rows, GRID_W, N_HEADS, HEAD_DIM)[:, row_idx]
    vg = v.reshape(b, rows, GRID_W, N_HEADS, HEAD_DIM)[:, row_idx]
    col = jnp.arange(GRID_W)
    col_start = jnp.clip(col - WIN_W // 2, 0, GRID_W - WIN_W)
    col_ok = (col[None, :] >= col_start[:, None]) & (col[None, :] < col_start[:, None] + WIN_W)
    scale = HEAD_DIM ** -0.5
    s_lat = jnp.einsum('brqhd,brikhd->bhrqik', qg, kg).astype(jnp.float32) * scale
    dy = row_idx - r[:, None] + (WIN_H - 1)
    dx = jnp.clip(col[None, :] - col[:, None] + (WIN_W - 1), 0, REL_W - 1)
    bias = rel_bias.astype(jnp.float32)[:, dy[:, None, :, None], dx[None, :, None, :]]
    s_lat = jnp.where(col_ok[:, None, :], s_lat + bias[None], NEG_INF)
    s_lat = s_lat.reshape(b, N_HEADS, n, wh * GRID_W)
    qh = q.transpose(0, 2, 1, 3)
    s_ctx = jnp.einsum('bhnd,bhld->bhnl', qh, ctx_k).astype(jnp.float32) * scale
    p = jax.nn.softmax(jnp.concatenate([s_lat, s_ctx], axis=-1), axis=-1).astype(v.dtype)
    p_lat = p[..., : wh * GRID_W].reshape(b, N_HEADS, rows, GRID_W, wh, GRID_W)
    p_ctx = p[..., wh * GRID_W:]
    o_lat = jnp.einsum('bhrqik,brikhd->brqhd', p_lat, vg).reshape(b, n, N_HEADS, HEAD_DIM)
    o_ctx = jnp.einsum('bhnl,bhld->bnhd', p_ctx, ctx_v)
    return (o_lat + o_ctx).reshape(b, n, D_MODEL)


def grouped_moe(h, w_router, b_router, w_gate, w_up, w_down):
    b, n, d = h.shape
    t = h.reshape(-1, d)
    ntok = t.shape[0]
    s = jax.nn.sigmoid((t @ w_router).astype(jnp.float32))
    sel = s + b_router.astype(jnp.float32)
    grp = sel.reshape(ntok, N_EXPERT_GROUPS, EXPERTS_PER_GROUP)
    grp_score = jnp.sum(lax.top_k(grp, 2)[0], axis=-1)
    _, gidx = lax.top_k(grp_score, TOPK_GROUP)
    gmask = jnp.sum(jax.nn.one_hot(gidx, N_EXPERT_GROUPS, dtype=jnp.float32), axis=1) > 0
    emask = jnp.repeat(gmask, EXPERTS_PER_GROUP, axis=1)
    _, eidx = lax.top_k(jnp.where(emask, sel, NEG_INF), TOP_K)
    wsel = jnp.take_along_axis(s, eidx, axis=1)
    wsel = wsel / jnp.sum(wsel, axis=-1, keepdims=True)
    combine = jnp.sum(jax.nn.one_hot(eidx, N_EXPERTS, dtype=jnp.float32) * wsel[..., None], axis=1).astype(t.dtype)
    out = jnp.zeros_like(t)
    for e in range(N_EXPERTS):
        he = jax.nn.silu(t @ w_gate[e]) * (t @ w_up[e])
        out = out + combine[:, e:e + 1] * (he @ w_down[e])
    return out.reshape(b, n, d)


def setup_inputs(seed: int = 0) -> dict:
    key = jax.random.key(seed)
    ks = jax.random.split(key, 24)
    nrm = jax.random.normal
    f32 = jnp.float32
    d, e, ff = D_MODEL, N_EXPERTS, D_FF_EXPERT
    qkv_scale = jnp.concatenate([jnp.ones((2 * d,), f32), jnp.full((d,), BETA, f32)])
    return {
        "x_prompt": nrm(ks[0], (BATCH, SEQ, d), f32),
        "x_sample": nrm(ks[1], (DEC_BATCH, DEC_SEQ, d), f32),
        "cache_k": nrm(ks[2], (DEC_BATCH, N_ATTN_LAYERS, N_HEADS, PAST_LEN, HEAD_DIM), f32),
        "cache_v": nrm(ks[3], (DEC_BATCH, N_ATTN_LAYERS, N_HEADS, PAST_LEN, HEAD_DIM), f32) * BETA,
        "c": nrm(ks[4], (DEC_BATCH, d), f32),
        "c_ctx": nrm(ks[5], (d,), f32),
        "w_ada": nrm(ks[6], (DEPTH, d, 6 * d), f32) * (0.2 * d ** -0.5),
        "b_ada": nrm(ks[7], (DEPTH, 6 * d), f32) * 0.02,
        "ln1_g": 1.0 + 0.02 * nrm(ks[8], (DEPTH, d), f32),
        "ln1_b": 0.02 * nrm(ks[9], (DEPTH, d), f32),
        "ln2_g": 1.0 + 0.02 * nrm(ks[10], (DEPTH, d), f32),
        "ln2_b": 0.02 * nrm(ks[11], (DEPTH, d), f32),
        "pool_w": nrm(ks[12], (N_POOL_LAYERS, N_POOL_GROUPS, POOL_GROUP_DIM, POOL_GROUP_DIM), f32) * (BETA * POOL_GROUP_DIM ** -0.5),
        "pool_scale": 1.0 + 0.02 * nrm(ks[13], (N_POOL_LAYERS, d), f32),
        "w_qkv": nrm(ks[14], (N_ATTN_LAYERS, d, 3 * d), f32) * (d ** -0.5) * qkv_scale,
        "w_o": nrm(ks[15], (N_ATTN_LAYERS, d, d), f32) * (BETA * d ** -0.5),
        "rel_bias": 0.02 * nrm(ks[16], (N_ATTN_LAYERS, N_HEADS, REL_H, REL_W), f32),
        "w_router": nrm(ks[17], (d, e), f32) * (d ** -0.5),
        "b_router": 0.01 * nrm(ks[18], (e,), f32),
        "w_gate": nrm(ks[19], (DEPTH, e, d, ff), f32) * (BETA * d ** -0.5),
        "w_up": nrm(ks[20], (DEPTH, e, d, ff), f32) * (BETA * d ** -0.5),
        "w_down": nrm(ks[21], (DEPTH, e, ff, d), f32) * (BETA * ff ** -0.5),
    }


def reference(x_prompt, x_sample, cache_k, cache_v, c, c_ctx, w_ada, b_ada, ln1_g, ln1_b, ln2_g, ln2_b,
              pool_w, pool_scale, w_qkv, w_o, rel_bias, w_router, b_router, w_gate, w_up, w_down):
    x = x_prompt
    new_k, new_v = [], []
    for i in range(DEPTH):
        j = i // N_MIXERS
        sh1, sc1, g1, sh2, sc2, g2 = ada_modulation(c_ctx[None, :], w_ada[i], b_ada[i])
        h = modulate(x, sh1, sc1)
        if i % N_MIXERS == 0:
            out = pool_mixer(h, pool_w[j], pool_scale[j])
        else:
            q, k, v = project_qkv(h, w_qkv[j])
            k_t = k.transpose(0, 2, 1, 3)
            v_t = v.transpose(0, 2, 1, 3)
            new_k.append(k_t)
            new_v.append(v_t)
            out = context_attention(q.transpose(0, 2, 1, 3), k_t, v_t) @ w_o[j]
        x = post_norm(x, g1 * out, ln1_g[i], ln1_b[i])
        h = modulate(x, sh2, sc2)
        x = post_norm(x, g2 * grouped_moe(h, w_router, b_router, w_gate[i], w_up[i], w_down[i]), ln2_g[i], ln2_b[i])
    y_prompt = x
    new_cache_k = jnp.stack(new_k, axis=1)
    new_cache_v = jnp.stack(new_v, axis=1)

    x = x_sample
    for i in range(DEPTH):
        j = i // N_MIXERS
        sh1, sc1, g1, sh2, sc2, g2 = ada_modulation(c, w_ada[i], b_ada[i])
        h = modulate(x, sh1, sc1)
        if i % N_MIXERS == 0:
            out = pool_mixer(h, pool_w[j], pool_scale[j])
        else:
            q, k, v = project_qkv(h, w_qkv[j])
            out = latent_neighbourhood_attention(q, k, v, cache_k[:, j], cache_v[:, j], rel_bias[j]) @ w_o[j]
        x = post_norm(x, g1 * out, ln1_g[i], ln1_b[i])
        h = modulate(x, sh2, sc2)
        x = post_norm(x, g2 * grouped_moe(h, w_router, b_router, w_gate[i], w_up[i], w_down[i]), ln2_g[i], ln2_b[i])
    y_sample = x
    return (y_prompt, y_sample, new_cache_k, new_cache_v)
```

```python
from contextlib import ExitStack

import numpy as np
import concourse.bass as bass
import concourse.mybir as mybir
from concourse.bass_utils import run_bass_kernel_spmd

F32 = mybir.dt.float32
BF16 = mybir.dt.bfloat16
AF = mybir.ActivationFunctionType
ALU = mybir.AluOpType
AX = mybir.AxisListType

N_CORES = 8
D = 1024
NT = 12
T = NT * 128
ALPHA = 4.0 ** 0.25
LN_EPS = 1e-5
NEG = -30000.0
POOL_SIZES = (2, 4, 8, 16)
SEQS = ((0, 2), (2, 2), (4, 8))


def tile_cond(t):
    return 0 if t < 4 else 1


class Buf:
    __slots__ = ("name", "writer", "readers", "excl")

    def __init__(self, name="", excl=False):
        self.name = name
        self.writer = None
        self.readers = []
        self.excl = excl


class Lane:
    __slots__ = ("sem", "n", "last")

    def __init__(self, sem):
        self.sem = sem
        self.n = 0
        self.last = None


class Ins:
    __slots__ = ("eng", "fn", "deps", "signal", "count", "lane", "lval", "idx")

    def __init__(self, eng, fn):
        self.eng = eng
        self.fn = fn
        self.deps = []
        self.signal = False
        self.count = 0
        self.lane = None
        self.lval = 0
        self.idx = 0


class Prog:
    ENG = ("pe", "act", "dve", "pool", "sp")
    NLANES = {"sp": 24, "pool": 12, "act": 4}

    def __init__(self, nc, es):
        self.nc = nc
        self.es = es
        self.eng = {"pe": nc.tensor, "act": nc.scalar, "dve": nc.vector,
                    "pool": nc.gpsimd, "sp": nc.sync}
        self.sem = {e: es.enter_context(nc.semaphore("s_" + e)) for e in self.ENG}
        self.cnt = {e: 0 for e in self.ENG}
        self.bar = es.enter_context(nc.semaphore("s_bar"))
        self.nbar = 0
        self.lanes = {q: [Lane(es.enter_context(nc.semaphore("l_%s%d" % (q, i)))) for i in range(n)]
                      for q, n in self.NLANES.items()}
        self.lane_rr = {q: 0 for q in self.NLANES}
        self.waited = {e: {} for e in self.ENG}
        self.streams = {e: [] for e in self.ENG}
        self.touched = []
        self.n_ins = 0

    def _add_dep(self, ins, p):
        if p is None or p is ins:
            return
        if p.lane is None and ins.lane is None and p.eng == ins.eng and p.eng == "pe":
            return
        deps = ins.deps
        if p.lane is None:
            for i, q in enumerate(deps):
                if q.lane is None and q.eng == p.eng:
                    if p.idx > q.idx:
                        deps[i] = p
                    return
        elif p in deps:
            return
        deps.append(p)

    def _deps(self, ins, reads, writes):
        if any(b.excl for b in reads):
            writes = list(writes) + [b for b in reads if b.excl and b not in writes]
            reads = [b for b in reads if not b.excl]
        same = lambda p: (p is not None and p.lane is None and ins.lane is None and p.eng == ins.eng)
        for b in reads:
            self._add_dep(ins, b.writer)
        for b in writes:
            if not same(b.writer):
                self._add_dep(ins, b.writer)
            for r in b.readers:
                if not same(r):
                    self._add_dep(ins, r)
        for b in reads:
            if not b.readers and b.writer is None:
                self.touched.append(b)
            b.readers.append(ins)
        for b in writes:
            if not b.readers and b.writer is None:
                self.touched.append(b)
            b.writer = ins
            b.readers = []

    def op(self, eng, fn, reads=(), writes=()):
        ins = Ins(eng, fn)
        ins.idx = len(self.streams[eng])
        self._deps(ins, reads, writes)
        self.streams[eng].append(ins)
        return ins

    def dma(self, eng, out, in_, reads=(), writes=(), **kw):
        E = self.eng[eng]
        ins = Ins(eng, lambda: E.dma_start(out=out, in_=in_, **kw))
        ins.idx = len(self.streams[eng])
        lanes = self.lanes[eng]
        lane = lanes[self.lane_rr[eng] % len(lanes)]
        self.lane_rr[eng] += 1
        ins.lane = lane
        lane.n += 1
        ins.lval = 16 * lane.n
        if lane.last is not None:
            ins.deps.append(lane.last)
        lane.last = ins
        self._deps(ins, reads, writes)
        self.streams[eng].append(ins)
        return ins

    def mm(self, out, lhsT, rhs, start, stop, reads=(), writes=(), skip=False):
        t = self.nc.tensor
        if skip:
            return self.op("pe", lambda: t.matmul(out, lhsT, rhs, start=start, stop=stop, skip_group_check=True),
                           reads, writes)
        return self.op("pe", lambda: t.matmul(out, lhsT, rhs, start=start, stop=stop), reads, writes)

    def tr(self, out, in_, ident, reads=(), writes=()):
        t = self.nc.tensor
        return self.op("pe", lambda: t.transpose(out, in_, ident), reads, writes)

    def v(self, eng, meth, *args, R=(), W=(), **kw):
        E = self.eng[eng]
        return self.op(eng, lambda: getattr(E, meth)(*args, **kw), R, W)

    def end_phase(self):
        ENG = self.ENG
        for e in ENG:
            for ins in self.streams[e]:
                for p in ins.deps:
                    if p.lane is None:
                        p.signal = True
            for ins in reversed(self.streams[e]):
                if ins.lane is None:
                    ins.signal = True
                    break
        for e in ENG:
            for ins in self.streams[e]:
                if ins.lane is None and ins.signal:
                    self.cnt[e] += 1
                    ins.count = self.cnt[e]
        for e in ENG:
            E = self.eng[e]
            wd = self.waited[e]
            for ins in self.streams[e]:
                for p in ins.deps:
                    if p.lane is not None:
                        sem, val = p.lane.sem, p.lval
                    else:
                        sem, val = self.sem[p.eng], p.count
                    if wd.get(sem.num, 0) >= val:
                        continue
                    wd[sem.num] = val
                    E.wait_ge(sem, val)
                bi = ins.fn()
                if ins.lane is not None:
                    bi.then_inc(ins.lane.sem, 16)
                elif ins.signal:
                    bi.then_inc(self.sem[e], 1)
                self.n_ins += 1
        sp = self.eng["sp"]
        wd = self.waited["sp"]
        all_lanes = [l for q in self.lanes.values() for l in q]
        for e in ENG:
            if e == "sp" or self.cnt[e] == 0:
                continue
            if wd.get(self.sem[e].num, 0) < self.cnt[e]:
                sp.wait_ge(self.sem[e], self.cnt[e])
        for l in all_lanes:
            if l.n and wd.get(l.sem.num, 0) < 16 * l.n:
                sp.wait_ge(l.sem, 16 * l.n)
        self.nbar += 1
        sp.sem_inc(self.bar, 1)
        for e in ENG:
            if e != "sp":
                self.eng[e].wait_ge(self.bar, self.nbar)
            for e2 in ENG:
                self.waited[e][self.sem[e2].num] = self.cnt[e2]
            for l in all_lanes:
                self.waited[e][l.sem.num] = 16 * l.n
        for l in all_lanes:
            l.last = None
        for b in self.touched:
            b.writer = None
            b.readers = []
        self.touched = []
        self.streams = {e: [] for e in ENG}


def _pool_tables():
    n = 1024
    out = np.zeros((4, 5, 128, 128), np.float32)
    t = np.arange(n)
    for wi, w in enumerate(POOL_SIZES):
        lo = np.clip(t - w // 2, 0, n)
        hi = np.clip(t - w // 2 + w, 0, n)
        cnt = (hi - lo).astype(np.float64)
        s = np.arange(n)[:, None]
        Pm = ((s >= lo[None, :]) & (s < hi[None, :])) / cnt[None, :] - np.eye(n)
        Pm = Pm.astype(np.float32)
        out[wi, 0] = Pm[0:128, 0:128]
        out[wi, 1] = Pm[128:256, 128:256]
        out[wi, 2] = Pm[896:1024, 896:1024]
        out[wi, 3] = Pm[0:128, 128:256]
        out[wi, 4] = Pm[256:384, 128:256]
    return out


def _chunks_for(j):
    if j < 2:
        return list(range(0, 4))
    if j > 5:
        return list(range(4, 8))
    return list(range(j - 2, j + 3))


def _mask_tables():
    tiles = []
    index = {}
    keymap = {}
    col = np.arange(64)
    cs = np.clip(col - 8, 0, 48)
    colok = (col[None, :] >= cs[:, None]) & (col[None, :] < cs[:, None] + 16)
    for j in range(8):
        for m in _chunks_for(j):
            mt = np.full((2, 64, 2, 64), NEG, np.float32)
            for rq in range(2):
                r = 2 * j + rq
                rs = min(max(r - 4, 0), 8)
                for rk in range(2):
                    kr = 2 * m + rk
                    if rs <= kr < rs + 8:
                        mt[rk, :, rq, :] = np.where(colok.T, 0.0, NEG)
            key = mt.tobytes()
            if key not in keymap:
                keymap[key] = len(tiles)
                tiles.append(mt.reshape(128, 128))
            index[(j, m)] = keymap[key]
    return np.stack(tiles), index


_MASKS, _MASK_IDX = _mask_tables()
_M_FULL, _M_A, _M_B = _MASK_IDX[(0, 0)], _MASK_IDX[(2, 0)], _MASK_IDX[(2, 4)]


def _mask01_table():
    full = (_MASKS[_M_FULL] == 0).astype(np.float32)
    t = np.stack([full] * 14)
    t[7 + (3 - 2)] = (_MASKS[_M_B] == 0)
    t[7 + (3 + 2)] = (_MASKS[_M_A] == 0)
    return t


def _segments():
    segs = {0: [], 1: []}
    for m in range(8):
        for qb in range(2):
            js = [j for j in range(8) if m in _chunks_for(j) and j // 4 == qb]
            if not js:
                continue
            assert js == list(range(js[0], js[-1] + 1))
            e2 = any(_MASK_IDX[(j, m)] != _M_FULL for j in js)
            for j in js:
                want = _M_FULL
                if e2 and m - j == -2:
                    want = _M_A
                if e2 and m - j == 2:
                    want = _M_B
                assert _MASK_IDX[(j, m)] == want, (j, m)
            segs[qb].append((m, js[0], len(js), int(e2)))
    return segs


_SEGS = _segments()


def _rel_bias_layout(rel_bias):
    rk = np.arange(2)[:, None, None, None]
    ck = np.arange(64)[None, :, None, None]
    rq = np.arange(2)[None, None, :, None]
    cq = np.arange(64)[None, None, None, :]
    out = np.empty((16, 7, 128, 128), np.float32)
    for i in range(7):
        dl = 3 - i
        dy = np.clip(2 * dl + rk - rq + 7, 0, 14)
        dx = np.clip(ck - cq + 15, 0, 30)
        dy, dx = np.broadcast_arrays(dy, dx)
        out[:, i] = rel_bias[:, dy, dx].reshape(16, 128, 128)
    return out


def build_nc(stop=None):
    nc = bass.Bass("TRN2", target_bir_lowering=False)

    def din(name, shape):
        return nc.dram_tensor(name, list(shape), F32, kind="ExternalInput").ap()

    def dout(name, shape):
        return nc.dram_tensor(name, list(shape), F32, kind="ExternalOutput").ap()

    xin = din("xin", [T, D])
    ck_d = din("ck", [16, 512, 64])
    cv_d = din("cv", [16, 512, 64])
    condT_d = din("condT", [128, 8, 2])
    w_ada_d = din("w_ada", [2, D, 6 * D])
    b_adaT_d = din("b_adaT", [2, 128, 48])
    ln1g_d = din("ln1_g", [2, D])
    ln1b_d = din("ln1_b", [2, D])
    ln2g_d = din("ln2_g", [2, D])
    ln2b_d = din("ln2_b", [2, D])
    pool_w_d = din("pool_w", [4, 256, 256])
    pool_scale_d = din("pool_scale", [1, D])
    w_qkv_d = din("w_qkv", [D, 3 * D])
    w_o_d = din("w_o", [D, D])
    rbt_d = din("rbt", [16, 7, 128, 128])
    w_router_d = din("w_router", [D, 16])
    b_router_d = din("b_router", [1, 16])
    w_gate_d = din("w_gate", [2, 16, D, 512])
    w_up_d = din("w_up", [2, 16, D, 512])
    w_down_d = din("w_down", [2, 16, 512, D])
    ptab_d = din("ptab", [4, 5, 128, 128])
    m01_d = din("m01", [14, 128, 128])

    y_d = dout("y", [T, D])
    nk_d = dout("nk", [2, 16, 256, 64])
    nv_d = dout("nv", [2, 16, 256, 64])
    dbg_d = dout("dbg", [128, 2, 48, 2]) if stop == "ada" else None

    with ExitStack() as es:
        P = Prog(nc, es)

        _uid = [0]

        def sbuf(scope, name, shape, dt=F32):
            _uid[0] += 1
            return scope.enter_context(nc.sbuf_tensor("sb%d_%s" % (_uid[0], name), list(shape), dt))

        x_sb = sbuf(es, "x_sb", [128, NT, D])
        hT = sbuf(es, "hT", [128, 8, T], BF16)
        ident_f = sbuf(es, "ident_f", [128, 128])
        ident_b = sbuf(es, "ident_b", [128, 128], BF16)
        ones_f = sbuf(es, "ones_f", [128, 128])
        modcol = sbuf(es, "modcol", [128, 2, 48, 2])
        cmb = sbuf(es, "cmb", [128, NT, 16])
        small = sbuf(es, "small", [128, 64])
        psum = [es.enter_context(nc.psum_tensor("ps%d" % i, [128, 512], F32)) for i in range(8)]
        PB = [Buf("ps%d" % i, excl=True) for i in range(8)]

        Bx = [Buf("x%d" % t) for t in range(NT)]
        BhT = [(Buf("hTd%d" % t), Buf("hTa%d" % t)) for t in range(NT)]
        Bconst = Buf("const")
        Bmod = Buf("modcol")
        Bcmb = [Buf("cmb%d" % t) for t in range(NT)]


        def alpha_col(layer, v, ch, c):
            return modcol[:, layer, v * 8 + ch, c:c + 1]

        with ExitStack() as ph:
            condT = sbuf(ph, "condT", [128, 8, 2])
            sT = sbuf(ph, "sT", [128, 8, 2])
            sTb = sbuf(ph, "sTb", [128, 8, 2], BF16)
            b_col = sbuf(ph, "b_col", [128, 2, 48])
            wa = [sbuf(ph, "wa%d" % i, [128, 8, 512], BF16) for i in range(3)]
            Bwa = [Buf("wa%d" % i) for i in range(3)]
            BsT = Buf("sT")
            Bbcol = Buf("bcol")

            P.v("pool", "memset", ident_f[:], 0.0, W=[Bconst])
            P.v("pool", "affine_select", ident_f[:], ident_f[:], [[-1, 128]], ALU.not_equal, 1.0,
                base=0, channel_multiplier=1, R=[Bconst], W=[Bconst])
            P.v("pool", "tensor_copy", ident_b[:], ident_f[:], R=[Bconst], W=[Bconst])
            P.v("pool", "memset", ones_f[:], 1.0, W=[Bconst])

            for t in range(NT):
                P.dma("sp", x_sb[:, t, :], xin[t * 128:(t + 1) * 128, :], writes=[Bx[t]])
            P.dma("sp", condT[:], condT_d[:, :, :], writes=[BsT])
            P.dma("sp", b_col[:], b_adaT_d.rearrange("l p c -> p l c"), writes=[Bbcol])
            P.v("act", "activation", sT[:], condT[:], AF.Silu, R=[BsT], W=[BsT])
            P.v("dve", "tensor_copy", sTb[:], sT[:], R=[BsT], W=[BsT])

            nblk = 0
            for layer in range(2):
                pc = psum[layer]
                for nb in range(12):
                    slot = nblk % 3
                    src = w_ada_d[layer].rearrange("(kc k) n -> k kc n", k=128)[:, :, nb * 512:(nb + 1) * 512]
                    P.dma("pool", wa[slot][:], src, writes=[Bwa[slot]])
                    for n4 in range(4):
                        ch = nb * 4 + n4
                        for kc in range(8):
                            P.mm(pc[:, ch * 2:ch * 2 + 2], wa[slot][:, kc, n4 * 128:(n4 + 1) * 128],
                                 sTb[:, kc, :], kc == 0, kc == 7,
                                 reads=[Bwa[slot], BsT], writes=[PB[layer]])
                    nblk += 1
                P.v("dve", "tensor_tensor", modcol[:, layer, :, :],
                    pc[:, 0:96].rearrange("p (c k) -> p c k", k=2),
                    b_col[:, layer, :].unsqueeze(2).to_broadcast([128, 48, 2]), ALU.add,
                    R=[PB[layer], Bbcol], W=[Bmod])
                for v in (1, 4):
                    P.v("dve", "tensor_scalar_add", modcol[:, layer, v * 8:(v + 1) * 8, :],
                        modcol[:, layer, v * 8:(v + 1) * 8, :], 1.0, R=[Bmod], W=[Bmod])
            if stop == "ada":
                P.dma("sp", dbg_d[:, :, :, :], modcol[:], reads=[Bmod])
            P.end_phase()

        def load_bc(dst, src_row, W):
            P.dma("sp", dst, src_row.partition_broadcast(128), writes=W)

        def make_bc(dst, layer, v, c, dgall, Bdg, banks, W):
            P.v("dve", "tensor_tensor", dgall[:], ident_f[:].unsqueeze(1).to_broadcast([128, 8, 128]),
                modcol[:, layer, v * 8:(v + 1) * 8, c:c + 1].to_broadcast([128, 8, 128]), ALU.mult,
                R=[Bconst, Bmod], W=[Bdg])
            for half in range(2):
                pb, Bp = banks[half]
                P.mm(pb[:], ones_f[:], dgall[:, half * 4:(half + 1) * 4, :].rearrange("p a b -> p (a b)"),
                     True, True, reads=[Bconst, Bdg], writes=[Bp])
                P.v("act", "copy", dst[:, half * 512:(half + 1) * 512], pb[:], R=[Bp], W=W)

        Bsmall = [Buf("mv%d" % i) for i in range(4)]

        def ln_group(tiles, ys, Bys, g_bc, b_bc, Bvec, next_mod, banks, gmul_eng):
            n = len(tiles)
            sts = [small[:, 16 * i:16 * i + 12].rearrange("p (a b) -> p a b", b=6) for i in range(n)]
            mvv = [small[:, 16 * i + 12:16 * i + 16] for i in range(n)]
            add_eng = "pool" if gmul_eng == "dve" else "dve"
            for i in range(n):
                P.v("dve", "bn_stats", sts[i][:, 0, :], ys[i][:, 0:512], R=[Bys[i]], W=[Bsmall[i]])
                P.v("dve", "bn_stats", sts[i][:, 1, :], ys[i][:, 512:1024], R=[Bys[i]], W=[Bsmall[i]])
                P.v("dve", "bn_aggr", mvv[i][:, 0:2], sts[i], R=[Bsmall[i]], W=[Bsmall[i]])
            for i in range(n):
                P.v("act", "activation", mvv[i][:, 2:3], mvv[i][:, 1:2], AF.Sqrt, bias=LN_EPS,
                    R=[Bsmall[i]], W=[Bsmall[i]])
            for i in range(n):
                P.v("dve", "reciprocal", mvv[i][:, 2:3], mvv[i][:, 2:3], R=[Bsmall[i]], W=[Bsmall[i]])
                P.v("dve", "tensor_scalar", mvv[i][:, 3:4], mvv[i][:, 0:1], mvv[i][:, 2:3], -1.0,
                    op0=ALU.mult, op1=ALU.mult, R=[Bsmall[i]], W=[Bsmall[i]])
            for i in range(n):
                P.v("act", "activation", ys[i], ys[i], AF.Identity, scale=mvv[i][:, 2:3], bias=mvv[i][:, 3:4],
                    R=[Bys[i], Bsmall[i]], W=[Bys[i]])
            for i in range(n):
                P.v(gmul_eng, "tensor_tensor", ys[i], ys[i], g_bc, ALU.mult, R=[Bys[i]] + list(Bvec), W=[Bys[i]])
            for i, t in enumerate(tiles):
                P.v("pool", "tensor_tensor", x_sb[:, t, :], ys[i], b_bc, ALU.add, R=[Bys[i]] + list(Bvec), W=[Bx[t]])
            if next_mod is None:
                return
            layer, vA, vB = next_mod
            nb = len(banks) // 2
            for i, t in enumerate(tiles):
                c = tile_cond(t)
                pair = banks[2 * (i % nb):2 * (i % nb) + 2]
                for half in range(2):
                    pb, Bp = pair[half]
                    for q in range(4):
                        kc = half * 4 + q
                        P.tr(pb[:, q * 128:(q + 1) * 128], x_sb[:, t, kc * 128:(kc + 1) * 128], ident_f[:],
                             reads=[Bx[t], Bconst], writes=[Bp])
                for half in range(2):
                    pb, Bp = pair[half]
                    for q in range(4):
                        kc = half * 4 + q
                        if half == 0:
                            P.v("dve", "tensor_scalar", hT[:, kc, t * 128:(t + 1) * 128], pb[:, q * 128:(q + 1) * 128],
                                alpha_col(layer, vA, kc, c), alpha_col(layer, vB, kc, c),
                                op0=ALU.mult, op1=ALU.add, R=[Bp, Bmod], W=[BhT[t][0]])
                        else:
                            P.v("act", "activation", hT[:, kc, t * 128:(t + 1) * 128], pb[:, q * 128:(q + 1) * 128],
                                AF.Identity, scale=alpha_col(layer, vA, kc, c), bias=alpha_col(layer, vB, kc, c),
                                R=[Bp, Bmod], W=[BhT[t][1]])

        PBK = [(psum[i], PB[i]) for i in range(8)]

        def store_x(label):
            for t in range(NT):
                P.dma("sp", y_d[t * 128:(t + 1) * 128, :], x_sb[:, t, :], reads=[Bx[t]])

        if stop == "ada":
            store_x("ada")
            P.end_phase()
            return nc

        with ExitStack() as ph:
            bc = [sbuf(ph, "bc%d" % i, [128, D]) for i in range(8)]
            Bbc = [Buf("bc%d" % i) for i in range(8)]
            psc = sbuf(ph, "psc", [128, D])
            Bpsc = Buf("psc")
            dgall = sbuf(ph, "dgall", [128, 8, 128])
            Bdg = Buf("dgall")
            h_tok = sbuf(ph, "h_tok", [128, NT, D])
            Bh = [Buf("h%d" % t) for t in range(NT)]
            ptab = sbuf(ph, "ptab", [128, 4, 5, 128])
            Bptab = Buf("ptab")
            pw = sbuf(ph, "pw", [128, 4, 2, 256], BF16)
            Bpw = Buf("pw")
            pTt = [sbuf(ph, "pTt%d" % i, [128, 8, 128], BF16) for i in range(2)]
            BpTt = [Buf("pTt0"), Buf("pTt1")]
            ybuf = [sbuf(ph, "ybuf%d" % i, [128, D]) for i in range(4)]
            By = [Buf("y%d" % i) for i in range(4)]

            P.dma("sp", ptab[:], ptab_d.rearrange("w k s t -> s w k t"), writes=[Bptab])
            P.dma("pool", pw[:], pool_w_d.rearrange("g (i c) d -> c g i d", c=128), writes=[Bpw])
            load_bc(psc[:], pool_scale_d[0:1, :], [Bpsc])
            load_bc(bc[6][:], ln1g_d[0:1, :], [Bbc[6]])
            load_bc(bc[7][:], ln1b_d[0:1, :], [Bbc[7]])
            for c in range(2):
                make_bc(bc[2 * c], 0, 1, c, dgall, Bdg, PBK[0:2], [Bbc[2 * c]])
                make_bc(bc[2 * c + 1], 0, 0, c, dgall, Bdg, PBK[2:4], [Bbc[2 * c + 1]])
                make_bc(bc[4 + c], 0, 2, c, dgall, Bdg, PBK[4:6], [Bbc[4 + c]])
                P.v("pool", "tensor_tensor", bc[4 + c][:], bc[4 + c][:], psc[:], ALU.mult,
                    R=[Bbc[4 + c], Bpsc], W=[Bbc[4 + c]])
            for t in range(NT):
                c = tile_cond(t)
                P.v("dve", "tensor_tensor", h_tok[:, t, :], x_sb[:, t, :], bc[2 * c][:], ALU.mult,
                    R=[Bx[t], Bbc[2 * c]], W=[Bh[t]])
                P.v("pool", "tensor_tensor", h_tok[:, t, :], h_tok[:, t, :], bc[2 * c + 1][:], ALU.add,
                    R=[Bh[t], Bbc[2 * c + 1]], W=[Bh[t]])
            seq_of = {}
            for (t0, ntl) in SEQS:
                for q in range(ntl):
                    seq_of[t0 + q] = (q, ntl)
            for grp in range(3):
                tiles = list(range(4 * grp, 4 * grp + 4))
                for i, t in enumerate(tiles):
                    q, ntl = seq_of[t]
                    c = tile_cond(t)
                    k = i % 2
                    (pA, BA), (pBk, BB) = PBK[2 * k], PBK[2 * k + 1]
                    kind = 0 if q == 0 else (2 if q == ntl - 1 else 1)
                    srcs = [(t, kind)]
                    if q > 0:
                        srcs.append((t - 1, 3))
                    if q < ntl - 1:
                        srcs.append((t + 1, 4))
                    for kc in range(8):
                        wi = kc // 2
                        bank, Bb = (pA, BA) if kc < 4 else (pBk, BB)
                        dst = bank[:, (kc % 4) * 128:(kc % 4 + 1) * 128]
                        for si, (ts, kd) in enumerate(srcs):
                            P.mm(dst, h_tok[:, ts, kc * 128:(kc + 1) * 128], ptab[:, wi, kd, :],
                                 si == 0, si == len(srcs) - 1, reads=[Bh[ts], Bptab], writes=[Bb])
                    P.v("act", "copy", pTt[k][:, 0:4, :], pA[:].rearrange("p (a b) -> p a b", b=128),
                        R=[BA], W=[BpTt[k]])
                    P.v("act", "copy", pTt[k][:, 4:8, :], pBk[:].rearrange("p (a b) -> p a b", b=128),
                        R=[BB], W=[BpTt[k]])
                    for half in range(2):
                        bank, Bb = (pA, BA) if half == 0 else (pBk, BB)
                        for gg in range(2):
                            g = half * 2 + gg
                            for i2 in range(2):
                                P.mm(bank[:, gg * 256:(gg + 1) * 256], pTt[k][:, 2 * g + i2, :], pw[:, g, i2, :],
                                     i2 == 0, i2 == 1, reads=[BpTt[k], Bpw], writes=[Bb])
                        P.v("dve", "tensor_tensor", ybuf[i][:, half * 512:(half + 1) * 512], bank[:],
                            bc[4 + c][:, half * 512:(half + 1) * 512], ALU.mult, R=[Bb, Bbc[4 + c]], W=[By[i]])
                    P.v("dve", "scalar_tensor_tensor", ybuf[i][:], x_sb[:, t, :], ALPHA, ybuf[i][:],
                        op0=ALU.mult, op1=ALU.add, R=[Bx[t], By[i]], W=[By[i]])
                ln_group(tiles, [ybuf[i][:] for i in range(4)], By, bc[6][:], bc[7][:], [Bbc[6], Bbc[7]],
                         (0, 4, 3), PBK[4:8], "pool")
            if stop == "l0mix":
                store_x("l0mix")
            P.end_phase()
        if stop == "l0mix":
            return nc


        def moe_phase(layer, next_mod, final):
            with ExitStack() as ph:
                bc = [sbuf(ph, "bc%d" % i, [128, D]) for i in range(4)]
                Bbc = [Buf("mbc%d" % i) for i in range(4)]
                dgall = sbuf(ph, "dgall", [128, 8, 128])
                Bdg = Buf("dgall")
                wg = [sbuf(ph, "wg%d" % i, [128, 8, 512], BF16) for i in range(2)]
                wu = [sbuf(ph, "wu%d" % i, [128, 8, 512], BF16) for i in range(2)]
                wd = [sbuf(ph, "wd%d" % i, [128, 4, D], BF16) for i in range(2)]
                Bwg = [Buf("wg0"), Buf("wg1")]
                Bwu = [Buf("wu0"), Buf("wu1")]
                Bwd = [Buf("wd0"), Buf("wd1")]
                heT = [sbuf(ph, "heT%d" % i, [128, 4, 512], BF16) for i in range(2)]
                Bhe = [Buf("he0"), Buf("he1")]
                sg = [sbuf(ph, "sg%d" % i, [128, 512], BF16) for i in range(2)]
                Bsg = [Buf("sg0"), Buf("sg1")]
                yacc = sbuf(ph, "yacc", [128, NT, D])
                Byacc = [Buf("yacc%d" % t) for t in range(NT)]
                wr_b = sbuf(ph, "wr_b", [128, 8, 16], BF16)
                brt = sbuf(ph, "brt", [128, 16])
                Bwr = Buf("wr")
                Bbrt = Buf("brt")
                rt = sbuf(ph, "rt", [128, 1472])
                Brt = Buf("rt")
                G = [psum[0], psum[1]]
                U = [psum[2], psum[3]]
                Dn = [psum[4], psum[5], psum[6], psum[7]]
                BG, BU, BD = [PB[0], PB[1]], [PB[2], PB[3]], [PB[4], PB[5], PB[6], PB[7]]

                def load_expert(e):
                    sl = e % 2
                    P.dma("pool", wg[sl][:], w_gate_d[layer, e].rearrange("(kc k) f -> k kc f", k=128), writes=[Bwg[sl]])
                    P.dma("pool", wu[sl][:], w_up_d[layer, e].rearrange("(kc k) f -> k kc f", k=128), writes=[Bwu[sl]])
                    P.dma("pool", wd[sl][:], w_down_d[layer, e].rearrange("(fc f) d -> f fc d", f=128), writes=[Bwd[sl]])

                P.dma("pool", wr_b[:], w_router_d.rearrange("(kc k) e -> k kc e", k=128), writes=[Bwr])
                load_expert(0)
                load_expert(1)
                P.dma("sp", brt[:], b_router_d[0:1, :].partition_broadcast(128), writes=[Bbrt])
                load_bc(bc[2][:], ln2g_d[layer:layer + 1, :], [Bbc[2]])
                load_bc(bc[3][:], ln2b_d[layer:layer + 1, :], [Bbc[3]])
                for c in range(2):
                    make_bc(bc[c], layer, 5, c, dgall, Bdg, PBK[6:8], [Bbc[c]])

                lg = psum[4]
                for t in range(NT):
                    for kc in range(8):
                        P.mm(lg[:, t * 16:(t + 1) * 16], hT[:, kc, t * 128:(t + 1) * 128], wr_b[:, kc, :],
                             kc == 0, kc == 7, reads=[BhT[t][0], BhT[t][1], Bwr], writes=[PB[4]])
                o = [0]

                def carve(nel):
                    v_ = rt[:, o[0]:o[0] + nel]
                    o[0] += nel
                    return v_
                s_sb, sel, t6, gs, gm = carve(192), carve(192), carve(288), carve(48), carve(12)
                gmask, msel, top8, emask, wsel, wsum = carve(48), carve(192), carve(96), carve(192), carve(192), carve(12)
                Rr, Wr = [Brt], [Brt]
                P.v("act", "activation", s_sb, lg[:, 0:192], AF.Sigmoid, R=[PB[4]], W=Wr)
                P.v("dve", "tensor_tensor", sel.rearrange("p (t e) -> p t e", e=16),
                    s_sb.rearrange("p (t e) -> p t e", e=16), brt[:].unsqueeze(1).to_broadcast([128, NT, 16]),
                    ALU.add, R=Rr + [Bbrt], W=Wr)
                sel4 = sel.rearrange("p (g e) -> p g e", e=4)
                t6v = t6.rearrange("p (g e) -> p g e", e=6)
                P.v("dve", "tensor_tensor", t6v[:, :, 0:3], sel4[:, :, 0:3], sel4[:, :, 1:4], ALU.add, R=Rr, W=Wr)
                P.v("dve", "tensor_tensor", t6v[:, :, 3:5], sel4[:, :, 0:2], sel4[:, :, 2:4], ALU.add, R=Rr, W=Wr)
                P.v("dve", "tensor_tensor", t6v[:, :, 5:6], sel4[:, :, 0:1], sel4[:, :, 3:4], ALU.add, R=Rr, W=Wr)
                P.v("dve", "tensor_reduce", gs, t6v, AX.X, ALU.max, R=Rr, W=Wr)
                gs3 = gs.rearrange("p (t g) -> p t g", g=4)
                P.v("dve", "tensor_reduce", gm, gs3, AX.X, ALU.max, R=Rr, W=Wr)
                P.v("dve", "tensor_tensor", gmask.rearrange("p (t g) -> p t g", g=4), gs3,
                    gm.unsqueeze(2).to_broadcast([128, NT, 4]), ALU.is_equal, R=Rr, W=Wr)
                P.v("dve", "scalar_tensor_tensor", msel.rearrange("p (g e) -> p g e", e=4), sel4, 2.0,
                    gmask.unsqueeze(2).to_broadcast([128, 48, 4]), op0=ALU.add, op1=ALU.mult, R=Rr, W=Wr)
                top83 = top8.rearrange("p (t e) -> p t e", e=8)
                for t in range(NT):
                    P.v("dve", "max", top83[:, t, :], msel[:, t * 16:(t + 1) * 16], R=Rr, W=Wr)
                P.v("dve", "tensor_tensor", emask.rearrange("p (t e) -> p t e", e=16),
                    msel.rearrange("p (t e) -> p t e", e=16), top83[:, :, 1:2].to_broadcast([128, NT, 16]),
                    ALU.is_ge, R=Rr, W=Wr)
                P.v("dve", "tensor_tensor", wsel, s_sb, emask, ALU.mult, R=Rr, W=Wr)
                wsel3 = wsel.rearrange("p (t e) -> p t e", e=16)
                P.v("dve", "tensor_reduce", wsum, wsel3, AX.X, ALU.add, R=Rr, W=Wr)
                P.v("dve", "reciprocal", wsum, wsum, R=Rr, W=Wr)
                P.v("dve", "tensor_tensor", cmb[:], wsel3, wsum.unsqueeze(2).to_broadcast([128, NT, 16]),
                    ALU.mult, R=Rr, W=Bcmb)

                blocks = [(e, b) for e in range(16) for b in range(3)]
                hT_bufs = [[x for t in range(4 * b, 4 * b + 4) for x in BhT[t]] for b in range(3)]

                def GU(k):
                    e, b = blocks[k]
                    sl = e % 2
                    for fc in range(4):
                        gi = (k * 4 + fc) % 2
                        for kc in range(8):
                            P.mm(G[gi][:], wg[sl][:, kc, fc * 128:(fc + 1) * 128], hT[:, kc, b * 512:(b + 1) * 512],
                                 kc == 0, kc == 7, reads=[Bwg[sl]] + hT_bufs[b], writes=[BG[gi]])
                        for kc in range(8):
                            P.mm(U[gi][:], wu[sl][:, kc, fc * 128:(fc + 1) * 128], hT[:, kc, b * 512:(b + 1) * 512],
                                 kc == 0, kc == 7, reads=[Bwu[sl]] + hT_bufs[b], writes=[BU[gi]])
                        P.v("act", "activation", sg[gi][:], G[gi][:], AF.Silu, R=[BG[gi]], W=[Bsg[gi]])
                        P.v("dve", "tensor_tensor", heT[k % 2][:, fc, :], U[gi][:], sg[gi][:], ALU.mult,
                            R=[BU[gi], Bsg[gi]], W=[Bhe[k % 2]])

                def DNp(k):
                    e, b = blocks[k]
                    sl = e % 2
                    for tt in range(4):
                        t = 4 * b + tt
                        for half in range(2):
                            di = (k * 8 + tt * 2 + half) % 4
                            for fc in range(4):
                                P.mm(Dn[di][:], heT[k % 2][:, fc, tt * 128:(tt + 1) * 128],
                                     wd[sl][:, fc, half * 512:(half + 1) * 512], fc == 0, fc == 3,
                                     reads=[Bhe[k % 2], Bwd[sl]], writes=[BD[di]])
                            dst = yacc[:, t, half * 512:(half + 1) * 512]
                            if e == 0:
                                P.v("dve", "tensor_scalar", dst, Dn[di][:], cmb[:, t, e:e + 1], None, op0=ALU.mult,
                                    R=[BD[di], Bcmb[t]], W=[Byacc[t]])
                            else:
                                P.v("dve", "scalar_tensor_tensor", dst, Dn[di][:], cmb[:, t, e:e + 1], dst,
                                    op0=ALU.mult, op1=ALU.add, R=[BD[di], Bcmb[t], Byacc[t]], W=[Byacc[t]])

                GU(0)
                for k in range(len(blocks)):
                    if k + 1 < len(blocks):
                        GU(k + 1)
                    DNp(k)
                    e_, b_ = blocks[k]
                    if b_ == 2 and e_ + 2 < 16:
                        load_expert(e_ + 2)

                def pre(grp):
                    for t in range(4 * grp, 4 * grp + 4):
                        c = tile_cond(t)
                        ya = yacc[:, t, :]
                        P.v("pool", "tensor_tensor", ya, ya, bc[c][:], ALU.mult, R=[Byacc[t], Bbc[c]], W=[Byacc[t]])
                    for t in range(4 * grp, 4 * grp + 4):
                        ya = yacc[:, t, :]
                        P.v("dve", "scalar_tensor_tensor", ya, x_sb[:, t, :], ALPHA, ya, op0=ALU.mult, op1=ALU.add,
                            R=[Bx[t], Byacc[t]], W=[Byacc[t]])

                pre(0)
                for grp in range(3):
                    if grp + 1 < 3:
                        pre(grp + 1)
                    tiles = list(range(4 * grp, 4 * grp + 4))
                    ln_group(tiles, [yacc[:, t, :] for t in tiles], [Byacc[t] for t in tiles], bc[2][:], bc[3][:],
                             [Bbc[2], Bbc[3]], next_mod, PBK, "dve")
                    if final:
                        for t in tiles:
                            P.dma("sp", y_d[t * 128:(t + 1) * 128, :], x_sb[:, t, :], reads=[Bx[t]])
                P.end_phase()

        moe_phase(0, (1, 1, 0), False)
        if stop == "l0moe":
            store_x("l0moe")
            P.end_phase()
            return nc


        with ExitStack() as pa:
            QT_s = sbuf(pa, "QT_s", [128, 8, 1024], BF16)
            KT_s = sbuf(pa, "KT_s", [128, 8, 1024], BF16)
            Vp_s = sbuf(pa, "Vp_s", [128, 8, 8, 192], BF16)
            O_CTXV, O_M01, O_RBB, O_E, O_ONES, RG_N = 4096, 10240, 12032, 13824, 19200, 19392
            Rg = sbuf(pa, "Rg", [128, RG_N], BF16)
            QT_p = Rg[:, 0:4096].rearrange("p (c t) -> p c t", c=8)
            KT_p = Rg[:, 4096:8192].rearrange("p (c t) -> p c t", c=8)
            Vp_p = Rg[:, 8192:14336].rearrange("p (m a e) -> p m a e", m=4, a=8)
            ctxKT = Rg[:, 0:4096].rearrange("p (c t) -> p c t", c=8)
            ctxVp = Rg[:, O_CTXV:O_M01].rearrange("p (m a e) -> p m a e", m=4, a=8)
            m01 = Rg[:, O_M01:O_RBB].rearrange("p (m q) -> p m q", q=128)
            rbb = Rg[:, O_RBB:O_E].rearrange("p (b d q) -> p b d q", b=2, d=7)
            Etab = Rg[:, O_E:O_ONES].rearrange("p (b d q) -> p b d q", b=3, d=14)
            onesp = Rg[:, O_ONES:RG_N]
            BQs, BKs, BVs = Buf("QTs"), Buf("KTs"), Buf("Vs")
            BQp, BKp, BVp = Buf("QTp"), Buf("KTp"), Buf("Vp")
            Bones = Buf("onesp")

            with ExitStack() as ph:
                wb = [sbuf(ph, "wqkv%d" % i, [128, 8, 512], BF16) for i in range(2)]
                Bwb = [Buf("wb0"), Buf("wb1")]
                stg = [sbuf(ph, "stg%d" % i, [128, 512]) for i in range(2)]
                Bstg = [Buf("stg0"), Buf("stg1")]
                P.v("pool", "memset", Vp_s[:, :, :, 64:128], 0.0, W=[BVs])
                P.v("pool", "memset", Vp_p[:, :, :, 64:128], 0.0, W=[BVp])
                P.v("pool", "memset", onesp[:, 0:64], 1.0, W=[Bones])
                P.v("pool", "memset", onesp[:, 64:128], 0.0, W=[Bones])
                P.v("pool", "memset", onesp[:, 128:192], 1.0, W=[Bones])
                nb = [0]
                ns = [0]

                def bank():
                    i = nb[0] % 8
                    nb[0] += 1
                    return psum[i], PB[i], ("dve" if i % 2 == 0 else "act")

                def evac(eng, dst, src, Bsrc, W, mul=None):
                    if eng == "dve":
                        if mul is None:
                            P.v("dve", "tensor_copy", dst, src, R=[Bsrc], W=W)
                        else:
                            P.v("dve", "tensor_scalar_mul", dst, src, mul, R=[Bsrc], W=W)
                    else:
                        if mul is None:
                            P.v("act", "copy", dst, src, R=[Bsrc], W=W)
                        else:
                            P.v("act", "mul", dst, src, mul, R=[Bsrc], W=W)

                hTb = lambda tiles: [x for t in tiles for x in BhT[t]]
                for cb in range(6):
                    sl = cb % 2
                    P.dma("pool", wb[sl][:], w_qkv_d.rearrange("(kc k) n -> k kc n", k=128)[:, :, cb * 512:(cb + 1) * 512],
                          writes=[Bwb[sl]])
                    if cb < 4:
                        isq = cb < 2
                        for i in range(4):
                            a = (cb % 2) * 4 + i
                            for tb in range(3):
                                pb, Bp, eng = bank()
                                for kc in range(8):
                                    P.mm(pb[:], wb[sl][:, kc, i * 128:(i + 1) * 128], hT[:, kc, tb * 512:(tb + 1) * 512],
                                         kc == 0, kc == 7, reads=[Bwb[sl]] + hTb(range(4 * tb, 4 * tb + 4)), writes=[Bp])
                                if tb == 0:
                                    dst = (QT_p if isq else KT_p)[:, a, :]
                                    W = [BQp if isq else BKp]
                                else:
                                    dst = (QT_s if isq else KT_s)[:, a, (tb - 1) * 512:tb * 512]
                                    W = [BQs if isq else BKs]
                                evac(eng, dst, pb[:], Bp, W, mul=(0.125 if isq else None))
                    if cb in (2, 3, 4, 5):
                        isv = cb >= 4
                        hh = cb % 2
                        for t in (range(NT) if isv else range(4)):
                            pb, Bp, eng = bank()
                            for kc in range(8):
                                P.mm(pb[:], hT[:, kc, t * 128:(t + 1) * 128], wb[sl][:, kc, :], kc == 0, kc == 7,
                                     reads=[Bwb[sl]] + hTb([t]), writes=[Bp])
                            if isv:
                                src4 = pb[:].rearrange("p (a two d) -> p a two d", two=2, d=64)
                                for par in range(2):
                                    if t < 4:
                                        evac(eng, Vp_p[:, t, hh * 4:(hh + 1) * 4, par * 128:par * 128 + 64],
                                             src4[:, :, par, :], Bp, [BVp])
                                    else:
                                        evac(eng, Vp_s[:, t - 4, hh * 4:(hh + 1) * 4, par * 128:par * 128 + 64],
                                             src4[:, :, par, :], Bp, [BVs])
                            if t < 4:
                                k = ns[0] % 2
                                ns[0] += 1
                                evac("dve" if eng == "act" else "act", stg[k][:], pb[:], Bp, [Bstg[k]])
                                dd = nv_d if isv else nk_d
                                s_, l0 = t // 2, (t % 2) * 128
                                P.dma("sp", dd[s_, hh * 8:(hh + 1) * 8, l0:l0 + 128, :].rearrange("h l d -> l h d"),
                                      stg[k][:].rearrange("p (h d) -> p h d", d=64), reads=[Bstg[k]])
                P.end_phase()

            with ExitStack() as pb_:
                OT_all = sbuf(pb_, "OT_all", [128, 8, 1024], BF16)
                BOT = [Buf("OT%d" % a) for a in range(8)]
                wo = sbuf(pb_, "wo", [128, 8, D], BF16)
                Bwo = Buf("wo")
                By = [Buf("ay%d" % i) for i in range(4)]
                ST = [PBK[i] for i in range(4)]
                ACC = [(PBK[4], PBK[5]), (PBK[6], PBK[7])]
                cn = {"st": 0, "pt": 0, "em": 0}

                def heads_scope(scope):
                    PT = [sbuf(scope, "PT%d" % i, [128, 512], BF16) for i in range(4)]
                    BPT = [Buf("PT%d" % i) for i in range(4)]
                    rS = sbuf(scope, "rS", [128, 512])
                    BrS = Buf("rS")
                    return PT, BPT, rS, BrS

                def segment(PT, BPT, Kst, Qmv, nq, Bk, Bq, esl, BE_, Vblk, BV_, ones_blk, acc, c0):
                    st, Bst = ST[cn["st"] % 4]
                    cn["st"] += 1
                    k = cn["pt"] % 4
                    cn["pt"] += 1
                    P.mm(st[:, 0:nq], Kst, Qmv, True, True, reads=[Bk, Bq], writes=[Bst])
                    P.v("act", "activation", PT[k][:, 0:nq], st[:, 0:nq], AF.Exp, R=[Bst], W=[BPT[k]])
                    if esl is not None:
                        eng = "dve" if cn["em"] % 2 == 0 else "pool"
                        cn["em"] += 1
                        P.v(eng, "tensor_tensor", PT[k][:, 0:nq], PT[k][:, 0:nq], esl, ALU.mult,
                            R=[BPT[k], BE_], W=[BPT[k]])
                    (pO, BO_), (pS, BS_) = acc
                    P.mm(pO[:, c0:c0 + nq], Vblk, PT[k][:, 0:nq], False, False, reads=[BPT[k], BV_], writes=[BO_], skip=True)
                    P.mm(pS[:, c0:c0 + nq], ones_blk, PT[k][:, 0:nq], False, False, reads=[BPT[k], Bones], writes=[BS_],
                         skip=True)

                def zero_acc(acc, ncols):
                    (pO, BO_), (pS, BS_) = acc
                    P.v("dve", "memset", pO[:, 0:ncols], 0.0, W=[BO_])
                    P.v("dve", "memset", pS[:, 0:ncols], 0.0, W=[BS_])

                def normalize(acc, ncols, rS, BrS, dst, Wd):
                    (pO, BO_), (pS, BS_) = acc
                    P.v("dve", "reciprocal", rS[:, 0:ncols], pS[:, 0:ncols], R=[BS_], W=[BrS])
                    P.v("dve", "tensor_tensor", dst, pO[:, 0:ncols], rS[:, 0:ncols], ALU.mult, R=[BO_, BrS], W=Wd)

                def proj_group(tiles, tcols, ybufs, deadW, bcv, Bbcv):
                    for i, t in enumerate(tiles):
                        tc = tcols[i]
                        for half in range(2):
                            pb, Bp = PBK[2 * (i % 2) + half]
                            for kc in range(8):
                                P.mm(pb[:], OT_all[:, kc, tc:tc + 128], wo[:, kc, half * 512:(half + 1) * 512],
                                     kc == 0, kc == 7, reads=[BOT[kc], Bwo], writes=[Bp])
                            P.v("dve", "tensor_tensor", ybufs[i][:, half * 512:(half + 1) * 512], pb[:],
                                bcv[0][:, half * 512:(half + 1) * 512], ALU.mult, R=[Bp, Bbcv[0]],
                                W=[By[i]] + list(deadW))
                        P.v("dve", "scalar_tensor_tensor", ybufs[i], x_sb[:, t, :], ALPHA, ybufs[i],
                            op0=ALU.mult, op1=ALU.add, R=[Bx[t], By[i]], W=[By[i]])
                    ln_group(tiles, ybufs, By, bcv[1], bcv[2], [Bbcv[1], Bbcv[2]], (1, 4, 3), PBK[4:8], "pool")

                def prep_vectors(cidx, bcv, Bbcv, dgall_v, Bdg_v, deadW):
                    load_bc(bcv[1], ln1g_d[1:2, :], [Bbcv[1]] + list(deadW))
                    load_bc(bcv[2], ln1b_d[1:2, :], [Bbcv[2]] + list(deadW))
                    make_bc(bcv[0], 1, 2, cidx, dgall_v, Bdg_v, PBK[6:8], [Bbcv[0]] + list(deadW))

                with ExitStack() as ph:
                    PT, BPT, rS, BrS = heads_scope(ph)
                    P.dma("pool", wo[:], w_o_d.rearrange("(kc k) n -> k kc n", k=128), writes=[Bwo])
                    for a in range(8):
                        for s_ in range(2):
                            acc = ACC[(2 * a + s_) % 2]
                            zero_acc(acc, 256)
                            for rho in range(2):
                                pr = slice(64 * rho, 64 * rho + 64)
                                vc = slice(64 * rho, 64 * rho + 128)
                                for c in range(2):
                                    k0 = s_ * 256 + c * 128
                                    segment(PT, BPT, KT_p[pr, a, k0:k0 + 128], QT_p[pr, a, s_ * 256:(s_ + 1) * 256], 256,
                                            BKp, BQp, None, None, Vp_p[:, 2 * s_ + c, a, vc], BVp, onesp[:, vc], acc, 0)
                            normalize(acc, 256, rS, BrS, OT_all[:, a, s_ * 256:(s_ + 1) * 256], [BOT[a]])
                    P.end_phase()

                yb_p = [Rg[:, 2048 * i:2048 * (i + 1)].bitcast(F32) for i in range(4)]
                bc_p = [Rg[:, 8192 + 2048 * i:8192 + 2048 * (i + 1)].bitcast(F32) for i in range(3)]
                dg_p = Rg[:, 14336:16384].bitcast(F32).rearrange("p (a b) -> p a b", b=128)
                Bbc_p = [Buf("pbc%d" % i) for i in range(3)]
                prep_vectors(0, bc_p, Bbc_p, dg_p, Buf("pdg"), [])
                proj_group([0, 1, 2, 3], [0, 128, 256, 384], yb_p, [], bc_p, Bbc_p)
                if stop == "l1mixp":
                    store_x("l1mixp")
                P.end_phase()
                if stop == "l1mixp":
                    return nc

                with ExitStack() as ph:
                    PT, BPT, rS, BrS = heads_scope(ph)
                    ck_tok = OT_all[:, 0:4, :].rearrange("p c (h d) -> p c h d", d=64)
                    Bck = Buf("ck_tok")
                    BctxK, BctxV, Bm01 = Buf("ctxK"), Buf("ctxV"), Buf("m01")
                    Brbb = [Buf("rbb0"), Buf("rbb1")]
                    BE = [Buf("E0"), Buf("E1"), Buf("E2")]
                    for c in range(4):
                        P.dma("pool", ck_tok[:, c, :, :], ck_d[:, c * 128:(c + 1) * 128, :].rearrange("h l d -> l h d"),
                              writes=[Bck])
                    P.dma("pool", m01[:], m01_d.rearrange("m k q -> k m q"), writes=[Bm01])
                    P.v("pool", "memset", ctxVp[:, :, :, 64:128], 0.0, W=[BctxV])
                    cv4 = cv_d.rearrange("(a two) l d -> two l a d", two=2)
                    for c in range(4):
                        for par in range(2):
                            P.dma("pool", ctxVp[:, c, :, par * 128:par * 128 + 64], cv4[par, c * 128:(c + 1) * 128, :, :],
                                  writes=[BctxV])

                    def build_E(h):
                        hb, eb = h % 2, h % 3
                        P.dma("pool", rbb[:, hb, :, :], rbt_d[h].rearrange("d k q -> k d q"), writes=[Brbb[hb]])
                        P.v("act", "activation", rbb[:, hb, :, :], rbb[:, hb, :, :], AF.Exp, R=[Brbb[hb]], W=[Brbb[hb]])
                        for s2 in range(2):
                            P.v("pool", "tensor_tensor", Etab[:, eb, s2 * 7:(s2 + 1) * 7, :], rbb[:, hb, :, :],
                                m01[:, s2 * 7:(s2 + 1) * 7, :], ALU.mult, R=[Brbb[hb], Bm01], W=[BE[eb]])

                    for c in range(4):
                        pTb = psum[c % 2][:].bitcast(BF16)
                        for a in range(8):
                            P.tr(pTb[:, a * 128:(a + 1) * 128], ck_tok[:, c, 2 * a:2 * a + 2, :].rearrange("p h d -> p (h d)"),
                                 ident_b[:], reads=[Bck, Bconst], writes=[PB[c % 2]])
                        P.v("dve" if c % 2 == 0 else "act", "tensor_copy" if c % 2 == 0 else "copy",
                            ctxKT[:, :, c * 128:(c + 1) * 128], pTb.rearrange("p (a t) -> p a t", a=8),
                            R=[PB[c % 2]], W=[BctxK])
                    build_E(0)
                    build_E(1)
                    build_E(2)
                    for a in range(8):
                        for qb in range(2):
                            acc = ACC[(2 * a + qb) % 2]
                            zero_acc(acc, 512)
                            for rho in range(2):
                                h = 2 * a + rho
                                eb = h % 3
                                pr = slice(64 * rho, 64 * rho + 64)
                                vc = slice(64 * rho, 64 * rho + 128)
                                for (m, jlo, cnt, e2) in _SEGS[qb]:
                                    nq, q0 = cnt * 128, jlo * 128
                                    i0 = e2 * 7 + 3 - m + jlo
                                    esl = Etab[:, eb, i0:i0 + cnt, :].rearrange("p a b -> p (a b)")
                                    segment(PT, BPT, KT_s[pr, a, m * 128:(m + 1) * 128], QT_s[pr, a, q0:q0 + nq], nq,
                                            BKs, BQs, esl, BE[eb], Vp_s[:, m, a, vc], BVs, onesp[:, vc], acc, q0 - 512 * qb)
                                for c in range(4):
                                    segment(PT, BPT, ctxKT[pr, a, c * 128:(c + 1) * 128], QT_s[pr, a, qb * 512:(qb + 1) * 512],
                                            512, BctxK, BQs, None, None, ctxVp[:, c, a, vc], BctxV, onesp[:, vc], acc, 0)
                            normalize(acc, 512, rS, BrS, OT_all[:, a, qb * 512:(qb + 1) * 512], [BOT[a], Bck])
                        for h2 in (2 * a + 3, 2 * a + 4):
                            if h2 < 16:
                                build_E(h2)
                    P.end_phase()

                yb_s = [QT_s[:, 2 * i:2 * i + 2, :].rearrange("p a t -> p (a t)").bitcast(F32) for i in range(4)]
                bc_s = [KT_s[:, 2 * i:2 * i + 2, :].rearrange("p a t -> p (a t)").bitcast(F32) for i in range(3)]
                dg_s = KT_s[:, 6:8, :].rearrange("p a t -> p (a t)").bitcast(F32).rearrange("p (a b) -> p a b", b=128)
                Bbc_s = [Buf("sbc%d" % i) for i in range(3)]
                prep_vectors(1, bc_s, Bbc_s, dg_s, Buf("sdg"), [])
                proj_group([4, 5, 6, 7], [0, 128, 256, 384], yb_s, [], bc_s, Bbc_s)
                proj_group([8, 9, 10, 11], [512, 640, 768, 896], yb_s, [], bc_s, Bbc_s)
                if stop == "l1mix":
                    store_x("l1mix")
                P.end_phase()
        if stop == "l1mix":
            return nc

        moe_phase(1, None, True)
        return nc


_CACHE = {}


def make_in_maps(inputs):
    f = lambda a: np.ascontiguousarray(np.asarray(a, dtype=np.float32))
    x_prompt, x_sample = f(inputs["x_prompt"]), f(inputs["x_sample"])
    cache_k, cache_v = f(inputs["cache_k"]), f(inputs["cache_v"])
    c, c_ctx = f(inputs["c"]), f(inputs["c_ctx"])
    b_ada = f(inputs["b_ada"])
    shared = {
        "w_ada": f(inputs["w_ada"]),
        "b_adaT": np.ascontiguousarray(b_ada.reshape(2, 48, 128).transpose(0, 2, 1)),
        "ln1_g": f(inputs["ln1_g"]), "ln1_b": f(inputs["ln1_b"]),
        "ln2_g": f(inputs["ln2_g"]), "ln2_b": f(inputs["ln2_b"]),
        "pool_w": f(inputs["pool_w"])[0],
        "pool_scale": f(inputs["pool_scale"]).reshape(1, D),
        "w_qkv": f(inputs["w_qkv"])[0],
        "w_o": f(inputs["w_o"])[0],
        "rbt": _rel_bias_layout(f(inputs["rel_bias"])[0]),
        "w_router": f(inputs["w_router"]),
        "b_router": f(inputs["b_router"]).reshape(1, 16),
        "w_gate": f(inputs["w_gate"]), "w_up": f(inputs["w_up"]), "w_down": f(inputs["w_down"]),
        "ptab": _pool_tables(),
        "m01": _mask01_table(),
    }
    in_maps = []
    for i in range(N_CORES):
        m = dict(shared)
        m["xin"] = np.ascontiguousarray(np.concatenate(
            [x_prompt[2 * i].reshape(256, D), x_prompt[2 * i + 1].reshape(256, D), x_sample[i]], axis=0))
        m["ck"] = np.ascontiguousarray(cache_k[i, 0])
        m["cv"] = np.ascontiguousarray(cache_v[i, 0])
        cond = np.stack([c_ctx, c[i]], axis=-1)
        m["condT"] = np.ascontiguousarray(cond.reshape(8, 128, 2).transpose(1, 0, 2))
        in_maps.append(m)
    return in_maps


def kernel(**inputs):
    if "nc" not in _CACHE:
        _CACHE["nc"] = build_nc()
    nc = _CACHE["nc"]
    in_maps = make_in_maps(inputs)
    res = run_bass_kernel_spmd(nc, in_maps, core_ids=list(range(N_CORES)))
    ys = [r["y"] for r in res.results]
    y_prompt = np.stack([y[:512].reshape(2, 256, D) for y in ys]).reshape(16, 256, D)
    y_sample = np.stack([y[512:] for y in ys])
    nk = np.stack([r["nk"] for r in res.results]).reshape(16, 1, 16, 256, 64)
    nv = np.stack([r["nv"] for r in res.results]).reshape(16, 1, 16, 256, 64)
    return (y_prompt.astype(np.float32), y_sample.astype(np.float32),
            nk.astype(np.float32), nv.astype(np.float32))
```

```python
from contextlib import ExitStack

import numpy as np
import concourse.bass as bass
import concourse.mybir as mybir
from concourse.bass_utils import run_bass_kernel_spmd

F32 = mybir.dt.float32
BF16 = mybir.dt.bfloat16
AF = mybir.ActivationFunctionType
ALU = mybir.AluOpType
AX = mybir.AxisListType

N_CORES = 8
D = 1024
NT = 12
T = NT * 128
ALPHA = 4.0 ** 0.25
LN_EPS = 1e-5
NEG = -30000.0
POOL_SIZES = (2, 4, 8, 16)
SEQS = ((0, 2), (2, 2), (4, 8))


def tile_cond(t):
    return 0 if t < 4 else 1


class Buf:
    __slots__ = ("name", "writer", "readers", "excl")

    def __init__(self, name="", excl=False):
        self.name = name
        self.writer = None
        self.readers = []
        self.excl = excl


class Lane:
    __slots__ = ("sem", "n", "last")

    def __init__(self, sem):
        self.sem = sem
        self.n = 0
        self.last = None


class Ins:
    __slots__ = ("eng", "fn", "deps", "signal", "count", "lane", "lval", "idx")

    def __init__(self, eng, fn):
        self.eng = eng
        self.fn = fn
        self.deps = []
        self.signal = False
        self.count = 0
        self.lane = None
        self.lval = 0
        self.idx = 0


class Prog:
    ENG = ("pe", "act", "dve", "pool", "sp")
    NLANES = {"sp": 24, "pool": 12, "act": 4}

    def __init__(self, nc, es):
        self.nc = nc
        self.es = es
        self.eng = {"pe": nc.tensor, "act": nc.scalar, "dve": nc.vector,
                    "pool": nc.gpsimd, "sp": nc.sync}
        self.sem = {e: es.enter_context(nc.semaphore("s_" + e)) for e in self.ENG}
        self.cnt = {e: 0 for e in self.ENG}
        self.bar = es.enter_context(nc.semaphore("s_bar"))
        self.nbar = 0
        self.lanes = {q: [Lane(es.enter_context(nc.semaphore("l_%s%d" % (q, i)))) for i in range(n)]
                      for q, n in self.NLANES.items()}
        self.lane_rr = {q: 0 for q in self.NLANES}
        self.waited = {e: {} for e in self.ENG}
        self.streams = {e: [] for e in self.ENG}
        self.touched = []
        self.n_ins = 0

    def _add_dep(self, ins, p):
        if p is None or p is ins:
            return
        if p.lane is None and ins.lane is None and p.eng == ins.eng and p.eng == "pe":
            return
        deps = ins.deps
        if p.lane is None:
            for i, q in enumerate(deps):
                if q.lane is None and q.eng == p.eng:
                    if p.idx > q.idx:
                        deps[i] = p
                    return
        elif p in deps:
            return
        deps.append(p)

    def _deps(self, ins, reads, writes):
        if any(b.excl for b in reads):
            writes = list(writes) + [b for b in reads if b.excl and b not in writes]
            reads = [b for b in reads if not b.excl]
        same = lambda p: (p is not None and p.lane is None and ins.lane is None and p.eng == ins.eng)
        for b in reads:
            self._add_dep(ins, b.writer)
        for b in writes:
            if not same(b.writer):
                self._add_dep(ins, b.writer)
            for r in b.readers:
                if not same(r):
                    self._add_dep(ins, r)
        for b in reads:
            if not b.readers and b.writer is None:
                self.touched.append(b)
            b.readers.append(ins)
        for b in writes:
            if not b.readers and b.writer is None:
                self.touched.append(b)
            b.writer = ins
            b.readers = []

    def op(self, eng, fn, reads=(), writes=()):
        ins = Ins(eng, fn)
        ins.idx = len(self.streams[eng])
        self._deps(ins, reads, writes)
        self.streams[eng].append(ins)
        return ins

    def dma(self, eng, out, in_, reads=(), writes=(), **kw):
        E = self.eng[eng]
        ins = Ins(eng, lambda: E.dma_start(out=out, in_=in_, **kw))
        ins.idx = len(self.streams[eng])
        lanes = self.lanes[eng]
        lane = lanes[self.lane_rr[eng] % len(lanes)]
        self.lane_rr[eng] += 1
        ins.lane = lane
        lane.n += 1
        ins.lval = 16 * lane.n
        if lane.last is not None:
            ins.deps.append(lane.last)
        lane.last = ins
        self._deps(ins, reads, writes)
        self.streams[eng].append(ins)
        return ins

    def mm(self, out, lhsT, rhs, start, stop, reads=(), writes=(), skip=False):
        t = self.nc.tensor
        if skip:
            return self.op("pe", lambda: t.matmul(out, lhsT, rhs, start=start, stop=stop, skip_group_check=True),
                           reads, writes)
        return self.op("pe", lambda: t.matmul(out, lhsT, rhs, start=start, stop=stop), reads, writes)

    def tr(self, out, in_, ident, reads=(), writes=()):
        t = self.nc.tensor
        return self.op("pe", lambda: t.transpose(out, in_, ident), reads, writes)

    def v(self, eng, meth, *args, R=(), W=(), **kw):
        E = self.eng[eng]
        return self.op(eng, lambda: getattr(E, meth)(*args, **kw), R, W)

    def end_phase(self):
        ENG = self.ENG
        for e in ENG:
            for ins in self.streams[e]:
                for p in ins.deps:
                    if p.lane is None:
                        p.signal = True
            for ins in reversed(self.streams[e]):
                if ins.lane is None:
                    ins.signal = True
                    break
        for e in ENG:
            for ins in self.streams[e]:
                if ins.lane is None and ins.signal:
                    self.cnt[e] += 1
                    ins.count = self.cnt[e]
        for e in ENG:
            E = self.eng[e]
            wd = self.waited[e]
            for ins in self.streams[e]:
                for p in ins.deps:
                    if p.lane is not None:
                        sem, val = p.lane.sem, p.lval
                    else:
                        sem, val = self.sem[p.eng], p.count
                    if wd.get(sem.num, 0) >= val:
                        continue
                    wd[sem.num] = val
                    E.wait_ge(sem, val)
                bi = ins.fn()
                if ins.lane is not None:
                    bi.then_inc(ins.lane.sem, 16)
                elif ins.signal:
                    bi.then_inc(self.sem[e], 1)
                self.n_ins += 1
        sp = self.eng["sp"]
        wd = self.waited["sp"]
        all_lanes = [l for q in self.lanes.values() for l in q]
        for e in ENG:
            if e == "sp" or self.cnt[e] == 0:
                continue
            if wd.get(self.sem[e].num, 0) < self.cnt[e]:
                sp.wait_ge(self.sem[e], self.cnt[e])
        for l in all_lanes:
            if l.n and wd.get(l.sem.num, 0) < 16 * l.n:
                sp.wait_ge(l.sem, 16 * l.n)
        self.nbar += 1
        sp.sem_inc(self.bar, 1)
        for e in ENG:
            if e != "sp":
                self.eng[e].wait_ge(self.bar, self.nbar)
            for e2 in ENG:
                self.waited[e][self.sem[e2].num] = self.cnt[e2]
            for l in all_lanes:
                self.waited[e][l.sem.num] = 16 * l.n
        for l in all_lanes:
            l.last = None
        for b in self.touched:
            b.writer = None
            b.readers = []
        self.touched = []
        self.streams = {e: [] for e in ENG}


def _pool_tables():
    n = 1024
    out = np.zeros((4, 5, 128, 128), np.float32)
    t = np.arange(n)
    for wi, w in enumerate(POOL_SIZES):
        lo = np.clip(t - w // 2, 0, n)
        hi = np.clip(t - w // 2 + w, 0, n)
        cnt = (hi - lo).astype(np.float64)
        s = np.arange(n)[:, None]
        Pm = ((s >= lo[None, :]) & (s < hi[None, :])) / cnt[None, :] - np.eye(n)
        Pm = Pm.astype(np.float32)
        out[wi, 0] = Pm[0:128, 0:128]
        out[wi, 1] = Pm[128:256, 128:256]
        out[wi, 2] = Pm[896:1024, 896:1024]
        out[wi, 3] = Pm[0:128, 128:256]
        out[wi, 4] = Pm[256:384, 128:256]
    return out


def _chunks_for(j):
    if j < 2:
        return list(range(0, 4))
    if j > 5:
        return list(range(4, 8))
    return list(range(j - 2, j + 3))


def _mask_tables():
    tiles = []
    index = {}
    keymap = {}
    col = np.arange(64)
    cs = np.clip(col - 8, 0, 48)
    colok = (col[None, :] >= cs[:, None]) & (col[None, :] < cs[:, None] + 16)
    for j in range(8):
        for m in _chunks_for(j):
            mt = np.full((2, 64, 2, 64), NEG, np.float32)
            for rq in range(2):
                r = 2 * j + rq
                rs = min(max(r - 4, 0), 8)
                for rk in range(2):
                    kr = 2 * m + rk
                    if rs <= kr < rs + 8:
                        mt[rk, :, rq, :] = np.where(colok.T, 0.0, NEG)
            key = mt.tobytes()
            if key not in keymap:
                keymap[key] = len(tiles)
                tiles.append(mt.reshape(128, 128))
            index[(j, m)] = keymap[key]
    return np.stack(tiles), index


_MASKS, _MASK_IDX = _mask_tables()
_M_FULL, _M_A, _M_B = _MASK_IDX[(0, 0)], _MASK_IDX[(2, 0)], _MASK_IDX[(2, 4)]


def _mask01_table():
    full = (_MASKS[_M_FULL] == 0).astype(np.float32)
    t = np.stack([full] * 14)
    t[7 + (3 - 2)] = (_MASKS[_M_B] == 0)
    t[7 + (3 + 2)] = (_MASKS[_M_A] == 0)
    return t


def _segments():
    segs = {0: [], 1: []}
    for m in range(8):
        for qb in range(2):
            js = [j for j in range(8) if m in _chunks_for(j) and j // 4 == qb]
            if not js:
                continue
            assert js == list(range(js[0], js[-1] + 1))
            e2 = any(_MASK_IDX[(j, m)] != _M_FULL for j in js)
            for j in js:
                want = _M_FULL
                if e2 and m - j == -2:
                    want = _M_A
                if e2 and m - j == 2:
                    want = _M_B
                assert _MASK_IDX[(j, m)] == want, (j, m)
            segs[qb].append((m, js[0], len(js), int(e2)))
    return segs


_SEGS = _segments()


def _rel_bias_layout(rel_bias):
    rk = np.arange(2)[:, None, None, None]
    ck = np.arange(64)[None, :, None, None]
    rq = np.arange(2)[None, None, :, None]
    cq = np.arange(64)[None, None, None, :]
    out = np.empty((16, 7, 128, 128), np.float32)
    for i in range(7):
        dl = 3 - i
        dy = np.clip(2 * dl + rk - rq + 7, 0, 14)
        dx = np.clip(ck - cq + 15, 0, 30)
        dy, dx = np.broadcast_arrays(dy, dx)
        out[:, i] = rel_bias[:, dy, dx].reshape(16, 128, 128)
    return out


def build_nc(stop=None):
    nc = bass.Bass("TRN2", target_bir_lowering=False)

    def din(name, shape):
        return nc.dram_tensor(name, list(shape), F32, kind="ExternalInput").ap()

    def dout(name, shape):
        return nc.dram_tensor(name, list(shape), F32, kind="ExternalOutput").ap()

    xin = din("xin", [T, D])
    ck_d = din("ck", [16, 512, 64])
    cv_d = din("cv", [16, 512, 64])
    condT_d = din("condT", [128, 8, 2])
    w_ada_d = din("w_ada", [2, D, 6 * D])
    b_adaT_d = din("b_adaT", [2, 128, 48])
    ln1g_d = din("ln1_g", [2, D])
    ln1b_d = din("ln1_b", [2, D])
    ln2g_d = din("ln2_g", [2, D])
    ln2b_d = din("ln2_b", [2, D])
    pool_w_d = din("pool_w", [4, 256, 256])
    pool_scale_d = din("pool_scale", [1, D])
    w_qkv_d = din("w_qkv", [D, 3 * D])
    w_o_d = din("w_o", [D, D])
    rbt_d = din("rbt", [16, 7, 128, 128])
    w_router_d = din("w_router", [D, 16])
    b_router_d = din("b_router", [1, 16])
    w_gate_d = din("w_gate", [2, 16, D, 512])
    w_up_d = din("w_up", [2, 16, D, 512])
    w_down_d = din("w_down", [2, 16, 512, D])
    ptab_d = din("ptab", [4, 5, 128, 128])
    m01_d = din("m01", [14, 128, 128])

    y_d = dout("y", [T, D])
    nk_d = dout("nk", [2, 16, 256, 64])
    nv_d = dout("nv", [2, 16, 256, 64])
    dbg_d = dout("dbg", [128, 2, 48, 2]) if stop == "ada" else None

    with ExitStack() as es:
        P = Prog(nc, es)

        _uid = [0]

        def sbuf(scope, name, shape, dt=F32):
            _uid[0] += 1
            return scope.enter_context(nc.sbuf_tensor("sb%d_%s" % (_uid[0], name), list(shape), dt))

        x_sb = sbuf(es, "x_sb", [128, NT, D])
        hT = sbuf(es, "hT", [128, 8, T], BF16)
        ident_f = sbuf(es, "ident_f", [128, 128])
        ident_b = sbuf(es, "ident_b", [128, 128], BF16)
        ones_f = sbuf(es, "ones_f", [128, 128])
        modcol = sbuf(es, "modcol", [128, 2, 48, 2])
        cmb = sbuf(es, "cmb", [128, NT, 16])
        small = sbuf(es, "small", [128, 64])
        psum = [es.enter_context(nc.psum_tensor("ps%d" % i, [128, 512], F32)) for i in range(8)]
        PB = [Buf("ps%d" % i, excl=True) for i in range(8)]

        Bx = [Buf("x%d" % t) for t in range(NT)]
        BhT = [(Buf("hTd%d" % t), Buf("hTa%d" % t)) for t in range(NT)]
        Bconst = Buf("const")
        Bmod = Buf("modcol")
        Bcmb = [Buf("cmb%d" % t) for t in range(NT)]


        def alpha_col(layer, v, ch, c):
            return modcol[:, layer, v * 8 + ch, c:c + 1]

        with ExitStack() as ph:
            condT = sbuf(ph, "condT", [128, 8, 2])
            sT = sbuf(ph, "sT", [128, 8, 2])
            sTb = sbuf(ph, "sTb", [128, 8, 2], BF16)
            b_col = sbuf(ph, "b_col", [128, 2, 48])
            wa = [sbuf(ph, "wa%d" % i, [128, 8, 512], BF16) for i in range(3)]
            Bwa = [Buf("wa%d" % i) for i in range(3)]
            BsT = Buf("sT")
            Bbcol = Buf("bcol")

            P.v("pool", "memset", ident_f[:], 0.0, W=[Bconst])
            P.v("pool", "affine_select", ident_f[:], ident_f[:], [[-1, 128]], ALU.not_equal, 1.0,
                base=0, channel_multiplier=1, R=[Bconst], W=[Bconst])
            P.v("pool", "tensor_copy", ident_b[:], ident_f[:], R=[Bconst], W=[Bconst])
            P.v("pool", "memset", ones_f[:], 1.0, W=[Bconst])

            for t in range(NT):
                P.dma("sp", x_sb[:, t, :], xin[t * 128:(t + 1) * 128, :], writes=[Bx[t]])
            P.dma("sp", condT[:], condT_d[:, :, :], writes=[BsT])
            P.dma("sp", b_col[:], b_adaT_d.rearrange("l p c -> p l c"), writes=[Bbcol])
            P.v("act", "activation", sT[:], condT[:], AF.Silu, R=[BsT], W=[BsT])
            P.v("dve", "tensor_copy", sTb[:], sT[:], R=[BsT], W=[BsT])

            nblk = 0
            for layer in range(2):
                pc = psum[layer]
                for nb in range(12):
                    slot = nblk % 3
                    src = w_ada_d[layer].rearrange("(kc k) n -> k kc n", k=128)[:, :, nb * 512:(nb + 1) * 512]
                    P.dma("pool", wa[slot][:], src, writes=[Bwa[slot]])
                    for n4 in range(4):
                        ch = nb * 4 + n4
                        for kc in range(8):
                            P.mm(pc[:, ch * 2:ch * 2 + 2], wa[slot][:, kc, n4 * 128:(n4 + 1) * 128],
                                 sTb[:, kc, :], kc == 0, kc == 7,
                                 reads=[Bwa[slot], BsT], writes=[PB[layer]])
                    nblk += 1
                P.v("dve", "tensor_tensor", modcol[:, layer, :, :],
                    pc[:, 0:96].rearrange("p (c k) -> p c k", k=2),
                    b_col[:, layer, :].unsqueeze(2).to_broadcast([128, 48, 2]), ALU.add,
                    R=[PB[layer], Bbcol], W=[Bmod])
                for v in (1, 4):
                    P.v("dve", "tensor_scalar_add", modcol[:, layer, v * 8:(v + 1) * 8, :],
                        modcol[:, layer, v * 8:(v + 1) * 8, :], 1.0, R=[Bmod], W=[Bmod])
            if stop == "ada":
                P.dma("sp", dbg_d[:, :, :, :], modcol[:], reads=[Bmod])
            P.end_phase()

        def load_bc(dst, src_row, W):
            P.dma("sp", dst, src_row.partition_broadcast(128), writes=W)

        def make_bc(dst, layer, v, c, dgall, Bdg, banks, W):
            P.v("dve", "tensor_tensor", dgall[:], ident_f[:].unsqueeze(1).to_broadcast([128, 8, 128]),
                modcol[:, layer, v * 8:(v + 1) * 8, c:c + 1].to_broadcast([128, 8, 128]), ALU.mult,
                R=[Bconst, Bmod], W=[Bdg])
            for half in range(2):
                pb, Bp = banks[half]
                P.mm(pb[:], ones_f[:], dgall[:, half * 4:(half + 1) * 4, :].rearrange("p a b -> p (a b)"),
                     True, True, reads=[Bconst, Bdg], writes=[Bp])
                P.v("act", "copy", dst[:, half * 512:(half + 1) * 512], pb[:], R=[Bp], W=W)

        Bsmall = [Buf("mv%d" % i) for i in range(4)]

        def ln_group(tiles, ys, Bys, g_bc, b_bc, Bvec, next_mod, banks, gmul_eng):
            n = len(tiles)
            sts = [small[:, 16 * i:16 * i + 12].rearrange("p (a b) -> p a b", b=6) for i in range(n)]
            mvv = [small[:, 16 * i + 12:16 * i + 16] for i in range(n)]
            add_eng = "pool" if gmul_eng == "dve" else "dve"
            for i in range(n):
                P.v("dve", "bn_stats", sts[i][:, 0, :], ys[i][:, 0:512], R=[Bys[i]], W=[Bsmall[i]])
                P.v("dve", "bn_stats", sts[i][:, 1, :], ys[i][:, 512:1024], R=[Bys[i]], W=[Bsmall[i]])
                P.v("dve", "bn_aggr", mvv[i][:, 0:2], sts[i], R=[Bsmall[i]], W=[Bsmall[i]])
            for i in range(n):
                P.v("act", "activation", mvv[i][:, 2:3], mvv[i][:, 1:2], AF.Sqrt, bias=LN_EPS,
                    R=[Bsmall[i]], W=[Bsmall[i]])
            for i in range(n):
                P.v("dve", "reciprocal", mvv[i][:, 2:3], mvv[i][:, 2:3], R=[Bsmall[i]], W=[Bsmall[i]])
                P.v("dve", "tensor_scalar", mvv[i][:, 3:4], mvv[i][:, 0:1], mvv[i][:, 2:3], -1.0,
                    op0=ALU.mult, op1=ALU.mult, R=[Bsmall[i]], W=[Bsmall[i]])
            for i in range(n):
                P.v("act", "activation", ys[i], ys[i], AF.Identity, scale=mvv[i][:, 2:3], bias=mvv[i][:, 3:4],
                    R=[Bys[i], Bsmall[i]], W=[Bys[i]])
            for i in range(n):
                P.v(gmul_eng, "tensor_tensor", ys[i], ys[i], g_bc, ALU.mult, R=[Bys[i]] + list(Bvec), W=[Bys[i]])
            for i, t in enumerate(tiles):
                P.v("pool", "tensor_tensor", x_sb[:, t, :], ys[i], b_bc, ALU.add, R=[Bys[i]] + list(Bvec), W=[Bx[t]])
            if next_mod is None:
                return
            layer, vA, vB = next_mod
            nb = len(banks) // 2
            for i, t in enumerate(tiles):
                c = tile_cond(t)
                pair = banks[2 * (i % nb):2 * (i % nb) + 2]
                for half in range(2):
                    pb, Bp = pair[half]
                    for q in range(4):
                        kc = half * 4 + q
                        P.tr(pb[:, q * 128:(q + 1) * 128], x_sb[:, t, kc * 128:(kc + 1) * 128], ident_f[:],
                             reads=[Bx[t], Bconst], writes=[Bp])
                for half in range(2):
                    pb, Bp = pair[half]
                    for q in range(4):
                        kc = half * 4 + q
                        if half == 0:
                            P.v("dve", "tensor_scalar", hT[:, kc, t * 128:(t + 1) * 128], pb[:, q * 128:(q + 1) * 128],
                                alpha_col(layer, vA, kc, c), alpha_col(layer, vB, kc, c),
                                op0=ALU.mult, op1=ALU.add, R=[Bp, Bmod], W=[BhT[t][0]])
                        else:
                            P.v("act", "activation", hT[:, kc, t * 128:(t + 1) * 128], pb[:, q * 128:(q + 1) * 128],
                                AF.Identity, scale=alpha_col(layer, vA, kc, c), bias=alpha_col(layer, vB, kc, c),
                                R=[Bp, Bmod], W=[BhT[t][1]])

        PBK = [(psum[i], PB[i]) for i in range(8)]

        def store_x(label):
            for t in range(NT):
                P.dma("sp", y_d[t * 128:(t + 1) * 128, :], x_sb[:, t, :], reads=[Bx[t]])

        if stop == "ada":
            store_x("ada")
            P.end_phase()
            return nc

        with ExitStack() as ph:
            bc = [sbuf(ph, "bc%d" % i, [128, D]) for i in range(8)]
            Bbc = [Buf("bc%d" % i) for i in range(8)]
            psc = sbuf(ph, "psc", [128, D])
            Bpsc = Buf("psc")
            dgall = sbuf(ph, "dgall", [128, 8, 128])
            Bdg = Buf("dgall")
            h_tok = sbuf(ph, "h_tok", [128, NT, D])
            Bh = [Buf("h%d" % t) for t in range(NT)]
            ptab = sbuf(ph, "ptab", [128, 4, 5, 128])
            Bptab = Buf("ptab")
            pw = sbuf(ph, "pw", [128, 4, 2, 256], BF16)
            Bpw = Buf("pw")
            pTt = [sbuf(ph, "pTt%d" % i, [128, 8, 128], BF16) for i in range(2)]
            BpTt = [Buf("pTt0"), Buf("pTt1")]
            ybuf = [sbuf(ph, "ybuf%d" % i, [128, D]) for i in range(4)]
            By = [Buf("y%d" % i) for i in range(4)]

            P.dma("sp", ptab[:], ptab_d.rearrange("w k s t -> s w k t"), writes=[Bptab])
            P.dma("pool", pw[:], pool_w_d.rearrange("g (i c) d -> c g i d", c=128), writes=[Bpw])
            load_bc(psc[:], pool_scale_d[0:1, :], [Bpsc])
            load_bc(bc[6][:], ln1g_d[0:1, :], [Bbc[6]])
            load_bc(bc[7][:], ln1b_d[0:1, :], [Bbc[7]])
            for c in range(2):
                make_bc(bc[2 * c], 0, 1, c, dgall, Bdg, PBK[0:2], [Bbc[2 * c]])
                make_bc(bc[2 * c + 1], 0, 0, c, dgall, Bdg, PBK[2:4], [Bbc[2 * c + 1]])
                make_bc(bc[4 + c], 0, 2, c, dgall, Bdg, PBK[4:6], [Bbc[4 + c]])
                P.v("pool", "tensor_tensor", bc[4 + c][:], bc[4 + c][:], psc[:], ALU.mult,
                    R=[Bbc[4 + c], Bpsc], W=[Bbc[4 + c]])
            for t in range(NT):
                c = tile_cond(t)
                P.v("dve", "tensor_tensor", h_tok[:, t, :], x_sb[:, t, :], bc[2 * c][:], ALU.mult,
                    R=[Bx[t], Bbc[2 * c]], W=[Bh[t]])
                P.v("pool", "tensor_tensor", h_tok[:, t, :], h_tok[:, t, :], bc[2 * c + 1][:], ALU.add,
                    R=[Bh[t], Bbc[2 * c + 1]], W=[Bh[t]])
            seq_of = {}
            for (t0, ntl) in SEQS:
                for q in range(ntl):
                    seq_of[t0 + q] = (q, ntl)
            for grp in range(3):
                tiles = list(range(4 * grp, 4 * grp + 4))
                for i, t in enumerate(tiles):
                    q, ntl = seq_of[t]
                    c = tile_cond(t)
                    k = i % 2
                    (pA, BA), (pBk, BB) = PBK[2 * k], PBK[2 * k + 1]
                    kind = 0 if q == 0 else (2 if q == ntl - 1 else 1)
                    srcs = [(t, kind)]
                    if q > 0:
                        srcs.append((t - 1, 3))
                    if q < ntl - 1:
                        srcs.append((t + 1, 4))
                    for kc in range(8):
                        wi = kc // 2
                        bank, Bb = (pA, BA) if kc < 4 else (pBk, BB)
                        dst = bank[:, (kc % 4) * 128:(kc % 4 + 1) * 128]
                        for si, (ts, kd) in enumerate(srcs):
                            P.mm(dst, h_tok[:, ts, kc * 128:(kc + 1) * 128], ptab[:, wi, kd, :],
                                 si == 0, si == len(srcs) - 1, reads=[Bh[ts], Bptab], writes=[Bb])
                    P.v("act", "copy", pTt[k][:, 0:4, :], pA[:].rearrange("p (a b) -> p a b", b=128),
                        R=[BA], W=[BpTt[k]])
                    P.v("act", "copy", pTt[k][:, 4:8, :], pBk[:].rearrange("p (a b) -> p a b", b=128),
                        R=[BB], W=[BpTt[k]])
                    for half in range(2):
                        bank, Bb = (pA, BA) if half == 0 else (pBk, BB)
                        for gg in range(2):
                            g = half * 2 + gg
                            for i2 in range(2):
                                P.mm(bank[:, gg * 256:(gg + 1) * 256], pTt[k][:, 2 * g + i2, :], pw[:, g, i2, :],
                                     i2 == 0, i2 == 1, reads=[BpTt[k], Bpw], writes=[Bb])
                        P.v("dve", "tensor_tensor", ybuf[i][:, half * 512:(half + 1) * 512], bank[:],
                            bc[4 + c][:, half * 512:(half + 1) * 512], ALU.mult, R=[Bb, Bbc[4 + c]], W=[By[i]])
                    P.v("dve", "scalar_tensor_tensor", ybuf[i][:], x_sb[:, t, :], ALPHA, ybuf[i][:],
                        op0=ALU.mult, op1=ALU.add, R=[Bx[t], By[i]], W=[By[i]])
                ln_group(tiles, [ybuf[i][:] for i in range(4)], By, bc[6][:], bc[7][:], [Bbc[6], Bbc[7]],
                         (0, 4, 3), PBK[4:8], "pool")
            if stop == "l0mix":
                store_x("l0mix")
            P.end_phase()
        if stop == "l0mix":
            return nc


        def moe_phase(layer, next_mod, final):
            with ExitStack() as ph:
                bc = [sbuf(ph, "bc%d" % i, [128, D]) for i in range(4)]
                Bbc = [Buf("mbc%d" % i) for i in range(4)]
                dgall = sbuf(ph, "dgall", [128, 8, 128])
                Bdg = Buf("dgall")
                wg = [sbuf(ph, "wg%d" % i, [128, 8, 512], BF16) for i in range(2)]
                wu = [sbuf(ph, "wu%d" % i, [128, 8, 512], BF16) for i in range(2)]
                wd = [sbuf(ph, "wd%d" % i, [128, 4, D], BF16) for i in range(2)]
                Bwg = [Buf("wg0"), Buf("wg1")]
                Bwu = [Buf("wu0"), Buf("wu1")]
                Bwd = [Buf("wd0"), Buf("wd1")]
                heT = [sbuf(ph, "heT%d" % i, [128, 4, 512], BF16) for i in range(2)]
                Bhe = [Buf("he0"), Buf("he1")]
                sg = [sbuf(ph, "sg%d" % i, [128, 512], BF16) for i in range(2)]
                Bsg = [Buf("sg0"), Buf("sg1")]
                yacc = sbuf(ph, "yacc", [128, NT, D])
                Byacc = [Buf("yacc%d" % t) for t in range(NT)]
                wr_b = sbuf(ph, "wr_b", [128, 8, 16], BF16)
                brt = sbuf(ph, "brt", [128, 16])
                Bwr = Buf("wr")
                Bbrt = Buf("brt")
                rt = sbuf(ph, "rt", [128, 1472])
                Brt = Buf("rt")
                G = [psum[0], psum[1]]
                U = [psum[2], psum[3]]
                Dn = [psum[4], psum[5], psum[6], psum[7]]
                BG, BU, BD = [PB[0], PB[1]], [PB[2], PB[3]], [PB[4], PB[5], PB[6], PB[7]]

                def load_expert(e):
                    sl = e % 2
                    P.dma("pool", wg[sl][:], w_gate_d[layer, e].rearrange("(kc k) f -> k kc f", k=128), writes=[Bwg[sl]])
                    P.dma("pool", wu[sl][:], w_up_d[layer, e].rearrange("(kc k) f -> k kc f", k=128), writes=[Bwu[sl]])
                    P.dma("pool", wd[sl][:], w_down_d[layer, e].rearrange("(fc f) d -> f fc d", f=128), writes=[Bwd[sl]])

                P.dma("pool", wr_b[:], w_router_d.rearrange("(kc k) e -> k kc e", k=128), writes=[Bwr])
                load_expert(0)
                load_expert(1)
                P.dma("sp", brt[:], b_router_d[0:1, :].partition_broadcast(128), writes=[Bbrt])
                load_bc(bc[2][:], ln2g_d[layer:layer + 1, :], [Bbc[2]])
                load_bc(bc[3][:], ln2b_d[layer:layer + 1, :], [Bbc[3]])
                for c in range(2):
                    make_bc(bc[c], layer, 5, c, dgall, Bdg, PBK[6:8], [Bbc[c]])

                lg = psum[4]
                for t in range(NT):
                    for kc in range(8):
                        P.mm(lg[:, t * 16:(t + 1) * 16], hT[:, kc, t * 128:(t + 1) * 128], wr_b[:, kc, :],
                             kc == 0, kc == 7, reads=[BhT[t][0], BhT[t][1], Bwr], writes=[PB[4]])
                o = [0]

                def carve(nel):
                    v_ = rt[:, o[0]:o[0] + nel]
                    o[0] += nel
                    return v_
                s_sb, sel, t6, gs, gm = carve(192), carve(192), carve(288), carve(48), carve(12)
                gmask, msel, top8, emask, wsel, wsum = carve(48), carve(192), carve(96), carve(192), carve(192), carve(12)
                Rr, Wr = [Brt], [Brt]
                P.v("act", "activation", s_sb, lg[:, 0:192], AF.Sigmoid, R=[PB[4]], W=Wr)
                P.v("dve", "tensor_tensor", sel.rearrange("p (t e) -> p t e", e=16),
                    s_sb.rearrange("p (t e) -> p t e", e=16), brt[:].unsqueeze(1).to_broadcast([128, NT, 16]),
                    ALU.add, R=Rr + [Bbrt], W=Wr)
                sel4 = sel.rearrange("p (g e) -> p g e", e=4)
                t6v = t6.rearrange("p (g e) -> p g e", e=6)
                P.v("dve", "tensor_tensor", t6v[:, :, 0:3], sel4[:, :, 0:3], sel4[:, :, 1:4], ALU.add, R=Rr, W=Wr)
                P.v("dve", "tensor_tensor", t6v[:, :, 3:5], sel4[:, :, 0:2], sel4[:, :, 2:4], ALU.add, R=Rr, W=Wr)
                P.v("dve", "tensor_tensor", t6v[:, :, 5:6], sel4[:, :, 0:1], sel4[:, :, 3:4], ALU.add, R=Rr, W=Wr)
                P.v("dve", "tensor_reduce", gs, t6v, AX.X, ALU.max, R=Rr, W=Wr)
                gs3 = gs.rearrange("p (t g) -> p t g", g=4)
                P.v("dve", "tensor_reduce", gm, gs3, AX.X, ALU.max, R=Rr, W=Wr)
                P.v("dve", "tensor_tensor", gmask.rearrange("p (t g) -> p t g", g=4), gs3,
                    gm.unsqueeze(2).to_broadcast([128, NT, 4]), ALU.is_equal, R=Rr, W=Wr)
                P.v("dve", "scalar_tensor_tensor", msel.rearrange("p (g e) -> p g e", e=4), sel4, 2.0,
                    gmask.unsqueeze(2).to_broadcast([128, 48, 4]), op0=ALU.add, op1=ALU.mult, R=Rr, W=Wr)
                top83 = top8.rearrange("p (t e) -> p t e", e=8)
                for t in range(NT):
                    P.v("dve", "max", top83[:, t, :], msel[:, t * 16:(t + 1) * 16], R=Rr, W=Wr)
                P.v("dve", "tensor_tensor", emask.rearrange("p (t e) -> p t e", e=16),
                    msel.rearrange("p (t e) -> p t e", e=16), top83[:, :, 1:2].to_broadcast([128, NT, 16]),
                    ALU.is_ge, R=Rr, W=Wr)
                P.v("dve", "tensor_tensor", wsel, s_sb, emask, ALU.mult, R=Rr, W=Wr)
                wsel3 = wsel.rearrange("p (t e) -> p t e", e=16)
                P.v("dve", "tensor_reduce", wsum, wsel3, AX.X, ALU.add, R=Rr, W=Wr)
                P.v("dve", "reciprocal", wsum, wsum, R=Rr, W=Wr)
                P.v("dve", "tensor_tensor", cmb[:], wsel3, wsum.unsqueeze(2).to_broadcast([128, NT, 16]),
                    ALU.mult, R=Rr, W=Bcmb)

                blocks = [(e, b) for e in range(16) for b in range(3)]
                hT_bufs = [[x for t in range(4 * b, 4 * b + 4) for x in BhT[t]] for b in range(3)]

                def GU(k):
                    e, b = blocks[k]
                    sl = e % 2
                    for fc in range(4):
                        gi = (k * 4 + fc) % 2
                        for kc in range(8):
                            P.mm(G[gi][:], wg[sl][:, kc, fc * 128:(fc + 1) * 128], hT[:, kc, b * 512:(b + 1) * 512],
                                 kc == 0, kc == 7, reads=[Bwg[sl]] + hT_bufs[b], writes=[BG[gi]])
                        for kc in range(8):
                            P.mm(U[gi][:], wu[sl][:, kc, fc * 128:(fc + 1) * 128], hT[:, kc, b * 512:(b + 1) * 512],
                                 kc == 0, kc == 7, reads=[Bwu[sl]] + hT_bufs[b], writes=[BU[gi]])
                        P.v("act", "activation", sg[gi][:], G[gi][:], AF.Silu, R=[BG[gi]], W=[Bsg[gi]])
                        P.v("dve", "tensor_tensor", heT[k % 2][:, fc, :], U[gi][:], sg[gi][:], ALU.mult,
                            R=[BU[gi], Bsg[gi]], W=[Bhe[k % 2]])

                def DNp(k):
                    e, b = blocks[k]
                    sl = e % 2
                    for tt in range(4):
                        t = 4 * b + tt
                        for half in range(2):
                            di = (k * 8 + tt * 2 + half) % 4
                            for fc in range(4):
                                P.mm(Dn[di][:], heT[k % 2][:, fc, tt * 128:(tt + 1) * 128],
                                     wd[sl][:, fc, half * 512:(half + 1) * 512], fc == 0, fc == 3,
                                     reads=[Bhe[k % 2], Bwd[sl]], writes=[BD[di]])
                            dst = yacc[:, t, half * 512:(half + 1) * 512]
                            if e == 0:
                                P.v("dve", "tensor_scalar", dst, Dn[di][:], cmb[:, t, e:e + 1], None, op0=ALU.mult,
                                    R=[BD[di], Bcmb[t]], W=[Byacc[t]])
                            else:
                                P.v("dve", "scalar_tensor_tensor", dst, Dn[di][:], cmb[:, t, e:e + 1], dst,
                                    op0=ALU.mult, op1=ALU.add, R=[BD[di], Bcmb[t], Byacc[t]], W=[Byacc[t]])

                GU(0)
                for k in range(len(blocks)):
                    if k + 1 < len(blocks):
                        GU(k + 1)
                    DNp(k)
                    e_, b_ = blocks[k]
                    if b_ == 2 and e_ + 2 < 16:
                        load_expert(e_ + 2)

                def pre(grp):
                    for t in range(4 * grp, 4 * grp + 4):
                        c = tile_cond(t)
                        ya = yacc[:, t, :]
                        P.v("pool", "tensor_tensor", ya, ya, bc[c][:], ALU.mult, R=[Byacc[t], Bbc[c]], W=[Byacc[t]])
                    for t in range(4 * grp, 4 * grp + 4):
                        ya = yacc[:, t, :]
                        P.v("dve", "scalar_tensor_tensor", ya, x_sb[:, t, :], ALPHA, ya, op0=ALU.mult, op1=ALU.add,
                            R=[Bx[t], Byacc[t]], W=[Byacc[t]])

                pre(0)
                for grp in range(3):
                    if grp + 1 < 3:
                        pre(grp + 1)
                    tiles = list(range(4 * grp, 4 * grp + 4))
                    ln_group(tiles, [yacc[:, t, :] for t in tiles], [Byacc[t] for t in tiles], bc[2][:], bc[3][:],
                             [Bbc[2], Bbc[3]], next_mod, PBK, "dve")
                    if final:
                        for t in tiles:
                            P.dma("sp", y_d[t * 128:(t + 1) * 128, :], x_sb[:, t, :], reads=[Bx[t]])
                P.end_phase()

        moe_phase(0, (1, 1, 0), False)
        if stop == "l0moe":
            store_x("l0moe")
            P.end_phase()
            return nc


        with ExitStack() as pa:
            QT_s = sbuf(pa, "QT_s", [128, 8, 1024], BF16)
            KT_s = sbuf(pa, "KT_s", [128, 8, 1024], BF16)
            Vp_s = sbuf(pa, "Vp_s", [128, 8, 8, 192], BF16)
            O_CTXV, O_M01, O_RBB, O_E, O_ONES, RG_N = 4096, 10240, 12032, 13824, 19200, 19392
            Rg = sbuf(pa, "Rg", [128, RG_N], BF16)
            QT_p = Rg[:, 0:4096].rearrange("p (c t) -> p c t", c=8)
            KT_p = Rg[:, 4096:8192].rearrange("p (c t) -> p c t", c=8)
            Vp_p = Rg[:, 8192:14336].rearrange("p (m a e) -> p m a e", m=4, a=8)
            ctxKT = Rg[:, 0:4096].rearrange("p (c t) -> p c t", c=8)
            ctxVp = Rg[:, O_CTXV:O_M01].rearrange("p (m a e) -> p m a e", m=4, a=8)
            m01 = Rg[:, O_M01:O_RBB].rearrange("p (m q) -> p m q", q=128)
            rbb = Rg[:, O_RBB:O_E].rearrange("p (b d q) -> p b d q", b=2, d=7)
            Etab = Rg[:, O_E:O_ONES].rearrange("p (b d q) -> p b d q", b=3, d=14)
            onesp = Rg[:, O_ONES:RG_N]
            BQs, BKs, BVs = Buf("QTs"), Buf("KTs"), Buf("Vs")
            BQp, BKp, BVp = Buf("QTp"), Buf("KTp"), Buf("Vp")
            Bones = Buf("onesp")

            with ExitStack() as ph:
                wb = [sbuf(ph, "wqkv%d" % i, [128, 8, 512], BF16) for i in range(2)]
                Bwb = [Buf("wb0"), Buf("wb1")]
                stg = [sbuf(ph, "stg%d" % i, [128, 512]) for i in range(2)]
                Bstg = [Buf("stg0"), Buf("stg1")]
                P.v("pool", "memset", Vp_s[:, :, :, 64:128], 0.0, W=[BVs])
                P.v("pool", "memset", Vp_p[:, :, :, 64:128], 0.0, W=[BVp])
                P.v("pool", "memset", onesp[:, 0:64], 1.0, W=[Bones])
                P.v("pool", "memset", onesp[:, 64:128], 0.0, W=[Bones])
                P.v("pool", "memset", onesp[:, 128:192], 1.0, W=[Bones])
                nb = [0]
                ns = [0]

                def bank():
                    i = nb[0] % 8
                    nb[0] += 1
                    return psum[i], PB[i], ("dve" if i % 2 == 0 else "act")

                def evac(eng, dst, src, Bsrc, W, mul=None):
                    if eng == "dve":
                        if mul is None:
                            P.v("dve", "tensor_copy", dst, src, R=[Bsrc], W=W)
                        else:
                            P.v("dve", "tensor_scalar_mul", dst, src, mul, R=[Bsrc], W=W)
                    else:
                        if mul is None:
                            P.v("act", "copy", dst, src, R=[Bsrc], W=W)
                        else:
                            P.v("act", "mul", dst, src, mul, R=[Bsrc], W=W)

                hTb = lambda tiles: [x for t in tiles for x in BhT[t]]
                for cb in range(6):
                    sl = cb % 2
                    P.dma("pool", wb[sl][:], w_qkv_d.rearrange("(kc k) n -> k kc n", k=128)[:, :, cb * 512:(cb + 1) * 512],
                          writes=[Bwb[sl]])
                    if cb < 4:
                        isq = cb < 2
                        for i in range(4):
                            a = (cb % 2) * 4 + i
                            for tb in range(3):
                                pb, Bp, eng = bank()
                                for kc in range(8):
                                    P.mm(pb[:], wb[sl][:, kc, i * 128:(i + 1) * 128], hT[:, kc, tb * 512:(tb + 1) * 512],
                                         kc == 0, kc == 7, reads=[Bwb[sl]] + hTb(range(4 * tb, 4 * tb + 4)), writes=[Bp])
                                if tb == 0:
                                    dst = (QT_p if isq else KT_p)[:, a, :]
                                    W = [BQp if isq else BKp]
                                else:
                                    dst = (QT_s if isq else KT_s)[:, a, (tb - 1) * 512:tb * 512]
                                    W = [BQs if isq else BKs]
                                evac(eng, dst, pb[:], Bp, W, mul=(0.125 if isq else None))
                    if cb in (2, 3, 4, 5):
                        isv = cb >= 4
                        hh = cb % 2
                        for t in (range(NT) if isv else range(4)):
                            pb, Bp, eng = bank()
                            for kc in range(8):
                                P.mm(pb[:], hT[:, kc, t * 128:(t + 1) * 128], wb[sl][:, kc, :], kc == 0, kc == 7,
                                     reads=[Bwb[sl]] + hTb([t]), writes=[Bp])
                            if isv:
                                src4 = pb[:].rearrange("p (a two d) -> p a two d", two=2, d=64)
                                for par in range(2):
                                    if t < 4:
                                        evac(eng, Vp_p[:, t, hh * 4:(hh + 1) * 4, par * 128:par * 128 + 64],
                                             src4[:, :, par, :], Bp, [BVp])
                                    else:
                                        evac(eng, Vp_s[:, t - 4, hh * 4:(hh + 1) * 4, par * 128:par * 128 + 64],
                                             src4[:, :, par, :], Bp, [BVs])
                            if t < 4:
                                k = ns[0] % 2
                                ns[0] += 1
                                evac("dve" if eng == "act" else "act", stg[k][:], pb[:], Bp, [Bstg[k]])
                                dd = nv_d if isv else nk_d
                                s_, l0 = t // 2, (t % 2) * 128
                                P.dma("sp", dd[s_, hh * 8:(hh + 1) * 8, l0:l0 + 128, :].rearrange("h l d -> l h d"),
                                      stg[k][:].rearrange("p (h d) -> p h d", d=64), reads=[Bstg[k]])
                P.end_phase()

            with ExitStack() as pb_:
                OT_all = sbuf(pb_, "OT_all", [128, 8, 1024], BF16)
                BOT = [Buf("OT%d" % a) for a in range(8)]
                wo = sbuf(pb_, "wo", [128, 8, D], BF16)
                Bwo = Buf("wo")
                By = [Buf("ay%d" % i) for i in range(4)]
                ST = [PBK[i] for i in range(4)]
                ACC = [(PBK[4], PBK[5]), (PBK[6], PBK[7])]
                cn = {"st": 0, "pt": 0, "em": 0}

                def heads_scope(scope):
                    PT = [sbuf(scope, "PT%d" % i, [128, 512], BF16) for i in range(4)]
                    BPT = [Buf("PT%d" % i) for i in range(4)]
                    rS = sbuf(scope, "rS", [128, 512])
                    BrS = Buf("rS")
                    return PT, BPT, rS, BrS

                LOOKAHEAD = 3
                pend = []

                def drain(keep):
                    while len(pend) > keep:
                        pend.pop(0)()

                def segment(PT, BPT, Kst, Qmv, nq, Bk, Bq, esl, BE_, Vblk, BV_, ones_blk, acc, c0):
                    st, Bst = ST[cn["st"] % 4]
                    cn["st"] += 1
                    k = cn["pt"] % 4
                    cn["pt"] += 1
                    P.mm(st[:, 0:nq], Kst, Qmv, True, True, reads=[Bk, Bq], writes=[Bst])
                    P.v("act", "activation", PT[k][:, 0:nq], st[:, 0:nq], AF.Exp, R=[Bst], W=[BPT[k]])
                    if esl is not None:
                        eng = "pool" if cn["em"] % 3 == 2 else "dve"
                        cn["em"] += 1
                        P.v(eng, "tensor_tensor", PT[k][:, 0:nq], PT[k][:, 0:nq], esl, ALU.mult,
                            R=[BPT[k], BE_], W=[BPT[k]])
                    (pO, BO_), (pS, BS_) = acc

                    def back():
                        P.mm(pO[:, c0:c0 + nq], Vblk, PT[k][:, 0:nq], False, False, reads=[BPT[k], BV_], writes=[BO_],
                             skip=True)
                        P.mm(pS[:, c0:c0 + nq], ones_blk, PT[k][:, 0:nq], False, False, reads=[BPT[k], Bones],
                             writes=[BS_], skip=True)
                    pend.append(back)
                    drain(LOOKAHEAD)

                def zero_acc(acc, ncols):
                    (pO, BO_), (pS, BS_) = acc
                    P.v("dve", "memset", pO[:, 0:ncols], 0.0, W=[BO_])
                    P.v("dve", "memset", pS[:, 0:ncols], 0.0, W=[BS_])

                def normalize(acc, ncols, rS, BrS, dst, Wd):
                    (pO, BO_), (pS, BS_) = acc

                    def fin():
                        P.v("dve", "reciprocal", rS[:, 0:ncols], pS[:, 0:ncols], R=[BS_], W=[BrS])
                        P.v("dve", "tensor_tensor", dst, pO[:, 0:ncols], rS[:, 0:ncols], ALU.mult, R=[BO_, BrS], W=Wd)
                    pend.append(fin)

                def proj_group(tiles, tcols, ybufs, deadW, bcv, Bbcv):
                    for i, t in enumerate(tiles):
                        tc = tcols[i]
                        for half in range(2):
                            pb, Bp = PBK[2 * (i % 2) + half]
                            for kc in range(8):
                                P.mm(pb[:], OT_all[:, kc, tc:tc + 128], wo[:, kc, half * 512:(half + 1) * 512],
                                     kc == 0, kc == 7, reads=[BOT[kc], Bwo], writes=[Bp])
                            P.v("dve", "tensor_tensor", ybufs[i][:, half * 512:(half + 1) * 512], pb[:],
                                bcv[0][:, half * 512:(half + 1) * 512], ALU.mult, R=[Bp, Bbcv[0]],
                                W=[By[i]] + list(deadW))
                        P.v("dve", "scalar_tensor_tensor", ybufs[i], x_sb[:, t, :], ALPHA, ybufs[i],
                            op0=ALU.mult, op1=ALU.add, R=[Bx[t], By[i]], W=[By[i]])
                    ln_group(tiles, ybufs, By, bcv[1], bcv[2], [Bbcv[1], Bbcv[2]], (1, 4, 3), PBK[4:8], "pool")

                def prep_vectors(cidx, bcv, Bbcv, dgall_v, Bdg_v, deadW):
                    load_bc(bcv[1], ln1g_d[1:2, :], [Bbcv[1]] + list(deadW))
                    load_bc(bcv[2], ln1b_d[1:2, :], [Bbcv[2]] + list(deadW))
                    make_bc(bcv[0], 1, 2, cidx, dgall_v, Bdg_v, PBK[6:8], [Bbcv[0]] + list(deadW))

                with ExitStack() as ph:
                    PT, BPT, rS, BrS = heads_scope(ph)
                    P.dma("pool", wo[:], w_o_d.rearrange("(kc k) n -> k kc n", k=128), writes=[Bwo])
                    for a in range(8):
                        for s_ in range(2):
                            acc = ACC[(2 * a + s_) % 2]
                            zero_acc(acc, 256)
                            for rho in range(2):
                                pr = slice(64 * rho, 64 * rho + 64)
                                vc = slice(64 * rho, 64 * rho + 128)
                                for c in range(2):
                                    k0 = s_ * 256 + c * 128
                                    segment(PT, BPT, KT_p[pr, a, k0:k0 + 128], QT_p[pr, a, s_ * 256:(s_ + 1) * 256], 256,
                                            BKp, BQp, None, None, Vp_p[:, 2 * s_ + c, a, vc], BVp, onesp[:, vc], acc, 0)
                            normalize(acc, 256, rS, BrS, OT_all[:, a, s_ * 256:(s_ + 1) * 256], [BOT[a]])
                    drain(0)
                    P.end_phase()

                yb_p = [Rg[:, 2048 * i:2048 * (i + 1)].bitcast(F32) for i in range(4)]
                bc_p = [Rg[:, 8192 + 2048 * i:8192 + 2048 * (i + 1)].bitcast(F32) for i in range(3)]
                dg_p = Rg[:, 14336:16384].bitcast(F32).rearrange("p (a b) -> p a b", b=128)
                Bbc_p = [Buf("pbc%d" % i) for i in range(3)]
                prep_vectors(0, bc_p, Bbc_p, dg_p, Buf("pdg"), [])
                proj_group([0, 1, 2, 3], [0, 128, 256, 384], yb_p, [], bc_p, Bbc_p)
                if stop == "l1mixp":
                    store_x("l1mixp")
                P.end_phase()
                if stop == "l1mixp":
                    return nc

                with ExitStack() as ph:
                    PT, BPT, rS, BrS = heads_scope(ph)
                    ck_tok = OT_all[:, 0:4, :].rearrange("p c (h d) -> p c h d", d=64)
                    Bck = Buf("ck_tok")
                    BctxK, BctxV, Bm01 = Buf("ctxK"), Buf("ctxV"), Buf("m01")
                    Brbb = [Buf("rbb0"), Buf("rbb1")]
                    BE = [Buf("E0"), Buf("E1"), Buf("E2")]
                    for c in range(4):
                        P.dma("pool", ck_tok[:, c, :, :], ck_d[:, c * 128:(c + 1) * 128, :].rearrange("h l d -> l h d"),
                              writes=[Bck])
                    P.dma("pool", m01[:], m01_d.rearrange("m k q -> k m q"), writes=[Bm01])
                    P.v("pool", "memset", ctxVp[:, :, :, 64:128], 0.0, W=[BctxV])
                    cv4 = cv_d.rearrange("(a two) l d -> two l a d", two=2)
                    for c in range(4):
                        for par in range(2):
                            P.dma("pool", ctxVp[:, c, :, par * 128:par * 128 + 64], cv4[par, c * 128:(c + 1) * 128, :, :],
                                  writes=[BctxV])

                    def build_E(h):
                        hb, eb = h % 2, h % 3
                        P.dma("pool", rbb[:, hb, :, :], rbt_d[h].rearrange("d k q -> k d q"), writes=[Brbb[hb]])
                        P.v("act", "activation", rbb[:, hb, :, :], rbb[:, hb, :, :], AF.Exp, R=[Brbb[hb]], W=[Brbb[hb]])
                        for s2 in range(2):
                            P.v("pool", "tensor_tensor", Etab[:, eb, s2 * 7:(s2 + 1) * 7, :], rbb[:, hb, :, :],
                                m01[:, s2 * 7:(s2 + 1) * 7, :], ALU.mult, R=[Brbb[hb], Bm01], W=[BE[eb]])

                    for c in range(4):
                        pTb = psum[c % 2][:].bitcast(BF16)
                        for a in range(8):
                            P.tr(pTb[:, a * 128:(a + 1) * 128], ck_tok[:, c, 2 * a:2 * a + 2, :].rearrange("p h d -> p (h d)"),
                                 ident_b[:], reads=[Bck, Bconst], writes=[PB[c % 2]])
                        P.v("dve" if c % 2 == 0 else "act", "tensor_copy" if c % 2 == 0 else "copy",
                            ctxKT[:, :, c * 128:(c + 1) * 128], pTb.rearrange("p (a t) -> p a t", a=8),
                            R=[PB[c % 2]], W=[BctxK])
                    build_E(0)
                    build_E(1)
                    build_E(2)
                    for a in range(8):
                        for qb in range(2):
                            acc = ACC[(2 * a + qb) % 2]
                            zero_acc(acc, 512)
                            for rho in range(2):
                                h = 2 * a + rho
                                eb = h % 3
                                pr = slice(64 * rho, 64 * rho + 64)
                                vc = slice(64 * rho, 64 * rho + 128)
                                for (m, jlo, cnt, e2) in _SEGS[qb]:
                                    nq, q0 = cnt * 128, jlo * 128
                                    i0 = e2 * 7 + 3 - m + jlo
                                    esl = Etab[:, eb, i0:i0 + cnt, :].rearrange("p a b -> p (a b)")
                                    segment(PT, BPT, KT_s[pr, a, m * 128:(m + 1) * 128], QT_s[pr, a, q0:q0 + nq], nq,
                                            BKs, BQs, esl, BE[eb], Vp_s[:, m, a, vc], BVs, onesp[:, vc], acc, q0 - 512 * qb)
                                for c in range(4):
                                    segment(PT, BPT, ctxKT[pr, a, c * 128:(c + 1) * 128], QT_s[pr, a, qb * 512:(qb + 1) * 512],
                                            512, BctxK, BQs, None, None, ctxVp[:, c, a, vc], BctxV, onesp[:, vc], acc, 0)
                            normalize(acc, 512, rS, BrS, OT_all[:, a, qb * 512:(qb + 1) * 512], [BOT[a], Bck])
                        for h2 in (2 * a + 3, 2 * a + 4):
                            if h2 < 16:
                                build_E(h2)
                    drain(0)
                    P.end_phase()

                yb_s = [QT_s[:, 2 * i:2 * i + 2, :].rearrange("p a t -> p (a t)").bitcast(F32) for i in range(4)]
                bc_s = [KT_s[:, 2 * i:2 * i + 2, :].rearrange("p a t -> p (a t)").bitcast(F32) for i in range(3)]
                dg_s = KT_s[:, 6:8, :].rearrange("p a t -> p (a t)").bitcast(F32).rearrange("p (a b) -> p a b", b=128)
                Bbc_s = [Buf("sbc%d" % i) for i in range(3)]
                prep_vectors(1, bc_s, Bbc_s, dg_s, Buf("sdg"), [])
                proj_group([4, 5, 6, 7], [0, 128, 256, 384], yb_s, [], bc_s, Bbc_s)
                proj_group([8, 9, 10, 11], [512, 640, 768, 896], yb_s, [], bc_s, Bbc_s)
                if stop == "l1mix":
                    store_x("l1mix")
                P.end_phase()
        if stop == "l1mix":
            return nc

        moe_phase(1, None, True)
        return nc


_CACHE = {}


def make_in_maps(inputs):
    f = lambda a: np.ascontiguousarray(np.asarray(a, dtype=np.float32))
    x_prompt, x_sample = f(inputs["x_prompt"]), f(inputs["x_sample"])
    cache_k, cache_v = f(inputs["cache_k"]), f(inputs["cache_v"])
    c, c_ctx = f(inputs["c"]), f(inputs["c_ctx"])
    b_ada = f(inputs["b_ada"])
    shared = {
        "w_ada": f(inputs["w_ada"]),
        "b_adaT": np.ascontiguousarray(b_ada.reshape(2, 48, 128).transpose(0, 2, 1)),
        "ln1_g": f(inputs["ln1_g"]), "ln1_b": f(inputs["ln1_b"]),
        "ln2_g": f(inputs["ln2_g"]), "ln2_b": f(inputs["ln2_b"]),
        "pool_w": f(inputs["pool_w"])[0],
        "pool_scale": f(inputs["pool_scale"]).reshape(1, D),
        "w_qkv": f(inputs["w_qkv"])[0],
        "w_o": f(inputs["w_o"])[0],
        "rbt": _rel_bias_layout(f(inputs["rel_bias"])[0]),
        "w_router": f(inputs["w_router"]),
        "b_router": f(inputs["b_router"]).reshape(1, 16),
        "w_gate": f(inputs["w_gate"]), "w_up": f(inputs["w_up"]), "w_down": f(inputs["w_down"]),
        "ptab": _pool_tables(),
        "m01": _mask01_table(),
    }
    in_maps = []
    for i in range(N_CORES):
        m = dict(shared)
        m["xin"] = np.ascontiguousarray(np.concatenate(
            [x_prompt[2 * i].reshape(256, D), x_prompt[2 * i + 1].reshape(256, D), x_sample[i]], axis=0))
        m["ck"] = np.ascontiguousarray(cache_k[i, 0])
        m["cv"] = np.ascontiguousarray(cache_v[i, 0])
        cond = np.stack([c_ctx, c[i]], axis=-1)
        m["condT"] = np.ascontiguousarray(cond.reshape(8, 128, 2).transpose(1, 0, 2))
        in_maps.append(m)
    return in_maps


def kernel(**inputs):
    if "nc" not in _CACHE:
        _CACHE["nc"] = build_nc()
    nc = _CACHE["nc"]
    in_maps = make_in_maps(inputs)
    res = run_bass_kernel_spmd(nc, in_maps, core_ids=list(range(N_CORES)))
    ys = [r["y"] for r in res.results]
    y_prompt = np.stack([y[:512].reshape(2, 256, D) for y in ys]).reshape(16, 256, D)
    y_sample = np.stack([y[512:] for y in ys])
    nk = np.stack([r["nk"] for r in res.results]).reshape(16, 1, 16, 256, 64)
    nv = np.stack([r["nv"] for r in res.results]).reshape(16, 1, 16, 256, 64)
    return (y_prompt.astype(np.float32), y_sample.astype(np.float32),
            nk.astype(np.float32), nv.astype(np.float32))
```

```python
from contextlib import ExitStack

import numpy as np
import concourse.bass as bass
import concourse.mybir as mybir
from concourse.bass_utils import run_bass_kernel_spmd

F32 = mybir.dt.float32
BF16 = mybir.dt.bfloat16
AF = mybir.ActivationFunctionType
ALU = mybir.AluOpType
AX = mybir.AxisListType

N_CORES = 8
D = 1024
NT = 12
T = NT * 128
ALPHA = 4.0 ** 0.25
LN_EPS = 1e-5
NEG = -30000.0
POOL_SIZES = (2, 4, 8, 16)
SEQS = ((0, 2), (2, 2), (4, 8))


def tile_cond(t):
    return 0 if t < 4 else 1


class Buf:
    __slots__ = ("name", "writer", "readers", "excl")

    def __init__(self, name="", excl=False):
        self.name = name
        self.writer = None
        self.readers = []
        self.excl = excl


class Lane:
    __slots__ = ("sem", "n", "last")

    def __init__(self, sem):
        self.sem = sem
        self.n = 0
        self.last = None


class Ins:
    __slots__ = ("eng", "fn", "deps", "signal", "count", "lane", "lval", "idx")

    def __init__(self, eng, fn):
        self.eng = eng
        self.fn = fn
        self.deps = []
        self.signal = False
        self.count = 0
        self.lane = None
        self.lval = 0
        self.idx = 0


class Prog:
    ENG = ("pe", "act", "dve", "pool", "sp")
    NLANES = {"sp": 24, "pool": 12, "act": 4}

    def __init__(self, nc, es):
        self.nc = nc
        self.es = es
        self.eng = {"pe": nc.tensor, "act": nc.scalar, "dve": nc.vector,
                    "pool": nc.gpsimd, "sp": nc.sync}
        self.sem = {e: es.enter_context(nc.semaphore("s_" + e)) for e in self.ENG}
        self.cnt = {e: 0 for e in self.ENG}
        self.bar = es.enter_context(nc.semaphore("s_bar"))
        self.nbar = 0
        self.lanes = {q: [Lane(es.enter_context(nc.semaphore("l_%s%d" % (q, i)))) for i in range(n)]
                      for q, n in self.NLANES.items()}
        self.lane_rr = {q: 0 for q in self.NLANES}
        self.waited = {e: {} for e in self.ENG}
        self.streams = {e: [] for e in self.ENG}
        self.touched = []
        self.n_ins = 0

    def _add_dep(self, ins, p):
        if p is None or p is ins:
            return
        if p.lane is None and ins.lane is None and p.eng == ins.eng and p.eng == "pe":
            return
        deps = ins.deps
        if p.lane is None:
            for i, q in enumerate(deps):
                if q.lane is None and q.eng == p.eng:
                    if p.idx > q.idx:
                        deps[i] = p
                    return
        elif p in deps:
            return
        deps.append(p)

    def _deps(self, ins, reads, writes):
        if any(b.excl for b in reads):
            writes = list(writes) + [b for b in reads if b.excl and b not in writes]
            reads = [b for b in reads if not b.excl]
        same = lambda p: (p is not None and p.lane is None and ins.lane is None and p.eng == ins.eng)
        for b in reads:
            self._add_dep(ins, b.writer)
        for b in writes:
            if not same(b.writer):
                self._add_dep(ins, b.writer)
            for r in b.readers:
                if not same(r):
                    self._add_dep(ins, r)
        for b in reads:
            if not b.readers and b.writer is None:
                self.touched.append(b)
            b.readers.append(ins)
        for b in writes:
            if not b.readers and b.writer is None:
                self.touched.append(b)
            b.writer = ins
            b.readers = []

    def op(self, eng, fn, reads=(), writes=()):
        ins = Ins(eng, fn)
        ins.idx = len(self.streams[eng])
        self._deps(ins, reads, writes)
        self.streams[eng].append(ins)
        return ins

    def dma(self, eng, out, in_, reads=(), writes=(), **kw):
        E = self.eng[eng]
        ins = Ins(eng, lambda: E.dma_start(out=out, in_=in_, **kw))
        ins.idx = len(self.streams[eng])
        lanes = self.lanes[eng]
        lane = lanes[self.lane_rr[eng] % len(lanes)]
        self.lane_rr[eng] += 1
        ins.lane = lane
        lane.n += 1
        ins.lval = 16 * lane.n
        if lane.last is not None:
            ins.deps.append(lane.last)
        lane.last = ins
        self._deps(ins, reads, writes)
        self.streams[eng].append(ins)
        return ins

    def mm(self, out, lhsT, rhs, start, stop, reads=(), writes=(), skip=False):
        t = self.nc.tensor
        if skip:
            return self.op("pe", lambda: t.matmul(out, lhsT, rhs, start=start, stop=stop, skip_group_check=True),
                           reads, writes)
        return self.op("pe", lambda: t.matmul(out, lhsT, rhs, start=start, stop=stop), reads, writes)

    def tr(self, out, in_, ident, reads=(), writes=()):
        t = self.nc.tensor
        return self.op("pe", lambda: t.transpose(out, in_, ident), reads, writes)

    def v(self, eng, meth, *args, R=(), W=(), **kw):
        E = self.eng[eng]
        return self.op(eng, lambda: getattr(E, meth)(*args, **kw), R, W)

    def end_phase(self):
        ENG = self.ENG
        for e in ENG:
            for ins in self.streams[e]:
                for p in ins.deps:
                    if p.lane is None:
                        p.signal = True
            for ins in reversed(self.streams[e]):
                if ins.lane is None:
                    ins.signal = True
                    break
        for e in ENG:
            for ins in self.streams[e]:
                if ins.lane is None and ins.signal:
                    self.cnt[e] += 1
                    ins.count = self.cnt[e]
        for e in ENG:
            E = self.eng[e]
            wd = self.waited[e]
            for ins in self.streams[e]:
                for p in ins.deps:
                    if p.lane is not None:
                        sem, val = p.lane.sem, p.lval
                    else:
                        sem, val = self.sem[p.eng], p.count
                    if wd.get(sem.num, 0) >= val:
                        continue
                    wd[sem.num] = val
                    E.wait_ge(sem, val)
                bi = ins.fn()
                if ins.lane is not None:
                    bi.then_inc(ins.lane.sem, 16)
                elif ins.signal:
                    bi.then_inc(self.sem[e], 1)
                self.n_ins += 1
        sp = self.eng["sp"]
        wd = self.waited["sp"]
        all_lanes = [l for q in self.lanes.values() for l in q]
        for e in ENG:
            if e == "sp" or self.cnt[e] == 0:
                continue
            if wd.get(self.sem[e].num, 0) < self.cnt[e]:
                sp.wait_ge(self.sem[e], self.cnt[e])
        for l in all_lanes:
            if l.n and wd.get(l.sem.num, 0) < 16 * l.n:
                sp.wait_ge(l.sem, 16 * l.n)
        self.nbar += 1
        sp.sem_inc(self.bar, 1)
        for e in ENG:
            if e != "sp":
                self.eng[e].wait_ge(self.bar, self.nbar)
            for e2 in ENG:
                self.waited[e][self.sem[e2].num] = self.cnt[e2]
            for l in all_lanes:
                self.waited[e][l.sem.num] = 16 * l.n
        for l in all_lanes:
            l.last = None
        for b in self.touched:
            b.writer = None
            b.readers = []
        self.touched = []
        self.streams = {e: [] for e in ENG}


def _pool_tables():
    n = 1024
    out = np.zeros((4, 5, 128, 128), np.float32)
    t = np.arange(n)
    for wi, w in enumerate(POOL_SIZES):
        lo = np.clip(t - w // 2, 0, n)
        hi = np.clip(t - w // 2 + w, 0, n)
        cnt = (hi - lo).astype(np.float64)
        s = np.arange(n)[:, None]
        Pm = ((s >= lo[None, :]) & (s < hi[None, :])) / cnt[None, :] - np.eye(n)
        Pm = Pm.astype(np.float32)
        out[wi, 0] = Pm[0:128, 0:128]
        out[wi, 1] = Pm[128:256, 128:256]
        out[wi, 2] = Pm[896:1024, 896:1024]
        out[wi, 3] = Pm[0:128, 128:256]
        out[wi, 4] = Pm[256:384, 128:256]
    return out


def _chunks_for(j):
    if j < 2:
        return list(range(0, 4))
    if j > 5:
        return list(range(4, 8))
    return list(range(j - 2, j + 3))


def _mask_tables():
    tiles = []
    index = {}
    keymap = {}
    col = np.arange(64)
    cs = np.clip(col - 8, 0, 48)
    colok = (col[None, :] >= cs[:, None]) & (col[None, :] < cs[:, None] + 16)
    for j in range(8):
        for m in _chunks_for(j):
            mt = np.full((2, 64, 2, 64), NEG, np.float32)
            for rq in range(2):
                r = 2 * j + rq
                rs = min(max(r - 4, 0), 8)
                for rk in range(2):
                    kr = 2 * m + rk
                    if rs <= kr < rs + 8:
                        mt[rk, :, rq, :] = np.where(colok.T, 0.0, NEG)
            key = mt.tobytes()
            if key not in keymap:
                keymap[key] = len(tiles)
                tiles.append(mt.reshape(128, 128))
            index[(j, m)] = keymap[key]
    return np.stack(tiles), index


_MASKS, _MASK_IDX = _mask_tables()
_M_FULL, _M_A, _M_B = _MASK_IDX[(0, 0)], _MASK_IDX[(2, 0)], _MASK_IDX[(2, 4)]


def _mask01_table():
    full = (_MASKS[_M_FULL] == 0).astype(np.float32)
    t = np.stack([full] * 14)
    t[7 + (3 - 2)] = (_MASKS[_M_B] == 0)
    t[7 + (3 + 2)] = (_MASKS[_M_A] == 0)
    return t


def _segments():
    segs = {0: [], 1: []}
    for m in range(8):
        for qb in range(2):
            js = [j for j in range(8) if m in _chunks_for(j) and j // 4 == qb]
            if not js:
                continue
            assert js == list(range(js[0], js[-1] + 1))
            e2 = any(_MASK_IDX[(j, m)] != _M_FULL for j in js)
            for j in js:
                want = _M_FULL
                if e2 and m - j == -2:
                    want = _M_A
                if e2 and m - j == 2:
                    want = _M_B
                assert _MASK_IDX[(j, m)] == want, (j, m)
            segs[qb].append((m, js[0], len(js), int(e2)))
    return segs


_SEGS = _segments()


def _rel_bias_layout(rel_bias):
    rk = np.arange(2)[:, None, None, None]
    ck = np.arange(64)[None, :, None, None]
    rq = np.arange(2)[None, None, :, None]
    cq = np.arange(64)[None, None, None, :]
    out = np.empty((16, 7, 128, 128), np.float32)
    for i in range(7):
        dl = 3 - i
        dy = np.clip(2 * dl + rk - rq + 7, 0, 14)
        dx = np.clip(ck - cq + 15, 0, 30)
        dy, dx = np.broadcast_arrays(dy, dx)
        out[:, i] = rel_bias[:, dy, dx].reshape(16, 128, 128)
    return out


def build_nc(stop=None):
    nc = bass.Bass("TRN2", target_bir_lowering=False)

    def din(name, shape):
        return nc.dram_tensor(name, list(shape), F32, kind="ExternalInput").ap()

    def dout(name, shape):
        return nc.dram_tensor(name, list(shape), F32, kind="ExternalOutput").ap()

    xin = din("xin", [T, D])
    ck_d = din("ck", [16, 512, 64])
    cv_d = din("cv", [16, 512, 64])
    condT_d = din("condT", [128, 8, 2])
    w_ada_d = din("w_ada", [2, D, 6 * D])
    b_adaT_d = din("b_adaT", [2, 128, 48])
    ln1g_d = din("ln1_g", [2, D])
    ln1b_d = din("ln1_b", [2, D])
    ln2g_d = din("ln2_g", [2, D])
    ln2b_d = din("ln2_b", [2, D])
    pool_w_d = din("pool_w", [4, 256, 256])
    pool_scale_d = din("pool_scale", [1, D])
    w_qkv_d = din("w_qkv", [D, 3 * D])
    w_o_d = din("w_o", [D, D])
    rbt_d = din("rbt", [16, 7, 128, 128])
    w_router_d = din("w_router", [D, 16])
    b_router_d = din("b_router", [1, 16])
    w_gate_d = din("w_gate", [2, 16, D, 512])
    w_up_d = din("w_up", [2, 16, D, 512])
    w_down_d = din("w_down", [2, 16, 512, D])
    ptab_d = din("ptab", [4, 5, 128, 128])
    m01_d = din("m01", [14, 128, 128])

    y_d = dout("y", [T, D])
    nk_d = dout("nk", [2, 16, 256, 64])
    nv_d = dout("nv", [2, 16, 256, 64])
    dbg_d = dout("dbg", [128, 2, 48, 2]) if stop == "ada" else None

    with ExitStack() as es:
        P = Prog(nc, es)

        _uid = [0]

        def sbuf(scope, name, shape, dt=F32):
            _uid[0] += 1
            return scope.enter_context(nc.sbuf_tensor("sb%d_%s" % (_uid[0], name), list(shape), dt))

        x_sb = sbuf(es, "x_sb", [128, NT, D])
        hT = sbuf(es, "hT", [128, 8, T], BF16)
        ident_f = sbuf(es, "ident_f", [128, 128])
        ident_b = sbuf(es, "ident_b", [128, 128], BF16)
        ones_f = sbuf(es, "ones_f", [128, 128])
        modcol = sbuf(es, "modcol", [128, 2, 48, 2])
        cmb = sbuf(es, "cmb", [128, NT, 16])
        small = sbuf(es, "small", [128, 64])
        psum = [es.enter_context(nc.psum_tensor("ps%d" % i, [128, 512], F32)) for i in range(8)]
        PB = [Buf("ps%d" % i, excl=True) for i in range(8)]

        Bx = [Buf("x%d" % t) for t in range(NT)]
        BhT = [(Buf("hTd%d" % t), Buf("hTa%d" % t)) for t in range(NT)]
        Bconst = Buf("const")
        Bmod = Buf("modcol")
        Bcmb = [Buf("cmb%d" % t) for t in range(NT)]


        def alpha_col(layer, v, ch, c):
            return modcol[:, layer, v * 8 + ch, c:c + 1]

        with ExitStack() as ph:
            condT = sbuf(ph, "condT", [128, 8, 2])
            sT = sbuf(ph, "sT", [128, 8, 2])
            sTb = sbuf(ph, "sTb", [128, 8, 2], BF16)
            b_col = sbuf(ph, "b_col", [128, 2, 48])
            wa = [sbuf(ph, "wa%d" % i, [128, 8, 512], BF16) for i in range(3)]
            Bwa = [Buf("wa%d" % i) for i in range(3)]
            BsT = Buf("sT")
            Bbcol = Buf("bcol")

            P.v("pool", "memset", ident_f[:], 0.0, W=[Bconst])
            P.v("pool", "affine_select", ident_f[:], ident_f[:], [[-1, 128]], ALU.not_equal, 1.0,
                base=0, channel_multiplier=1, R=[Bconst], W=[Bconst])
            P.v("pool", "tensor_copy", ident_b[:], ident_f[:], R=[Bconst], W=[Bconst])
            P.v("pool", "memset", ones_f[:], 1.0, W=[Bconst])

            for t in range(NT):
                P.dma("sp", x_sb[:, t, :], xin[t * 128:(t + 1) * 128, :], writes=[Bx[t]])
            P.dma("sp", condT[:], condT_d[:, :, :], writes=[BsT])
            P.dma("sp", b_col[:], b_adaT_d.rearrange("l p c -> p l c"), writes=[Bbcol])
            P.v("act", "activation", sT[:], condT[:], AF.Silu, R=[BsT], W=[BsT])
            P.v("dve", "tensor_copy", sTb[:], sT[:], R=[BsT], W=[BsT])

            nblk = 0
            for layer in range(2):
                pc = psum[layer]
                for nb in range(12):
                    slot = nblk % 3
                    src = w_ada_d[layer].rearrange("(kc k) n -> k kc n", k=128)[:, :, nb * 512:(nb + 1) * 512]
                    P.dma("pool", wa[slot][:], src, writes=[Bwa[slot]])
                    for n4 in range(4):
                        ch = nb * 4 + n4
                        for kc in range(8):
                            P.mm(pc[:, ch * 2:ch * 2 + 2], wa[slot][:, kc, n4 * 128:(n4 + 1) * 128],
                                 sTb[:, kc, :], kc == 0, kc == 7,
                                 reads=[Bwa[slot], BsT], writes=[PB[layer]])
                    nblk += 1
                P.v("dve", "tensor_tensor", modcol[:, layer, :, :],
                    pc[:, 0:96].rearrange("p (c k) -> p c k", k=2),
                    b_col[:, layer, :].unsqueeze(2).to_broadcast([128, 48, 2]), ALU.add,
                    R=[PB[layer], Bbcol], W=[Bmod])
                for v in (1, 4):
                    P.v("dve", "tensor_scalar_add", modcol[:, layer, v * 8:(v + 1) * 8, :],
                        modcol[:, layer, v * 8:(v + 1) * 8, :], 1.0, R=[Bmod], W=[Bmod])
            if stop == "ada":
                P.dma("sp", dbg_d[:, :, :, :], modcol[:], reads=[Bmod])
            P.end_phase()

        def load_bc(dst, src_row, W):
            P.dma("sp", dst, src_row.partition_broadcast(128), writes=W)

        def make_bc(dst, layer, v, c, dgall, Bdg, banks, W):
            P.v("dve", "tensor_tensor", dgall[:], ident_f[:].unsqueeze(1).to_broadcast([128, 8, 128]),
                modcol[:, layer, v * 8:(v + 1) * 8, c:c + 1].to_broadcast([128, 8, 128]), ALU.mult,
                R=[Bconst, Bmod], W=[Bdg])
            for half in range(2):
                pb, Bp = banks[half]
                P.mm(pb[:], ones_f[:], dgall[:, half * 4:(half + 1) * 4, :].rearrange("p a b -> p (a b)"),
                     True, True, reads=[Bconst, Bdg], writes=[Bp])
                P.v("act", "copy", dst[:, half * 512:(half + 1) * 512], pb[:], R=[Bp], W=W)

        Bsmall = [Buf("mv%d" % i) for i in range(4)]

        def ln_group(tiles, ys, Bys, g_bc, b_bc, Bvec, next_mod, banks, gmul_eng):
            n = len(tiles)
            sts = [small[:, 16 * i:16 * i + 12].rearrange("p (a b) -> p a b", b=6) for i in range(n)]
            mvv = [small[:, 16 * i + 12:16 * i + 16] for i in range(n)]
            add_eng = "pool" if gmul_eng == "dve" else "dve"
            for i in range(n):
                P.v("dve", "bn_stats", sts[i][:, 0, :], ys[i][:, 0:512], R=[Bys[i]], W=[Bsmall[i]])
                P.v("dve", "bn_stats", sts[i][:, 1, :], ys[i][:, 512:1024], R=[Bys[i]], W=[Bsmall[i]])
                P.v("dve", "bn_aggr", mvv[i][:, 0:2], sts[i], R=[Bsmall[i]], W=[Bsmall[i]])
            for i in range(n):
                P.v("act", "activation", mvv[i][:, 2:3], mvv[i][:, 1:2], AF.Sqrt, bias=LN_EPS,
                    R=[Bsmall[i]], W=[Bsmall[i]])
            for i in range(n):
                P.v("dve", "reciprocal", mvv[i][:, 2:3], mvv[i][:, 2:3], R=[Bsmall[i]], W=[Bsmall[i]])
                P.v("dve", "tensor_scalar", mvv[i][:, 3:4], mvv[i][:, 0:1], mvv[i][:, 2:3], -1.0,
                    op0=ALU.mult, op1=ALU.mult, R=[Bsmall[i]], W=[Bsmall[i]])
            for i in range(n):
                P.v("act", "activation", ys[i], ys[i], AF.Identity, scale=mvv[i][:, 2:3], bias=mvv[i][:, 3:4],
                    R=[Bys[i], Bsmall[i]], W=[Bys[i]])
            for i in range(n):
                P.v(gmul_eng, "tensor_tensor", ys[i], ys[i], g_bc, ALU.mult, R=[Bys[i]] + list(Bvec), W=[Bys[i]])
            for i, t in enumerate(tiles):
                P.v("pool", "tensor_tensor", x_sb[:, t, :], ys[i], b_bc, ALU.add, R=[Bys[i]] + list(Bvec), W=[Bx[t]])
            if next_mod is None:
                return
            layer, vA, vB = next_mod
            nb = len(banks) // 2
            for i, t in enumerate(tiles):
                c = tile_cond(t)
                pair = banks[2 * (i % nb):2 * (i % nb) + 2]
                for half in range(2):
                    pb, Bp = pair[half]
                    for q in range(4):
                        kc = half * 4 + q
                        P.tr(pb[:, q * 128:(q + 1) * 128], x_sb[:, t, kc * 128:(kc + 1) * 128], ident_f[:],
                             reads=[Bx[t], Bconst], writes=[Bp])
                for half in range(2):
                    pb, Bp = pair[half]
                    for q in range(4):
                        kc = half * 4 + q
                        if half == 0:
                            P.v("dve", "tensor_scalar", hT[:, kc, t * 128:(t + 1) * 128], pb[:, q * 128:(q + 1) * 128],
                                alpha_col(layer, vA, kc, c), alpha_col(layer, vB, kc, c),
                                op0=ALU.mult, op1=ALU.add, R=[Bp, Bmod], W=[BhT[t][0]])
                        else:
                            P.v("act", "activation", hT[:, kc, t * 128:(t + 1) * 128], pb[:, q * 128:(q + 1) * 128],
                                AF.Identity, scale=alpha_col(layer, vA, kc, c), bias=alpha_col(layer, vB, kc, c),
                                R=[Bp, Bmod], W=[BhT[t][1]])

        PBK = [(psum[i], PB[i]) for i in range(8)]

        def store_x(label):
            for t in range(NT):
                P.dma("sp", y_d[t * 128:(t + 1) * 128, :], x_sb[:, t, :], reads=[Bx[t]])

        if stop == "ada":
            store_x("ada")
            P.end_phase()
            return nc

        with ExitStack() as ph:
            bc = [sbuf(ph, "bc%d" % i, [128, D]) for i in range(8)]
            Bbc = [Buf("bc%d" % i) for i in range(8)]
            psc = sbuf(ph, "psc", [128, D])
            Bpsc = Buf("psc")
            dgall = sbuf(ph, "dgall", [128, 8, 128])
            Bdg = Buf("dgall")
            h_tok = sbuf(ph, "h_tok", [128, NT, D])
            Bh = [Buf("h%d" % t) for t in range(NT)]
            ptab = sbuf(ph, "ptab", [128, 4, 5, 128])
            Bptab = Buf("ptab")
            pw = sbuf(ph, "pw", [128, 4, 2, 256], BF16)
            Bpw = Buf("pw")
            pTt = [sbuf(ph, "pTt%d" % i, [128, 8, 128], BF16) for i in range(2)]
            BpTt = [Buf("pTt0"), Buf("pTt1")]
            ybuf = [sbuf(ph, "ybuf%d" % i, [128, D]) for i in range(4)]
            By = [Buf("y%d" % i) for i in range(4)]

            P.dma("sp", ptab[:], ptab_d.rearrange("w k s t -> s w k t"), writes=[Bptab])
            P.dma("pool", pw[:], pool_w_d.rearrange("g (i c) d -> c g i d", c=128), writes=[Bpw])
            load_bc(psc[:], pool_scale_d[0:1, :], [Bpsc])
            load_bc(bc[6][:], ln1g_d[0:1, :], [Bbc[6]])
            load_bc(bc[7][:], ln1b_d[0:1, :], [Bbc[7]])
            for c in range(2):
                make_bc(bc[2 * c], 0, 1, c, dgall, Bdg, PBK[0:2], [Bbc[2 * c]])
                make_bc(bc[2 * c + 1], 0, 0, c, dgall, Bdg, PBK[2:4], [Bbc[2 * c + 1]])
                make_bc(bc[4 + c], 0, 2, c, dgall, Bdg, PBK[4:6], [Bbc[4 + c]])
                P.v("pool", "tensor_tensor", bc[4 + c][:], bc[4 + c][:], psc[:], ALU.mult,
                    R=[Bbc[4 + c], Bpsc], W=[Bbc[4 + c]])
            for t in range(NT):
                c = tile_cond(t)
                P.v("dve", "tensor_tensor", h_tok[:, t, :], x_sb[:, t, :], bc[2 * c][:], ALU.mult,
                    R=[Bx[t], Bbc[2 * c]], W=[Bh[t]])
                P.v("pool", "tensor_tensor", h_tok[:, t, :], h_tok[:, t, :], bc[2 * c + 1][:], ALU.add,
                    R=[Bh[t], Bbc[2 * c + 1]], W=[Bh[t]])
            seq_of = {}
            for (t0, ntl) in SEQS:
                for q in range(ntl):
                    seq_of[t0 + q] = (q, ntl)
            for grp in range(3):
                tiles = list(range(4 * grp, 4 * grp + 4))
                for i, t in enumerate(tiles):
                    q, ntl = seq_of[t]
                    c = tile_cond(t)
                    k = i % 2
                    (pA, BA), (pBk, BB) = PBK[2 * k], PBK[2 * k + 1]
                    kind = 0 if q == 0 else (2 if q == ntl - 1 else 1)
                    srcs = [(t, kind)]
                    if q > 0:
                        srcs.append((t - 1, 3))
                    if q < ntl - 1:
                        srcs.append((t + 1, 4))
                    for kc in range(8):
                        wi = kc // 2
                        bank, Bb = (pA, BA) if kc < 4 else (pBk, BB)
                        dst = bank[:, (kc % 4) * 128:(kc % 4 + 1) * 128]
                        for si, (ts, kd) in enumerate(srcs):
                            P.mm(dst, h_tok[:, ts, kc * 128:(kc + 1) * 128], ptab[:, wi, kd, :],
                                 si == 0, si == len(srcs) - 1, reads=[Bh[ts], Bptab], writes=[Bb])
                    P.v("act", "copy", pTt[k][:, 0:4, :], pA[:].rearrange("p (a b) -> p a b", b=128),
                        R=[BA], W=[BpTt[k]])
                    P.v("act", "copy", pTt[k][:, 4:8, :], pBk[:].rearrange("p (a b) -> p a b", b=128),
                        R=[BB], W=[BpTt[k]])
                    for half in range(2):
                        bank, Bb = (pA, BA) if half == 0 else (pBk, BB)
                        for gg in range(2):
                            g = half * 2 + gg
                            for i2 in range(2):
                                P.mm(bank[:, gg * 256:(gg + 1) * 256], pTt[k][:, 2 * g + i2, :], pw[:, g, i2, :],
                                     i2 == 0, i2 == 1, reads=[BpTt[k], Bpw], writes=[Bb])
                        P.v("dve", "tensor_tensor", ybuf[i][:, half * 512:(half + 1) * 512], bank[:],
                            bc[4 + c][:, half * 512:(half + 1) * 512], ALU.mult, R=[Bb, Bbc[4 + c]], W=[By[i]])
                    P.v("dve", "scalar_tensor_tensor", ybuf[i][:], x_sb[:, t, :], ALPHA, ybuf[i][:],
                        op0=ALU.mult, op1=ALU.add, R=[Bx[t], By[i]], W=[By[i]])
                ln_group(tiles, [ybuf[i][:] for i in range(4)], By, bc[6][:], bc[7][:], [Bbc[6], Bbc[7]],
                         (0, 4, 3), PBK[4:8], "pool")
            if stop == "l0mix":
                store_x("l0mix")
            P.end_phase()
        if stop == "l0mix":
            return nc


        def moe_phase(layer, next_mod, final):
            with ExitStack() as ph:
                bc = [sbuf(ph, "bc%d" % i, [128, D]) for i in range(4)]
                Bbc = [Buf("mbc%d" % i) for i in range(4)]
                dgall = sbuf(ph, "dgall", [128, 8, 128])
                Bdg = Buf("dgall")
                wg = [sbuf(ph, "wg%d" % i, [128, 8, 512], BF16) for i in range(2)]
                wu = [sbuf(ph, "wu%d" % i, [128, 8, 512], BF16) for i in range(2)]
                wd = [sbuf(ph, "wd%d" % i, [128, 4, D], BF16) for i in range(2)]
                Bwg = [Buf("wg0"), Buf("wg1")]
                Bwu = [Buf("wu0"), Buf("wu1")]
                Bwd = [Buf("wd0"), Buf("wd1")]
                heT = [sbuf(ph, "heT%d" % i, [128, 4, 512], BF16) for i in range(2)]
                Bhe = [Buf("he0"), Buf("he1")]
                sg = [sbuf(ph, "sg%d" % i, [128, 512], BF16) for i in range(2)]
                Bsg = [Buf("sg0"), Buf("sg1")]
                yacc = sbuf(ph, "yacc", [128, NT, D])
                Byacc = [Buf("yacc%d" % t) for t in range(NT)]
                wr_b = sbuf(ph, "wr_b", [128, 8, 16], BF16)
                brt = sbuf(ph, "brt", [128, 16])
                Bwr = Buf("wr")
                Bbrt = Buf("brt")
                rt = sbuf(ph, "rt", [128, 1472])
                Brt = Buf("rt")
                G = [psum[0], psum[1]]
                U = [psum[2], psum[3]]
                Dn = [psum[4], psum[5], psum[6], psum[7]]
                BG, BU, BD = [PB[0], PB[1]], [PB[2], PB[3]], [PB[4], PB[5], PB[6], PB[7]]

                def load_expert(e):
                    sl = e % 2
                    P.dma("pool", wg[sl][:], w_gate_d[layer, e].rearrange("(kc k) f -> k kc f", k=128), writes=[Bwg[sl]])
                    P.dma("pool", wu[sl][:], w_up_d[layer, e].rearrange("(kc k) f -> k kc f", k=128), writes=[Bwu[sl]])
                    P.dma("pool", wd[sl][:], w_down_d[layer, e].rearrange("(fc f) d -> f fc d", f=128), writes=[Bwd[sl]])

                P.dma("pool", wr_b[:], w_router_d.rearrange("(kc k) e -> k kc e", k=128), writes=[Bwr])
                load_expert(0)
                load_expert(1)
                P.dma("sp", brt[:], b_router_d[0:1, :].partition_broadcast(128), writes=[Bbrt])
                load_bc(bc[2][:], ln2g_d[layer:layer + 1, :], [Bbc[2]])
                load_bc(bc[3][:], ln2b_d[layer:layer + 1, :], [Bbc[3]])
                for c in range(2):
                    make_bc(bc[c], layer, 5, c, dgall, Bdg, PBK[6:8], [Bbc[c]])

                lg = psum[4]
                for t in range(NT):
                    for kc in range(8):
                        P.mm(lg[:, t * 16:(t + 1) * 16], hT[:, kc, t * 128:(t + 1) * 128], wr_b[:, kc, :],
                             kc == 0, kc == 7, reads=[BhT[t][0], BhT[t][1], Bwr], writes=[PB[4]])
                o = [0]

                def carve(nel):
                    v_ = rt[:, o[0]:o[0] + nel]
                    o[0] += nel
                    return v_
                s_sb, sel, t6, gs, gm = carve(192), carve(192), carve(288), carve(48), carve(12)
                gmask, msel, top8, emask, wsel, wsum = carve(48), carve(192), carve(96), carve(192), carve(192), carve(12)
                Rr, Wr = [Brt], [Brt]
                P.v("act", "activation", s_sb, lg[:, 0:192], AF.Sigmoid, R=[PB[4]], W=Wr)
                P.v("dve", "tensor_tensor", sel.rearrange("p (t e) -> p t e", e=16),
                    s_sb.rearrange("p (t e) -> p t e", e=16), brt[:].unsqueeze(1).to_broadcast([128, NT, 16]),
                    ALU.add, R=Rr + [Bbrt], W=Wr)
                sel4 = sel.rearrange("p (g e) -> p g e", e=4)
                t6v = t6.rearrange("p (g e) -> p g e", e=6)
                P.v("dve", "tensor_tensor", t6v[:, :, 0:3], sel4[:, :, 0:3], sel4[:, :, 1:4], ALU.add, R=Rr, W=Wr)
                P.v("dve", "tensor_tensor", t6v[:, :, 3:5], sel4[:, :, 0:2], sel4[:, :, 2:4], ALU.add, R=Rr, W=Wr)
                P.v("dve", "tensor_tensor", t6v[:, :, 5:6], sel4[:, :, 0:1], sel4[:, :, 3:4], ALU.add, R=Rr, W=Wr)
                P.v("dve", "tensor_reduce", gs, t6v, AX.X, ALU.max, R=Rr, W=Wr)
                gs3 = gs.rearrange("p (t g) -> p t g", g=4)
                P.v("dve", "tensor_reduce", gm, gs3, AX.X, ALU.max, R=Rr, W=Wr)
                P.v("dve", "tensor_tensor", gmask.rearrange("p (t g) -> p t g", g=4), gs3,
                    gm.unsqueeze(2).to_broadcast([128, NT, 4]), ALU.is_equal, R=Rr, W=Wr)
                P.v("dve", "scalar_tensor_tensor", msel.rearrange("p (g e) -> p g e", e=4), sel4, 2.0,
                    gmask.unsqueeze(2).to_broadcast([128, 48, 4]), op0=ALU.add, op1=ALU.mult, R=Rr, W=Wr)
                top83 = top8.rearrange("p (t e) -> p t e", e=8)
                for t in range(NT):
                    P.v("dve", "max", top83[:, t, :], msel[:, t * 16:(t + 1) * 16], R=Rr, W=Wr)
                P.v("dve", "tensor_tensor", emask.rearrange("p (t e) -> p t e", e=16),
                    msel.rearrange("p (t e) -> p t e", e=16), top83[:, :, 1:2].to_broadcast([128, NT, 16]),
                    ALU.is_ge, R=Rr, W=Wr)
                P.v("dve", "tensor_tensor", wsel, s_sb, emask, ALU.mult, R=Rr, W=Wr)
                wsel3 = wsel.rearrange("p (t e) -> p t e", e=16)
                P.v("dve", "tensor_reduce", wsum, wsel3, AX.X, ALU.add, R=Rr, W=Wr)
                P.v("dve", "reciprocal", wsum, wsum, R=Rr, W=Wr)
                P.v("dve", "tensor_tensor", cmb[:], wsel3, wsum.unsqueeze(2).to_broadcast([128, NT, 16]),
                    ALU.mult, R=Rr, W=Bcmb)

                blocks = [(e, b) for e in range(16) for b in range(3)]
                hT_bufs = [[x for t in range(4 * b, 4 * b + 4) for x in BhT[t]] for b in range(3)]

                def GU(k):
                    e, b = blocks[k]
                    sl = e % 2
                    for fc in range(4):
                        gi = (k * 4 + fc) % 2
                        for kc in range(8):
                            P.mm(G[gi][:], wg[sl][:, kc, fc * 128:(fc + 1) * 128], hT[:, kc, b * 512:(b + 1) * 512],
                                 kc == 0, kc == 7, reads=[Bwg[sl]] + hT_bufs[b], writes=[BG[gi]])
                        for kc in range(8):
                            P.mm(U[gi][:], wu[sl][:, kc, fc * 128:(fc + 1) * 128], hT[:, kc, b * 512:(b + 1) * 512],
                                 kc == 0, kc == 7, reads=[Bwu[sl]] + hT_bufs[b], writes=[BU[gi]])
                        P.v("act", "activation", sg[gi][:], G[gi][:], AF.Silu, R=[BG[gi]], W=[Bsg[gi]])
                        P.v("dve", "tensor_tensor", heT[k % 2][:, fc, :], U[gi][:], sg[gi][:], ALU.mult,
                            R=[BU[gi], Bsg[gi]], W=[Bhe[k % 2]])

                def DNp(k):
                    e, b = blocks[k]
                    sl = e % 2
                    for tt in range(4):
                        t = 4 * b + tt
                        for half in range(2):
                            di = (k * 8 + tt * 2 + half) % 4
                            for fc in range(4):
                                P.mm(Dn[di][:], heT[k % 2][:, fc, tt * 128:(tt + 1) * 128],
                                     wd[sl][:, fc, half * 512:(half + 1) * 512], fc == 0, fc == 3,
                                     reads=[Bhe[k % 2], Bwd[sl]], writes=[BD[di]])
                            dst = yacc[:, t, half * 512:(half + 1) * 512]
                            if e == 0:
                                P.v("dve", "tensor_scalar", dst, Dn[di][:], cmb[:, t, e:e + 1], None, op0=ALU.mult,
                                    R=[BD[di], Bcmb[t]], W=[Byacc[t]])
                            else:
                                P.v("dve", "scalar_tensor_tensor", dst, Dn[di][:], cmb[:, t, e:e + 1], dst,
                                    op0=ALU.mult, op1=ALU.add, R=[BD[di], Bcmb[t], Byacc[t]], W=[Byacc[t]])

                GU(0)
                for k in range(len(blocks)):
                    if k + 1 < len(blocks):
                        GU(k + 1)
                    DNp(k)
                    e_, b_ = blocks[k]
                    if b_ == 2 and e_ + 2 < 16:
                        load_expert(e_ + 2)

                def pre(grp):
                    for t in range(4 * grp, 4 * grp + 4):
                        c = tile_cond(t)
                        ya = yacc[:, t, :]
                        P.v("pool", "tensor_tensor", ya, ya, bc[c][:], ALU.mult, R=[Byacc[t], Bbc[c]], W=[Byacc[t]])
                    for t in range(4 * grp, 4 * grp + 4):
                        ya = yacc[:, t, :]
                        P.v("dve", "scalar_tensor_tensor", ya, x_sb[:, t, :], ALPHA, ya, op0=ALU.mult, op1=ALU.add,
                            R=[Bx[t], Byacc[t]], W=[Byacc[t]])

                pre(0)
                for grp in range(3):
                    if grp + 1 < 3:
                        pre(grp + 1)
                    tiles = list(range(4 * grp, 4 * grp + 4))
                    ln_group(tiles, [yacc[:, t, :] for t in tiles], [Byacc[t] for t in tiles], bc[2][:], bc[3][:],
                             [Bbc[2], Bbc[3]], next_mod, PBK, "dve")
                    if final:
                        for t in tiles:
                            P.dma("sp", y_d[t * 128:(t + 1) * 128, :], x_sb[:, t, :], reads=[Bx[t]])
                P.end_phase()

        moe_phase(0, (1, 1, 0), False)
        if stop == "l0moe":
            store_x("l0moe")
            P.end_phase()
            return nc


        with ExitStack() as pa:
            QT_s = sbuf(pa, "QT_s", [128, 8, 1024], BF16)
            KT_s = sbuf(pa, "KT_s", [128, 8, 1024], BF16)
            Vp_s = sbuf(pa, "Vp_s", [128, 8, 8, 192], BF16)
            O_CTXV, O_M01, O_RBB, O_E, O_ONES, RG_N = 4096, 10240, 12032, 13824, 19200, 19392
            Rg = sbuf(pa, "Rg", [128, RG_N], BF16)
            QT_p = Rg[:, 0:4096].rearrange("p (c t) -> p c t", c=8)
            KT_p = Rg[:, 4096:8192].rearrange("p (c t) -> p c t", c=8)
            Vp_p = Rg[:, 8192:14336].rearrange("p (m a e) -> p m a e", m=4, a=8)
            ctxKT = Rg[:, 0:4096].rearrange("p (c t) -> p c t", c=8)
            ctxVp = Rg[:, O_CTXV:O_M01].rearrange("p (m a e) -> p m a e", m=4, a=8)
            m01 = Rg[:, O_M01:O_RBB].rearrange("p (m q) -> p m q", q=128)
            rbb = Rg[:, O_RBB:O_E].rearrange("p (b d q) -> p b d q", b=2, d=7)
            Etab = Rg[:, O_E:O_ONES].rearrange("p (b d q) -> p b d q", b=3, d=14)
            onesp = Rg[:, O_ONES:RG_N]
            BQs, BKs, BVs = Buf("QTs"), Buf("KTs"), Buf("Vs")
            BQp, BKp, BVp = Buf("QTp"), Buf("KTp"), Buf("Vp")
            Bones = Buf("onesp")

            with ExitStack() as ph:
                wb = [sbuf(ph, "wqkv%d" % i, [128, 8, 512], BF16) for i in range(2)]
                Bwb = [Buf("wb0"), Buf("wb1")]
                stg = [sbuf(ph, "stg%d" % i, [128, 512]) for i in range(2)]
                Bstg = [Buf("stg0"), Buf("stg1")]
                P.v("pool", "memset", Vp_s[:, :, :, 64:128], 0.0, W=[BVs])
                P.v("pool", "memset", Vp_p[:, :, :, 64:128], 0.0, W=[BVp])
                P.v("pool", "memset", onesp[:, 0:64], 1.0, W=[Bones])
                P.v("pool", "memset", onesp[:, 64:128], 0.0, W=[Bones])
                P.v("pool", "memset", onesp[:, 128:192], 1.0, W=[Bones])
                nb = [0]
                ns = [0]

                def bank():
                    i = nb[0] % 8
                    nb[0] += 1
                    return psum[i], PB[i], ("dve" if i % 2 == 0 else "act")

                def evac(eng, dst, src, Bsrc, W, mul=None):
                    if eng == "dve":
                        if mul is None:
                            P.v("dve", "tensor_copy", dst, src, R=[Bsrc], W=W)
                        else:
                            P.v("dve", "tensor_scalar_mul", dst, src, mul, R=[Bsrc], W=W)
                    else:
                        if mul is None:
                            P.v("act", "copy", dst, src, R=[Bsrc], W=W)
                        else:
                            P.v("act", "mul", dst, src, mul, R=[Bsrc], W=W)

                hTb = lambda tiles: [x for t in tiles for x in BhT[t]]
                for cb in range(6):
                    sl = cb % 2
                    P.dma("pool", wb[sl][:], w_qkv_d.rearrange("(kc k) n -> k kc n", k=128)[:, :, cb * 512:(cb + 1) * 512],
                          writes=[Bwb[sl]])
                    if cb < 4:
                        isq = cb < 2
                        for i in range(4):
                            a = (cb % 2) * 4 + i
                            for tb in range(3):
                                pb, Bp, eng = bank()
                                for kc in range(8):
                                    P.mm(pb[:], wb[sl][:, kc, i * 128:(i + 1) * 128], hT[:, kc, tb * 512:(tb + 1) * 512],
                                         kc == 0, kc == 7, reads=[Bwb[sl]] + hTb(range(4 * tb, 4 * tb + 4)), writes=[Bp])
                                if tb == 0:
                                    dst = (QT_p if isq else KT_p)[:, a, :]
                                    W = [BQp if isq else BKp]
                                else:
                                    dst = (QT_s if isq else KT_s)[:, a, (tb - 1) * 512:tb * 512]
                                    W = [BQs if isq else BKs]
                                evac(eng, dst, pb[:], Bp, W, mul=(0.125 if isq else None))
                    if cb in (2, 3, 4, 5):
                        isv = cb >= 4
                        hh = cb % 2
                        for t in (range(NT) if isv else range(4)):
                            pb, Bp, eng = bank()
                            for kc in range(8):
                                P.mm(pb[:], hT[:, kc, t * 128:(t + 1) * 128], wb[sl][:, kc, :], kc == 0, kc == 7,
                                     reads=[Bwb[sl]] + hTb([t]), writes=[Bp])
                            if isv:
                                src4 = pb[:].rearrange("p (a two d) -> p a two d", two=2, d=64)
                                for par in range(2):
                                    if t < 4:
                                        evac(eng, Vp_p[:, t, hh * 4:(hh + 1) * 4, par * 128:par * 128 + 64],
                                             src4[:, :, par, :], Bp, [BVp])
                                    else:
                                        evac(eng, Vp_s[:, t - 4, hh * 4:(hh + 1) * 4, par * 128:par * 128 + 64],
                                             src4[:, :, par, :], Bp, [BVs])
                            if t < 4:
                                k = ns[0] % 2
                                ns[0] += 1
                                evac("dve" if eng == "act" else "act", stg[k][:], pb[:], Bp, [Bstg[k]])
                                dd = nv_d if isv else nk_d
                                s_, l0 = t // 2, (t % 2) * 128
                                P.dma("sp", dd[s_, hh * 8:(hh + 1) * 8, l0:l0 + 128, :].rearrange("h l d -> l h d"),
                                      stg[k][:].rearrange("p (h d) -> p h d", d=64), reads=[Bstg[k]])
                P.end_phase()

            with ExitStack() as pb_:
                OT_all = sbuf(pb_, "OT_all", [128, 8, 1024], BF16)
                BOT = [Buf("OT%d" % a) for a in range(8)]
                wo = sbuf(pb_, "wo", [128, 8, D], BF16)
                Bwo = Buf("wo")
                By = [Buf("ay%d" % i) for i in range(4)]
                ST = [PBK[i] for i in range(4)]
                ACC = [(PBK[4], PBK[5]), (PBK[6], PBK[7])]
                cn = {"st": 0, "pt": 0, "em": 0}

                def heads_scope(scope):
                    PT = [sbuf(scope, "PT%d" % i, [128, 512], BF16) for i in range(4)]
                    BPT = [Buf("PT%d" % i) for i in range(4)]
                    rS = sbuf(scope, "rS", [128, 512])
                    BrS = Buf("rS")
                    return PT, BPT, rS, BrS

                LOOKAHEAD = 3
                pend = []

                def drain(keep):
                    while len(pend) > keep:
                        pend.pop(0)()

                def segment(PT, BPT, Kst, Qmv, nq, Bk, Bq, esl, BE_, Vblk, BV_, ones_blk, acc, c0):
                    st, Bst = ST[cn["st"] % 4]
                    cn["st"] += 1
                    k = cn["pt"] % 4
                    cn["pt"] += 1
                    P.mm(st[:, 0:nq], Kst, Qmv, True, True, reads=[Bk, Bq], writes=[Bst])
                    P.v("act", "activation", PT[k][:, 0:nq], st[:, 0:nq], AF.Exp, R=[Bst], W=[BPT[k]])
                    if esl is not None:
                        eng = "pool" if cn["em"] % 3 == 2 else "dve"
                        cn["em"] += 1
                        P.v(eng, "tensor_tensor", PT[k][:, 0:nq], PT[k][:, 0:nq], esl, ALU.mult,
                            R=[BPT[k], BE_], W=[BPT[k]])
                    (pO, BO_), (pS, BS_) = acc

                    def back():
                        P.mm(pO[:, c0:c0 + nq], Vblk, PT[k][:, 0:nq], False, False, reads=[BPT[k], BV_], writes=[BO_],
                             skip=True)
                        P.mm(pS[:, c0:c0 + nq], ones_blk, PT[k][:, 0:nq], False, False, reads=[BPT[k], Bones],
                             writes=[BS_], skip=True)
                    pend.append(back)
                    drain(LOOKAHEAD)

                def zero_acc(acc, ncols):
                    (pO, BO_), (pS, BS_) = acc
                    P.v("dve", "memset", pO[:, 0:ncols], 0.0, W=[BO_])
                    P.v("dve", "memset", pS[:, 0:ncols], 0.0, W=[BS_])

                def normalize(acc, ncols, rS, BrS, dst, Wd):
                    (pO, BO_), (pS, BS_) = acc

                    def fin():
                        P.v("dve", "reciprocal", rS[:, 0:ncols], pS[:, 0:ncols], R=[BS_], W=[BrS])
                        P.v("dve", "tensor_tensor", dst, pO[:, 0:ncols], rS[:, 0:ncols], ALU.mult, R=[BO_, BrS], W=Wd)
                    pend.append(fin)

                def proj_group(tiles, tcols, ybufs, deadW, bcv, Bbcv):
                    for i, t in enumerate(tiles):
                        tc = tcols[i]
                        for half in range(2):
                            pb, Bp = PBK[2 * (i % 2) + half]
                            for kc in range(8):
                                P.mm(pb[:], OT_all[:, kc, tc:tc + 128], wo[:, kc, half * 512:(half + 1) * 512],
                                     kc == 0, kc == 7, reads=[BOT[kc], Bwo], writes=[Bp])
                            P.v("dve", "tensor_tensor", ybufs[i][:, half * 512:(half + 1) * 512], pb[:],
                                bcv[0][:, half * 512:(half + 1) * 512], ALU.mult, R=[Bp, Bbcv[0]],
                                W=[By[i]] + list(deadW))
                        P.v("dve", "scalar_tensor_tensor", ybufs[i], x_sb[:, t, :], ALPHA, ybufs[i],
                            op0=ALU.mult, op1=ALU.add, R=[Bx[t], By[i]], W=[By[i]])
                    ln_group(tiles, ybufs, By, bcv[1], bcv[2], [Bbcv[1], Bbcv[2]], (1, 4, 3), PBK[4:8], "pool")

                def prep_vectors(cidx, bcv, Bbcv, dgall_v, Bdg_v, deadW):
                    load_bc(bcv[1], ln1g_d[1:2, :], [Bbcv[1]] + list(deadW))
                    load_bc(bcv[2], ln1b_d[1:2, :], [Bbcv[2]] + list(deadW))
                    make_bc(bcv[0], 1, 2, cidx, dgall_v, Bdg_v, PBK[6:8], [Bbcv[0]] + list(deadW))

                with ExitStack() as ph:
                    PT, BPT, rS, BrS = heads_scope(ph)
                    P.dma("pool", wo[:], w_o_d.rearrange("(kc k) n -> k kc n", k=128), writes=[Bwo])
                    for a in range(8):
                        for s_ in range(2):
                            acc = ACC[(2 * a + s_) % 2]
                            zero_acc(acc, 256)
                            for c in range(2):
                                for rho in range(2):
                                    pr = slice(64 * rho, 64 * rho + 64)
                                    vc = slice(64 * rho, 64 * rho + 128)
                                    k0 = s_ * 256 + c * 128
                                    segment(PT, BPT, KT_p[pr, a, k0:k0 + 128], QT_p[pr, a, s_ * 256:(s_ + 1) * 256], 256,
                                            BKp, BQp, None, None, Vp_p[:, 2 * s_ + c, a, vc], BVp, onesp[:, vc], acc, 0)
                            normalize(acc, 256, rS, BrS, OT_all[:, a, s_ * 256:(s_ + 1) * 256], [BOT[a]])
                    drain(0)
                    P.end_phase()

                yb_p = [Rg[:, 2048 * i:2048 * (i + 1)].bitcast(F32) for i in range(4)]
                bc_p = [Rg[:, 8192 + 2048 * i:8192 + 2048 * (i + 1)].bitcast(F32) for i in range(3)]
                dg_p = Rg[:, 14336:16384].bitcast(F32).rearrange("p (a b) -> p a b", b=128)
                Bbc_p = [Buf("pbc%d" % i) for i in range(3)]
                prep_vectors(0, bc_p, Bbc_p, dg_p, Buf("pdg"), [])
                proj_group([0, 1, 2, 3], [0, 128, 256, 384], yb_p, [], bc_p, Bbc_p)
                if stop == "l1mixp":
                    store_x("l1mixp")
                P.end_phase()
                if stop == "l1mixp":
                    return nc

                with ExitStack() as ph:
                    PT, BPT, rS, BrS = heads_scope(ph)
                    ck_tok = OT_all[:, 0:4, :].rearrange("p c (h d) -> p c h d", d=64)
                    Bck = Buf("ck_tok")
                    BctxK, BctxV, Bm01 = Buf("ctxK"), Buf("ctxV"), Buf("m01")
                    Brbb = [Buf("rbb0"), Buf("rbb1")]
                    BE = [Buf("E0"), Buf("E1"), Buf("E2")]
                    for c in range(4):
                        P.dma("pool", ck_tok[:, c, :, :], ck_d[:, c * 128:(c + 1) * 128, :].rearrange("h l d -> l h d"),
                              writes=[Bck])
                    P.dma("pool", m01[:], m01_d.rearrange("m k q -> k m q"), writes=[Bm01])
                    P.v("pool", "memset", ctxVp[:, :, :, 64:128], 0.0, W=[BctxV])
                    cv4 = cv_d.rearrange("(a two) l d -> two l a d", two=2)
                    for c in range(4):
                        for par in range(2):
                            P.dma("pool", ctxVp[:, c, :, par * 128:par * 128 + 64], cv4[par, c * 128:(c + 1) * 128, :, :],
                                  writes=[BctxV])

                    def build_E(h):
                        hb, eb = h % 2, h % 3
                        P.dma("pool", rbb[:, hb, :, :], rbt_d[h].rearrange("d k q -> k d q"), writes=[Brbb[hb]])
                        P.v("act", "activation", rbb[:, hb, :, :], rbb[:, hb, :, :], AF.Exp, R=[Brbb[hb]], W=[Brbb[hb]])
                        for s2 in range(2):
                            P.v("pool", "tensor_tensor", Etab[:, eb, s2 * 7:(s2 + 1) * 7, :], rbb[:, hb, :, :],
                                m01[:, s2 * 7:(s2 + 1) * 7, :], ALU.mult, R=[Brbb[hb], Bm01], W=[BE[eb]])

                    for c in range(4):
                        pTb = psum[c % 2][:].bitcast(BF16)
                        for a in range(8):
                            P.tr(pTb[:, a * 128:(a + 1) * 128], ck_tok[:, c, 2 * a:2 * a + 2, :].rearrange("p h d -> p (h d)"),
                                 ident_b[:], reads=[Bck, Bconst], writes=[PB[c % 2]])
                        P.v("dve" if c % 2 == 0 else "act", "tensor_copy" if c % 2 == 0 else "copy",
                            ctxKT[:, :, c * 128:(c + 1) * 128], pTb.rearrange("p (a t) -> p a t", a=8),
                            R=[PB[c % 2]], W=[BctxK])
                    build_E(0)
                    build_E(1)
                    build_E(2)
                    for a in range(8):
                        for qb in range(2):
                            acc = ACC[(2 * a + qb) % 2]
                            zero_acc(acc, 512)
                            for (m, jlo, cnt, e2) in _SEGS[qb]:
                                for rho in range(2):
                                    eb = (2 * a + rho) % 3
                                    pr = slice(64 * rho, 64 * rho + 64)
                                    vc = slice(64 * rho, 64 * rho + 128)
                                    nq, q0 = cnt * 128, jlo * 128
                                    i0 = e2 * 7 + 3 - m + jlo
                                    esl = Etab[:, eb, i0:i0 + cnt, :].rearrange("p a b -> p (a b)")
                                    segment(PT, BPT, KT_s[pr, a, m * 128:(m + 1) * 128], QT_s[pr, a, q0:q0 + nq], nq,
                                            BKs, BQs, esl, BE[eb], Vp_s[:, m, a, vc], BVs, onesp[:, vc], acc, q0 - 512 * qb)
                            for c in range(4):
                                for rho in range(2):
                                    pr = slice(64 * rho, 64 * rho + 64)
                                    vc = slice(64 * rho, 64 * rho + 128)
                                    segment(PT, BPT, ctxKT[pr, a, c * 128:(c + 1) * 128], QT_s[pr, a, qb * 512:(qb + 1) * 512],
                                            512, BctxK, BQs, None, None, ctxVp[:, c, a, vc], BctxV, onesp[:, vc], acc, 0)
                            normalize(acc, 512, rS, BrS, OT_all[:, a, qb * 512:(qb + 1) * 512], [BOT[a], Bck])
                        for h2 in (2 * a + 3, 2 * a + 4):
                            if h2 < 16:
                                build_E(h2)
                    drain(0)
                    P.end_phase()

                yb_s = [QT_s[:, 2 * i:2 * i + 2, :].rearrange("p a t -> p (a t)").bitcast(F32) for i in range(4)]
                bc_s = [KT_s[:, 2 * i:2 * i + 2, :].rearrange("p a t -> p (a t)").bitcast(F32) for i in range(3)]
                dg_s = KT_s[:, 6:8, :].rearrange("p a t -> p (a t)").bitcast(F32).rearrange("p (a b) -> p a b", b=128)
                Bbc_s = [Buf("sbc%d" % i) for i in range(3)]
                prep_vectors(1, bc_s, Bbc_s, dg_s, Buf("sdg"), [])
                proj_group([4, 5, 6, 7], [0, 128, 256, 384], yb_s, [], bc_s, Bbc_s)
                proj_group([8, 9, 10, 11], [512, 640, 768, 896], yb_s, [], bc_s, Bbc_s)
                if stop == "l1mix":
                    store_x("l1mix")
                P.end_phase()
        if stop == "l1mix":
            return nc

        moe_phase(1, None, True)
        return nc


_CACHE = {}


def make_in_maps(inputs):
    f = lambda a: np.ascontiguousarray(np.asarray(a, dtype=np.float32))
    x_prompt, x_sample = f(inputs["x_prompt"]), f(inputs["x_sample"])
    cache_k, cache_v = f(inputs["cache_k"]), f(inputs["cache_v"])
    c, c_ctx = f(inputs["c"]), f(inputs["c_ctx"])
    b_ada = f(inputs["b_ada"])
    shared = {
        "w_ada": f(inputs["w_ada"]),
        "b_adaT": np.ascontiguousarray(b_ada.reshape(2, 48, 128).transpose(0, 2, 1)),
        "ln1_g": f(inputs["ln1_g"]), "ln1_b": f(inputs["ln1_b"]),
        "ln2_g": f(inputs["ln2_g"]), "ln2_b": f(inputs["ln2_b"]),
        "pool_w": f(inputs["pool_w"])[0],
        "pool_scale": f(inputs["pool_scale"]).reshape(1, D),
        "w_qkv": f(inputs["w_qkv"])[0],
        "w_o": f(inputs["w_o"])[0],
        "rbt": _rel_bias_layout(f(inputs["rel_bias"])[0]),
        "w_router": f(inputs["w_router"]),
        "b_router": f(inputs["b_router"]).reshape(1, 16),
        "w_gate": f(inputs["w_gate"]), "w_up": f(inputs["w_up"]), "w_down": f(inputs["w_down"]),
        "ptab": _pool_tables(),
        "m01": _mask01_table(),
    }
    in_maps = []
    for i in range(N_CORES):
        m = dict(shared)
        m["xin"] = np.ascontiguousarray(np.concatenate(
            [x_prompt[2 * i].reshape(256, D), x_prompt[2 * i + 1].reshape(256, D), x_sample[i]], axis=0))
        m["ck"] = np.ascontiguousarray(cache_k[i, 0])
        m["cv"] = np.ascontiguousarray(cache_v[i, 0])
        cond = np.stack([c_ctx, c[i]], axis=-1)
        m["condT"] = np.ascontiguousarray(cond.reshape(8, 128, 2).transpose(1, 0, 2))
        in_maps.append(m)
    return in_maps


def kernel(**inputs):
    if "nc" not in _CACHE:
        _CACHE["nc"] = build_nc()
    nc = _CACHE["nc"]
    in_maps = make_in_maps(inputs)
    res = run_bass_kernel_spmd(nc, in_maps, core_ids=list(range(N_CORES)))
    ys = [r["y"] for r in res.results]
    y_prompt = np.stack([y[:512].reshape(2, 256, D) for y in ys]).reshape(16, 256, D)
    y_sample = np.stack([y[512:] for y in ys])
    nk = np.stack([r["nk"] for r in res.results]).reshape(16, 1, 16, 256, 64)
    nv = np.stack([r["nv"] for r in res.results]).reshape(16, 1, 16, 256, 64)
    return (y_prompt.astype(np.float32), y_sample.astype(np.float32),
            nk.astype(np.float32), nv.astype(np.float32))
```

```python
from contextlib import ExitStack

import numpy as np
import concourse.bass as bass
import concourse.mybir as mybir
from concourse.bass_utils import run_bass_kernel_spmd

F32 = mybir.dt.float32
BF16 = mybir.dt.bfloat16
AF = mybir.ActivationFunctionType
ALU = mybir.AluOpType
AX = mybir.AxisListType

N_CORES = 8
D = 1024
NT = 12
T = NT * 128
ALPHA = 4.0 ** 0.25
LN_EPS = 1e-5
NEG = -30000.0
POOL_SIZES = (2, 4, 8, 16)
SEQS = ((0, 2), (2, 2), (4, 8))


def tile_cond(t):
    return 0 if t < 4 else 1


class Buf:
    __slots__ = ("name", "writer", "readers", "excl")

    def __init__(self, name="", excl=False):
        self.name = name
        self.writer = None
        self.readers = []
        self.excl = excl


class Lane:
    __slots__ = ("sem", "n", "last")

    def __init__(self, sem):
        self.sem = sem
        self.n = 0
        self.last = None


class Ins:
    __slots__ = ("eng", "fn", "deps", "signal", "count", "lane", "lval", "idx")

    def __init__(self, eng, fn):
        self.eng = eng
        self.fn = fn
        self.deps = []
        self.signal = False
        self.count = 0
        self.lane = None
        self.lval = 0
        self.idx = 0


class Prog:
    ENG = ("pe", "act", "dve", "pool", "sp")
    NLANES = {"sp": 24, "pool": 12, "act": 4}

    def __init__(self, nc, es):
        self.nc = nc
        self.es = es
        self.eng = {"pe": nc.tensor, "act": nc.scalar, "dve": nc.vector,
                    "pool": nc.gpsimd, "sp": nc.sync}
        self.sem = {e: es.enter_context(nc.semaphore("s_" + e)) for e in self.ENG}
        self.cnt = {e: 0 for e in self.ENG}
        self.bar = es.enter_context(nc.semaphore("s_bar"))
        self.nbar = 0
        self.lanes = {q: [Lane(es.enter_context(nc.semaphore("l_%s%d" % (q, i)))) for i in range(n)]
                      for q, n in self.NLANES.items()}
        self.lane_rr = {q: 0 for q in self.NLANES}
        self.waited = {e: {} for e in self.ENG}
        self.streams = {e: [] for e in self.ENG}
        self.touched = []
        self.n_ins = 0

    def _add_dep(self, ins, p):
        if p is None or p is ins:
            return
        if p.lane is None and ins.lane is None and p.eng == ins.eng and p.eng == "pe":
            return
        deps = ins.deps
        if p.lane is None:
            for i, q in enumerate(deps):
                if q.lane is None and q.eng == p.eng:
                    if p.idx > q.idx:
                        deps[i] = p
                    return
        elif p in deps:
            return
        deps.append(p)

    def _deps(self, ins, reads, writes):
        if any(b.excl for b in reads):
            writes = list(writes) + [b for b in reads if b.excl and b not in writes]
            reads = [b for b in reads if not b.excl]
        same = lambda p: (p is not None and p.lane is None and ins.lane is None and p.eng == ins.eng)
        for b in reads:
            self._add_dep(ins, b.writer)
        for b in writes:
            if not same(b.writer):
                self._add_dep(ins, b.writer)
            for r in b.readers:
                if not same(r):
                    self._add_dep(ins, r)
        for b in reads:
            if not b.readers and b.writer is None:
                self.touched.append(b)
            b.readers.append(ins)
        for b in writes:
            if not b.readers and b.writer is None:
                self.touched.append(b)
            b.writer = ins
            b.readers = []

    def op(self, eng, fn, reads=(), writes=()):
        ins = Ins(eng, fn)
        ins.idx = len(self.streams[eng])
        self._deps(ins, reads, writes)
        self.streams[eng].append(ins)
        return ins

    def dma(self, eng, out, in_, reads=(), writes=(), **kw):
        E = self.eng[eng]
        ins = Ins(eng, lambda: E.dma_start(out=out, in_=in_, **kw))
        ins.idx = len(self.streams[eng])
        lanes = self.lanes[eng]
        lane = lanes[self.lane_rr[eng] % len(lanes)]
        self.lane_rr[eng] += 1
        ins.lane = lane
        lane.n += 1
        ins.lval = 16 * lane.n
        if lane.last is not None:
            ins.deps.append(lane.last)
        lane.last = ins
        self._deps(ins, reads, writes)
        self.streams[eng].append(ins)
        return ins

    def mm(self, out, lhsT, rhs, start, stop, reads=(), writes=(), skip=False):
        t = self.nc.tensor
        if skip:
            return self.op("pe", lambda: t.matmul(out, lhsT, rhs, start=start, stop=stop, skip_group_check=True),
                           reads, writes)
        return self.op("pe", lambda: t.matmul(out, lhsT, rhs, start=start, stop=stop), reads, writes)

    def tr(self, out, in_, ident, reads=(), writes=()):
        t = self.nc.tensor
        return self.op("pe", lambda: t.transpose(out, in_, ident), reads, writes)

    def v(self, eng, meth, *args, R=(), W=(), **kw):
        E = self.eng[eng]
        return self.op(eng, lambda: getattr(E, meth)(*args, **kw), R, W)

    def end_phase(self):
        ENG = self.ENG
        for e in ENG:
            for ins in self.streams[e]:
                for p in ins.deps:
                    if p.lane is None:
                        p.signal = True
            for ins in reversed(self.streams[e]):
                if ins.lane is None:
                    ins.signal = True
                    break
        for e in ENG:
            for ins in self.streams[e]:
                if ins.lane is None and ins.signal:
                    self.cnt[e] += 1
                    ins.count = self.cnt[e]
        for e in ENG:
            E = self.eng[e]
            wd = self.waited[e]
            for ins in self.streams[e]:
                for p in ins.deps:
                    if p.lane is not None:
                        sem, val = p.lane.sem, p.lval
                    else:
                        sem, val = self.sem[p.eng], p.count
                    if wd.get(sem.num, 0) >= val:
                        continue
                    wd[sem.num] = val
                    E.wait_ge(sem, val)
                bi = ins.fn()
                if ins.lane is not None:
                    bi.then_inc(ins.lane.sem, 16)
                elif ins.signal:
                    bi.then_inc(self.sem[e], 1)
                self.n_ins += 1
        sp = self.eng["sp"]
        wd = self.waited["sp"]
        all_lanes = [l for q in self.lanes.values() for l in q]
        for e in ENG:
            if e == "sp" or self.cnt[e] == 0:
                continue
            if wd.get(self.sem[e].num, 0) < self.cnt[e]:
                sp.wait_ge(self.sem[e], self.cnt[e])
        for l in all_lanes:
            if l.n and wd.get(l.sem.num, 0) < 16 * l.n:
                sp.wait_ge(l.sem, 16 * l.n)
        self.nbar += 1
        sp.sem_inc(self.bar, 1)
        for e in ENG:
            if e != "sp":
                self.eng[e].wait_ge(self.bar, self.nbar)
            for e2 in ENG:
                self.waited[e][self.sem[e2].num] = self.cnt[e2]
            for l in all_lanes:
                self.waited[e][l.sem.num] = 16 * l.n
        for l in all_lanes:
            l.last = None
        for b in self.touched:
            b.writer = None
            b.readers = []
        self.touched = []
        self.streams = {e: [] for e in ENG}


def _pool_tables():
    n = 1024
    out = np.zeros((4, 5, 128, 128), np.float32)
    t = np.arange(n)
    for wi, w in enumerate(POOL_SIZES):
        lo = np.clip(t - w // 2, 0, n)
        hi = np.clip(t - w // 2 + w, 0, n)
        cnt = (hi - lo).astype(np.float64)
        s = np.arange(n)[:, None]
        Pm = ((s >= lo[None, :]) & (s < hi[None, :])) / cnt[None, :] - np.eye(n)
        Pm = Pm.astype(np.float32)
        out[wi, 0] = Pm[0:128, 0:128]
        out[wi, 1] = Pm[128:256, 128:256]
        out[wi, 2] = Pm[896:1024, 896:1024]
        out[wi, 3] = Pm[0:128, 128:256]
        out[wi, 4] = Pm[256:384, 128:256]
    return out


def _chunks_for(j):
    if j < 2:
        return list(range(0, 4))
    if j > 5:
        return list(range(4, 8))
    return list(range(j - 2, j + 3))


def _mask_tables():
    tiles = []
    index = {}
    keymap = {}
    col = np.arange(64)
    cs = np.clip(col - 8, 0, 48)
    colok = (col[None, :] >= cs[:, None]) & (col[None, :] < cs[:, None] + 16)
    for j in range(8):
        for m in _chunks_for(j):
            mt = np.full((2, 64, 2, 64), NEG, np.float32)
            for rq in range(2):
                r = 2 * j + rq
                rs = min(max(r - 4, 0), 8)
                for rk in range(2):
                    kr = 2 * m + rk
                    if rs <= kr < rs + 8:
                        mt[rk, :, rq, :] = np.where(colok.T, 0.0, NEG)
            key = mt.tobytes()
            if key not in keymap:
                keymap[key] = len(tiles)
                tiles.append(mt.reshape(128, 128))
            index[(j, m)] = keymap[key]
    return np.stack(tiles), index


_MASKS, _MASK_IDX = _mask_tables()
_M_FULL, _M_A, _M_B = _MASK_IDX[(0, 0)], _MASK_IDX[(2, 0)], _MASK_IDX[(2, 4)]


def _mask01_table():
    full = (_MASKS[_M_FULL] == 0).astype(np.float32)
    t = np.stack([full] * 14)
    t[7 + (3 - 2)] = (_MASKS[_M_B] == 0)
    t[7 + (3 + 2)] = (_MASKS[_M_A] == 0)
    return t


def _segments():
    segs = {0: [], 1: []}
    for m in range(8):
        for qb in range(2):
            js = [j for j in range(8) if m in _chunks_for(j) and j // 4 == qb]
            if not js:
                continue
            assert js == list(range(js[0], js[-1] + 1))
            e2 = any(_MASK_IDX[(j, m)] != _M_FULL for j in js)
            for j in js:
                want = _M_FULL
                if e2 and m - j == -2:
                    want = _M_A
                if e2 and m - j == 2:
                    want = _M_B
                assert _MASK_IDX[(j, m)] == want, (j, m)
            segs[qb].append((m, js[0], len(js), int(e2)))
    return segs


_SEGS = _segments()


def _rel_bias_layout(rel_bias):
    rk = np.arange(2)[:, None, None, None]
    ck = np.arange(64)[None, :, None, None]
    rq = np.arange(2)[None, None, :, None]
    cq = np.arange(64)[None, None, None, :]
    out = np.empty((16, 7, 128, 128), np.float32)
    for i in range(7):
        dl = 3 - i
        dy = np.clip(2 * dl + rk - rq + 7, 0, 14)
        dx = np.clip(ck - cq + 15, 0, 30)
        dy, dx = np.broadcast_arrays(dy, dx)
        out[:, i] = rel_bias[:, dy, dx].reshape(16, 128, 128)
    return out


def build_nc(stop=None):
    nc = bass.Bass("TRN2", target_bir_lowering=False)

    def din(name, shape):
        return nc.dram_tensor(name, list(shape), F32, kind="ExternalInput").ap()

    def dout(name, shape):
        return nc.dram_tensor(name, list(shape), F32, kind="ExternalOutput").ap()

    xin = din("xin", [T, D])
    ck_d = din("ck", [16, 512, 64])
    cv_d = din("cv", [16, 512, 64])
    condT_d = din("condT", [128, 8, 2])
    w_ada_d = din("w_ada", [2, D, 6 * D])
    b_adaT_d = din("b_adaT", [2, 128, 48])
    ln1g_d = din("ln1_g", [2, D])
    ln1b_d = din("ln1_b", [2, D])
    ln2g_d = din("ln2_g", [2, D])
    ln2b_d = din("ln2_b", [2, D])
    pool_w_d = din("pool_w", [4, 256, 256])
    pool_scale_d = din("pool_scale", [1, D])
    w_qkv_d = din("w_qkv", [D, 3 * D])
    w_o_d = din("w_o", [D, D])
    rbt_d = din("rbt", [16, 7, 128, 128])
    w_router_d = din("w_router", [D, 16])
    b_router_d = din("b_router", [1, 16])
    w_gate_d = din("w_gate", [2, 16, D, 512])
    w_up_d = din("w_up", [2, 16, D, 512])
    w_down_d = din("w_down", [2, 16, 512, D])
    ptab_d = din("ptab", [4, 5, 128, 128])
    m01_d = din("m01", [14, 128, 128])

    y_d = dout("y", [T, D])
    nk_d = dout("nk", [2, 16, 256, 64])
    nv_d = dout("nv", [2, 16, 256, 64])
    dbg_d = dout("dbg", [128, 2, 48, 2]) if stop == "ada" else None

    with ExitStack() as es:
        P = Prog(nc, es)

        _uid = [0]

        def sbuf(scope, name, shape, dt=F32):
            _uid[0] += 1
            return scope.enter_context(nc.sbuf_tensor("sb%d_%s" % (_uid[0], name), list(shape), dt))

        x_sb = sbuf(es, "x_sb", [128, NT, D])
        hT = sbuf(es, "hT", [128, 8, T], BF16)
        ident_f = sbuf(es, "ident_f", [128, 128])
        ident_b = sbuf(es, "ident_b", [128, 128], BF16)
        ones_f = sbuf(es, "ones_f", [128, 128])
        modcol = sbuf(es, "modcol", [128, 2, 48, 2])
        cmb = sbuf(es, "cmb", [128, NT, 16])
        small = sbuf(es, "small", [128, 64])
        psum = [es.enter_context(nc.psum_tensor("ps%d" % i, [128, 512], F32)) for i in range(8)]
        PB = [Buf("ps%d" % i, excl=True) for i in range(8)]

        Bx = [Buf("x%d" % t) for t in range(NT)]
        BhT = [(Buf("hTd%d" % t), Buf("hTa%d" % t)) for t in range(NT)]
        Bconst = Buf("const")
        Bmod = Buf("modcol")
        Bcmb = [Buf("cmb%d" % t) for t in range(NT)]


        def alpha_col(layer, v, ch, c):
            return modcol[:, layer, v * 8 + ch, c:c + 1]

        with ExitStack() as ph:
            condT = sbuf(ph, "condT", [128, 8, 2])
            sT = sbuf(ph, "sT", [128, 8, 2])
            sTb = sbuf(ph, "sTb", [128, 8, 2], BF16)
            b_col = sbuf(ph, "b_col", [128, 2, 48])
            wa = [sbuf(ph, "wa%d" % i, [128, 8, 512], BF16) for i in range(3)]
            Bwa = [Buf("wa%d" % i) for i in range(3)]
            BsT = Buf("sT")
            Bbcol = Buf("bcol")

            P.v("pool", "memset", ident_f[:], 0.0, W=[Bconst])
            P.v("pool", "affine_select", ident_f[:], ident_f[:], [[-1, 128]], ALU.not_equal, 1.0,
                base=0, channel_multiplier=1, R=[Bconst], W=[Bconst])
            P.v("pool", "tensor_copy", ident_b[:], ident_f[:], R=[Bconst], W=[Bconst])
            P.v("pool", "memset", ones_f[:], 1.0, W=[Bconst])

            for t in range(NT):
                P.dma("sp", x_sb[:, t, :], xin[t * 128:(t + 1) * 128, :], writes=[Bx[t]])
            P.dma("sp", condT[:], condT_d[:, :, :], writes=[BsT])
            P.dma("sp", b_col[:], b_adaT_d.rearrange("l p c -> p l c"), writes=[Bbcol])
            P.v("act", "activation", sT[:], condT[:], AF.Silu, R=[BsT], W=[BsT])
            P.v("dve", "tensor_copy", sTb[:], sT[:], R=[BsT], W=[BsT])

            nblk = 0
            for layer in range(2):
                pc = psum[layer]
                for nb in range(12):
                    slot = nblk % 3
                    src = w_ada_d[layer].rearrange("(kc k) n -> k kc n", k=128)[:, :, nb * 512:(nb + 1) * 512]
                    P.dma("pool", wa[slot][:], src, writes=[Bwa[slot]])
                    for n4 in range(4):
                        ch = nb * 4 + n4
                        for kc in range(8):
                            P.mm(pc[:, ch * 2:ch * 2 + 2], wa[slot][:, kc, n4 * 128:(n4 + 1) * 128],
                                 sTb[:, kc, :], kc == 0, kc == 7,
                                 reads=[Bwa[slot], BsT], writes=[PB[layer]])
                    nblk += 1
                P.v("dve", "tensor_tensor", modcol[:, layer, :, :],
                    pc[:, 0:96].rearrange("p (c k) -> p c k", k=2),
                    b_col[:, layer, :].unsqueeze(2).to_broadcast([128, 48, 2]), ALU.add,
                    R=[PB[layer], Bbcol], W=[Bmod])
                for v in (1, 4):
                    P.v("dve", "tensor_scalar_add", modcol[:, layer, v * 8:(v + 1) * 8, :],
                        modcol[:, layer, v * 8:(v + 1) * 8, :], 1.0, R=[Bmod], W=[Bmod])
            if stop == "ada":
                P.dma("sp", dbg_d[:, :, :, :], modcol[:], reads=[Bmod])
            P.end_phase()

        def load_bc(dst, src_row, W):
            P.dma("sp", dst, src_row.partition_broadcast(128), writes=W)

        def make_bc(dst, layer, v, c, dgall, Bdg, banks, W):
            P.v("dve", "tensor_tensor", dgall[:], ident_f[:].unsqueeze(1).to_broadcast([128, 8, 128]),
                modcol[:, layer, v * 8:(v + 1) * 8, c:c + 1].to_broadcast([128, 8, 128]), ALU.mult,
                R=[Bconst, Bmod], W=[Bdg])
            for half in range(2):
                pb, Bp = banks[half]
                P.mm(pb[:], ones_f[:], dgall[:, half * 4:(half + 1) * 4, :].rearrange("p a b -> p (a b)"),
                     True, True, reads=[Bconst, Bdg], writes=[Bp])
                P.v("act", "copy", dst[:, half * 512:(half + 1) * 512], pb[:], R=[Bp], W=W)

        Bsmall = [Buf("mv%d" % i) for i in range(4)]

        def ln_group(tiles, ys, Bys, g_bc, b_bc, Bvec, next_mod, banks, gmul_eng):
            n = len(tiles)
            sts = [small[:, 16 * i:16 * i + 12].rearrange("p (a b) -> p a b", b=6) for i in range(n)]
            mvv = [small[:, 16 * i + 12:16 * i + 16] for i in range(n)]
            add_eng = "pool" if gmul_eng == "dve" else "dve"
            for i in range(n):
                P.v("dve", "bn_stats", sts[i][:, 0, :], ys[i][:, 0:512], R=[Bys[i]], W=[Bsmall[i]])
                P.v("dve", "bn_stats", sts[i][:, 1, :], ys[i][:, 512:1024], R=[Bys[i]], W=[Bsmall[i]])
                P.v("dve", "bn_aggr", mvv[i][:, 0:2], sts[i], R=[Bsmall[i]], W=[Bsmall[i]])
            for i in range(n):
                P.v("act", "activation", mvv[i][:, 2:3], mvv[i][:, 1:2], AF.Sqrt, bias=LN_EPS,
                    R=[Bsmall[i]], W=[Bsmall[i]])
            for i in range(n):
                P.v("dve", "reciprocal", mvv[i][:, 2:3], mvv[i][:, 2:3], R=[Bsmall[i]], W=[Bsmall[i]])
                P.v("dve", "tensor_scalar", mvv[i][:, 3:4], mvv[i][:, 0:1], mvv[i][:, 2:3], -1.0,
                    op0=ALU.mult, op1=ALU.mult, R=[Bsmall[i]], W=[Bsmall[i]])
            for i in range(n):
                P.v("act", "activation", ys[i], ys[i], AF.Identity, scale=mvv[i][:, 2:3], bias=mvv[i][:, 3:4],
                    R=[Bys[i], Bsmall[i]], W=[Bys[i]])
            for i in range(n):
                P.v(gmul_eng, "tensor_tensor", ys[i], ys[i], g_bc, ALU.mult, R=[Bys[i]] + list(Bvec), W=[Bys[i]])
            for i, t in enumerate(tiles):
                P.v("pool", "tensor_tensor", x_sb[:, t, :], ys[i], b_bc, ALU.add, R=[Bys[i]] + list(Bvec), W=[Bx[t]])
            if next_mod is None:
                return
            layer, vA, vB = next_mod
            nb = len(banks) // 2
            for i, t in enumerate(tiles):
                c = tile_cond(t)
                pair = banks[2 * (i % nb):2 * (i % nb) + 2]
                for half in range(2):
                    pb, Bp = pair[half]
                    for q in range(4):
                        kc = half * 4 + q
                        P.tr(pb[:, q * 128:(q + 1) * 128], x_sb[:, t, kc * 128:(kc + 1) * 128], ident_f[:],
                             reads=[Bx[t], Bconst], writes=[Bp])
                for half in range(2):
                    pb, Bp = pair[half]
                    for q in range(4):
                        kc = half * 4 + q
                        if half == 0:
                            P.v("dve", "tensor_scalar", hT[:, kc, t * 128:(t + 1) * 128], pb[:, q * 128:(q + 1) * 128],
                                alpha_col(layer, vA, kc, c), alpha_col(layer, vB, kc, c),
                                op0=ALU.mult, op1=ALU.add, R=[Bp, Bmod], W=[BhT[t][0]])
                        else:
                            P.v("act", "activation", hT[:, kc, t * 128:(t + 1) * 128], pb[:, q * 128:(q + 1) * 128],
                                AF.Identity, scale=alpha_col(layer, vA, kc, c), bias=alpha_col(layer, vB, kc, c),
                                R=[Bp, Bmod], W=[BhT[t][1]])

        PBK = [(psum[i], PB[i]) for i in range(8)]

        def store_x(label):
            for t in range(NT):
                P.dma("sp", y_d[t * 128:(t + 1) * 128, :], x_sb[:, t, :], reads=[Bx[t]])

        if stop == "ada":
            store_x("ada")
            P.end_phase()
            return nc

        with ExitStack() as ph:
            bc = [sbuf(ph, "bc%d" % i, [128, D]) for i in range(8)]
            Bbc = [Buf("bc%d" % i) for i in range(8)]
            psc = sbuf(ph, "psc", [128, D])
            Bpsc = Buf("psc")
            dgall = sbuf(ph, "dgall", [128, 8, 128])
            Bdg = Buf("dgall")
            h_tok = sbuf(ph, "h_tok", [128, NT, D])
            Bh = [Buf("h%d" % t) for t in range(NT)]
            ptab = sbuf(ph, "ptab", [128, 4, 5, 128])
            Bptab = Buf("ptab")
            pw = sbuf(ph, "pw", [128, 4, 2, 256], BF16)
            Bpw = Buf("pw")
            pTt = [sbuf(ph, "pTt%d" % i, [128, 8, 128], BF16) for i in range(2)]
            BpTt = [Buf("pTt0"), Buf("pTt1")]
            ybuf = [sbuf(ph, "ybuf%d" % i, [128, D]) for i in range(4)]
            By = [Buf("y%d" % i) for i in range(4)]

            P.dma("sp", ptab[:], ptab_d.rearrange("w k s t -> s w k t"), writes=[Bptab])
            P.dma("pool", pw[:], pool_w_d.rearrange("g (i c) d -> c g i d", c=128), writes=[Bpw])
            load_bc(psc[:], pool_scale_d[0:1, :], [Bpsc])
            load_bc(bc[6][:], ln1g_d[0:1, :], [Bbc[6]])
            load_bc(bc[7][:], ln1b_d[0:1, :], [Bbc[7]])
            for c in range(2):
                make_bc(bc[2 * c], 0, 1, c, dgall, Bdg, PBK[0:2], [Bbc[2 * c]])
                make_bc(bc[2 * c + 1], 0, 0, c, dgall, Bdg, PBK[2:4], [Bbc[2 * c + 1]])
                make_bc(bc[4 + c], 0, 2, c, dgall, Bdg, PBK[4:6], [Bbc[4 + c]])
                P.v("pool", "tensor_tensor", bc[4 + c][:], bc[4 + c][:], psc[:], ALU.mult,
                    R=[Bbc[4 + c], Bpsc], W=[Bbc[4 + c]])
            for t in range(NT):
                c = tile_cond(t)
                P.v("dve", "tensor_tensor", h_tok[:, t, :], x_sb[:, t, :], bc[2 * c][:], ALU.mult,
                    R=[Bx[t], Bbc[2 * c]], W=[Bh[t]])
                P.v("pool", "tensor_tensor", h_tok[:, t, :], h_tok[:, t, :], bc[2 * c + 1][:], ALU.add,
                    R=[Bh[t], Bbc[2 * c + 1]], W=[Bh[t]])
            seq_of = {}
            for (t0, ntl) in SEQS:
                for q in range(ntl):
                    seq_of[t0 + q] = (q, ntl)
            for grp in range(3):
                tiles = list(range(4 * grp, 4 * grp + 4))
                for i, t in enumerate(tiles):
                    q, ntl = seq_of[t]
                    c = tile_cond(t)
                    k = i % 2
                    (pA, BA), (pBk, BB) = PBK[2 * k], PBK[2 * k + 1]
                    kind = 0 if q == 0 else (2 if q == ntl - 1 else 1)
                    srcs = [(t, kind)]
                    if q > 0:
                        srcs.append((t - 1, 3))
                    if q < ntl - 1:
                        srcs.append((t + 1, 4))
                    for kc in range(8):
                        wi = kc // 2
                        bank, Bb = (pA, BA) if kc < 4 else (pBk, BB)
                        dst = bank[:, (kc % 4) * 128:(kc % 4 + 1) * 128]
                        for si, (ts, kd) in enumerate(srcs):
                            P.mm(dst, h_tok[:, ts, kc * 128:(kc + 1) * 128], ptab[:, wi, kd, :],
                                 si == 0, si == len(srcs) - 1, reads=[Bh[ts], Bptab], writes=[Bb])
                    P.v("act", "copy", pTt[k][:, 0:4, :], pA[:].rearrange("p (a b) -> p a b", b=128),
                        R=[BA], W=[BpTt[k]])
                    P.v("act", "copy", pTt[k][:, 4:8, :], pBk[:].rearrange("p (a b) -> p a b", b=128),
                        R=[BB], W=[BpTt[k]])
                    for half in range(2):
                        bank, Bb = (pA, BA) if half == 0 else (pBk, BB)
                        for gg in range(2):
                            g = half * 2 + gg
                            for i2 in range(2):
                                P.mm(bank[:, gg * 256:(gg + 1) * 256], pTt[k][:, 2 * g + i2, :], pw[:, g, i2, :],
                                     i2 == 0, i2 == 1, reads=[BpTt[k], Bpw], writes=[Bb])
                        P.v("dve", "tensor_tensor", ybuf[i][:, half * 512:(half + 1) * 512], bank[:],
                            bc[4 + c][:, half * 512:(half + 1) * 512], ALU.mult, R=[Bb, Bbc[4 + c]], W=[By[i]])
                    P.v("dve", "scalar_tensor_tensor", ybuf[i][:], x_sb[:, t, :], ALPHA, ybuf[i][:],
                        op0=ALU.mult, op1=ALU.add, R=[Bx[t], By[i]], W=[By[i]])
                ln_group(tiles, [ybuf[i][:] for i in range(4)], By, bc[6][:], bc[7][:], [Bbc[6], Bbc[7]],
                         (0, 4, 3), PBK[4:8], "pool")
            if stop == "l0mix":
                store_x("l0mix")
            P.end_phase()
        if stop == "l0mix":
            return nc


        def moe_phase(layer, next_mod, final):
            with ExitStack() as ph:
                bc = [sbuf(ph, "bc%d" % i, [128, D]) for i in range(4)]
                Bbc = [Buf("mbc%d" % i) for i in range(4)]
                dgall = sbuf(ph, "dgall", [128, 8, 128])
                Bdg = Buf("dgall")
                wg = [sbuf(ph, "wg%d" % i, [128, 8, 512], BF16) for i in range(2)]
                wu = [sbuf(ph, "wu%d" % i, [128, 8, 512], BF16) for i in range(2)]
                wd = [sbuf(ph, "wd%d" % i, [128, 4, D], BF16) for i in range(2)]
                Bwg = [Buf("wg0"), Buf("wg1")]
                Bwu = [Buf("wu0"), Buf("wu1")]
                Bwd = [Buf("wd0"), Buf("wd1")]
                heT = [sbuf(ph, "heT%d" % i, [128, 4, 512], BF16) for i in range(2)]
                Bhe = [Buf("he0"), Buf("he1")]
                sg = [sbuf(ph, "sg%d" % i, [128, 512], BF16) for i in range(2)]
                Bsg = [Buf("sg0"), Buf("sg1")]
                yacc = sbuf(ph, "yacc", [128, NT, D])
                Byacc = [Buf("yacc%d" % t) for t in range(NT)]
                wr_b = sbuf(ph, "wr_b", [128, 8, 16], BF16)
                brt = sbuf(ph, "brt", [128, 16])
                Bwr = Buf("wr")
                Bbrt = Buf("brt")
                rt = sbuf(ph, "rt", [128, 1472])
                Brt = Buf("rt")
                G = [psum[0], psum[1]]
                U = [psum[2], psum[3]]
                Dn = [psum[4], psum[5], psum[6], psum[7]]
                BG, BU, BD = [PB[0], PB[1]], [PB[2], PB[3]], [PB[4], PB[5], PB[6], PB[7]]

                def load_expert(e):
                    sl = e % 2
                    P.dma("pool", wg[sl][:], w_gate_d[layer, e].rearrange("(kc k) f -> k kc f", k=128), writes=[Bwg[sl]])
                    P.dma("pool", wu[sl][:], w_up_d[layer, e].rearrange("(kc k) f -> k kc f", k=128), writes=[Bwu[sl]])
                    P.dma("pool", wd[sl][:], w_down_d[layer, e].rearrange("(fc f) d -> f fc d", f=128), writes=[Bwd[sl]])

                P.dma("pool", wr_b[:], w_router_d.rearrange("(kc k) e -> k kc e", k=128), writes=[Bwr])
                load_expert(0)
                load_expert(1)
                P.dma("sp", brt[:], b_router_d[0:1, :].partition_broadcast(128), writes=[Bbrt])
                load_bc(bc[2][:], ln2g_d[layer:layer + 1, :], [Bbc[2]])
                load_bc(bc[3][:], ln2b_d[layer:layer + 1, :], [Bbc[3]])
                for c in range(2):
                    make_bc(bc[c], layer, 5, c, dgall, Bdg, PBK[6:8], [Bbc[c]])

                lg = psum[4]
                for t in range(NT):
                    for kc in range(8):
                        P.mm(lg[:, t * 16:(t + 1) * 16], hT[:, kc, t * 128:(t + 1) * 128], wr_b[:, kc, :],
                             kc == 0, kc == 7, reads=[BhT[t][0], BhT[t][1], Bwr], writes=[PB[4]])
                o = [0]

                def carve(nel):
                    v_ = rt[:, o[0]:o[0] + nel]
                    o[0] += nel
                    return v_
                s_sb, sel, t6, gs, gm = carve(192), carve(192), carve(288), carve(48), carve(12)
                gmask, msel, top8, emask, wsel, wsum = carve(48), carve(192), carve(96), carve(192), carve(192), carve(12)
                Rr, Wr = [Brt], [Brt]
                P.v("act", "activation", s_sb, lg[:, 0:192], AF.Sigmoid, R=[PB[4]], W=Wr)
                P.v("dve", "tensor_tensor", sel.rearrange("p (t e) -> p t e", e=16),
                    s_sb.rearrange("p (t e) -> p t e", e=16), brt[:].unsqueeze(1).to_broadcast([128, NT, 16]),
                    ALU.add, R=Rr + [Bbrt], W=Wr)
                sel4 = sel.rearrange("p (g e) -> p g e", e=4)
                t6v = t6.rearrange("p (g e) -> p g e", e=6)
                P.v("dve", "tensor_tensor", t6v[:, :, 0:3], sel4[:, :, 0:3], sel4[:, :, 1:4], ALU.add, R=Rr, W=Wr)
                P.v("dve", "tensor_tensor", t6v[:, :, 3:5], sel4[:, :, 0:2], sel4[:, :, 2:4], ALU.add, R=Rr, W=Wr)
                P.v("dve", "tensor_tensor", t6v[:, :, 5:6], sel4[:, :, 0:1], sel4[:, :, 3:4], ALU.add, R=Rr, W=Wr)
                P.v("dve", "tensor_reduce", gs, t6v, AX.X, ALU.max, R=Rr, W=Wr)
                gs3 = gs.rearrange("p (t g) -> p t g", g=4)
                P.v("dve", "tensor_reduce", gm, gs3, AX.X, ALU.max, R=Rr, W=Wr)
                P.v("dve", "tensor_tensor", gmask.rearrange("p (t g) -> p t g", g=4), gs3,
                    gm.unsqueeze(2).to_broadcast([128, NT, 4]), ALU.is_equal, R=Rr, W=Wr)
                P.v("dve", "scalar_tensor_tensor", msel.rearrange("p (g e) -> p g e", e=4), sel4, 2.0,
                    gmask.unsqueeze(2).to_broadcast([128, 48, 4]), op0=ALU.add, op1=ALU.mult, R=Rr, W=Wr)
                top83 = top8.rearrange("p (t e) -> p t e", e=8)
                for t in range(NT):
                    P.v("dve", "max", top83[:, t, :], msel[:, t * 16:(t + 1) * 16], R=Rr, W=Wr)
                P.v("dve", "tensor_tensor", emask.rearrange("p (t e) -> p t e", e=16),
                    msel.rearrange("p (t e) -> p t e", e=16), top83[:, :, 1:2].to_broadcast([128, NT, 16]),
                    ALU.is_ge, R=Rr, W=Wr)
                P.v("dve", "tensor_tensor", wsel, s_sb, emask, ALU.mult, R=Rr, W=Wr)
                wsel3 = wsel.rearrange("p (t e) -> p t e", e=16)
                P.v("dve", "tensor_reduce", wsum, wsel3, AX.X, ALU.add, R=Rr, W=Wr)
                P.v("dve", "reciprocal", wsum, wsum, R=Rr, W=Wr)
                P.v("dve", "tensor_tensor", cmb[:], wsel3, wsum.unsqueeze(2).to_broadcast([128, NT, 16]),
                    ALU.mult, R=Rr, W=Bcmb)

                blocks = [(e, b) for e in range(16) for b in range(3)]
                hT_bufs = [[x for t in range(4 * b, 4 * b + 4) for x in BhT[t]] for b in range(3)]

                def GU(k):
                    e, b = blocks[k]
                    sl = e % 2
                    for fc in range(4):
                        gi = (k * 4 + fc) % 2
                        for kc in range(8):
                            P.mm(G[gi][:], wg[sl][:, kc, fc * 128:(fc + 1) * 128], hT[:, kc, b * 512:(b + 1) * 512],
                                 kc == 0, kc == 7, reads=[Bwg[sl]] + hT_bufs[b], writes=[BG[gi]])
                        for kc in range(8):
                            P.mm(U[gi][:], wu[sl][:, kc, fc * 128:(fc + 1) * 128], hT[:, kc, b * 512:(b + 1) * 512],
                                 kc == 0, kc == 7, reads=[Bwu[sl]] + hT_bufs[b], writes=[BU[gi]])
                        P.v("act", "activation", sg[gi][:], G[gi][:], AF.Silu, R=[BG[gi]], W=[Bsg[gi]])
                        P.v("dve", "tensor_tensor", heT[k % 2][:, fc, :], U[gi][:], sg[gi][:], ALU.mult,
                            R=[BU[gi], Bsg[gi]], W=[Bhe[k % 2]])

                def DNp(k):
                    e, b = blocks[k]
                    sl = e % 2
                    for tt in range(4):
                        t = 4 * b + tt
                        for half in range(2):
                            di = (k * 8 + tt * 2 + half) % 4
                            for fc in range(4):
                                P.mm(Dn[di][:], heT[k % 2][:, fc, tt * 128:(tt + 1) * 128],
                                     wd[sl][:, fc, half * 512:(half + 1) * 512], fc == 0, fc == 3,
                                     reads=[Bhe[k % 2], Bwd[sl]], writes=[BD[di]])
                            dst = yacc[:, t, half * 512:(half + 1) * 512]
                            if e == 0:
                                P.v("dve", "tensor_scalar", dst, Dn[di][:], cmb[:, t, e:e + 1], None, op0=ALU.mult,
                                    R=[BD[di], Bcmb[t]], W=[Byacc[t]])
                            else:
                                P.v("dve", "scalar_tensor_tensor", dst, Dn[di][:], cmb[:, t, e:e + 1], dst,
                                    op0=ALU.mult, op1=ALU.add, R=[BD[di], Bcmb[t], Byacc[t]], W=[Byacc[t]])

                GU(0)
                for k in range(len(blocks)):
                    if k + 1 < len(blocks):
                        GU(k + 1)
                    DNp(k)
                    e_, b_ = blocks[k]
                    if b_ == 2 and e_ + 2 < 16:
                        load_expert(e_ + 2)

                def pre(grp):
                    for t in range(4 * grp, 4 * grp + 4):
                        c = tile_cond(t)
                        ya = yacc[:, t, :]
                        P.v("pool", "tensor_tensor", ya, ya, bc[c][:], ALU.mult, R=[Byacc[t], Bbc[c]], W=[Byacc[t]])
                    for t in range(4 * grp, 4 * grp + 4):
                        ya = yacc[:, t, :]
                        P.v("dve", "scalar_tensor_tensor", ya, x_sb[:, t, :], ALPHA, ya, op0=ALU.mult, op1=ALU.add,
                            R=[Bx[t], Byacc[t]], W=[Byacc[t]])

                pre(0)
                for grp in range(3):
                    if grp + 1 < 3:
                        pre(grp + 1)
                    tiles = list(range(4 * grp, 4 * grp + 4))
                    ln_group(tiles, [yacc[:, t, :] for t in tiles], [Byacc[t] for t in tiles], bc[2][:], bc[3][:],
                             [Bbc[2], Bbc[3]], next_mod, PBK, "dve")
                    if final:
                        for t in tiles:
                            P.dma("sp", y_d[t * 128:(t + 1) * 128, :], x_sb[:, t, :], reads=[Bx[t]])
                P.end_phase()

        moe_phase(0, (1, 1, 0), False)
        if stop == "l0moe":
            store_x("l0moe")
            P.end_phase()
            return nc


        with ExitStack() as pa:
            QT_s = sbuf(pa, "QT_s", [128, 8, 1024], BF16)
            KT_s = sbuf(pa, "KT_s", [128, 8, 1024], BF16)
            Vp_s = sbuf(pa, "Vp_s", [128, 8, 8, 192], BF16)
            O_CTXV, O_M01, O_RBB, O_E, O_ONES, RG_N = 4096, 10240, 12032, 13824, 19200, 19392
            Rg = sbuf(pa, "Rg", [128, RG_N], BF16)
            QT_p = Rg[:, 0:4096].rearrange("p (c t) -> p c t", c=8)
            KT_p = Rg[:, 4096:8192].rearrange("p (c t) -> p c t", c=8)
            Vp_p = Rg[:, 8192:14336].rearrange("p (m a e) -> p m a e", m=4, a=8)
            ctxKT = Rg[:, 0:4096].rearrange("p (c t) -> p c t", c=8)
            ctxVp = Rg[:, O_CTXV:O_M01].rearrange("p (m a e) -> p m a e", m=4, a=8)
            m01 = Rg[:, O_M01:O_RBB].rearrange("p (m q) -> p m q", q=128)
            rbb = Rg[:, O_RBB:O_E].rearrange("p (b d q) -> p b d q", b=2, d=7)
            Etab = Rg[:, O_E:O_ONES].rearrange("p (b d q) -> p b d q", b=3, d=14)
            onesp = Rg[:, O_ONES:RG_N]
            BQs, BKs, BVs = Buf("QTs"), Buf("KTs"), Buf("Vs")
            BQp, BKp, BVp = Buf("QTp"), Buf("KTp"), Buf("Vp")
            Bones = Buf("onesp")

            with ExitStack() as ph:
                wb = [sbuf(ph, "wqkv%d" % i, [128, 8, 512], BF16) for i in range(2)]
                Bwb = [Buf("wb0"), Buf("wb1")]
                stg = [sbuf(ph, "stg%d" % i, [128, 512]) for i in range(2)]
                Bstg = [Buf("stg0"), Buf("stg1")]
                P.v("pool", "memset", Vp_s[:, :, :, 64:128], 0.0, W=[BVs])
                P.v("pool", "memset", Vp_p[:, :, :, 64:128], 0.0, W=[BVp])
                P.v("pool", "memset", onesp[:, 0:64], 1.0, W=[Bones])
                P.v("pool", "memset", onesp[:, 64:128], 0.0, W=[Bones])
                P.v("pool", "memset", onesp[:, 128:192], 1.0, W=[Bones])
                nb = [0]
                ns = [0]

                def bank():
                    i = nb[0] % 8
                    nb[0] += 1
                    return psum[i], PB[i], ("dve" if i % 2 == 0 else "act")

                def evac(eng, dst, src, Bsrc, W, mul=None):
                    if eng == "dve":
                        if mul is None:
                            P.v("dve", "tensor_copy", dst, src, R=[Bsrc], W=W)
                        else:
                            P.v("dve", "tensor_scalar_mul", dst, src, mul, R=[Bsrc], W=W)
                    else:
                        if mul is None:
                            P.v("act", "copy", dst, src, R=[Bsrc], W=W)
                        else:
                            P.v("act", "mul", dst, src, mul, R=[Bsrc], W=W)

                hTb = lambda tiles: [x for t in tiles for x in BhT[t]]
                for cb in range(6):
                    sl = cb % 2
                    P.dma("pool", wb[sl][:], w_qkv_d.rearrange("(kc k) n -> k kc n", k=128)[:, :, cb * 512:(cb + 1) * 512],
                          writes=[Bwb[sl]])
                    if cb < 4:
                        isq = cb < 2
                        for i in range(4):
                            a = (cb % 2) * 4 + i
                            for tb in range(3):
                                pb, Bp, eng = bank()
                                for kc in range(8):
                                    P.mm(pb[:], wb[sl][:, kc, i * 128:(i + 1) * 128], hT[:, kc, tb * 512:(tb + 1) * 512],
                                         kc == 0, kc == 7, reads=[Bwb[sl]] + hTb(range(4 * tb, 4 * tb + 4)), writes=[Bp])
                                if tb == 0:
                                    dst = (QT_p if isq else KT_p)[:, a, :]
                                    W = [BQp if isq else BKp]
                                else:
                                    dst = (QT_s if isq else KT_s)[:, a, (tb - 1) * 512:tb * 512]
                                    W = [BQs if isq else BKs]
                                evac(eng, dst, pb[:], Bp, W, mul=(0.125 if isq else None))
                    if cb in (2, 3, 4, 5):
                        isv = cb >= 4
                        hh = cb % 2
                        for t in (range(NT) if isv else range(4)):
                            pb, Bp, eng = bank()
                            for kc in range(8):
                                P.mm(pb[:], hT[:, kc, t * 128:(t + 1) * 128], wb[sl][:, kc, :], kc == 0, kc == 7,
                                     reads=[Bwb[sl]] + hTb([t]), writes=[Bp])
                            if isv:
                                src4 = pb[:].rearrange("p (a two d) -> p a two d", two=2, d=64)
                                for par in range(2):
                                    if t < 4:
                                        evac(eng, Vp_p[:, t, hh * 4:(hh + 1) * 4, par * 128:par * 128 + 64],
                                             src4[:, :, par, :], Bp, [BVp])
                                    else:
                                        evac(eng, Vp_s[:, t - 4, hh * 4:(hh + 1) * 4, par * 128:par * 128 + 64],
                                             src4[:, :, par, :], Bp, [BVs])
                            if t < 4:
                                k = ns[0] % 2
                                ns[0] += 1
                                evac("dve" if eng == "act" else "act", stg[k][:], pb[:], Bp, [Bstg[k]])
                                dd = nv_d if isv else nk_d
                                s_, l0 = t // 2, (t % 2) * 128
                                P.dma("sp", dd[s_, hh * 8:(hh + 1) * 8, l0:l0 + 128, :].rearrange("h l d -> l h d"),
                                      stg[k][:].rearrange("p (h d) -> p h d", d=64), reads=[Bstg[k]])
                P.end_phase()

            with ExitStack() as pb_:
                OT_all = sbuf(pb_, "OT_all", [128, 8, 1024], BF16)
                BOT = [Buf("OT%d" % a) for a in range(8)]
                wo = sbuf(pb_, "wo", [128, 8, D], BF16)
                Bwo = Buf("wo")
                By = [Buf("ay%d" % i) for i in range(4)]
                ST = [PBK[i] for i in range(4)]
                ACC = [(PBK[4], PBK[5]), (PBK[6], PBK[7])]
                cn = {"st": 0, "pt": 0, "em": 0}

                def heads_scope(scope):
                    PT = [sbuf(scope, "PT%d" % i, [128, 512], BF16) for i in range(4)]
                    BPT = [Buf("PT%d" % i) for i in range(4)]
                    rS = sbuf(scope, "rS", [128, 512])
                    BrS = Buf("rS")
                    return PT, BPT, rS, BrS

                LOOKAHEAD = 3
                pend = []

                def drain(keep):
                    while len(pend) > keep:
                        pend.pop(0)()

                def segment(PT, BPT, Kst, Qmv, nq, Bk, Bq, esl, BE_, Vblk, BV_, ones_blk, acc, c0):
                    st, Bst = ST[cn["st"] % 4]
                    cn["st"] += 1
                    k = cn["pt"] % 4
                    cn["pt"] += 1
                    P.mm(st[:, 0:nq], Kst, Qmv, True, True, reads=[Bk, Bq], writes=[Bst])
                    P.v("act", "activation", PT[k][:, 0:nq], st[:, 0:nq], AF.Exp, R=[Bst], W=[BPT[k]])
                    if esl is not None:
                        P.v("dve", "tensor_tensor", PT[k][:, 0:nq], PT[k][:, 0:nq], esl, ALU.mult,
                            R=[BPT[k], BE_], W=[BPT[k]])
                    (pO, BO_), (pS, BS_) = acc

                    def back():
                        P.mm(pO[:, c0:c0 + nq], Vblk, PT[k][:, 0:nq], False, False, reads=[BPT[k], BV_], writes=[BO_],
                             skip=True)
                        P.mm(pS[:, c0:c0 + nq], ones_blk, PT[k][:, 0:nq], False, False, reads=[BPT[k], Bones],
                             writes=[BS_], skip=True)
                    pend.append(back)
                    drain(LOOKAHEAD)

                def zero_acc(acc, ncols):
                    (pO, BO_), (pS, BS_) = acc
                    P.v("dve", "memset", pO[:, 0:ncols], 0.0, W=[BO_])
                    P.v("dve", "memset", pS[:, 0:ncols], 0.0, W=[BS_])

                def normalize(acc, ncols, rS, BrS, dst, Wd):
                    (pO, BO_), (pS, BS_) = acc

                    def fin():
                        P.v("dve", "reciprocal", rS[:, 0:ncols], pS[:, 0:ncols], R=[BS_], W=[BrS])
                        P.v("dve", "tensor_tensor", dst, pO[:, 0:ncols], rS[:, 0:ncols], ALU.mult, R=[BO_, BrS], W=Wd)
                    pend.append(fin)

                def proj_group(tiles, tcols, ybufs, deadW, bcv, Bbcv):
                    for i, t in enumerate(tiles):
                        tc = tcols[i]
                        for half in range(2):
                            pb, Bp = PBK[2 * (i % 2) + half]
                            for kc in range(8):
                                P.mm(pb[:], OT_all[:, kc, tc:tc + 128], wo[:, kc, half * 512:(half + 1) * 512],
                                     kc == 0, kc == 7, reads=[BOT[kc], Bwo], writes=[Bp])
                            P.v("dve", "tensor_tensor", ybufs[i][:, half * 512:(half + 1) * 512], pb[:],
                                bcv[0][:, half * 512:(half + 1) * 512], ALU.mult, R=[Bp, Bbcv[0]],
                                W=[By[i]] + list(deadW))
                        P.v("dve", "scalar_tensor_tensor", ybufs[i], x_sb[:, t, :], ALPHA, ybufs[i],
                            op0=ALU.mult, op1=ALU.add, R=[Bx[t], By[i]], W=[By[i]])
                    ln_group(tiles, ybufs, By, bcv[1], bcv[2], [Bbcv[1], Bbcv[2]], (1, 4, 3), PBK[4:8], "pool")

                def prep_vectors(cidx, bcv, Bbcv, dgall_v, Bdg_v, deadW):
                    load_bc(bcv[1], ln1g_d[1:2, :], [Bbcv[1]] + list(deadW))
                    load_bc(bcv[2], ln1b_d[1:2, :], [Bbcv[2]] + list(deadW))
                    make_bc(bcv[0], 1, 2, cidx, dgall_v, Bdg_v, PBK[6:8], [Bbcv[0]] + list(deadW))

                with ExitStack() as ph:
                    PT, BPT, rS, BrS = heads_scope(ph)
                    P.dma("pool", wo[:], w_o_d.rearrange("(kc k) n -> k kc n", k=128), writes=[Bwo])
                    for a in range(8):
                        for s_ in range(2):
                            acc = ACC[(2 * a + s_) % 2]
                            zero_acc(acc, 256)
                            for c in range(2):
                                for rho in range(2):
                                    pr = slice(64 * rho, 64 * rho + 64)
                                    vc = slice(64 * rho, 64 * rho + 128)
                                    k0 = s_ * 256 + c * 128
                                    segment(PT, BPT, KT_p[pr, a, k0:k0 + 128], QT_p[pr, a, s_ * 256:(s_ + 1) * 256], 256,
                                            BKp, BQp, None, None, Vp_p[:, 2 * s_ + c, a, vc], BVp, onesp[:, vc], acc, 0)
                            normalize(acc, 256, rS, BrS, OT_all[:, a, s_ * 256:(s_ + 1) * 256], [BOT[a]])
                    drain(0)
                    P.end_phase()

                yb_p = [Rg[:, 2048 * i:2048 * (i + 1)].bitcast(F32) for i in range(4)]
                bc_p = [Rg[:, 8192 + 2048 * i:8192 + 2048 * (i + 1)].bitcast(F32) for i in range(3)]
                dg_p = Rg[:, 14336:16384].bitcast(F32).rearrange("p (a b) -> p a b", b=128)
                Bbc_p = [Buf("pbc%d" % i) for i in range(3)]
                prep_vectors(0, bc_p, Bbc_p, dg_p, Buf("pdg"), [])
                proj_group([0, 1, 2, 3], [0, 128, 256, 384], yb_p, [], bc_p, Bbc_p)
                if stop == "l1mixp":
                    store_x("l1mixp")
                P.end_phase()
                if stop == "l1mixp":
                    return nc

                with ExitStack() as ph:
                    PT, BPT, rS, BrS = heads_scope(ph)
                    ck_tok = OT_all[:, 0:4, :].rearrange("p c (h d) -> p c h d", d=64)
                    Bck = Buf("ck_tok")
                    BctxK, BctxV, Bm01 = Buf("ctxK"), Buf("ctxV"), Buf("m01")
                    Brbb = [Buf("rbb0"), Buf("rbb1")]
                    BE = [Buf("E0"), Buf("E1"), Buf("E2")]
                    for c in range(4):
                        P.dma("pool", ck_tok[:, c, :, :], ck_d[:, c * 128:(c + 1) * 128, :].rearrange("h l d -> l h d"),
                              writes=[Bck])
                    P.dma("pool", m01[:], m01_d.rearrange("m k q -> k m q"), writes=[Bm01])
                    P.v("pool", "memset", ctxVp[:, :, :, 64:128], 0.0, W=[BctxV])
                    cv4 = cv_d.rearrange("(a two) l d -> two l a d", two=2)
                    for c in range(4):
                        for par in range(2):
                            P.dma("pool", ctxVp[:, c, :, par * 128:par * 128 + 64], cv4[par, c * 128:(c + 1) * 128, :, :],
                                  writes=[BctxV])

                    def build_E(h):
                        hb, eb = h % 2, h % 3
                        P.dma("pool", rbb[:, hb, :, :], rbt_d[h].rearrange("d k q -> k d q"), writes=[Brbb[hb]])
                        P.v("act", "activation", rbb[:, hb, :, :], rbb[:, hb, :, :], AF.Exp, R=[Brbb[hb]], W=[Brbb[hb]])
                        for s2 in range(2):
                            P.v("pool", "tensor_tensor", Etab[:, eb, s2 * 7:(s2 + 1) * 7, :], rbb[:, hb, :, :],
                                m01[:, s2 * 7:(s2 + 1) * 7, :], ALU.mult, R=[Brbb[hb], Bm01], W=[BE[eb]])

                    for c in range(4):
                        pTb = psum[c % 2][:].bitcast(BF16)
                        for a in range(8):
                            P.tr(pTb[:, a * 128:(a + 1) * 128], ck_tok[:, c, 2 * a:2 * a + 2, :].rearrange("p h d -> p (h d)"),
                                 ident_b[:], reads=[Bck, Bconst], writes=[PB[c % 2]])
                        P.v("dve" if c % 2 == 0 else "act", "tensor_copy" if c % 2 == 0 else "copy",
                            ctxKT[:, :, c * 128:(c + 1) * 128], pTb.rearrange("p (a t) -> p a t", a=8),
                            R=[PB[c % 2]], W=[BctxK])
                    build_E(0)
                    build_E(1)
                    build_E(2)
                    for a in range(8):
                        for qb in range(2):
                            acc = ACC[(2 * a + qb) % 2]
                            zero_acc(acc, 512)
                            for (m, jlo, cnt, e2) in _SEGS[qb]:
                                for rho in range(2):
                                    eb = (2 * a + rho) % 3
                                    pr = slice(64 * rho, 64 * rho + 64)
                                    vc = slice(64 * rho, 64 * rho + 128)
                                    nq, q0 = cnt * 128, jlo * 128
                                    i0 = e2 * 7 + 3 - m + jlo
                                    esl = Etab[:, eb, i0:i0 + cnt, :].rearrange("p a b -> p (a b)")
                                    segment(PT, BPT, KT_s[pr, a, m * 128:(m + 1) * 128], QT_s[pr, a, q0:q0 + nq], nq,
                                            BKs, BQs, esl, BE[eb], Vp_s[:, m, a, vc], BVs, onesp[:, vc], acc, q0 - 512 * qb)
                            for c in range(4):
                                for rho in range(2):
                                    pr = slice(64 * rho, 64 * rho + 64)
                                    vc = slice(64 * rho, 64 * rho + 128)
                                    segment(PT, BPT, ctxKT[pr, a, c * 128:(c + 1) * 128], QT_s[pr, a, qb * 512:(qb + 1) * 512],
                                            512, BctxK, BQs, None, None, ctxVp[:, c, a, vc], BctxV, onesp[:, vc], acc, 0)
                            normalize(acc, 512, rS, BrS, OT_all[:, a, qb * 512:(qb + 1) * 512], [BOT[a], Bck])
                        for h2 in (2 * a + 3, 2 * a + 4):
                            if h2 < 16:
                                build_E(h2)
                    drain(0)
                    P.end_phase()

                yb_s = [QT_s[:, 2 * i:2 * i + 2, :].rearrange("p a t -> p (a t)").bitcast(F32) for i in range(4)]
                bc_s = [KT_s[:, 2 * i:2 * i + 2, :].rearrange("p a t -> p (a t)").bitcast(F32) for i in range(3)]
                dg_s = KT_s[:, 6:8, :].rearrange("p a t -> p (a t)").bitcast(F32).rearrange("p (a b) -> p a b", b=128)
                Bbc_s = [Buf("sbc%d" % i) for i in range(3)]
                prep_vectors(1, bc_s, Bbc_s, dg_s, Buf("sdg"), [])
                proj_group([4, 5, 6, 7], [0, 128, 256, 384], yb_s, [], bc_s, Bbc_s)
                proj_group([8, 9, 10, 11], [512, 640, 768, 896], yb_s, [], bc_s, Bbc_s)
                if stop == "l1mix":
                    store_x("l1mix")
                P.end_phase()
        if stop == "l1mix":
            return nc

        moe_phase(1, None, True)
        return nc


_CACHE = {}


def make_in_maps(inputs):
    f = lambda a: np.ascontiguousarray(np.asarray(a, dtype=np.float32))
    x_prompt, x_sample = f(inputs["x_prompt"]), f(inputs["x_sample"])
    cache_k, cache_v = f(inputs["cache_k"]), f(inputs["cache_v"])
    c, c_ctx = f(inputs["c"]), f(inputs["c_ctx"])
    b_ada = f(inputs["b_ada"])
    shared = {
        "w_ada": f(inputs["w_ada"]),
        "b_adaT": np.ascontiguousarray(b_ada.reshape(2, 48, 128).transpose(0, 2, 1)),
        "ln1_g": f(inputs["ln1_g"]), "ln1_b": f(inputs["ln1_b"]),
        "ln2_g": f(inputs["ln2_g"]), "ln2_b": f(inputs["ln2_b"]),
        "pool_w": f(inputs["pool_w"])[0],
        "pool_scale": f(inputs["pool_scale"]).reshape(1, D),
        "w_qkv": f(inputs["w_qkv"])[0],
        "w_o": f(inputs["w_o"])[0],
        "rbt": _rel_bias_layout(f(inputs["rel_bias"])[0]),
        "w_router": f(inputs["w_router"]),
        "b_router": f(inputs["b_router"]).reshape(1, 16),
        "w_gate": f(inputs["w_gate"]), "w_up": f(inputs["w_up"]), "w_down": f(inputs["w_down"]),
        "ptab": _pool_tables(),
        "m01": _mask01_table(),
    }
    in_maps = []
    for i in range(N_CORES):
        m = dict(shared)
        m["xin"] = np.ascontiguousarray(np.concatenate(
            [x_prompt[2 * i].reshape(256, D), x_prompt[2 * i + 1].reshape(256, D), x_sample[i]], axis=0))
        m["ck"] = np.ascontiguousarray(cache_k[i, 0])
        m["cv"] = np.ascontiguousarray(cache_v[i, 0])
        cond = np.stack([c_ctx, c[i]], axis=-1)
        m["condT"] = np.ascontiguousarray(cond.reshape(8, 128, 2).transpose(1, 0, 2))
        in_maps.append(m)
    return in_maps


def kernel(**inputs):
    if "nc" not in _CACHE:
        _CACHE["nc"] = build_nc()
    nc = _CACHE["nc"]
    in_maps = make_in_maps(inputs)
    res = run_bass_kernel_spmd(nc, in_maps, core_ids=list(range(N_CORES)))
    ys = [r["y"] for r in res.results]
    y_prompt = np.stack([y[:512].reshape(2, 256, D) for y in ys]).reshape(16, 256, D)
    y_sample = np.stack([y[512:] for y in ys])
    nk = np.stack([r["nk"] for r in res.results]).reshape(16, 1, 16, 256, 64)
    nv = np.stack([r["nv"] for r in res.results]).reshape(16, 1, 16, 256, 64)
    return (y_prompt.astype(np.float32), y_sample.astype(np.float32),
            nk.astype(np.float32), nv.astype(np.float32))
```

```python
from contextlib import ExitStack

import numpy as np
import concourse.bass as bass
import concourse.mybir as mybir
from concourse.bass_utils import run_bass_kernel_spmd

F32 = mybir.dt.float32
BF16 = mybir.dt.bfloat16
AF = mybir.ActivationFunctionType
ALU = mybir.AluOpType
AX = mybir.AxisListType

N_CORES = 8
D = 1024
NT = 12
T = NT * 128
ALPHA = 4.0 ** 0.25
LN_EPS = 1e-5
NEG = -30000.0
POOL_SIZES = (2, 4, 8, 16)
SEQS = ((0, 2), (2, 2), (4, 8))


def tile_cond(t):
    return 0 if t < 4 else 1


class Buf:
    __slots__ = ("name", "writer", "readers", "excl")

    def __init__(self, name="", excl=False):
        self.name = name
        self.writer = None
        self.readers = []
        self.excl = excl


class Lane:
    __slots__ = ("sem", "n", "last")

    def __init__(self, sem):
        self.sem = sem
        self.n = 0
        self.last = None


class Ins:
    __slots__ = ("eng", "fn", "deps", "signal", "count", "lane", "lval", "idx")

    def __init__(self, eng, fn):
        self.eng = eng
        self.fn = fn
        self.deps = []
        self.signal = False
        self.count = 0
        self.lane = None
        self.lval = 0
        self.idx = 0


class Prog:
    ENG = ("pe", "act", "dve", "pool", "sp")
    NLANES = {"sp": 24, "pool": 12, "act": 4}

    def __init__(self, nc, es):
        self.nc = nc
        self.es = es
        self.eng = {"pe": nc.tensor, "act": nc.scalar, "dve": nc.vector,
                    "pool": nc.gpsimd, "sp": nc.sync}
        self.sem = {e: es.enter_context(nc.semaphore("s_" + e)) for e in self.ENG}
        self.cnt = {e: 0 for e in self.ENG}
        self.bar = es.enter_context(nc.semaphore("s_bar"))
        self.nbar = 0
        self.lanes = {q: [Lane(es.enter_context(nc.semaphore("l_%s%d" % (q, i)))) for i in range(n)]
                      for q, n in self.NLANES.items()}
        self.lane_rr = {q: 0 for q in self.NLANES}
        self.waited = {e: {} for e in self.ENG}
        self.streams = {e: [] for e in self.ENG}
        self.touched = []
        self.n_ins = 0

    def _add_dep(self, ins, p):
        if p is None or p is ins:
            return
        if p.lane is None and ins.lane is None and p.eng == ins.eng and p.eng == "pe":
            return
        deps = ins.deps
        if p.lane is None:
            for i, q in enumerate(deps):
                if q.lane is None and q.eng == p.eng:
                    if p.idx > q.idx:
                        deps[i] = p
                    return
        elif p in deps:
            return
        deps.append(p)

    def _deps(self, ins, reads, writes):
        if any(b.excl for b in reads):
            writes = list(writes) + [b for b in reads if b.excl and b not in writes]
            reads = [b for b in reads if not b.excl]
        same = lambda p: (p is not None and p.lane is None and ins.lane is None and p.eng == ins.eng)
        for b in reads:
            self._add_dep(ins, b.writer)
        for b in writes:
            if not same(b.writer):
                self._add_dep(ins, b.writer)
            for r in b.readers:
                if not same(r):
                    self._add_dep(ins, r)
        for b in reads:
            if not b.readers and b.writer is None:
                self.touched.append(b)
            b.readers.append(ins)
        for b in writes:
            if not b.readers and b.writer is None:
                self.touched.append(b)
            b.writer = ins
            b.readers = []

    def op(self, eng, fn, reads=(), writes=()):
        ins = Ins(eng, fn)
        ins.idx = len(self.streams[eng])
        self._deps(ins, reads, writes)
        self.streams[eng].append(ins)
        return ins

    def dma(self, eng, out, in_, reads=(), writes=(), **kw):
        E = self.eng[eng]
        ins = Ins(eng, lambda: E.dma_start(out=out, in_=in_, **kw))
        ins.idx = len(self.streams[eng])
        lanes = self.lanes[eng]
        lane = lanes[self.lane_rr[eng] % len(lanes)]
        self.lane_rr[eng] += 1
        ins.lane = lane
        lane.n += 1
        ins.lval = 16 * lane.n
        if lane.last is not None:
            ins.deps.append(lane.last)
        lane.last = ins
        self._deps(ins, reads, writes)
        self.streams[eng].append(ins)
        return ins

    def mm(self, out, lhsT, rhs, start, stop, reads=(), writes=(), skip=False):
        t = self.nc.tensor
        if skip:
            return self.op("pe", lambda: t.matmul(out, lhsT, rhs, start=start, stop=stop, skip_group_check=True),
                           reads, writes)
        return self.op("pe", lambda: t.matmul(out, lhsT, rhs, start=start, stop=stop), reads, writes)

    def tr(self, out, in_, ident, reads=(), writes=()):
        t = self.nc.tensor
        return self.op("pe", lambda: t.transpose(out, in_, ident), reads, writes)

    def v(self, eng, meth, *args, R=(), W=(), **kw):
        E = self.eng[eng]
        return self.op(eng, lambda: getattr(E, meth)(*args, **kw), R, W)

    def end_phase(self):
        ENG = self.ENG
        for e in ENG:
            for ins in self.streams[e]:
                for p in ins.deps:
                    if p.lane is None:
                        p.signal = True
            for ins in reversed(self.streams[e]):
                if ins.lane is None:
                    ins.signal = True
                    break
        for e in ENG:
            for ins in self.streams[e]:
                if ins.lane is None and ins.signal:
                    self.cnt[e] += 1
                    ins.count = self.cnt[e]
        for e in ENG:
            E = self.eng[e]
            wd = self.waited[e]
            for ins in self.streams[e]:
                for p in ins.deps:
                    if p.lane is not None:
                        sem, val = p.lane.sem, p.lval
                    else:
                        sem, val = self.sem[p.eng], p.count
                    if wd.get(sem.num, 0) >= val:
                        continue
                    wd[sem.num] = val
                    E.wait_ge(sem, val)
                bi = ins.fn()
                if ins.lane is not None:
                    bi.then_inc(ins.lane.sem, 16)
                elif ins.signal:
                    bi.then_inc(self.sem[e], 1)
                self.n_ins += 1
        sp = self.eng["sp"]
        wd = self.waited["sp"]
        all_lanes = [l for q in self.lanes.values() for l in q]
        for e in ENG:
            if e == "sp" or self.cnt[e] == 0:
                continue
            if wd.get(self.sem[e].num, 0) < self.cnt[e]:
                sp.wait_ge(self.sem[e], self.cnt[e])
        for l in all_lanes:
            if l.n and wd.get(l.sem.num, 0) < 16 * l.n:
                sp.wait_ge(l.sem, 16 * l.n)
        self.nbar += 1
        sp.sem_inc(self.bar, 1)
        for e in ENG:
            if e != "sp":
                self.eng[e].wait_ge(self.bar, self.nbar)
            for e2 in ENG:
                self.waited[e][self.sem[e2].num] = self.cnt[e2]
            for l in all_lanes:
                self.waited[e][l.sem.num] = 16 * l.n
        for l in all_lanes:
            l.last = None
        for b in self.touched:
            b.writer = None
            b.readers = []
        self.touched = []
        self.streams = {e: [] for e in ENG}


def _pool_tables():
    n = 1024
    out = np.zeros((4, 5, 128, 128), np.float32)
    t = np.arange(n)
    for wi, w in enumerate(POOL_SIZES):
        lo = np.clip(t - w // 2, 0, n)
        hi = np.clip(t - w // 2 + w, 0, n)
        cnt = (hi - lo).astype(np.float64)
        s = np.arange(n)[:, None]
        Pm = ((s >= lo[None, :]) & (s < hi[None, :])) / cnt[None, :] - np.eye(n)
        Pm = Pm.astype(np.float32)
        out[wi, 0] = Pm[0:128, 0:128]
        out[wi, 1] = Pm[128:256, 128:256]
        out[wi, 2] = Pm[896:1024, 896:1024]
        out[wi, 3] = Pm[0:128, 128:256]
        out[wi, 4] = Pm[256:384, 128:256]
    return out


def _chunks_for(j):
    if j < 2:
        return list(range(0, 4))
    if j > 5:
        return list(range(4, 8))
    return list(range(j - 2, j + 3))


def _mask_tables():
    tiles = []
    index = {}
    keymap = {}
    col = np.arange(64)
    cs = np.clip(col - 8, 0, 48)
    colok = (col[None, :] >= cs[:, None]) & (col[None, :] < cs[:, None] + 16)
    for j in range(8):
        for m in _chunks_for(j):
            mt = np.full((2, 64, 2, 64), NEG, np.float32)
            for rq in range(2):
                r = 2 * j + rq
                rs = min(max(r - 4, 0), 8)
                for rk in range(2):
                    kr = 2 * m + rk
                    if rs <= kr < rs + 8:
                        mt[rk, :, rq, :] = np.where(colok.T, 0.0, NEG)
            key = mt.tobytes()
            if key not in keymap:
                keymap[key] = len(tiles)
                tiles.append(mt.reshape(128, 128))
            index[(j, m)] = keymap[key]
    return np.stack(tiles), index


_MASKS, _MASK_IDX = _mask_tables()
_M_FULL, _M_A, _M_B = _MASK_IDX[(0, 0)], _MASK_IDX[(2, 0)], _MASK_IDX[(2, 4)]


def _mask01_table():
    full = (_MASKS[_M_FULL] == 0).astype(np.float32)
    t = np.stack([full] * 14)
    t[7 + (3 - 2)] = (_MASKS[_M_B] == 0)
    t[7 + (3 + 2)] = (_MASKS[_M_A] == 0)
    return t


def _segments():
    segs = {0: [], 1: []}
    for m in range(8):
        for qb in range(2):
            js = [j for j in range(8) if m in _chunks_for(j) and j // 4 == qb]
            if not js:
                continue
            assert js == list(range(js[0], js[-1] + 1))
            e2 = any(_MASK_IDX[(j, m)] != _M_FULL for j in js)
            for j in js:
                want = _M_FULL
                if e2 and m - j == -2:
                    want = _M_A
                if e2 and m - j == 2:
                    want = _M_B
                assert _MASK_IDX[(j, m)] == want, (j, m)
            segs[qb].append((m, js[0], len(js), int(e2)))
    return segs


_SEGS = _segments()


def _rel_bias_layout(rel_bias):
    rk = np.arange(2)[:, None, None, None]
    ck = np.arange(64)[None, :, None, None]
    rq = np.arange(2)[None, None, :, None]
    cq = np.arange(64)[None, None, None, :]
    out = np.empty((16, 7, 128, 128), np.float32)
    for i in range(7):
        dl = 3 - i
        dy = np.clip(2 * dl + rk - rq + 7, 0, 14)
        dx = np.clip(ck - cq + 15, 0, 30)
        dy, dx = np.broadcast_arrays(dy, dx)
        out[:, i] = rel_bias[:, dy, dx].reshape(16, 128, 128)
    return out


def build_nc(stop=None):
    nc = bass.Bass("TRN2", target_bir_lowering=False)

    def din(name, shape):
        return nc.dram_tensor(name, list(shape), F32, kind="ExternalInput").ap()

    def dout(name, shape):
        return nc.dram_tensor(name, list(shape), F32, kind="ExternalOutput").ap()

    xin = din("xin", [T, D])
    ck_d = din("ck", [16, 512, 64])
    cv_d = din("cv", [16, 512, 64])
    condT_d = din("condT", [128, 8, 2])
    w_ada_d = din("w_ada", [2, D, 6 * D])
    b_adaT_d = din("b_adaT", [2, 128, 48])
    ln1g_d = din("ln1_g", [2, D])
    ln1b_d = din("ln1_b", [2, D])
    ln2g_d = din("ln2_g", [2, D])
    ln2b_d = din("ln2_b", [2, D])
    pool_w_d = din("pool_w", [4, 256, 256])
    pool_scale_d = din("pool_scale", [1, D])
    w_qkv_d = din("w_qkv", [D, 3 * D])
    w_o_d = din("w_o", [D, D])
    rbt_d = din("rbt", [16, 7, 128, 128])
    w_router_d = din("w_router", [D, 16])
    b_router_d = din("b_router", [1, 16])
    w_gate_d = din("w_gate", [2, 16, D, 512])
    w_up_d = din("w_up", [2, 16, D, 512])
    w_down_d = din("w_down", [2, 16, 512, D])
    ptab_d = din("ptab", [4, 5, 128, 128])
    m01_d = din("m01", [14, 128, 128])

    y_d = dout("y", [T, D])
    nk_d = dout("nk", [2, 16, 256, 64])
    nv_d = dout("nv", [2, 16, 256, 64])
    dbg_d = dout("dbg", [128, 2, 48, 2]) if stop == "ada" else None

    with ExitStack() as es:
        P = Prog(nc, es)

        _uid = [0]

        def sbuf(scope, name, shape, dt=F32):
            _uid[0] += 1
            return scope.enter_context(nc.sbuf_tensor("sb%d_%s" % (_uid[0], name), list(shape), dt))

        x_sb = sbuf(es, "x_sb", [128, NT, D])
        hT = sbuf(es, "hT", [128, 8, T], BF16)
        ident_f = sbuf(es, "ident_f", [128, 128])
        ident_b = sbuf(es, "ident_b", [128, 128], BF16)
        ones_f = sbuf(es, "ones_f", [128, 128])
        modcol = sbuf(es, "modcol", [128, 2, 48, 2])
        cmb = sbuf(es, "cmb", [128, NT, 16])
        small = sbuf(es, "small", [128, 64])
        psum = [es.enter_context(nc.psum_tensor("ps%d" % i, [128, 512], F32)) for i in range(8)]
        PB = [Buf("ps%d" % i, excl=True) for i in range(8)]

        Bx = [Buf("x%d" % t) for t in range(NT)]
        BhT = [(Buf("hTd%d" % t), Buf("hTa%d" % t)) for t in range(NT)]
        Bconst = Buf("const")
        Bmod = [Buf("modcol0"), Buf("modcol1")]
        sTb = sbuf(es, "sTb", [128, 8, 2], BF16)
        b_col = sbuf(es, "b_col", [128, 2, 48])
        BsT = Buf("sT")
        Bbcol = Buf("bcol")
        Bcmb = [Buf("cmb%d" % t) for t in range(NT)]


        def alpha_col(layer, v, ch, c):
            return modcol[:, layer, v * 8 + ch, c:c + 1]

        PBK = [(psum[i], PB[i]) for i in range(8)]

        def ada_block(layer, nb, wa_t, Bwa_t, bank):
            pc, Bpc = bank
            src = w_ada_d[layer].rearrange("(kc k) n -> k kc n", k=128)[:, :, nb * 512:(nb + 1) * 512]
            P.dma("pool", wa_t[:], src, writes=[Bwa_t])
            for n4 in range(4):
                for kc in range(8):
                    P.mm(pc[:, n4 * 2:n4 * 2 + 2], wa_t[:, kc, n4 * 128:(n4 + 1) * 128], sTb[:, kc, :],
                         kc == 0, kc == 7, reads=[Bwa_t, BsT], writes=[Bpc])
            dst = modcol[:, layer, nb * 4:(nb + 1) * 4, :]
            P.v("dve", "tensor_tensor", dst, pc[:, 0:8].rearrange("p (c k) -> p c k", k=2),
                b_col[:, layer, nb * 4:(nb + 1) * 4].unsqueeze(2).to_broadcast([128, 4, 2]), ALU.add,
                R=[Bpc, Bbcol], W=[Bmod[layer]])
            if nb // 2 in (1, 4):
                P.v("dve", "tensor_scalar_add", dst, dst, 1.0, R=[Bmod[layer]], W=[Bmod[layer]])

        with ExitStack() as ph:
            condT = sbuf(ph, "condT", [128, 8, 2])
            sT = sbuf(ph, "sT", [128, 8, 2])
            wa = [sbuf(ph, "wa%d" % i, [128, 8, 512], BF16) for i in range(3)]
            Bwa = [Buf("wa%d" % i) for i in range(3)]

            P.v("pool", "memset", ident_f[:], 0.0, W=[Bconst])
            P.v("pool", "affine_select", ident_f[:], ident_f[:], [[-1, 128]], ALU.not_equal, 1.0,
                base=0, channel_multiplier=1, R=[Bconst], W=[Bconst])
            P.v("pool", "tensor_copy", ident_b[:], ident_f[:], R=[Bconst], W=[Bconst])
            P.v("pool", "memset", ones_f[:], 1.0, W=[Bconst])

            for t in range(NT):
                P.dma("sp", x_sb[:, t, :], xin[t * 128:(t + 1) * 128, :], writes=[Bx[t]])
            P.dma("sp", condT[:], condT_d[:, :, :], writes=[BsT])
            P.dma("sp", b_col[:], b_adaT_d.rearrange("l p c -> p l c"), writes=[Bbcol])
            P.v("act", "activation", sT[:], condT[:], AF.Silu, R=[BsT], W=[BsT])
            P.v("dve", "tensor_copy", sTb[:], sT[:], R=[BsT], W=[BsT])

            for nb in range(12):
                ada_block(0, nb, wa[nb % 3], Bwa[nb % 3], PBK[nb % 2])
            if stop == "ada":
                for nb in range(12):
                    ada_block(1, nb, wa[nb % 3], Bwa[nb % 3], PBK[nb % 2])
            if stop == "ada":
                P.dma("sp", dbg_d[:, :, :, :], modcol[:], reads=Bmod)
            P.end_phase()

        def load_bc(dst, src_row, W):
            P.dma("sp", dst, src_row.partition_broadcast(128), writes=W)

        def make_bc(dst, layer, v, c, dgall, Bdg, banks, W):
            P.v("dve", "tensor_tensor", dgall[:], ident_f[:].unsqueeze(1).to_broadcast([128, 8, 128]),
                modcol[:, layer, v * 8:(v + 1) * 8, c:c + 1].to_broadcast([128, 8, 128]), ALU.mult,
                R=[Bconst, Bmod[layer]], W=[Bdg])
            for half in range(2):
                pb, Bp = banks[half]
                P.mm(pb[:], ones_f[:], dgall[:, half * 4:(half + 1) * 4, :].rearrange("p a b -> p (a b)"),
                     True, True, reads=[Bconst, Bdg], writes=[Bp])
                P.v("act", "copy", dst[:, half * 512:(half + 1) * 512], pb[:], R=[Bp], W=W)

        Bsmall = [Buf("mv%d" % i) for i in range(4)]

        def ln_group(tiles, ys, Bys, g_bc, b_bc, Bvec, next_mod, banks, gmul_eng, part="all"):
            n = len(tiles)
            sts = [small[:, 16 * i:16 * i + 12].rearrange("p (a b) -> p a b", b=6) for i in range(n)]
            mvv = [small[:, 16 * i + 12:16 * i + 16] for i in range(n)]
            if part != "tr":
                ln_elem(tiles, ys, Bys, g_bc, b_bc, Bvec, gmul_eng, n, sts, mvv)
            if next_mod is None or part == "elem":
                return
            ln_tr(tiles, next_mod, banks)

        def ln_elem(tiles, ys, Bys, g_bc, b_bc, Bvec, gmul_eng, n, sts, mvv):
            for i in range(n):
                P.v("dve", "bn_stats", sts[i][:, 0, :], ys[i][:, 0:512], R=[Bys[i]], W=[Bsmall[i]])
                P.v("dve", "bn_stats", sts[i][:, 1, :], ys[i][:, 512:1024], R=[Bys[i]], W=[Bsmall[i]])
                P.v("dve", "bn_aggr", mvv[i][:, 0:2], sts[i], R=[Bsmall[i]], W=[Bsmall[i]])
            for i in range(n):
                P.v("act", "activation", mvv[i][:, 2:3], mvv[i][:, 1:2], AF.Sqrt, bias=LN_EPS,
                    R=[Bsmall[i]], W=[Bsmall[i]])
            for i in range(n):
                P.v("dve", "reciprocal", mvv[i][:, 2:3], mvv[i][:, 2:3], R=[Bsmall[i]], W=[Bsmall[i]])
                P.v("dve", "tensor_scalar", mvv[i][:, 3:4], mvv[i][:, 0:1], mvv[i][:, 2:3], -1.0,
                    op0=ALU.mult, op1=ALU.mult, R=[Bsmall[i]], W=[Bsmall[i]])
            for i in range(n):
                P.v("act", "activation", ys[i], ys[i], AF.Identity, scale=mvv[i][:, 2:3], bias=mvv[i][:, 3:4],
                    R=[Bys[i], Bsmall[i]], W=[Bys[i]])
            for i in range(n):
                P.v(gmul_eng, "tensor_tensor", ys[i], ys[i], g_bc, ALU.mult, R=[Bys[i]] + list(Bvec), W=[Bys[i]])
            for i, t in enumerate(tiles):
                P.v("pool", "tensor_tensor", x_sb[:, t, :], ys[i], b_bc, ALU.add, R=[Bys[i]] + list(Bvec), W=[Bx[t]])

        def ln_tr(tiles, next_mod, banks):
            layer, vA, vB = next_mod
            nb = len(banks) // 2
            for i, t in enumerate(tiles):
                c = tile_cond(t)
                pair = banks[2 * (i % nb):2 * (i % nb) + 2]
                for half in range(2):
                    pb, Bp = pair[half]
                    for q in range(4):
                        kc = half * 4 + q
                        P.tr(pb[:, q * 128:(q + 1) * 128], x_sb[:, t, kc * 128:(kc + 1) * 128], ident_f[:],
                             reads=[Bx[t], Bconst], writes=[Bp])
                for half in range(2):
                    pb, Bp = pair[half]
                    for q in range(4):
                        kc = half * 4 + q
                        if half == 0:
                            P.v("dve", "tensor_scalar", hT[:, kc, t * 128:(t + 1) * 128], pb[:, q * 128:(q + 1) * 128],
                                alpha_col(layer, vA, kc, c), alpha_col(layer, vB, kc, c),
                                op0=ALU.mult, op1=ALU.add, R=[Bp, Bmod[layer]], W=[BhT[t][0]])
                        else:
                            P.v("act", "activation", hT[:, kc, t * 128:(t + 1) * 128], pb[:, q * 128:(q + 1) * 128],
                                AF.Identity, scale=alpha_col(layer, vA, kc, c), bias=alpha_col(layer, vB, kc, c),
                                R=[Bp, Bmod[layer]], W=[BhT[t][1]])

        PBK = [(psum[i], PB[i]) for i in range(8)]

        def store_x(label):
            for t in range(NT):
                P.dma("sp", y_d[t * 128:(t + 1) * 128, :], x_sb[:, t, :], reads=[Bx[t]])

        if stop == "ada":
            store_x("ada")
            P.end_phase()
            return nc

        with ExitStack() as ph:
            bc = [sbuf(ph, "bc%d" % i, [128, D]) for i in range(8)]
            Bbc = [Buf("bc%d" % i) for i in range(8)]
            psc = sbuf(ph, "psc", [128, D])
            Bpsc = Buf("psc")
            dgall = sbuf(ph, "dgall", [128, 8, 128])
            Bdg = Buf("dgall")
            h_tok = sbuf(ph, "h_tok", [128, NT, D])
            Bh = [Buf("h%d" % t) for t in range(NT)]
            ptab = sbuf(ph, "ptab", [128, 4, 5, 128])
            Bptab = Buf("ptab")
            pw = sbuf(ph, "pw", [128, 4, 2, 256], BF16)
            Bpw = Buf("pw")
            pTt = [sbuf(ph, "pTt%d" % i, [128, 8, 128], BF16) for i in range(2)]
            BpTt = [Buf("pTt0"), Buf("pTt1")]
            ybuf = [sbuf(ph, "ybuf%d" % i, [128, D]) for i in range(4)]
            By = [Buf("y%d" % i) for i in range(4)]
            wa1 = sbuf(ph, "wa1", [128, 8, 512], BF16)
            Bwa1 = Buf("wa1")

            P.dma("sp", ptab[:], ptab_d.rearrange("w k s t -> s w k t"), writes=[Bptab])
            P.dma("pool", pw[:], pool_w_d.rearrange("g (i c) d -> c g i d", c=128), writes=[Bpw])
            load_bc(psc[:], pool_scale_d[0:1, :], [Bpsc])
            load_bc(bc[6][:], ln1g_d[0:1, :], [Bbc[6]])
            load_bc(bc[7][:], ln1b_d[0:1, :], [Bbc[7]])
            for c in range(2):
                make_bc(bc[2 * c], 0, 1, c, dgall, Bdg, PBK[0:2], [Bbc[2 * c]])
                make_bc(bc[2 * c + 1], 0, 0, c, dgall, Bdg, PBK[2:4], [Bbc[2 * c + 1]])
                make_bc(bc[4 + c], 0, 2, c, dgall, Bdg, PBK[4:6], [Bbc[4 + c]])
                P.v("pool", "tensor_tensor", bc[4 + c][:], bc[4 + c][:], psc[:], ALU.mult,
                    R=[Bbc[4 + c], Bpsc], W=[Bbc[4 + c]])
            for t in range(NT):
                c = tile_cond(t)
                P.v("dve", "tensor_tensor", h_tok[:, t, :], x_sb[:, t, :], bc[2 * c][:], ALU.mult,
                    R=[Bx[t], Bbc[2 * c]], W=[Bh[t]])
                P.v("pool", "tensor_tensor", h_tok[:, t, :], h_tok[:, t, :], bc[2 * c + 1][:], ALU.add,
                    R=[Bh[t], Bbc[2 * c + 1]], W=[Bh[t]])
            seq_of = {}
            for (t0, ntl) in SEQS:
                for q in range(ntl):
                    seq_of[t0 + q] = (q, ntl)
            for grp in range(3):
                tiles = list(range(4 * grp, 4 * grp + 4))
                for i, t in enumerate(tiles):
                    q, ntl = seq_of[t]
                    c = tile_cond(t)
                    k = i % 2
                    (pA, BA), (pBk, BB) = PBK[2 * k], PBK[2 * k + 1]
                    kind = 0 if q == 0 else (2 if q == ntl - 1 else 1)
                    srcs = [(t, kind)]
                    if q > 0:
                        srcs.append((t - 1, 3))
                    if q < ntl - 1:
                        srcs.append((t + 1, 4))
                    for kc in range(8):
                        wi = kc // 2
                        bank, Bb = (pA, BA) if kc < 4 else (pBk, BB)
                        dst = bank[:, (kc % 4) * 128:(kc % 4 + 1) * 128]
                        for si, (ts, kd) in enumerate(srcs):
                            P.mm(dst, h_tok[:, ts, kc * 128:(kc + 1) * 128], ptab[:, wi, kd, :],
                                 si == 0, si == len(srcs) - 1, reads=[Bh[ts], Bptab], writes=[Bb])
                    P.v("act", "copy", pTt[k][:, 0:4, :], pA[:].rearrange("p (a b) -> p a b", b=128),
                        R=[BA], W=[BpTt[k]])
                    P.v("act", "copy", pTt[k][:, 4:8, :], pBk[:].rearrange("p (a b) -> p a b", b=128),
                        R=[BB], W=[BpTt[k]])
                    for half in range(2):
                        bank, Bb = (pA, BA) if half == 0 else (pBk, BB)
                        for gg in range(2):
                            g = half * 2 + gg
                            for i2 in range(2):
                                P.mm(bank[:, gg * 256:(gg + 1) * 256], pTt[k][:, 2 * g + i2, :], pw[:, g, i2, :],
                                     i2 == 0, i2 == 1, reads=[BpTt[k], Bpw], writes=[Bb])
                        P.v("dve", "tensor_tensor", ybuf[i][:, half * 512:(half + 1) * 512], bank[:],
                            bc[4 + c][:, half * 512:(half + 1) * 512], ALU.mult, R=[Bb, Bbc[4 + c]], W=[By[i]])
                    P.v("dve", "scalar_tensor_tensor", ybuf[i][:], x_sb[:, t, :], ALPHA, ybuf[i][:],
                        op0=ALU.mult, op1=ALU.add, R=[Bx[t], By[i]], W=[By[i]])
                    ada_block(1, t, wa1, Bwa1, PBK[6 + t % 2])
                ln_group(tiles, [ybuf[i][:] for i in range(4)], By, bc[6][:], bc[7][:], [Bbc[6], Bbc[7]],
                         (0, 4, 3), PBK[4:8], "pool")
            if stop == "l0mix":
                store_x("l0mix")
            P.end_phase()
        if stop == "l0mix":
            return nc


        def moe_phase(layer, next_mod, final):
            with ExitStack() as ph:
                bc = [sbuf(ph, "bc%d" % i, [128, D]) for i in range(4)]
                Bbc = [Buf("mbc%d" % i) for i in range(4)]
                dgall = sbuf(ph, "dgall", [128, 8, 128])
                Bdg = Buf("dgall")
                wg = [sbuf(ph, "wg%d" % i, [128, 8, 512], BF16) for i in range(2)]
                wu = [sbuf(ph, "wu%d" % i, [128, 8, 512], BF16) for i in range(2)]
                wd = [sbuf(ph, "wd%d" % i, [128, 4, D], BF16) for i in range(2)]
                Bwg = [Buf("wg0"), Buf("wg1")]
                Bwu = [Buf("wu0"), Buf("wu1")]
                Bwd = [Buf("wd0"), Buf("wd1")]
                heT = [sbuf(ph, "heT%d" % i, [128, 4, 512], BF16) for i in range(2)]
                Bhe = [Buf("he0"), Buf("he1")]
                sg = [sbuf(ph, "sg%d" % i, [128, 512], BF16) for i in range(2)]
                Bsg = [Buf("sg0"), Buf("sg1")]
                yacc = sbuf(ph, "yacc", [128, NT, D])
                Byacc = [Buf("yacc%d" % t) for t in range(NT)]
                wr_b = sbuf(ph, "wr_b", [128, 8, 16], BF16)
                brt = sbuf(ph, "brt", [128, 16])
                Bwr = Buf("wr")
                Bbrt = Buf("brt")
                rt = sbuf(ph, "rt", [128, 1472])
                Brt = Buf("rt")
                G = [psum[0], psum[1]]
                U = [psum[2], psum[3]]
                Dn = [psum[4], psum[5], psum[6], psum[7]]
                BG, BU, BD = [PB[0], PB[1]], [PB[2], PB[3]], [PB[4], PB[5], PB[6], PB[7]]

                def load_expert(e):
                    sl = e % 2
                    P.dma("pool", wg[sl][:], w_gate_d[layer, e].rearrange("(kc k) f -> k kc f", k=128), writes=[Bwg[sl]])
                    P.dma("pool", wu[sl][:], w_up_d[layer, e].rearrange("(kc k) f -> k kc f", k=128), writes=[Bwu[sl]])
                    P.dma("pool", wd[sl][:], w_down_d[layer, e].rearrange("(fc f) d -> f fc d", f=128), writes=[Bwd[sl]])

                P.dma("pool", wr_b[:], w_router_d.rearrange("(kc k) e -> k kc e", k=128), writes=[Bwr])
                load_expert(0)
                load_expert(1)
                P.dma("sp", brt[:], b_router_d[0:1, :].partition_broadcast(128), writes=[Bbrt])
                load_bc(bc[2][:], ln2g_d[layer:layer + 1, :], [Bbc[2]])
                load_bc(bc[3][:], ln2b_d[layer:layer + 1, :], [Bbc[3]])
                for c in range(2):
                    make_bc(bc[c], layer, 5, c, dgall, Bdg, PBK[6:8], [Bbc[c]])

                lg = psum[4]
                for t in range(NT):
                    for kc in range(8):
                        P.mm(lg[:, t * 16:(t + 1) * 16], hT[:, kc, t * 128:(t + 1) * 128], wr_b[:, kc, :],
                             kc == 0, kc == 7, reads=[BhT[t][0], BhT[t][1], Bwr], writes=[PB[4]])
                o = [0]

                def carve(nel):
                    v_ = rt[:, o[0]:o[0] + nel]
                    o[0] += nel
                    return v_
                s_sb, sel, t6, gs, gm = carve(192), carve(192), carve(288), carve(48), carve(12)
                gmask, msel, top8, emask, wsel, wsum = carve(48), carve(192), carve(96), carve(192), carve(192), carve(12)
                Rr, Wr = [Brt], [Brt]
                P.v("act", "activation", s_sb, lg[:, 0:192], AF.Sigmoid, R=[PB[4]], W=Wr)
                P.v("dve", "tensor_tensor", sel.rearrange("p (t e) -> p t e", e=16),
                    s_sb.rearrange("p (t e) -> p t e", e=16), brt[:].unsqueeze(1).to_broadcast([128, NT, 16]),
                    ALU.add, R=Rr + [Bbrt], W=Wr)
                sel4 = sel.rearrange("p (g e) -> p g e", e=4)
                t6v = t6.rearrange("p (g e) -> p g e", e=6)
                P.v("dve", "tensor_tensor", t6v[:, :, 0:3], sel4[:, :, 0:3], sel4[:, :, 1:4], ALU.add, R=Rr, W=Wr)
                P.v("dve", "tensor_tensor", t6v[:, :, 3:5], sel4[:, :, 0:2], sel4[:, :, 2:4], ALU.add, R=Rr, W=Wr)
                P.v("dve", "tensor_tensor", t6v[:, :, 5:6], sel4[:, :, 0:1], sel4[:, :, 3:4], ALU.add, R=Rr, W=Wr)
                P.v("dve", "tensor_reduce", gs, t6v, AX.X, ALU.max, R=Rr, W=Wr)
                gs3 = gs.rearrange("p (t g) -> p t g", g=4)
                P.v("dve", "tensor_reduce", gm, gs3, AX.X, ALU.max, R=Rr, W=Wr)
                P.v("dve", "tensor_tensor", gmask.rearrange("p (t g) -> p t g", g=4), gs3,
                    gm.unsqueeze(2).to_broadcast([128, NT, 4]), ALU.is_equal, R=Rr, W=Wr)
                P.v("dve", "scalar_tensor_tensor", msel.rearrange("p (g e) -> p g e", e=4), sel4, 2.0,
                    gmask.unsqueeze(2).to_broadcast([128, 48, 4]), op0=ALU.add, op1=ALU.mult, R=Rr, W=Wr)
                top83 = top8.rearrange("p (t e) -> p t e", e=8)
                for t in range(NT):
                    P.v("dve", "max", top83[:, t, :], msel[:, t * 16:(t + 1) * 16], R=Rr, W=Wr)
                P.v("dve", "tensor_tensor", emask.rearrange("p (t e) -> p t e", e=16),
                    msel.rearrange("p (t e) -> p t e", e=16), top83[:, :, 1:2].to_broadcast([128, NT, 16]),
                    ALU.is_ge, R=Rr, W=Wr)
                P.v("dve", "tensor_tensor", wsel, s_sb, emask, ALU.mult, R=Rr, W=Wr)
                wsel3 = wsel.rearrange("p (t e) -> p t e", e=16)
                P.v("dve", "tensor_reduce", wsum, wsel3, AX.X, ALU.add, R=Rr, W=Wr)
                P.v("dve", "reciprocal", wsum, wsum, R=Rr, W=Wr)
                P.v("dve", "tensor_tensor", cmb[:], wsel3, wsum.unsqueeze(2).to_broadcast([128, NT, 16]),
                    ALU.mult, R=Rr, W=Bcmb)

                blocks = [(e, b) for e in range(16) for b in range(3)]
                hT_bufs = [[x for t in range(4 * b, 4 * b + 4) for x in BhT[t]] for b in range(3)]

                def GU(k):
                    e, b = blocks[k]
                    sl = e % 2
                    for fc in range(4):
                        gi = (k * 4 + fc) % 2
                        for kc in range(8):
                            P.mm(G[gi][:], wg[sl][:, kc, fc * 128:(fc + 1) * 128], hT[:, kc, b * 512:(b + 1) * 512],
                                 kc == 0, kc == 7, reads=[Bwg[sl]] + hT_bufs[b], writes=[BG[gi]])
                        for kc in range(8):
                            P.mm(U[gi][:], wu[sl][:, kc, fc * 128:(fc + 1) * 128], hT[:, kc, b * 512:(b + 1) * 512],
                                 kc == 0, kc == 7, reads=[Bwu[sl]] + hT_bufs[b], writes=[BU[gi]])
                        P.v("act", "activation", sg[gi][:], G[gi][:], AF.Silu, R=[BG[gi]], W=[Bsg[gi]])
                        P.v("dve", "tensor_tensor", heT[k % 2][:, fc, :], U[gi][:], sg[gi][:], ALU.mult,
                            R=[BU[gi], Bsg[gi]], W=[Bhe[k % 2]])

                def DNp(k):
                    e, b = blocks[k]
                    sl = e % 2
                    for tt in range(4):
                        t = 4 * b + tt
                        for half in range(2):
                            di = (k * 8 + tt * 2 + half) % 4
                            for fc in range(4):
                                P.mm(Dn[di][:], heT[k % 2][:, fc, tt * 128:(tt + 1) * 128],
                                     wd[sl][:, fc, half * 512:(half + 1) * 512], fc == 0, fc == 3,
                                     reads=[Bhe[k % 2], Bwd[sl]], writes=[BD[di]])
                            dst = yacc[:, t, half * 512:(half + 1) * 512]
                            if e == 0:
                                P.v("dve", "tensor_scalar", dst, Dn[di][:], cmb[:, t, e:e + 1], None, op0=ALU.mult,
                                    R=[BD[di], Bcmb[t]], W=[Byacc[t]])
                            else:
                                P.v("dve", "scalar_tensor_tensor", dst, Dn[di][:], cmb[:, t, e:e + 1], dst,
                                    op0=ALU.mult, op1=ALU.add, R=[BD[di], Bcmb[t], Byacc[t]], W=[Byacc[t]])

                def pre(grp):
                    for t in range(4 * grp, 4 * grp + 4):
                        c = tile_cond(t)
                        ya = yacc[:, t, :]
                        P.v("pool", "tensor_tensor", ya, ya, bc[c][:], ALU.mult, R=[Byacc[t], Bbc[c]], W=[Byacc[t]])
                    for t in range(4 * grp, 4 * grp + 4):
                        ya = yacc[:, t, :]
                        P.v("dve", "scalar_tensor_tensor", ya, x_sb[:, t, :], ALPHA, ya, op0=ALU.mult, op1=ALU.add,
                            R=[Bx[t], Byacc[t]], W=[Byacc[t]])

                def epi(grp, part):
                    tiles = list(range(4 * grp, 4 * grp + 4))
                    ln_group(tiles, [yacc[:, t, :] for t in tiles], [Byacc[t] for t in tiles], bc[2][:], bc[3][:],
                             [Bbc[2], Bbc[3]], next_mod, PBK, "dve", part=part)

                GU(0)
                for k in range(len(blocks)):
                    if k + 1 < len(blocks):
                        GU(k + 1)
                    DNp(k)
                    e_, b_ = blocks[k]
                    if b_ == 2 and e_ + 2 < 16:
                        load_expert(e_ + 2)
                    if e_ == 15:
                        pre(b_)
                        epi(b_, "elem")
                for grp in range(3):
                    if next_mod is not None:
                        epi(grp, "tr")
                    if final:
                        for t in range(4 * grp, 4 * grp + 4):
                            P.dma("sp", y_d[t * 128:(t + 1) * 128, :], x_sb[:, t, :], reads=[Bx[t]])
                P.end_phase()

        moe_phase(0, (1, 1, 0), False)
        if stop == "l0moe":
            store_x("l0moe")
            P.end_phase()
            return nc


        with ExitStack() as pa:
            QT_s = sbuf(pa, "QT_s", [128, 8, 1024], BF16)
            KT_s = sbuf(pa, "KT_s", [128, 8, 1024], BF16)
            Vp_s = sbuf(pa, "Vp_s", [128, 8, 8, 192], BF16)
            O_CTXV, O_M01, O_RBB, O_E, O_ONES, RG_N = 4096, 10240, 12032, 13824, 19200, 19392
            Rg = sbuf(pa, "Rg", [128, RG_N], BF16)
            QT_p = Rg[:, 0:4096].rearrange("p (c t) -> p c t", c=8)
            KT_p = Rg[:, 4096:8192].rearrange("p (c t) -> p c t", c=8)
            Vp_p = Rg[:, 8192:14336].rearrange("p (m a e) -> p m a e", m=4, a=8)
            ctxKT = Rg[:, 0:4096].rearrange("p (c t) -> p c t", c=8)
            ctxVp = Rg[:, O_CTXV:O_M01].rearrange("p (m a e) -> p m a e", m=4, a=8)
            m01 = Rg[:, O_M01:O_RBB].rearrange("p (m q) -> p m q", q=128)
            rbb = Rg[:, O_RBB:O_E].rearrange("p (b d q) -> p b d q", b=2, d=7)
            Etab = Rg[:, O_E:O_ONES].rearrange("p (b d q) -> p b d q", b=3, d=14)
            onesp = Rg[:, O_ONES:RG_N]
            BQs, BKs, BVs = Buf("QTs"), Buf("KTs"), Buf("Vs")
            BQp, BKp, BVp = Buf("QTp"), Buf("KTp"), Buf("Vp")
            Bones = Buf("onesp")

            with ExitStack() as ph:
                wb = [sbuf(ph, "wqkv%d" % i, [128, 8, 512], BF16) for i in range(2)]
                Bwb = [Buf("wb0"), Buf("wb1")]
                stg = [sbuf(ph, "stg%d" % i, [128, 512]) for i in range(2)]
                Bstg = [Buf("stg0"), Buf("stg1")]
                P.v("pool", "memset", Vp_s[:, :, :, 64:128], 0.0, W=[BVs])
                P.v("pool", "memset", Vp_p[:, :, :, 64:128], 0.0, W=[BVp])
                P.v("pool", "memset", onesp[:, 0:64], 1.0, W=[Bones])
                P.v("pool", "memset", onesp[:, 64:128], 0.0, W=[Bones])
                P.v("pool", "memset", onesp[:, 128:192], 1.0, W=[Bones])
                nb = [0]
                ns = [0]

                def bank():
                    i = nb[0] % 8
                    nb[0] += 1
                    return psum[i], PB[i], ("dve" if i % 2 == 0 else "act")

                def evac(eng, dst, src, Bsrc, W, mul=None):
                    if eng == "dve":
                        if mul is None:
                            P.v("dve", "tensor_copy", dst, src, R=[Bsrc], W=W)
                        else:
                            P.v("dve", "tensor_scalar_mul", dst, src, mul, R=[Bsrc], W=W)
                    else:
                        if mul is None:
                            P.v("act", "copy", dst, src, R=[Bsrc], W=W)
                        else:
                            P.v("act", "mul", dst, src, mul, R=[Bsrc], W=W)

                hTb = lambda tiles: [x for t in tiles for x in BhT[t]]
                for cb in range(6):
                    sl = cb % 2
                    P.dma("pool", wb[sl][:], w_qkv_d.rearrange("(kc k) n -> k kc n", k=128)[:, :, cb * 512:(cb + 1) * 512],
                          writes=[Bwb[sl]])
                    if cb < 4:
                        isq = cb < 2
                        for i in range(4):
                            a = (cb % 2) * 4 + i
                            for tb in range(3):
                                pb, Bp, eng = bank()
                                for kc in range(8):
                                    P.mm(pb[:], wb[sl][:, kc, i * 128:(i + 1) * 128], hT[:, kc, tb * 512:(tb + 1) * 512],
                                         kc == 0, kc == 7, reads=[Bwb[sl]] + hTb(range(4 * tb, 4 * tb + 4)), writes=[Bp])
                                if tb == 0:
                                    dst = (QT_p if isq else KT_p)[:, a, :]
                                    W = [BQp if isq else BKp]
                                else:
                                    dst = (QT_s if isq else KT_s)[:, a, (tb - 1) * 512:tb * 512]
                                    W = [BQs if isq else BKs]
                                evac(eng, dst, pb[:], Bp, W, mul=(0.125 if isq else None))
                    if cb in (2, 3, 4, 5):
                        isv = cb >= 4
                        hh = cb % 2
                        for t in (range(NT) if isv else range(4)):
                            pb, Bp, eng = bank()
                            for kc in range(8):
                                P.mm(pb[:], hT[:, kc, t * 128:(t + 1) * 128], wb[sl][:, kc, :], kc == 0, kc == 7,
                                     reads=[Bwb[sl]] + hTb([t]), writes=[Bp])
                            if isv:
                                src4 = pb[:].rearrange("p (a two d) -> p a two d", two=2, d=64)
                                for par in range(2):
                                    if t < 4:
                                        evac(eng, Vp_p[:, t, hh * 4:(hh + 1) * 4, par * 128:par * 128 + 64],
                                             src4[:, :, par, :], Bp, [BVp])
                                    else:
                                        evac(eng, Vp_s[:, t - 4, hh * 4:(hh + 1) * 4, par * 128:par * 128 + 64],
                                             src4[:, :, par, :], Bp, [BVs])
                            if t < 4:
                                k = ns[0] % 2
                                ns[0] += 1
                                evac("dve" if eng == "act" else "act", stg[k][:], pb[:], Bp, [Bstg[k]])
                                dd = nv_d if isv else nk_d
                                s_, l0 = t // 2, (t % 2) * 128
                                P.dma("sp", dd[s_, hh * 8:(hh + 1) * 8, l0:l0 + 128, :].rearrange("h l d -> l h d"),
                                      stg[k][:].rearrange("p (h d) -> p h d", d=64), reads=[Bstg[k]])
                P.end_phase()

            with ExitStack() as pb_:
                OT_all = sbuf(pb_, "OT_all", [128, 8, 1024], BF16)
                BOT = [Buf("OT%d" % a) for a in range(8)]
                wo = sbuf(pb_, "wo", [128, 8, D], BF16)
                Bwo = Buf("wo")
                By = [Buf("ay%d" % i) for i in range(4)]
                ST = [PBK[i] for i in range(4)]
                ACC = [(PBK[4], PBK[5]), (PBK[6], PBK[7])]
                cn = {"st": 0, "pt": 0, "em": 0}

                def heads_scope(scope):
                    PT = [sbuf(scope, "PT%d" % i, [128, 512], BF16) for i in range(4)]
                    BPT = [Buf("PT%d" % i) for i in range(4)]
                    rS = sbuf(scope, "rS", [128, 512])
                    BrS = Buf("rS")
                    return PT, BPT, rS, BrS

                LOOKAHEAD = 3
                pend = []

                def drain(keep):
                    while len(pend) > keep:
                        pend.pop(0)()

                def segment(PT, BPT, Kst, Qmv, nq, Bk, Bq, esl, BE_, Vblk, BV_, ones_blk, acc, c0):
                    st, Bst = ST[cn["st"] % 4]
                    cn["st"] += 1
                    k = cn["pt"] % 4
                    cn["pt"] += 1
                    P.mm(st[:, 0:nq], Kst, Qmv, True, True, reads=[Bk, Bq], writes=[Bst])
                    P.v("act", "activation", PT[k][:, 0:nq], st[:, 0:nq], AF.Exp, R=[Bst], W=[BPT[k]])
                    if esl is not None:
                        P.v("dve", "tensor_tensor", PT[k][:, 0:nq], PT[k][:, 0:nq], esl, ALU.mult,
                            R=[BPT[k], BE_], W=[BPT[k]])
                    (pO, BO_), (pS, BS_) = acc

                    def back():
                        P.mm(pO[:, c0:c0 + nq], Vblk, PT[k][:, 0:nq], False, False, reads=[BPT[k], BV_], writes=[BO_],
                             skip=True)
                        P.mm(pS[:, c0:c0 + nq], ones_blk, PT[k][:, 0:nq], False, False, reads=[BPT[k], Bones],
                             writes=[BS_], skip=True)
                    pend.append(back)
                    drain(LOOKAHEAD)

                def zero_acc(acc, ncols):
                    (pO, BO_), (pS, BS_) = acc
                    P.v("dve", "memset", pO[:, 0:ncols], 0.0, W=[BO_])
                    P.v("dve", "memset", pS[:, 0:ncols], 0.0, W=[BS_])

                def normalize(acc, ncols, rS, BrS, dst, Wd):
                    (pO, BO_), (pS, BS_) = acc

                    def fin():
                        P.v("dve", "reciprocal", rS[:, 0:ncols], pS[:, 0:ncols], R=[BS_], W=[BrS])
                        P.v("dve", "tensor_tensor", dst, pO[:, 0:ncols], rS[:, 0:ncols], ALU.mult, R=[BO_, BrS], W=Wd)
                    pend.append(fin)

                def proj_group(tiles, tcols, ybufs, deadW, bcv, Bbcv):
                    for i, t in enumerate(tiles):
                        tc = tcols[i]
                        for half in range(2):
                            pb, Bp = PBK[2 * (i % 2) + half]
                            for kc in range(8):
                                P.mm(pb[:], OT_all[:, kc, tc:tc + 128], wo[:, kc, half * 512:(half + 1) * 512],
                                     kc == 0, kc == 7, reads=[BOT[kc], Bwo], writes=[Bp])
                            P.v("dve", "tensor_tensor", ybufs[i][:, half * 512:(half + 1) * 512], pb[:],
                                bcv[0][:, half * 512:(half + 1) * 512], ALU.mult, R=[Bp, Bbcv[0]],
                                W=[By[i]] + list(deadW))
                        P.v("dve", "scalar_tensor_tensor", ybufs[i], x_sb[:, t, :], ALPHA, ybufs[i],
                            op0=ALU.mult, op1=ALU.add, R=[Bx[t], By[i]], W=[By[i]])
                    ln_group(tiles, ybufs, By, bcv[1], bcv[2], [Bbcv[1], Bbcv[2]], (1, 4, 3), PBK[4:8], "pool")

                def prep_vectors(cidx, bcv, Bbcv, dgall_v, Bdg_v, deadW):
                    load_bc(bcv[1], ln1g_d[1:2, :], [Bbcv[1]] + list(deadW))
                    load_bc(bcv[2], ln1b_d[1:2, :], [Bbcv[2]] + list(deadW))
                    make_bc(bcv[0], 1, 2, cidx, dgall_v, Bdg_v, PBK[6:8], [Bbcv[0]] + list(deadW))

                with ExitStack() as ph:
                    PT, BPT, rS, BrS = heads_scope(ph)
                    P.dma("pool", wo[:], w_o_d.rearrange("(kc k) n -> k kc n", k=128), writes=[Bwo])
                    for a in range(8):
                        for s_ in range(2):
                            acc = ACC[(2 * a + s_) % 2]
                            zero_acc(acc, 256)
                            for c in range(2):
                                for rho in range(2):
                                    pr = slice(64 * rho, 64 * rho + 64)
                                    vc = slice(64 * rho, 64 * rho + 128)
                                    k0 = s_ * 256 + c * 128
                                    segment(PT, BPT, KT_p[pr, a, k0:k0 + 128], QT_p[pr, a, s_ * 256:(s_ + 1) * 256], 256,
                                            BKp, BQp, None, None, Vp_p[:, 2 * s_ + c, a, vc], BVp, onesp[:, vc], acc, 0)
                            normalize(acc, 256, rS, BrS, OT_all[:, a, s_ * 256:(s_ + 1) * 256], [BOT[a]])
                    drain(0)
                    P.end_phase()

                yb_p = [Rg[:, 2048 * i:2048 * (i + 1)].bitcast(F32) for i in range(4)]
                bc_p = [Rg[:, 8192 + 2048 * i:8192 + 2048 * (i + 1)].bitcast(F32) for i in range(3)]
                dg_p = Rg[:, 14336:16384].bitcast(F32).rearrange("p (a b) -> p a b", b=128)
                Bbc_p = [Buf("pbc%d" % i) for i in range(3)]
                prep_vectors(0, bc_p, Bbc_p, dg_p, Buf("pdg"), [])
                proj_group([0, 1, 2, 3], [0, 128, 256, 384], yb_p, [], bc_p, Bbc_p)
                if stop == "l1mixp":
                    store_x("l1mixp")
                P.end_phase()
                if stop == "l1mixp":
                    return nc

                with ExitStack() as ph:
                    PT, BPT, rS, BrS = heads_scope(ph)
                    ck_tok = OT_all[:, 0:4, :].rearrange("p c (h d) -> p c h d", d=64)
                    Bck = Buf("ck_tok")
                    BctxK, BctxV, Bm01 = Buf("ctxK"), Buf("ctxV"), Buf("m01")
                    Brbb = [Buf("rbb0"), Buf("rbb1")]
                    BE = [Buf("E0"), Buf("E1"), Buf("E2")]
                    for c in range(4):
                        P.dma("pool", ck_tok[:, c, :, :], ck_d[:, c * 128:(c + 1) * 128, :].rearrange("h l d -> l h d"),
                              writes=[Bck])
                    P.dma("pool", m01[:], m01_d.rearrange("m k q -> k m q"), writes=[Bm01])
                    P.v("pool", "memset", ctxVp[:, :, :, 64:128], 0.0, W=[BctxV])
                    cv4 = cv_d.rearrange("(a two) l d -> two l a d", two=2)
                    for c in range(4):
                        for par in range(2):
                            P.dma("pool", ctxVp[:, c, :, par * 128:par * 128 + 64], cv4[par, c * 128:(c + 1) * 128, :, :],
                                  writes=[BctxV])

                    def build_E(h):
                        hb, eb = h % 2, h % 3
                        P.dma("pool", rbb[:, hb, :, :], rbt_d[h].rearrange("d k q -> k d q"), writes=[Brbb[hb]])
                        P.v("act", "activation", rbb[:, hb, :, :], rbb[:, hb, :, :], AF.Exp, R=[Brbb[hb]], W=[Brbb[hb]])
                        for s2 in range(2):
                            P.v("pool", "tensor_tensor", Etab[:, eb, s2 * 7:(s2 + 1) * 7, :], rbb[:, hb, :, :],
                                m01[:, s2 * 7:(s2 + 1) * 7, :], ALU.mult, R=[Brbb[hb], Bm01], W=[BE[eb]])

                    for c in range(4):
                        pTb = psum[c % 2][:].bitcast(BF16)
                        for a in range(8):
                            P.tr(pTb[:, a * 128:(a + 1) * 128], ck_tok[:, c, 2 * a:2 * a + 2, :].rearrange("p h d -> p (h d)"),
                                 ident_b[:], reads=[Bck, Bconst], writes=[PB[c % 2]])
                        P.v("dve" if c % 2 == 0 else "act", "tensor_copy" if c % 2 == 0 else "copy",
                            ctxKT[:, :, c * 128:(c + 1) * 128], pTb.rearrange("p (a t) -> p a t", a=8),
                            R=[PB[c % 2]], W=[BctxK])
                    build_E(0)
                    build_E(1)
                    build_E(2)
                    for a in range(8):
                        for qb in range(2):
                            acc = ACC[(2 * a + qb) % 2]
                            zero_acc(acc, 512)
                            for (m, jlo, cnt, e2) in _SEGS[qb]:
                                for rho in range(2):
                                    eb = (2 * a + rho) % 3
                                    pr = slice(64 * rho, 64 * rho + 64)
                                    vc = slice(64 * rho, 64 * rho + 128)
                                    nq, q0 = cnt * 128, jlo * 128
                                    i0 = e2 * 7 + 3 - m + jlo
                                    esl = Etab[:, eb, i0:i0 + cnt, :].rearrange("p a b -> p (a b)")
                                    segment(PT, BPT, KT_s[pr, a, m * 128:(m + 1) * 128], QT_s[pr, a, q0:q0 + nq], nq,
                                            BKs, BQs, esl, BE[eb], Vp_s[:, m, a, vc], BVs, onesp[:, vc], acc, q0 - 512 * qb)
                            for c in range(4):
                                for rho in range(2):
                                    pr = slice(64 * rho, 64 * rho + 64)
                                    vc = slice(64 * rho, 64 * rho + 128)
                                    segment(PT, BPT, ctxKT[pr, a, c * 128:(c + 1) * 128], QT_s[pr, a, qb * 512:(qb + 1) * 512],
                                            512, BctxK, BQs, None, None, ctxVp[:, c, a, vc], BctxV, onesp[:, vc], acc, 0)
                            normalize(acc, 512, rS, BrS, OT_all[:, a, qb * 512:(qb + 1) * 512], [BOT[a], Bck])
                        for h2 in (2 * a + 3, 2 * a + 4):
                            if h2 < 16:
                                build_E(h2)
                    drain(0)
                    P.end_phase()

                yb_s = [QT_s[:, 2 * i:2 * i + 2, :].rearrange("p a t -> p (a t)").bitcast(F32) for i in range(4)]
                bc_s = [KT_s[:, 2 * i:2 * i + 2, :].rearrange("p a t -> p (a t)").bitcast(F32) for i in range(3)]
                dg_s = KT_s[:, 6:8, :].rearrange("p a t -> p (a t)").bitcast(F32).rearrange("p (a b) -> p a b", b=128)
                Bbc_s = [Buf("sbc%d" % i) for i in range(3)]
                prep_vectors(1, bc_s, Bbc_s, dg_s, Buf("sdg"), [])
                proj_group([4, 5, 6, 7], [0, 128, 256, 384], yb_s, [], bc_s, Bbc_s)
                proj_group([8, 9, 10, 11], [512, 640, 768, 896], yb_s, [], bc_s, Bbc_s)
                if stop == "l1mix":
                    store_x("l1mix")
                P.end_phase()
        if stop == "l1mix":
            return nc

        moe_phase(1, None, True)
        return nc


_CACHE = {}


def make_in_maps(inputs):
    f = lambda a: np.ascontiguousarray(np.asarray(a, dtype=np.float32))
    x_prompt, x_sample = f(inputs["x_prompt"]), f(inputs["x_sample"])
    cache_k, cache_v = f(inputs["cache_k"]), f(inputs["cache_v"])
    c, c_ctx = f(inputs["c"]), f(inputs["c_ctx"])
    b_ada = f(inputs["b_ada"])
    shared = {
        "w_ada": f(inputs["w_ada"]),
        "b_adaT": np.ascontiguousarray(b_ada.reshape(2, 48, 128).transpose(0, 2, 1)),
        "ln1_g": f(inputs["ln1_g"]), "ln1_b": f(inputs["ln1_b"]),
        "ln2_g": f(inputs["ln2_g"]), "ln2_b": f(inputs["ln2_b"]),
        "pool_w": f(inputs["pool_w"])[0],
        "pool_scale": f(inputs["pool_scale"]).reshape(1, D),
        "w_qkv": f(inputs["w_qkv"])[0],
        "w_o": f(inputs["w_o"])[0],
        "rbt": _rel_bias_layout(f(inputs["rel_bias"])[0]),
        "w_router": f(inputs["w_router"]),
        "b_router": f(inputs["b_router"]).reshape(1, 16),
        "w_gate": f(inputs["w_gate"]), "w_up": f(inputs["w_up"]), "w_down": f(inputs["w_down"]),
        "ptab": _pool_tables(),
        "m01": _mask01_table(),
    }
    in_maps = []
    for i in range(N_CORES):
        m = dict(shared)
        m["xin"] = np.ascontiguousarray(np.concatenate(
            [x_prompt[2 * i].reshape(256, D), x_prompt[2 * i + 1].reshape(256, D), x_sample[i]], axis=0))
        m["ck"] = np.ascontiguousarray(cache_k[i, 0])
        m["cv"] = np.ascontiguousarray(cache_v[i, 0])
        cond = np.stack([c_ctx, c[i]], axis=-1)
        m["condT"] = np.ascontiguousarray(cond.reshape(8, 128, 2).transpose(1, 0, 2))
        in_maps.append(m)
    return in_maps


def kernel(**inputs):
    if "nc" not in _CACHE:
        _CACHE["nc"] = build_nc()
    nc = _CACHE["nc"]
    in_maps = make_in_maps(inputs)
    res = run_bass_kernel_spmd(nc, in_maps, core_ids=list(range(N_CORES)))
    ys = [r["y"] for r in res.results]
    y_prompt = np.stack([y[:512].reshape(2, 256, D) for y in ys]).reshape(16, 256, D)
    y_sample = np.stack([y[512:] for y in ys])
    nk = np.stack([r["nk"] for r in res.results]).reshape(16, 1, 16, 256, 64)
    nv = np.stack([r["nv"] for r in res.results]).reshape(16, 1, 16, 256, 64)
    return (y_prompt.astype(np.float32), y_sample.astype(np.float32),
            nk.astype(np.float32), nv.astype(np.float32))
```

```python
from contextlib import ExitStack

import numpy as np
import concourse.bass as bass
import concourse.mybir as mybir
from concourse.bass_utils import run_bass_kernel_spmd

F32 = mybir.dt.float32
BF16 = mybir.dt.bfloat16
AF = mybir.ActivationFunctionType
ALU = mybir.AluOpType
AX = mybir.AxisListType

N_CORES = 8
D = 1024
NT = 12
T = NT * 128
ALPHA = 4.0 ** 0.25
LN_EPS = 1e-5
NEG = -30000.0
POOL_SIZES = (2, 4, 8, 16)
SEQS = ((0, 2), (2, 2), (4, 8))


def tile_cond(t):
    return 0 if t < 4 else 1


class Buf:
    __slots__ = ("name", "writer", "readers", "excl")

    def __init__(self, name="", excl=False):
        self.name = name
        self.writer = None
        self.readers = []
        self.excl = excl


class Lane:
    __slots__ = ("sem", "n", "last")

    def __init__(self, sem):
        self.sem = sem
        self.n = 0
        self.last = None


class Ins:
    __slots__ = ("eng", "fn", "deps", "signal", "count", "lane", "lval", "idx")

    def __init__(self, eng, fn):
        self.eng = eng
        self.fn = fn
        self.deps = []
        self.signal = False
        self.count = 0
        self.lane = None
        self.lval = 0
        self.idx = 0


class Prog:
    ENG = ("pe", "act", "dve", "pool", "sp")
    NLANES = {"sp": 24, "pool": 12, "act": 4}

    def __init__(self, nc, es):
        self.nc = nc
        self.es = es
        self.eng = {"pe": nc.tensor, "act": nc.scalar, "dve": nc.vector,
                    "pool": nc.gpsimd, "sp": nc.sync}
        self.sem = {e: es.enter_context(nc.semaphore("s_" + e)) for e in self.ENG}
        self.cnt = {e: 0 for e in self.ENG}
        self.bar = es.enter_context(nc.semaphore("s_bar"))
        self.nbar = 0
        self.lanes = {q: [Lane(es.enter_context(nc.semaphore("l_%s%d" % (q, i)))) for i in range(n)]
                      for q, n in self.NLANES.items()}
        self.lane_rr = {q: 0 for q in self.NLANES}
        self.waited = {e: {} for e in self.ENG}
        self.streams = {e: [] for e in self.ENG}
        self.touched = []
        self.n_ins = 0

    def _add_dep(self, ins, p):
        if p is None or p is ins:
            return
        if p.lane is None and ins.lane is None and p.eng == ins.eng and p.eng == "pe":
            return
        deps = ins.deps
        if p.lane is None:
            for i, q in enumerate(deps):
                if q.lane is None and q.eng == p.eng:
                    if p.idx > q.idx:
                        deps[i] = p
                    return
        elif p in deps:
            return
        deps.append(p)

    def _deps(self, ins, reads, writes):
        if any(b.excl for b in reads):
            writes = list(writes) + [b for b in reads if b.excl and b not in writes]
            reads = [b for b in reads if not b.excl]
        same = lambda p: (p is not None and p.lane is None and ins.lane is None and p.eng == ins.eng)
        for b in reads:
            self._add_dep(ins, b.writer)
        for b in writes:
            if not same(b.writer):
                self._add_dep(ins, b.writer)
            for r in b.readers:
                if not same(r):
                    self._add_dep(ins, r)
        for b in reads:
            if not b.readers and b.writer is None:
                self.touched.append(b)
            b.readers.append(ins)
        for b in writes:
            if not b.readers and b.writer is None:
                self.touched.append(b)
            b.writer = ins
            b.readers = []

    def op(self, eng, fn, reads=(), writes=()):
        ins = Ins(eng, fn)
        ins.idx = len(self.streams[eng])
        self._deps(ins, reads, writes)
        self.streams[eng].append(ins)
        return ins

    def dma(self, eng, out, in_, reads=(), writes=(), **kw):
        E = self.eng[eng]
        ins = Ins(eng, lambda: E.dma_start(out=out, in_=in_, **kw))
        ins.idx = len(self.streams[eng])
        lanes = self.lanes[eng]
        lane = lanes[self.lane_rr[eng] % len(lanes)]
        self.lane_rr[eng] += 1
        ins.lane = lane
        lane.n += 1
        ins.lval = 16 * lane.n
        if lane.last is not None:
            ins.deps.append(lane.last)
        lane.last = ins
        self._deps(ins, reads, writes)
        self.streams[eng].append(ins)
        return ins

    def mm(self, out, lhsT, rhs, start, stop, reads=(), writes=(), skip=False):
        t = self.nc.tensor
        if skip:
            return self.op("pe", lambda: t.matmul(out, lhsT, rhs, start=start, stop=stop, skip_group_check=True),
                           reads, writes)
        return self.op("pe", lambda: t.matmul(out, lhsT, rhs, start=start, stop=stop), reads, writes)

    def tr(self, out, in_, ident, reads=(), writes=()):
        t = self.nc.tensor
        return self.op("pe", lambda: t.transpose(out, in_, ident), reads, writes)

    def v(self, eng, meth, *args, R=(), W=(), **kw):
        E = self.eng[eng]
        return self.op(eng, lambda: getattr(E, meth)(*args, **kw), R, W)

    def end_phase(self):
        ENG = self.ENG
        for e in ENG:
            for ins in self.streams[e]:
                for p in ins.deps:
                    if p.lane is None:
                        p.signal = True
            for ins in reversed(self.streams[e]):
                if ins.lane is None:
                    ins.signal = True
                    break
        for e in ENG:
            for ins in self.streams[e]:
                if ins.lane is None and ins.signal:
                    self.cnt[e] += 1
                    ins.count = self.cnt[e]
        for e in ENG:
            E = self.eng[e]
            wd = self.waited[e]
            for ins in self.streams[e]:
                for p in ins.deps:
                    if p.lane is not None:
                        sem, val = p.lane.sem, p.lval
                    else:
                        sem, val = self.sem[p.eng], p.count
                    if wd.get(sem.num, 0) >= val:
                        continue
                    wd[sem.num] = val
                    E.wait_ge(sem, val)
                bi = ins.fn()
                if ins.lane is not None:
                    bi.then_inc(ins.lane.sem, 16)
                elif ins.signal:
                    bi.then_inc(self.sem[e], 1)
                self.n_ins += 1
        sp = self.eng["sp"]
        wd = self.waited["sp"]
        all_lanes = [l for q in self.lanes.values() for l in q]
        for e in ENG:
            if e == "sp" or self.cnt[e] == 0:
                continue
            if wd.get(self.sem[e].num, 0) < self.cnt[e]:
                sp.wait_ge(self.sem[e], self.cnt[e])
        for l in all_lanes:
            if l.n and wd.get(l.sem.num, 0) < 16 * l.n:
                sp.wait_ge(l.sem, 16 * l.n)
        self.nbar += 1
        sp.sem_inc(self.bar, 1)
        for e in ENG:
            if e != "sp":
                self.eng[e].wait_ge(self.bar, self.nbar)
            for e2 in ENG:
                self.waited[e][self.sem[e2].num] = self.cnt[e2]
            for l in all_lanes:
                self.waited[e][l.sem.num] = 16 * l.n
        for l in all_lanes:
            l.last = None
        for b in self.touched:
            b.writer = None
            b.readers = []
        self.touched = []
        self.streams = {e: [] for e in ENG}


def _pool_tables():
    n = 1024
    out = np.zeros((4, 5, 128, 128), np.float32)
    t = np.arange(n)
    for wi, w in enumerate(POOL_SIZES):
        lo = np.clip(t - w // 2, 0, n)
        hi = np.clip(t - w // 2 + w, 0, n)
        cnt = (hi - lo).astype(np.float64)
        s = np.arange(n)[:, None]
        Pm = ((s >= lo[None, :]) & (s < hi[None, :])) / cnt[None, :] - np.eye(n)
        Pm = Pm.astype(np.float32)
        out[wi, 0] = Pm[0:128, 0:128]
        out[wi, 1] = Pm[128:256, 128:256]
        out[wi, 2] = Pm[896:1024, 896:1024]
        out[wi, 3] = Pm[0:128, 128:256]
        out[wi, 4] = Pm[256:384, 128:256]
    return out


def _chunks_for(j):
    if j < 2:
        return list(range(0, 4))
    if j > 5:
        return list(range(4, 8))
    return list(range(j - 2, j + 3))


def _mask_tables():
    tiles = []
    index = {}
    keymap = {}
    col = np.arange(64)
    cs = np.clip(col - 8, 0, 48)
    colok = (col[None, :] >= cs[:, None]) & (col[None, :] < cs[:, None] + 16)
    for j in range(8):
        for m in _chunks_for(j):
            mt = np.full((2, 64, 2, 64), NEG, np.float32)
            for rq in range(2):
                r = 2 * j + rq
                rs = min(max(r - 4, 0), 8)
                for rk in range(2):
                    kr = 2 * m + rk
                    if rs <= kr < rs + 8:
                        mt[rk, :, rq, :] = np.where(colok.T, 0.0, NEG)
            key = mt.tobytes()
            if key not in keymap:
                keymap[key] = len(tiles)
                tiles.append(mt.reshape(128, 128))
            index[(j, m)] = keymap[key]
    return np.stack(tiles), index


_MASKS, _MASK_IDX = _mask_tables()
_M_FULL, _M_A, _M_B = _MASK_IDX[(0, 0)], _MASK_IDX[(2, 0)], _MASK_IDX[(2, 4)]


def _mask01_table():
    full = (_MASKS[_M_FULL] == 0).astype(np.float32)
    t = np.stack([full] * 14)
    t[7 + (3 - 2)] = (_MASKS[_M_B] == 0)
    t[7 + (3 + 2)] = (_MASKS[_M_A] == 0)
    return t


def _segments():
    segs = {0: [], 1: []}
    for m in range(8):
        for qb in range(2):
            js = [j for j in range(8) if m in _chunks_for(j) and j // 4 == qb]
            if not js:
                continue
            assert js == list(range(js[0], js[-1] + 1))
            e2 = any(_MASK_IDX[(j, m)] != _M_FULL for j in js)
            for j in js:
                want = _M_FULL
                if e2 and m - j == -2:
                    want = _M_A
                if e2 and m - j == 2:
                    want = _M_B
                assert _MASK_IDX[(j, m)] == want, (j, m)
            segs[qb].append((m, js[0], len(js), int(e2)))
    return segs


_SEGS = _segments()


def _rel_bias_layout(rel_bias):
    rk = np.arange(2)[:, None, None, None]
    ck = np.arange(64)[None, :, None, None]
    rq = np.arange(2)[None, None, :, None]
    cq = np.arange(64)[None, None, None, :]
    out = np.empty((16, 7, 128, 128), np.float32)
    for i in range(7):
        dl = 3 - i
        dy = np.clip(2 * dl + rk - rq + 7, 0, 14)
        dx = np.clip(ck - cq + 15, 0, 30)
        dy, dx = np.broadcast_arrays(dy, dx)
        out[:, i] = rel_bias[:, dy, dx].reshape(16, 128, 128)
    return out


def build_nc(stop=None):
    nc = bass.Bass("TRN2", target_bir_lowering=False)

    def din(name, shape):
        return nc.dram_tensor(name, list(shape), F32, kind="ExternalInput").ap()

    def dout(name, shape):
        return nc.dram_tensor(name, list(shape), F32, kind="ExternalOutput").ap()

    xin = din("xin", [T, D])
    ck_d = din("ck", [16, 512, 64])
    cv_d = din("cv", [16, 512, 64])
    condT_d = din("condT", [128, 8, 2])
    w_ada_d = din("w_ada", [2, D, 6 * D])
    b_adaT_d = din("b_adaT", [2, 128, 48])
    ln1g_d = din("ln1_g", [2, D])
    ln1b_d = din("ln1_b", [2, D])
    ln2g_d = din("ln2_g", [2, D])
    ln2b_d = din("ln2_b", [2, D])
    pool_w_d = din("pool_w", [4, 256, 256])
    pool_scale_d = din("pool_scale", [1, D])
    w_qkv_d = din("w_qkv", [D, 3 * D])
    w_o_d = din("w_o", [D, D])
    rbt_d = din("rbt", [16, 7, 128, 128])
    w_router_d = din("w_router", [D, 16])
    b_router_d = din("b_router", [1, 16])
    w_gate_d = din("w_gate", [2, 16, D, 512])
    w_up_d = din("w_up", [2, 16, D, 512])
    w_down_d = din("w_down", [2, 16, 512, D])
    ptab_d = din("ptab", [4, 5, 128, 128])
    m01_d = din("m01", [14, 128, 128])

    y_d = dout("y", [T, D])
    nk_d = dout("nk", [2, 16, 256, 64])
    nv_d = dout("nv", [2, 16, 256, 64])
    dbg_d = dout("dbg", [128, 2, 48, 2]) if stop == "ada" else None

    with ExitStack() as es:
        P = Prog(nc, es)

        _uid = [0]

        def sbuf(scope, name, shape, dt=F32):
            _uid[0] += 1
            return scope.enter_context(nc.sbuf_tensor("sb%d_%s" % (_uid[0], name), list(shape), dt))

        x_sb = sbuf(es, "x_sb", [128, NT, D])
        hT = sbuf(es, "hT", [128, 8, T], BF16)
        ident_f = sbuf(es, "ident_f", [128, 128])
        ident_b = sbuf(es, "ident_b", [128, 128], BF16)
        ones_f = sbuf(es, "ones_f", [128, 128])
        modcol = sbuf(es, "modcol", [128, 2, 48, 2])
        cmb = sbuf(es, "cmb", [128, NT, 16])
        small = sbuf(es, "small", [128, 64])
        psum = [es.enter_context(nc.psum_tensor("ps%d" % i, [128, 512], F32)) for i in range(8)]
        PB = [Buf("ps%d" % i, excl=True) for i in range(8)]

        Bx = [Buf("x%d" % t) for t in range(NT)]
        BhT = [(Buf("hTd%d" % t), Buf("hTa%d" % t)) for t in range(NT)]
        Bconst = Buf("const")
        Bmod = [Buf("modcol0"), Buf("modcol1")]
        sTb = sbuf(es, "sTb", [128, 8, 2], BF16)
        b_col = sbuf(es, "b_col", [128, 2, 48])
        BsT = Buf("sT")
        Bbcol = Buf("bcol")
        Bcmb = [Buf("cmb%d" % t) for t in range(NT)]


        def alpha_col(layer, v, ch, c):
            return modcol[:, layer, v * 8 + ch, c:c + 1]

        PBK = [(psum[i], PB[i]) for i in range(8)]

        def ada_block(layer, nb, wa_t, Bwa_t, bank):
            pc, Bpc = bank
            src = w_ada_d[layer].rearrange("(kc k) n -> k kc n", k=128)[:, :, nb * 512:(nb + 1) * 512]
            P.dma("pool", wa_t[:], src, writes=[Bwa_t])
            for n4 in range(4):
                for kc in range(8):
                    P.mm(pc[:, n4 * 2:n4 * 2 + 2], wa_t[:, kc, n4 * 128:(n4 + 1) * 128], sTb[:, kc, :],
                         kc == 0, kc == 7, reads=[Bwa_t, BsT], writes=[Bpc])
            dst = modcol[:, layer, nb * 4:(nb + 1) * 4, :]
            P.v("dve", "tensor_tensor", dst, pc[:, 0:8].rearrange("p (c k) -> p c k", k=2),
                b_col[:, layer, nb * 4:(nb + 1) * 4].unsqueeze(2).to_broadcast([128, 4, 2]), ALU.add,
                R=[Bpc, Bbcol], W=[Bmod[layer]])
            if nb // 2 in (1, 4):
                P.v("dve", "tensor_scalar_add", dst, dst, 1.0, R=[Bmod[layer]], W=[Bmod[layer]])

        with ExitStack() as ph:
            condT = sbuf(ph, "condT", [128, 8, 2])
            sT = sbuf(ph, "sT", [128, 8, 2])
            wa = [sbuf(ph, "wa%d" % i, [128, 8, 512], BF16) for i in range(3)]
            Bwa = [Buf("wa%d" % i) for i in range(3)]

            P.v("pool", "memset", ident_f[:], 0.0, W=[Bconst])
            P.v("pool", "affine_select", ident_f[:], ident_f[:], [[-1, 128]], ALU.not_equal, 1.0,
                base=0, channel_multiplier=1, R=[Bconst], W=[Bconst])
            P.v("pool", "tensor_copy", ident_b[:], ident_f[:], R=[Bconst], W=[Bconst])
            P.v("pool", "memset", ones_f[:], 1.0, W=[Bconst])

            for t in range(NT):
                P.dma("sp", x_sb[:, t, :], xin[t * 128:(t + 1) * 128, :], writes=[Bx[t]])
            P.dma("sp", condT[:], condT_d[:, :, :], writes=[BsT])
            P.dma("sp", b_col[:], b_adaT_d.rearrange("l p c -> p l c"), writes=[Bbcol])
            P.v("act", "activation", sT[:], condT[:], AF.Silu, R=[BsT], W=[BsT])
            P.v("dve", "tensor_copy", sTb[:], sT[:], R=[BsT], W=[BsT])

            for nb in range(12):
                ada_block(0, nb, wa[nb % 3], Bwa[nb % 3], PBK[nb % 2])
            if stop == "ada":
                for nb in range(12):
                    ada_block(1, nb, wa[nb % 3], Bwa[nb % 3], PBK[nb % 2])
            if stop == "ada":
                P.dma("sp", dbg_d[:, :, :, :], modcol[:], reads=Bmod)
            P.end_phase()

        def load_bc(dst, src_row, W):
            P.dma("sp", dst, src_row.partition_broadcast(128), writes=W)

        def make_bc(dst, layer, v, c, dgall, Bdg, banks, W):
            P.v("dve", "tensor_tensor", dgall[:], ident_f[:].unsqueeze(1).to_broadcast([128, 8, 128]),
                modcol[:, layer, v * 8:(v + 1) * 8, c:c + 1].to_broadcast([128, 8, 128]), ALU.mult,
                R=[Bconst, Bmod[layer]], W=[Bdg])
            for half in range(2):
                pb, Bp = banks[half]
                P.mm(pb[:], ones_f[:], dgall[:, half * 4:(half + 1) * 4, :].rearrange("p a b -> p (a b)"),
                     True, True, reads=[Bconst, Bdg], writes=[Bp])
                P.v("act", "copy", dst[:, half * 512:(half + 1) * 512], pb[:], R=[Bp], W=W)

        Bsmall = [Buf("mv%d" % i) for i in range(4)]

        def ln_group(tiles, ys, Bys, g_bc, b_bc, Bvec, next_mod, banks, gmul_eng, part="all"):
            n = len(tiles)
            sts = [small[:, 16 * i:16 * i + 12].rearrange("p (a b) -> p a b", b=6) for i in range(n)]
            mvv = [small[:, 16 * i + 12:16 * i + 16] for i in range(n)]
            if part != "tr":
                ln_elem(tiles, ys, Bys, g_bc, b_bc, Bvec, gmul_eng, n, sts, mvv)
            if next_mod is None or part == "elem":
                return
            ln_tr(tiles, next_mod, banks)

        def ln_elem(tiles, ys, Bys, g_bc, b_bc, Bvec, gmul_eng, n, sts, mvv):
            for i in range(n):
                P.v("dve", "bn_stats", sts[i][:, 0, :], ys[i][:, 0:512], R=[Bys[i]], W=[Bsmall[i]])
                P.v("dve", "bn_stats", sts[i][:, 1, :], ys[i][:, 512:1024], R=[Bys[i]], W=[Bsmall[i]])
                P.v("dve", "bn_aggr", mvv[i][:, 0:2], sts[i], R=[Bsmall[i]], W=[Bsmall[i]])
            for i in range(n):
                P.v("act", "activation", mvv[i][:, 2:3], mvv[i][:, 1:2], AF.Sqrt, bias=LN_EPS,
                    R=[Bsmall[i]], W=[Bsmall[i]])
            for i in range(n):
                P.v("dve", "reciprocal", mvv[i][:, 2:3], mvv[i][:, 2:3], R=[Bsmall[i]], W=[Bsmall[i]])
                P.v("dve", "tensor_scalar", mvv[i][:, 3:4], mvv[i][:, 0:1], mvv[i][:, 2:3], -1.0,
                    op0=ALU.mult, op1=ALU.mult, R=[Bsmall[i]], W=[Bsmall[i]])
            for i in range(n):
                P.v("act", "activation", ys[i], ys[i], AF.Identity, scale=mvv[i][:, 2:3], bias=mvv[i][:, 3:4],
                    R=[Bys[i], Bsmall[i]], W=[Bys[i]])
            for i in range(n):
                P.v(gmul_eng, "tensor_tensor", ys[i], ys[i], g_bc, ALU.mult, R=[Bys[i]] + list(Bvec), W=[Bys[i]])
            for i, t in enumerate(tiles):
                P.v("pool", "tensor_tensor", x_sb[:, t, :], ys[i], b_bc, ALU.add, R=[Bys[i]] + list(Bvec), W=[Bx[t]])

        def ln_tr(tiles, next_mod, banks):
            layer, vA, vB = next_mod
            nb = len(banks) // 2
            for i, t in enumerate(tiles):
                c = tile_cond(t)
                pair = banks[2 * (i % nb):2 * (i % nb) + 2]
                for half in range(2):
                    pb, Bp = pair[half]
                    for q in range(4):
                        kc = half * 4 + q
                        P.tr(pb[:, q * 128:(q + 1) * 128], x_sb[:, t, kc * 128:(kc + 1) * 128], ident_f[:],
                             reads=[Bx[t], Bconst], writes=[Bp])
                for half in range(2):
                    pb, Bp = pair[half]
                    for q in range(4):
                        kc = half * 4 + q
                        if half == 0:
                            P.v("dve", "tensor_scalar", hT[:, kc, t * 128:(t + 1) * 128], pb[:, q * 128:(q + 1) * 128],
                                alpha_col(layer, vA, kc, c), alpha_col(layer, vB, kc, c),
                                op0=ALU.mult, op1=ALU.add, R=[Bp, Bmod[layer]], W=[BhT[t][0]])
                        else:
                            P.v("act", "activation", hT[:, kc, t * 128:(t + 1) * 128], pb[:, q * 128:(q + 1) * 128],
                                AF.Identity, scale=alpha_col(layer, vA, kc, c), bias=alpha_col(layer, vB, kc, c),
                                R=[Bp, Bmod[layer]], W=[BhT[t][1]])

        PBK = [(psum[i], PB[i]) for i in range(8)]

        def store_x(label):
            for t in range(NT):
                P.dma("sp", y_d[t * 128:(t + 1) * 128, :], x_sb[:, t, :], reads=[Bx[t]])

        if stop == "ada":
            store_x("ada")
            P.end_phase()
            return nc

        with ExitStack() as ph:
            bc = [sbuf(ph, "bc%d" % i, [128, D]) for i in range(8)]
            Bbc = [Buf("bc%d" % i) for i in range(8)]
            psc = sbuf(ph, "psc", [128, D])
            Bpsc = Buf("psc")
            dgall = sbuf(ph, "dgall", [128, 8, 128])
            Bdg = Buf("dgall")
            h_tok = sbuf(ph, "h_tok", [128, NT, D])
            Bh = [Buf("h%d" % t) for t in range(NT)]
            ptab = sbuf(ph, "ptab", [128, 4, 5, 128])
            Bptab = Buf("ptab")
            pw = sbuf(ph, "pw", [128, 4, 2, 256], BF16)
            Bpw = Buf("pw")
            pTt = [sbuf(ph, "pTt%d" % i, [128, 8, 128], BF16) for i in range(2)]
            BpTt = [Buf("pTt0"), Buf("pTt1")]
            ybuf = [sbuf(ph, "ybuf%d" % i, [128, D]) for i in range(4)]
            By = [Buf("y%d" % i) for i in range(4)]
            wa1 = sbuf(ph, "wa1", [128, 8, 512], BF16)
            Bwa1 = Buf("wa1")

            P.dma("sp", ptab[:], ptab_d.rearrange("w k s t -> s w k t"), writes=[Bptab])
            P.dma("pool", pw[:], pool_w_d.rearrange("g (i c) d -> c g i d", c=128), writes=[Bpw])
            load_bc(psc[:], pool_scale_d[0:1, :], [Bpsc])
            load_bc(bc[6][:], ln1g_d[0:1, :], [Bbc[6]])
            load_bc(bc[7][:], ln1b_d[0:1, :], [Bbc[7]])
            for c in range(2):
                make_bc(bc[2 * c], 0, 1, c, dgall, Bdg, PBK[0:2], [Bbc[2 * c]])
                make_bc(bc[2 * c + 1], 0, 0, c, dgall, Bdg, PBK[2:4], [Bbc[2 * c + 1]])
                make_bc(bc[4 + c], 0, 2, c, dgall, Bdg, PBK[4:6], [Bbc[4 + c]])
                P.v("pool", "tensor_tensor", bc[4 + c][:], bc[4 + c][:], psc[:], ALU.mult,
                    R=[Bbc[4 + c], Bpsc], W=[Bbc[4 + c]])
            for t in range(NT):
                c = tile_cond(t)
                P.v("dve", "tensor_tensor", h_tok[:, t, :], x_sb[:, t, :], bc[2 * c][:], ALU.mult,
                    R=[Bx[t], Bbc[2 * c]], W=[Bh[t]])
                P.v("pool", "tensor_tensor", h_tok[:, t, :], h_tok[:, t, :], bc[2 * c + 1][:], ALU.add,
                    R=[Bh[t], Bbc[2 * c + 1]], W=[Bh[t]])
            seq_of = {}
            for (t0, ntl) in SEQS:
                for q in range(ntl):
                    seq_of[t0 + q] = (q, ntl)
            for grp in range(3):
                tiles = list(range(4 * grp, 4 * grp + 4))
                for i, t in enumerate(tiles):
                    q, ntl = seq_of[t]
                    c = tile_cond(t)
                    k = i % 2
                    (pA, BA), (pBk, BB) = PBK[2 * k], PBK[2 * k + 1]
                    kind = 0 if q == 0 else (2 if q == ntl - 1 else 1)
                    srcs = [(t, kind)]
                    if q > 0:
                        srcs.append((t - 1, 3))
                    if q < ntl - 1:
                        srcs.append((t + 1, 4))
                    for kc in range(8):
                        wi = kc // 2
                        bank, Bb = (pA, BA) if kc < 4 else (pBk, BB)
                        dst = bank[:, (kc % 4) * 128:(kc % 4 + 1) * 128]
                        for si, (ts, kd) in enumerate(srcs):
                            P.mm(dst, h_tok[:, ts, kc * 128:(kc + 1) * 128], ptab[:, wi, kd, :],
                                 si == 0, si == len(srcs) - 1, reads=[Bh[ts], Bptab], writes=[Bb])
                    P.v("act", "copy", pTt[k][:, 0:4, :], pA[:].rearrange("p (a b) -> p a b", b=128),
                        R=[BA], W=[BpTt[k]])
                    P.v("act", "copy", pTt[k][:, 4:8, :], pBk[:].rearrange("p (a b) -> p a b", b=128),
                        R=[BB], W=[BpTt[k]])
                    for half in range(2):
                        bank, Bb = (pA, BA) if half == 0 else (pBk, BB)
                        for gg in range(2):
                            g = half * 2 + gg
                            for i2 in range(2):
                                P.mm(bank[:, gg * 256:(gg + 1) * 256], pTt[k][:, 2 * g + i2, :], pw[:, g, i2, :],
                                     i2 == 0, i2 == 1, reads=[BpTt[k], Bpw], writes=[Bb])
                        P.v("dve", "tensor_tensor", ybuf[i][:, half * 512:(half + 1) * 512], bank[:],
                            bc[4 + c][:, half * 512:(half + 1) * 512], ALU.mult, R=[Bb, Bbc[4 + c]], W=[By[i]])
                    P.v("dve", "scalar_tensor_tensor", ybuf[i][:], x_sb[:, t, :], ALPHA, ybuf[i][:],
                        op0=ALU.mult, op1=ALU.add, R=[Bx[t], By[i]], W=[By[i]])
                    ada_block(1, t, wa1, Bwa1, PBK[6 + t % 2])
                ln_group(tiles, [ybuf[i][:] for i in range(4)], By, bc[6][:], bc[7][:], [Bbc[6], Bbc[7]],
                         (0, 4, 3), PBK[4:8], "pool")
            if stop == "l0mix":
                store_x("l0mix")
            P.end_phase()
        if stop == "l0mix":
            return nc


        def moe_phase(layer, next_mod, final):
            with ExitStack() as ph:
                bc = [sbuf(ph, "bc%d" % i, [128, D]) for i in range(4)]
                Bbc = [Buf("mbc%d" % i) for i in range(4)]
                dgall = sbuf(ph, "dgall", [128, 8, 128])
                Bdg = Buf("dgall")
                wg = [sbuf(ph, "wg%d" % i, [128, 8, 512], BF16) for i in range(2)]
                wu = [sbuf(ph, "wu%d" % i, [128, 8, 512], BF16) for i in range(2)]
                wd = [sbuf(ph, "wd%d" % i, [128, 4, D], BF16) for i in range(2)]
                Bwg = [Buf("wg0"), Buf("wg1")]
                Bwu = [Buf("wu0"), Buf("wu1")]
                Bwd = [Buf("wd0"), Buf("wd1")]
                heT = [sbuf(ph, "heT%d" % i, [128, 4, 512], BF16) for i in range(2)]
                Bhe = [Buf("he0"), Buf("he1")]
                sg = [sbuf(ph, "sg%d" % i, [128, 512], BF16) for i in range(2)]
                Bsg = [Buf("sg0"), Buf("sg1")]
                yacc = sbuf(ph, "yacc", [128, NT, D])
                Byacc = [Buf("yacc%d" % t) for t in range(NT)]
                wr_b = sbuf(ph, "wr_b", [128, 8, 16], BF16)
                brt = sbuf(ph, "brt", [128, 16])
                Bwr = Buf("wr")
                Bbrt = Buf("brt")
                rt = sbuf(ph, "rt", [128, 1472])
                Brt = Buf("rt")
                G = [psum[0], psum[1]]
                U = [psum[2], psum[3]]
                Dn = [psum[4], psum[5], psum[6], psum[7]]
                BG, BU, BD = [PB[0], PB[1]], [PB[2], PB[3]], [PB[4], PB[5], PB[6], PB[7]]

                def load_expert(e):
                    sl = e % 2
                    P.dma("pool", wg[sl][:], w_gate_d[layer, e].rearrange("(kc k) f -> k kc f", k=128), writes=[Bwg[sl]])
                    P.dma("pool", wu[sl][:], w_up_d[layer, e].rearrange("(kc k) f -> k kc f", k=128), writes=[Bwu[sl]])
                    P.dma("pool", wd[sl][:], w_down_d[layer, e].rearrange("(fc f) d -> f fc d", f=128), writes=[Bwd[sl]])

                P.dma("pool", wr_b[:], w_router_d.rearrange("(kc k) e -> k kc e", k=128), writes=[Bwr])
                load_expert(0)
                load_expert(1)
                P.dma("sp", brt[:], b_router_d[0:1, :].partition_broadcast(128), writes=[Bbrt])
                load_bc(bc[2][:], ln2g_d[layer:layer + 1, :], [Bbc[2]])
                load_bc(bc[3][:], ln2b_d[layer:layer + 1, :], [Bbc[3]])
                for c in range(2):
                    make_bc(bc[c], layer, 5, c, dgall, Bdg, PBK[6:8], [Bbc[c]])

                lg = psum[4]
                for t in range(NT):
                    for kc in range(8):
                        P.mm(lg[:, t * 16:(t + 1) * 16], hT[:, kc, t * 128:(t + 1) * 128], wr_b[:, kc, :],
                             kc == 0, kc == 7, reads=[BhT[t][0], BhT[t][1], Bwr], writes=[PB[4]])
                o = [0]

                def carve(nel):
                    v_ = rt[:, o[0]:o[0] + nel]
                    o[0] += nel
                    return v_
                s_sb, sel, t6, gs, gm = carve(192), carve(192), carve(288), carve(48), carve(12)
                gmask, msel, top8, emask, wsel, wsum = carve(48), carve(192), carve(96), carve(192), carve(192), carve(12)
                Rr, Wr = [Brt], [Brt]
                P.v("act", "activation", s_sb, lg[:, 0:192], AF.Sigmoid, R=[PB[4]], W=Wr)
                P.v("dve", "tensor_tensor", sel.rearrange("p (t e) -> p t e", e=16),
                    s_sb.rearrange("p (t e) -> p t e", e=16), brt[:].unsqueeze(1).to_broadcast([128, NT, 16]),
                    ALU.add, R=Rr + [Bbrt], W=Wr)
                sel4 = sel.rearrange("p (g e) -> p g e", e=4)
                t6v = t6.rearrange("p (g e) -> p g e", e=6)
                P.v("dve", "tensor_tensor", t6v[:, :, 0:3], sel4[:, :, 0:3], sel4[:, :, 1:4], ALU.add, R=Rr, W=Wr)
                P.v("dve", "tensor_tensor", t6v[:, :, 3:5], sel4[:, :, 0:2], sel4[:, :, 2:4], ALU.add, R=Rr, W=Wr)
                P.v("dve", "tensor_tensor", t6v[:, :, 5:6], sel4[:, :, 0:1], sel4[:, :, 3:4], ALU.add, R=Rr, W=Wr)
                P.v("dve", "tensor_reduce", gs, t6v, AX.X, ALU.max, R=Rr, W=Wr)
                gs3 = gs.rearrange("p (t g) -> p t g", g=4)
                P.v("dve", "tensor_reduce", gm, gs3, AX.X, ALU.max, R=Rr, W=Wr)
                P.v("dve", "tensor_tensor", gmask.rearrange("p (t g) -> p t g", g=4), gs3,
                    gm.unsqueeze(2).to_broadcast([128, NT, 4]), ALU.is_equal, R=Rr, W=Wr)
                P.v("dve", "scalar_tensor_tensor", msel.rearrange("p (g e) -> p g e", e=4), sel4, 2.0,
                    gmask.unsqueeze(2).to_broadcast([128, 48, 4]), op0=ALU.add, op1=ALU.mult, R=Rr, W=Wr)
                top83 = top8.rearrange("p (t e) -> p t e", e=8)
                for t in range(NT):
                    P.v("dve", "max", top83[:, t, :], msel[:, t * 16:(t + 1) * 16], R=Rr, W=Wr)
                P.v("dve", "tensor_tensor", emask.rearrange("p (t e) -> p t e", e=16),
                    msel.rearrange("p (t e) -> p t e", e=16), top83[:, :, 1:2].to_broadcast([128, NT, 16]),
                    ALU.is_ge, R=Rr, W=Wr)
                P.v("dve", "tensor_tensor", wsel, s_sb, emask, ALU.mult, R=Rr, W=Wr)
                wsel3 = wsel.rearrange("p (t e) -> p t e", e=16)
                P.v("dve", "tensor_reduce", wsum, wsel3, AX.X, ALU.add, R=Rr, W=Wr)
                P.v("dve", "reciprocal", wsum, wsum, R=Rr, W=Wr)
                P.v("dve", "tensor_tensor", cmb[:], wsel3, wsum.unsqueeze(2).to_broadcast([128, NT, 16]),
                    ALU.mult, R=Rr, W=Bcmb)

                blocks = [(e, b) for e in range(16) for b in range(3)]
                hT_bufs = [[x for t in range(4 * b, 4 * b + 4) for x in BhT[t]] for b in range(3)]

                def GU(k):
                    e, b = blocks[k]
                    sl = e % 2
                    for fc in range(4):
                        gi = (k * 4 + fc) % 2
                        for kc in range(8):
                            P.mm(G[gi][:], wg[sl][:, kc, fc * 128:(fc + 1) * 128], hT[:, kc, b * 512:(b + 1) * 512],
                                 kc == 0, kc == 7, reads=[Bwg[sl]] + hT_bufs[b], writes=[BG[gi]])
                        for kc in range(8):
                            P.mm(U[gi][:], wu[sl][:, kc, fc * 128:(fc + 1) * 128], hT[:, kc, b * 512:(b + 1) * 512],
                                 kc == 0, kc == 7, reads=[Bwu[sl]] + hT_bufs[b], writes=[BU[gi]])
                        P.v("act", "activation", sg[gi][:], G[gi][:], AF.Silu, R=[BG[gi]], W=[Bsg[gi]])
                        P.v("dve", "tensor_tensor", heT[k % 2][:, fc, :], U[gi][:], sg[gi][:], ALU.mult,
                            R=[BU[gi], Bsg[gi]], W=[Bhe[k % 2]])

                def DNp(k):
                    e, b = blocks[k]
                    sl = e % 2
                    for tt in range(4):
                        t = 4 * b + tt
                        for half in range(2):
                            di = (k * 8 + tt * 2 + half) % 4
                            for fc in range(4):
                                P.mm(Dn[di][:], heT[k % 2][:, fc, tt * 128:(tt + 1) * 128],
                                     wd[sl][:, fc, half * 512:(half + 1) * 512], fc == 0, fc == 3,
                                     reads=[Bhe[k % 2], Bwd[sl]], writes=[BD[di]])
                            dst = yacc[:, t, half * 512:(half + 1) * 512]
                            if e == 0:
                                P.v("dve", "tensor_scalar", dst, Dn[di][:], cmb[:, t, e:e + 1], None, op0=ALU.mult,
                                    R=[BD[di], Bcmb[t]], W=[Byacc[t]])
                            else:
                                P.v("dve", "scalar_tensor_tensor", dst, Dn[di][:], cmb[:, t, e:e + 1], dst,
                                    op0=ALU.mult, op1=ALU.add, R=[BD[di], Bcmb[t], Byacc[t]], W=[Byacc[t]])

                def pre(grp):
                    for t in range(4 * grp, 4 * grp + 4):
                        c = tile_cond(t)
                        ya = yacc[:, t, :]
                        P.v("pool", "tensor_tensor", ya, ya, bc[c][:], ALU.mult, R=[Byacc[t], Bbc[c]], W=[Byacc[t]])
                    for t in range(4 * grp, 4 * grp + 4):
                        ya = yacc[:, t, :]
                        P.v("dve", "scalar_tensor_tensor", ya, x_sb[:, t, :], ALPHA, ya, op0=ALU.mult, op1=ALU.add,
                            R=[Bx[t], Byacc[t]], W=[Byacc[t]])

                def epi(grp, part):
                    tiles = list(range(4 * grp, 4 * grp + 4))
                    ln_group(tiles, [yacc[:, t, :] for t in tiles], [Byacc[t] for t in tiles], bc[2][:], bc[3][:],
                             [Bbc[2], Bbc[3]], next_mod, PBK, "dve", part=part)

                GU(0)
                for k in range(len(blocks)):
                    if k + 1 < len(blocks):
                        GU(k + 1)
                    DNp(k)
                    e_, b_ = blocks[k]
                    if b_ == 2 and e_ + 2 < 16:
                        load_expert(e_ + 2)
                    if e_ == 15:
                        pre(b_)
                        epi(b_, "elem")
                for grp in range(3):
                    if next_mod is not None:
                        epi(grp, "tr")
                    if final:
                        for t in range(4 * grp, 4 * grp + 4):
                            P.dma("sp", y_d[t * 128:(t + 1) * 128, :], x_sb[:, t, :], reads=[Bx[t]])
                P.end_phase()

        moe_phase(0, (1, 1, 0), False)
        if stop == "l0moe":
            store_x("l0moe")
            P.end_phase()
            return nc


        with ExitStack() as pa:
            QT_s = sbuf(pa, "QT_s", [128, 8, 1024], BF16)
            KT_s = sbuf(pa, "KT_s", [128, 8, 1024], BF16)
            Vp_s = sbuf(pa, "Vp_s", [128, 8, 8, 192], BF16)
            O_CTXV, O_M01, O_RBB, O_E, O_ONES, RG_N = 4096, 10240, 12032, 13824, 19200, 19392
            Rg = sbuf(pa, "Rg", [128, RG_N], BF16)
            QT_p = Rg[:, 0:4096].rearrange("p (c t) -> p c t", c=8)
            KT_p = Rg[:, 4096:8192].rearrange("p (c t) -> p c t", c=8)
            Vp_p = Rg[:, 8192:14336].rearrange("p (m a e) -> p m a e", m=4, a=8)
            ctxKT = Rg[:, 0:4096].rearrange("p (c t) -> p c t", c=8)
            ctxVp = Rg[:, O_CTXV:O_M01].rearrange("p (m a e) -> p m a e", m=4, a=8)
            m01 = Rg[:, O_M01:O_RBB].rearrange("p (m q) -> p m q", q=128)
            rbb = Rg[:, O_RBB:O_E].rearrange("p (b d q) -> p b d q", b=2, d=7)
            Etab = Rg[:, O_E:O_ONES].rearrange("p (b d q) -> p b d q", b=3, d=14)
            onesp = Rg[:, O_ONES:RG_N]
            BQs, BKs, BVs = Buf("QTs"), Buf("KTs"), Buf("Vs")
            BQp, BKp, BVp = Buf("QTp"), Buf("KTp"), Buf("Vp")
            Bones = Buf("onesp")

            with ExitStack() as ph:
                wb = [sbuf(ph, "wqkv%d" % i, [128, 8, 512], BF16) for i in range(2)]
                Bwb = [Buf("wb0"), Buf("wb1")]
                stg = [sbuf(ph, "stg%d" % i, [128, 512]) for i in range(2)]
                Bstg = [Buf("stg0"), Buf("stg1")]
                P.v("pool", "memset", Vp_s[:, :, :, 64:128], 0.0, W=[BVs])
                P.v("pool", "memset", Vp_p[:, :, :, 64:128], 0.0, W=[BVp])
                P.v("pool", "memset", onesp[:, 0:64], 1.0, W=[Bones])
                P.v("pool", "memset", onesp[:, 64:128], 0.0, W=[Bones])
                P.v("pool", "memset", onesp[:, 128:192], 1.0, W=[Bones])
                nb = [0]
                ns = [0]

                def bank():
                    i = nb[0] % 8
                    nb[0] += 1
                    return psum[i], PB[i], ("dve" if i % 2 == 0 else "act")

                def evac(eng, dst, src, Bsrc, W, mul=None):
                    if eng == "dve":
                        if mul is None:
                            P.v("dve", "tensor_copy", dst, src, R=[Bsrc], W=W)
                        else:
                            P.v("dve", "tensor_scalar_mul", dst, src, mul, R=[Bsrc], W=W)
                    else:
                        if mul is None:
                            P.v("act", "copy", dst, src, R=[Bsrc], W=W)
                        else:
                            P.v("act", "mul", dst, src, mul, R=[Bsrc], W=W)

                hTb = lambda tiles: [x for t in tiles for x in BhT[t]]
                for cb in range(6):
                    sl = cb % 2
                    P.dma("pool", wb[sl][:], w_qkv_d.rearrange("(kc k) n -> k kc n", k=128)[:, :, cb * 512:(cb + 1) * 512],
                          writes=[Bwb[sl]])
                    if cb < 4:
                        isq = cb < 2
                        for i in range(4):
                            a = (cb % 2) * 4 + i
                            for tb in range(3):
                                pb, Bp, eng = bank()
                                for kc in range(8):
                                    P.mm(pb[:], wb[sl][:, kc, i * 128:(i + 1) * 128], hT[:, kc, tb * 512:(tb + 1) * 512],
                                         kc == 0, kc == 7, reads=[Bwb[sl]] + hTb(range(4 * tb, 4 * tb + 4)), writes=[Bp])
                                if tb == 0:
                                    dst = (QT_p if isq else KT_p)[:, a, :]
                                    W = [BQp if isq else BKp]
                                else:
                                    dst = (QT_s if isq else KT_s)[:, a, (tb - 1) * 512:tb * 512]
                                    W = [BQs if isq else BKs]
                                evac(eng, dst, pb[:], Bp, W, mul=(0.125 if isq else None))
                    if cb in (2, 3, 4, 5):
                        isv = cb >= 4
                        hh = cb % 2
                        for t in (range(NT) if isv else range(4)):
                            pb, Bp, eng = bank()
                            for kc in range(8):
                                P.mm(pb[:], hT[:, kc, t * 128:(t + 1) * 128], wb[sl][:, kc, :], kc == 0, kc == 7,
                                     reads=[Bwb[sl]] + hTb([t]), writes=[Bp])
                            if isv:
                                src4 = pb[:].rearrange("p (a two d) -> p a two d", two=2, d=64)
                                for par in range(2):
                                    if t < 4:
                                        evac(eng, Vp_p[:, t, hh * 4:(hh + 1) * 4, par * 128:par * 128 + 64],
                                             src4[:, :, par, :], Bp, [BVp])
                                    else:
                                        evac(eng, Vp_s[:, t - 4, hh * 4:(hh + 1) * 4, par * 128:par * 128 + 64],
                                             src4[:, :, par, :], Bp, [BVs])
                            if t < 4:
                                k = ns[0] % 2
                                ns[0] += 1
                                evac("dve" if eng == "act" else "act", stg[k][:], pb[:], Bp, [Bstg[k]])
                                dd = nv_d if isv else nk_d
                                s_, l0 = t // 2, (t % 2) * 128
                                P.dma("sp", dd[s_, hh * 8:(hh + 1) * 8, l0:l0 + 128, :].rearrange("h l d -> l h d"),
                                      stg[k][:].rearrange("p (h d) -> p h d", d=64), reads=[Bstg[k]])
                P.end_phase()

            with ExitStack() as pb_:
                OT_all = sbuf(pb_, "OT_all", [128, 8, 1024], BF16)
                BOT = [Buf("OT%d" % a) for a in range(8)]
                wo = sbuf(pb_, "wo", [128, 8, D], BF16)
                Bwo = Buf("wo")
                By = [Buf("ay%d" % i) for i in range(4)]
                ST = [PBK[i] for i in range(4)]
                ACC = [(PBK[4], PBK[5]), (PBK[6], PBK[7])]
                cn = {"st": 0, "pt": 0, "em": 0}

                def heads_scope(scope):
                    PT = [sbuf(scope, "PT%d" % i, [128, 512], BF16) for i in range(4)]
                    BPT = [Buf("PT%d" % i) for i in range(4)]
                    rS = sbuf(scope, "rS", [128, 512])
                    BrS = Buf("rS")
                    return PT, BPT, rS, BrS

                LOOKAHEAD = 2
                pend = []

                def drain(keep):
                    while len(pend) > keep:
                        pend.pop(0)()

                def segment(PT, BPT, Kst, Qmv, nq, Bk, Bq, esl, BE_, Vblk, BV_, ones_blk, acc, c0, hold=False):
                    st, Bst = ST[cn["st"] % 4]
                    cn["st"] += 1
                    k = cn["pt"] % 4
                    cn["pt"] += 1
                    P.mm(st[:, 0:nq], Kst, Qmv, True, True, reads=[Bk, Bq], writes=[Bst])
                    P.v("act", "activation", PT[k][:, 0:nq], st[:, 0:nq], AF.Exp, R=[Bst], W=[BPT[k]])
                    if esl is not None:
                        P.v("dve", "tensor_tensor", PT[k][:, 0:nq], PT[k][:, 0:nq], esl, ALU.mult,
                            R=[BPT[k], BE_], W=[BPT[k]])
                    (pO, BO_), (pS, BS_) = acc

                    def back():
                        P.mm(pO[:, c0:c0 + nq], Vblk, PT[k][:, 0:nq], False, False, reads=[BPT[k], BV_], writes=[BO_],
                             skip=True)
                        P.mm(pS[:, c0:c0 + nq], ones_blk, PT[k][:, 0:nq], False, False, reads=[BPT[k], Bones],
                             writes=[BS_], skip=True)
                    pend.append(back)
                    if not hold:
                        drain(LOOKAHEAD)

                def zero_acc(acc, ncols):
                    (pO, BO_), (pS, BS_) = acc
                    P.v("dve", "memset", pO[:, 0:ncols], 0.0, W=[BO_])
                    P.v("dve", "memset", pS[:, 0:ncols], 0.0, W=[BS_])

                def normalize(acc, ncols, rS, BrS, dst, Wd):
                    (pO, BO_), (pS, BS_) = acc

                    def fin():
                        P.v("dve", "reciprocal", rS[:, 0:ncols], pS[:, 0:ncols], R=[BS_], W=[BrS])
                        P.v("dve", "tensor_tensor", dst, pO[:, 0:ncols], rS[:, 0:ncols], ALU.mult, R=[BO_, BrS], W=Wd)
                    pend.append(fin)

                def proj_group(tiles, tcols, ybufs, deadW, bcv, Bbcv):
                    for i, t in enumerate(tiles):
                        tc = tcols[i]
                        for half in range(2):
                            pb, Bp = PBK[2 * (i % 2) + half]
                            for kc in range(8):
                                P.mm(pb[:], OT_all[:, kc, tc:tc + 128], wo[:, kc, half * 512:(half + 1) * 512],
                                     kc == 0, kc == 7, reads=[BOT[kc], Bwo], writes=[Bp])
                            P.v("dve", "tensor_tensor", ybufs[i][:, half * 512:(half + 1) * 512], pb[:],
                                bcv[0][:, half * 512:(half + 1) * 512], ALU.mult, R=[Bp, Bbcv[0]],
                                W=[By[i]] + list(deadW))
                        P.v("dve", "scalar_tensor_tensor", ybufs[i], x_sb[:, t, :], ALPHA, ybufs[i],
                            op0=ALU.mult, op1=ALU.add, R=[Bx[t], By[i]], W=[By[i]])
                    ln_group(tiles, ybufs, By, bcv[1], bcv[2], [Bbcv[1], Bbcv[2]], (1, 4, 3), PBK[4:8], "pool")

                def prep_vectors(cidx, bcv, Bbcv, dgall_v, Bdg_v, deadW):
                    load_bc(bcv[1], ln1g_d[1:2, :], [Bbcv[1]] + list(deadW))
                    load_bc(bcv[2], ln1b_d[1:2, :], [Bbcv[2]] + list(deadW))
                    make_bc(bcv[0], 1, 2, cidx, dgall_v, Bdg_v, PBK[6:8], [Bbcv[0]] + list(deadW))

                with ExitStack() as ph:
                    PT, BPT, rS, BrS = heads_scope(ph)
                    P.dma("pool", wo[:], w_o_d.rearrange("(kc k) n -> k kc n", k=128), writes=[Bwo])
                    for a in range(8):
                        for s_ in range(2):
                            acc = ACC[(2 * a + s_) % 2]
                            zero_acc(acc, 256)
                            for c in range(2):
                                for rho in range(2):
                                    pr = slice(64 * rho, 64 * rho + 64)
                                    vc = slice(64 * rho, 64 * rho + 128)
                                    k0 = s_ * 256 + c * 128
                                    segment(PT, BPT, KT_p[pr, a, k0:k0 + 128], QT_p[pr, a, s_ * 256:(s_ + 1) * 256], 256,
                                            BKp, BQp, None, None, Vp_p[:, 2 * s_ + c, a, vc], BVp, onesp[:, vc], acc, 0, hold=(rho == 0))
                            normalize(acc, 256, rS, BrS, OT_all[:, a, s_ * 256:(s_ + 1) * 256], [BOT[a]])
                    drain(0)
                    P.end_phase()

                yb_p = [Rg[:, 2048 * i:2048 * (i + 1)].bitcast(F32) for i in range(4)]
                bc_p = [Rg[:, 8192 + 2048 * i:8192 + 2048 * (i + 1)].bitcast(F32) for i in range(3)]
                dg_p = Rg[:, 14336:16384].bitcast(F32).rearrange("p (a b) -> p a b", b=128)
                Bbc_p = [Buf("pbc%d" % i) for i in range(3)]
                prep_vectors(0, bc_p, Bbc_p, dg_p, Buf("pdg"), [])
                proj_group([0, 1, 2, 3], [0, 128, 256, 384], yb_p, [], bc_p, Bbc_p)
                if stop == "l1mixp":
                    store_x("l1mixp")
                P.end_phase()
                if stop == "l1mixp":
                    return nc

                with ExitStack() as ph:
                    PT, BPT, rS, BrS = heads_scope(ph)
                    ck_tok = OT_all[:, 0:4, :].rearrange("p c (h d) -> p c h d", d=64)
                    Bck = Buf("ck_tok")
                    BctxK, BctxV, Bm01 = Buf("ctxK"), Buf("ctxV"), Buf("m01")
                    Brbb = [Buf("rbb0"), Buf("rbb1")]
                    BE = [Buf("E0"), Buf("E1"), Buf("E2")]
                    for c in range(4):
                        P.dma("pool", ck_tok[:, c, :, :], ck_d[:, c * 128:(c + 1) * 128, :].rearrange("h l d -> l h d"),
                              writes=[Bck])
                    P.dma("pool", m01[:], m01_d.rearrange("m k q -> k m q"), writes=[Bm01])
                    P.v("pool", "memset", ctxVp[:, :, :, 64:128], 0.0, W=[BctxV])
                    cv4 = cv_d.rearrange("(a two) l d -> two l a d", two=2)
                    for c in range(4):
                        for par in range(2):
                            P.dma("pool", ctxVp[:, c, :, par * 128:par * 128 + 64], cv4[par, c * 128:(c + 1) * 128, :, :],
                                  writes=[BctxV])

                    def build_E(h):
                        hb, eb = h % 2, h % 3
                        P.dma("pool", rbb[:, hb, :, :], rbt_d[h].rearrange("d k q -> k d q"), writes=[Brbb[hb]])
                        P.v("act", "activation", rbb[:, hb, :, :], rbb[:, hb, :, :], AF.Exp, R=[Brbb[hb]], W=[Brbb[hb]])
                        for s2 in range(2):
                            P.v("pool", "tensor_tensor", Etab[:, eb, s2 * 7:(s2 + 1) * 7, :], rbb[:, hb, :, :],
                                m01[:, s2 * 7:(s2 + 1) * 7, :], ALU.mult, R=[Brbb[hb], Bm01], W=[BE[eb]])

                    for c in range(4):
                        pTb = psum[c % 2][:].bitcast(BF16)
                        for a in range(8):
                            P.tr(pTb[:, a * 128:(a + 1) * 128], ck_tok[:, c, 2 * a:2 * a + 2, :].rearrange("p h d -> p (h d)"),
                                 ident_b[:], reads=[Bck, Bconst], writes=[PB[c % 2]])
                        P.v("dve" if c % 2 == 0 else "act", "tensor_copy" if c % 2 == 0 else "copy",
                            ctxKT[:, :, c * 128:(c + 1) * 128], pTb.rearrange("p (a t) -> p a t", a=8),
                            R=[PB[c % 2]], W=[BctxK])
                    build_E(0)
                    build_E(1)
                    build_E(2)
                    for a in range(8):
                        for qb in range(2):
                            acc = ACC[(2 * a + qb) % 2]
                            zero_acc(acc, 512)
                            for (m, jlo, cnt, e2) in _SEGS[qb]:
                                for rho in range(2):
                                    eb = (2 * a + rho) % 3
                                    pr = slice(64 * rho, 64 * rho + 64)
                                    vc = slice(64 * rho, 64 * rho + 128)
                                    nq, q0 = cnt * 128, jlo * 128
                                    i0 = e2 * 7 + 3 - m + jlo
                                    esl = Etab[:, eb, i0:i0 + cnt, :].rearrange("p a b -> p (a b)")
                                    segment(PT, BPT, KT_s[pr, a, m * 128:(m + 1) * 128], QT_s[pr, a, q0:q0 + nq], nq,
                                            BKs, BQs, esl, BE[eb], Vp_s[:, m, a, vc], BVs, onesp[:, vc], acc, q0 - 512 * qb, hold=(rho == 0))
                            for c in range(4):
                                for rho in range(2):
                                    pr = slice(64 * rho, 64 * rho + 64)
                                    vc = slice(64 * rho, 64 * rho + 128)
                                    segment(PT, BPT, ctxKT[pr, a, c * 128:(c + 1) * 128], QT_s[pr, a, qb * 512:(qb + 1) * 512],
                                            512, BctxK, BQs, None, None, ctxVp[:, c, a, vc], BctxV, onesp[:, vc], acc, 0, hold=(rho == 0))
                            normalize(acc, 512, rS, BrS, OT_all[:, a, qb * 512:(qb + 1) * 512], [BOT[a], Bck])
                        for h2 in (2 * a + 3, 2 * a + 4):
                            if h2 < 16:
                                build_E(h2)
                    drain(0)
                    P.end_phase()

                yb_s = [QT_s[:, 2 * i:2 * i + 2, :].rearrange("p a t -> p (a t)").bitcast(F32) for i in range(4)]
                bc_s = [KT_s[:, 2 * i:2 * i + 2, :].rearrange("p a t -> p (a t)").bitcast(F32) for i in range(3)]
                dg_s = KT_s[:, 6:8, :].rearrange("p a t -> p (a t)").bitcast(F32).rearrange("p (a b) -> p a b", b=128)
                Bbc_s = [Buf("sbc%d" % i) for i in range(3)]
                prep_vectors(1, bc_s, Bbc_s, dg_s, Buf("sdg"), [])
                proj_group([4, 5, 6, 7], [0, 128, 256, 384], yb_s, [], bc_s, Bbc_s)
                proj_group([8, 9, 10, 11], [512, 640, 768, 896], yb_s, [], bc_s, Bbc_s)
                if stop == "l1mix":
                    store_x("l1mix")
                P.end_phase()
        if stop == "l1mix":
            return nc

        moe_phase(1, None, True)
        return nc


_CACHE = {}


def make_in_maps(inputs):
    f = lambda a: np.ascontiguousarray(np.asarray(a, dtype=np.float32))
    x_prompt, x_sample = f(inputs["x_prompt"]), f(inputs["x_sample"])
    cache_k, cache_v = f(inputs["cache_k"]), f(inputs["cache_v"])
    c, c_ctx = f(inputs["c"]), f(inputs["c_ctx"])
    b_ada = f(inputs["b_ada"])
    shared = {
        "w_ada": f(inputs["w_ada"]),
        "b_adaT": np.ascontiguousarray(b_ada.reshape(2, 48, 128).transpose(0, 2, 1)),
        "ln1_g": f(inputs["ln1_g"]), "ln1_b": f(inputs["ln1_b"]),
        "ln2_g": f(inputs["ln2_g"]), "ln2_b": f(inputs["ln2_b"]),
        "pool_w": f(inputs["pool_w"])[0],
        "pool_scale": f(inputs["pool_scale"]).reshape(1, D),
        "w_qkv": f(inputs["w_qkv"])[0],
        "w_o": f(inputs["w_o"])[0],
        "rbt": _rel_bias_layout(f(inputs["rel_bias"])[0]),
        "w_router": f(inputs["w_router"]),
        "b_router": f(inputs["b_router"]).reshape(1, 16),
        "w_gate": f(inputs["w_gate"]), "w_up": f(inputs["w_up"]), "w_down": f(inputs["w_down"]),
        "ptab": _pool_tables(),
        "m01": _mask01_table(),
    }
    in_maps = []
    for i in range(N_CORES):
        m = dict(shared)
        m["xin"] = np.ascontiguousarray(np.concatenate(
            [x_prompt[2 * i].reshape(256, D), x_prompt[2 * i + 1].reshape(256, D), x_sample[i]], axis=0))
        m["ck"] = np.ascontiguousarray(cache_k[i, 0])
        m["cv"] = np.ascontiguousarray(cache_v[i, 0])
        cond = np.stack([c_ctx, c[i]], axis=-1)
        m["condT"] = np.ascontiguousarray(cond.reshape(8, 128, 2).transpose(1, 0, 2))
        in_maps.append(m)
    return in_maps


def kernel(**inputs):
    if "nc" not in _CACHE:
        _CACHE["nc"] = build_nc()
    nc = _CACHE["nc"]
    in_maps = make_in_maps(inputs)
    res = run_bass_kernel_spmd(nc, in_maps, core_ids=list(range(N_CORES)))
    ys = [r["y"] for r in res.results]
    y_prompt = np.stack([y[:512].reshape(2, 256, D) for y in ys]).reshape(16, 256, D)
    y_sample = np.stack([y[512:] for y in ys])
    nk = np.stack([r["nk"] for r in res.results]).reshape(16, 1, 16, 256, 64)
    nv = np.stack([r["nv"] for r in res.results]).reshape(16, 1, 16, 256, 64)
    return (y_prompt.astype(np.float32), y_sample.astype(np.float32),
            nk.astype(np.float32), nv.astype(np.float32))
```

```python
from contextlib import ExitStack

import numpy as np
import concourse.bass as bass
import concourse.mybir as mybir
from concourse.bass_utils import run_bass_kernel_spmd

F32 = mybir.dt.float32
BF16 = mybir.dt.bfloat16
AF = mybir.ActivationFunctionType
ALU = mybir.AluOpType
AX = mybir.AxisListType

N_CORES = 8
D = 1024
NT = 12
T = NT * 128
ALPHA = 4.0 ** 0.25
LN_EPS = 1e-5
NEG = -30000.0
POOL_SIZES = (2, 4, 8, 16)
SEQS = ((0, 2), (2, 2), (4, 8))


def tile_cond(t):
    return 0 if t < 4 else 1


class Buf:
    __slots__ = ("name", "writer", "readers", "excl")

    def __init__(self, name="", excl=False):
        self.name = name
        self.writer = None
        self.readers = []
        self.excl = excl


class Lane:
    __slots__ = ("sem", "n", "last")

    def __init__(self, sem):
        self.sem = sem
        self.n = 0
        self.last = None


class Ins:
    __slots__ = ("eng", "fn", "deps", "signal", "count", "lane", "lval", "idx")

    def __init__(self, eng, fn):
        self.eng = eng
        self.fn = fn
        self.deps = []
        self.signal = False
        self.count = 0
        self.lane = None
        self.lval = 0
        self.idx = 0


class Prog:
    ENG = ("pe", "act", "dve", "pool", "sp")
    NLANES = {"sp": 24, "pool": 12, "act": 4}

    def __init__(self, nc, es):
        self.nc = nc
        self.es = es
        self.eng = {"pe": nc.tensor, "act": nc.scalar, "dve": nc.vector,
                    "pool": nc.gpsimd, "sp": nc.sync}
        self.sem = {e: es.enter_context(nc.semaphore("s_" + e)) for e in self.ENG}
        self.cnt = {e: 0 for e in self.ENG}
        self.bar = es.enter_context(nc.semaphore("s_bar"))
        self.nbar = 0
        self.lanes = {q: [Lane(es.enter_context(nc.semaphore("l_%s%d" % (q, i)))) for i in range(n)]
                      for q, n in self.NLANES.items()}
        self.lane_rr = {q: 0 for q in self.NLANES}
        self.waited = {e: {} for e in self.ENG}
        self.streams = {e: [] for e in self.ENG}
        self.touched = []
        self.n_ins = 0

    def _add_dep(self, ins, p):
        if p is None or p is ins:
            return
        if p.lane is None and ins.lane is None and p.eng == ins.eng and p.eng == "pe":
            return
        deps = ins.deps
        if p.lane is None:
            for i, q in enumerate(deps):
                if q.lane is None and q.eng == p.eng:
                    if p.idx > q.idx:
                        deps[i] = p
                    return
        elif p in deps:
            return
        deps.append(p)

    def _deps(self, ins, reads, writes):
        if any(b.excl for b in reads):
            writes = list(writes) + [b for b in reads if b.excl and b not in writes]
            reads = [b for b in reads if not b.excl]
        same = lambda p: (p is not None and p.lane is None and ins.lane is None and p.eng == ins.eng)
        for b in reads:
            self._add_dep(ins, b.writer)
        for b in writes:
            if not same(b.writer):
                self._add_dep(ins, b.writer)
            for r in b.readers:
                if not same(r):
                    self._add_dep(ins, r)
        for b in reads:
            if not b.readers and b.writer is None:
                self.touched.append(b)
            b.readers.append(ins)
        for b in writes:
            if not b.readers and b.writer is None:
                self.touched.append(b)
            b.writer = ins
            b.readers = []

    def op(self, eng, fn, reads=(), writes=()):
        ins = Ins(eng, fn)
        ins.idx = len(self.streams[eng])
        self._deps(ins, reads, writes)
        self.streams[eng].append(ins)
        return ins

    def dma(self, eng, out, in_, reads=(), writes=(), **kw):
        E = self.eng[eng]
        ins = Ins(eng, lambda: E.dma_start(out=out, in_=in_, **kw))
        ins.idx = len(self.streams[eng])
        lanes = self.lanes[eng]
        lane = lanes[self.lane_rr[eng] % len(lanes)]
        self.lane_rr[eng] += 1
        ins.lane = lane
        lane.n += 1
        ins.lval = 16 * lane.n
        if lane.last is not None:
            ins.deps.append(lane.last)
        lane.last = ins
        self._deps(ins, reads, writes)
        self.streams[eng].append(ins)
        return ins

    def mm(self, out, lhsT, rhs, start, stop, reads=(), writes=(), skip=False):
        t = self.nc.tensor
        if skip:
            return self.op("pe", lambda: t.matmul(out, lhsT, rhs, start=start, stop=stop, skip_group_check=True),
                           reads, writes)
        return self.op("pe", lambda: t.matmul(out, lhsT, rhs, start=start, stop=stop), reads, writes)

    def tr(self, out, in_, ident, reads=(), writes=()):
        t = self.nc.tensor
        return self.op("pe", lambda: t.transpose(out, in_, ident), reads, writes)

    def v(self, eng, meth, *args, R=(), W=(), **kw):
        E = self.eng[eng]
        return self.op(eng, lambda: getattr(E, meth)(*args, **kw), R, W)

    def end_phase(self):
        ENG = self.ENG
        for e in ENG:
            for ins in self.streams[e]:
                for p in ins.deps:
                    if p.lane is None:
                        p.signal = True
            for ins in reversed(self.streams[e]):
                if ins.lane is None:
                    ins.signal = True
                    break
        for e in ENG:
            for ins in self.streams[e]:
                if ins.lane is None and ins.signal:
                    self.cnt[e] += 1
                    ins.count = self.cnt[e]
        for e in ENG:
            E = self.eng[e]
            wd = self.waited[e]
            for ins in self.streams[e]:
                for p in ins.deps:
                    if p.lane is not None:
                        sem, val = p.lane.sem, p.lval
                    else:
                        sem, val = self.sem[p.eng], p.count
                    if wd.get(sem.num, 0) >= val:
                        continue
                    wd[sem.num] = val
                    E.wait_ge(sem, val)
                bi = ins.fn()
                if ins.lane is not None:
                    bi.then_inc(ins.lane.sem, 16)
                elif ins.signal:
                    bi.then_inc(self.sem[e], 1)
                self.n_ins += 1
        sp = self.eng["sp"]
        wd = self.waited["sp"]
        all_lanes = [l for q in self.lanes.values() for l in q]
        for e in ENG:
            if e == "sp" or self.cnt[e] == 0:
                continue
            if wd.get(self.sem[e].num, 0) < self.cnt[e]:
                sp.wait_ge(self.sem[e], self.cnt[e])
        for l in all_lanes:
            if l.n and wd.get(l.sem.num, 0) < 16 * l.n:
                sp.wait_ge(l.sem, 16 * l.n)
        self.nbar += 1
        sp.sem_inc(self.bar, 1)
        for e in ENG:
            if e != "sp":
                self.eng[e].wait_ge(self.bar, self.nbar)
            for e2 in ENG:
                self.waited[e][self.sem[e2].num] = self.cnt[e2]
            for l in all_lanes:
                self.waited[e][l.sem.num] = 16 * l.n
        for l in all_lanes:
            l.last = None
        for b in self.touched:
            b.writer = None
            b.readers = []
        self.touched = []
        self.streams = {e: [] for e in ENG}


def _pool_tables():
    n = 1024
    out = np.zeros((4, 5, 128, 128), np.float32)
    t = np.arange(n)
    for wi, w in enumerate(POOL_SIZES):
        lo = np.clip(t - w // 2, 0, n)
        hi = np.clip(t - w // 2 + w, 0, n)
        cnt = (hi - lo).astype(np.float64)
        s = np.arange(n)[:, None]
        Pm = ((s >= lo[None, :]) & (s < hi[None, :])) / cnt[None, :] - np.eye(n)
        Pm = Pm.astype(np.float32)
        out[wi, 0] = Pm[0:128, 0:128]
        out[wi, 1] = Pm[128:256, 128:256]
        out[wi, 2] = Pm[896:1024, 896:1024]
        out[wi, 3] = Pm[0:128, 128:256]
        out[wi, 4] = Pm[256:384, 128:256]
    return out


def _chunks_for(j):
    if j < 2:
        return list(range(0, 4))
    if j > 5:
        return list(range(4, 8))
    return list(range(j - 2, j + 3))


def _mask_tables():
    tiles = []
    index = {}
    keymap = {}
    col = np.arange(64)
    cs = np.clip(col - 8, 0, 48)
    colok = (col[None, :] >= cs[:, None]) & (col[None, :] < cs[:, None] + 16)
    for j in range(8):
        for m in _chunks_for(j):
            mt = np.full((2, 64, 2, 64), NEG, np.float32)
            for rq in range(2):
                r = 2 * j + rq
                rs = min(max(r - 4, 0), 8)
                for rk in range(2):
                    kr = 2 * m + rk
                    if rs <= kr < rs + 8:
                        mt[rk, :, rq, :] = np.where(colok.T, 0.0, NEG)
            key = mt.tobytes()
            if key not in keymap:
                keymap[key] = len(tiles)
                tiles.append(mt.reshape(128, 128))
            index[(j, m)] = keymap[key]
    return np.stack(tiles), index


_MASKS, _MASK_IDX = _mask_tables()
_M_FULL, _M_A, _M_B = _MASK_IDX[(0, 0)], _MASK_IDX[(2, 0)], _MASK_IDX[(2, 4)]


def _mask01_table():
    full = (_MASKS[_M_FULL] == 0).astype(np.float32)
    t = np.stack([full] * 14)
    t[7 + (3 - 2)] = (_MASKS[_M_B] == 0)
    t[7 + (3 + 2)] = (_MASKS[_M_A] == 0)
    return t


def _segments():
    segs = {0: [], 1: []}
    for m in range(8):
        for qb in range(2):
            js = [j for j in range(8) if m in _chunks_for(j) and j // 4 == qb]
            if not js:
                continue
            assert js == list(range(js[0], js[-1] + 1))
            e2 = any(_MASK_IDX[(j, m)] != _M_FULL for j in js)
            for j in js:
                want = _M_FULL
                if e2 and m - j == -2:
                    want = _M_A
                if e2 and m - j == 2:
                    want = _M_B
                assert _MASK_IDX[(j, m)] == want, (j, m)
            segs[qb].append((m, js[0], len(js), int(e2)))
    return segs


_SEGS = _segments()


def _rel_bias_layout(rel_bias):
    rk = np.arange(2)[:, None, None, None]
    ck = np.arange(64)[None, :, None, None]
    rq = np.arange(2)[None, None, :, None]
    cq = np.arange(64)[None, None, None, :]
    out = np.empty((16, 7, 128, 128), np.float32)
    for i in range(7):
        dl = 3 - i
        dy = np.clip(2 * dl + rk - rq + 7, 0, 14)
        dx = np.clip(ck - cq + 15, 0, 30)
        dy, dx = np.broadcast_arrays(dy, dx)
        out[:, i] = rel_bias[:, dy, dx].reshape(16, 128, 128)
    return out


def build_nc(stop=None):
    nc = bass.Bass("TRN2", target_bir_lowering=False)

    def din(name, shape):
        return nc.dram_tensor(name, list(shape), F32, kind="ExternalInput").ap()

    def dout(name, shape):
        return nc.dram_tensor(name, list(shape), F32, kind="ExternalOutput").ap()

    xin = din("xin", [T, D])
    ck_d = din("ck", [16, 512, 64])
    cv_d = din("cv", [16, 512, 64])
    condT_d = din("condT", [128, 8, 2])
    w_ada_d = din("w_ada", [2, D, 6 * D])
    b_adaT_d = din("b_adaT", [2, 128, 48])
    ln1g_d = din("ln1_g", [2, D])
    ln1b_d = din("ln1_b", [2, D])
    ln2g_d = din("ln2_g", [2, D])
    ln2b_d = din("ln2_b", [2, D])
    pool_w_d = din("pool_w", [4, 256, 256])
    pool_scale_d = din("pool_scale", [1, D])
    w_qkv_d = din("w_qkv", [D, 3 * D])
    w_o_d = din("w_o", [D, D])
    rbt_d = din("rbt", [16, 7, 128, 128])
    w_router_d = din("w_router", [D, 16])
    b_router_d = din("b_router", [1, 16])
    w_gate_d = din("w_gate", [2, 16, D, 512])
    w_up_d = din("w_up", [2, 16, D, 512])
    w_down_d = din("w_down", [2, 16, 512, D])
    ptab_d = din("ptab", [4, 5, 128, 128])
    m01_d = din("m01", [14, 128, 128])

    y_d = dout("y", [T, D])
    nk_d = dout("nk", [2, 16, 256, 64])
    nv_d = dout("nv", [2, 16, 256, 64])
    dbg_d = dout("dbg", [128, 2, 48, 2]) if stop == "ada" else None

    with ExitStack() as es:
        P = Prog(nc, es)

        _uid = [0]

        def sbuf(scope, name, shape, dt=F32):
            _uid[0] += 1
            return scope.enter_context(nc.sbuf_tensor("sb%d_%s" % (_uid[0], name), list(shape), dt))

        x_sb = sbuf(es, "x_sb", [128, NT, D])
        hT = sbuf(es, "hT", [128, 8, T], BF16)
        ident_f = sbuf(es, "ident_f", [128, 128])
        ident_b = sbuf(es, "ident_b", [128, 128], BF16)
        ones_f = sbuf(es, "ones_f", [128, 128])
        modcol = sbuf(es, "modcol", [128, 2, 48, 2])
        cmb = sbuf(es, "cmb", [128, NT, 16])
        small = sbuf(es, "small", [128, 64])
        psum = [es.enter_context(nc.psum_tensor("ps%d" % i, [128, 512], F32)) for i in range(8)]
        PB = [Buf("ps%d" % i, excl=True) for i in range(8)]

        Bx = [Buf("x%d" % t) for t in range(NT)]
        BhT = [(Buf("hTd%d" % t), Buf("hTa%d" % t)) for t in range(NT)]
        Bconst = Buf("const")
        Bmod = [Buf("modcol0"), Buf("modcol1")]
        sTb = sbuf(es, "sTb", [128, 8, 2], BF16)
        b_col = sbuf(es, "b_col", [128, 2, 48])
        BsT = Buf("sT")
        Bbcol = Buf("bcol")
        Bcmb = [Buf("cmb%d" % t) for t in range(NT)]


        def alpha_col(layer, v, ch, c):
            return modcol[:, layer, v * 8 + ch, c:c + 1]

        PBK = [(psum[i], PB[i]) for i in range(8)]

        def ada_block(layer, nb, wa_t, Bwa_t, bank):
            pc, Bpc = bank
            src = w_ada_d[layer].rearrange("(kc k) n -> k kc n", k=128)[:, :, nb * 512:(nb + 1) * 512]
            P.dma("pool", wa_t[:], src, writes=[Bwa_t])
            for n4 in range(4):
                for kc in range(8):
                    P.mm(pc[:, n4 * 2:n4 * 2 + 2], wa_t[:, kc, n4 * 128:(n4 + 1) * 128], sTb[:, kc, :],
                         kc == 0, kc == 7, reads=[Bwa_t, BsT], writes=[Bpc])
            dst = modcol[:, layer, nb * 4:(nb + 1) * 4, :]
            P.v("dve", "tensor_tensor", dst, pc[:, 0:8].rearrange("p (c k) -> p c k", k=2),
                b_col[:, layer, nb * 4:(nb + 1) * 4].unsqueeze(2).to_broadcast([128, 4, 2]), ALU.add,
                R=[Bpc, Bbcol], W=[Bmod[layer]])
            if nb // 2 in (1, 4):
                P.v("dve", "tensor_scalar_add", dst, dst, 1.0, R=[Bmod[layer]], W=[Bmod[layer]])

        with ExitStack() as ph:
            condT = sbuf(ph, "condT", [128, 8, 2])
            sT = sbuf(ph, "sT", [128, 8, 2])
            wa = [sbuf(ph, "wa%d" % i, [128, 8, 512], BF16) for i in range(3)]
            Bwa = [Buf("wa%d" % i) for i in range(3)]

            P.v("pool", "memset", ident_f[:], 0.0, W=[Bconst])
            P.v("pool", "affine_select", ident_f[:], ident_f[:], [[-1, 128]], ALU.not_equal, 1.0,
                base=0, channel_multiplier=1, R=[Bconst], W=[Bconst])
            P.v("pool", "tensor_copy", ident_b[:], ident_f[:], R=[Bconst], W=[Bconst])
            P.v("pool", "memset", ones_f[:], 1.0, W=[Bconst])

            for t in range(NT):
                P.dma("sp", x_sb[:, t, :], xin[t * 128:(t + 1) * 128, :], writes=[Bx[t]])
            P.dma("sp", condT[:], condT_d[:, :, :], writes=[BsT])
            P.dma("sp", b_col[:], b_adaT_d.rearrange("l p c -> p l c"), writes=[Bbcol])
            P.v("act", "activation", sT[:], condT[:], AF.Silu, R=[BsT], W=[BsT])
            P.v("dve", "tensor_copy", sTb[:], sT[:], R=[BsT], W=[BsT])

            for nb in range(12):
                ada_block(0, nb, wa[nb % 3], Bwa[nb % 3], PBK[nb % 2])
            if stop == "ada":
                for nb in range(12):
                    ada_block(1, nb, wa[nb % 3], Bwa[nb % 3], PBK[nb % 2])
            if stop == "ada":
                P.dma("sp", dbg_d[:, :, :, :], modcol[:], reads=Bmod)
            P.end_phase()

        def load_bc(dst, src_row, W):
            P.dma("sp", dst, src_row.partition_broadcast(128), writes=W)

        def make_bc(dst, layer, v, c, dgall, Bdg, banks, W):
            P.v("dve", "tensor_tensor", dgall[:], ident_f[:].unsqueeze(1).to_broadcast([128, 8, 128]),
                modcol[:, layer, v * 8:(v + 1) * 8, c:c + 1].to_broadcast([128, 8, 128]), ALU.mult,
                R=[Bconst, Bmod[layer]], W=[Bdg])
            for half in range(2):
                pb, Bp = banks[half]
                P.mm(pb[:], ones_f[:], dgall[:, half * 4:(half + 1) * 4, :].rearrange("p a b -> p (a b)"),
                     True, True, reads=[Bconst, Bdg], writes=[Bp])
                P.v("act", "copy", dst[:, half * 512:(half + 1) * 512], pb[:], R=[Bp], W=W)

        Bsmall = [Buf("mv%d" % i) for i in range(4)]

        def ln_group(tiles, ys, Bys, g_bc, b_bc, Bvec, next_mod, banks, gmul_eng, part="all"):
            n = len(tiles)
            sts = [small[:, 16 * i:16 * i + 12].rearrange("p (a b) -> p a b", b=6) for i in range(n)]
            mvv = [small[:, 16 * i + 12:16 * i + 16] for i in range(n)]
            if part != "tr":
                ln_elem(tiles, ys, Bys, g_bc, b_bc, Bvec, gmul_eng, n, sts, mvv)
            if next_mod is None or part == "elem":
                return
            ln_tr(tiles, next_mod, banks)

        def ln_elem(tiles, ys, Bys, g_bc, b_bc, Bvec, gmul_eng, n, sts, mvv):
            for i in range(n):
                P.v("dve", "bn_stats", sts[i][:, 0, :], ys[i][:, 0:512], R=[Bys[i]], W=[Bsmall[i]])
                P.v("dve", "bn_stats", sts[i][:, 1, :], ys[i][:, 512:1024], R=[Bys[i]], W=[Bsmall[i]])
                P.v("dve", "bn_aggr", mvv[i][:, 0:2], sts[i], R=[Bsmall[i]], W=[Bsmall[i]])
            for i in range(n):
                P.v("act", "activation", mvv[i][:, 2:3], mvv[i][:, 1:2], AF.Sqrt, bias=LN_EPS,
                    R=[Bsmall[i]], W=[Bsmall[i]])
            for i in range(n):
                P.v("dve", "reciprocal", mvv[i][:, 2:3], mvv[i][:, 2:3], R=[Bsmall[i]], W=[Bsmall[i]])
                P.v("dve", "tensor_scalar", mvv[i][:, 3:4], mvv[i][:, 0:1], mvv[i][:, 2:3], -1.0,
                    op0=ALU.mult, op1=ALU.mult, R=[Bsmall[i]], W=[Bsmall[i]])
            for i in range(n):
                P.v("act", "activation", ys[i], ys[i], AF.Identity, scale=mvv[i][:, 2:3], bias=mvv[i][:, 3:4],
                    R=[Bys[i], Bsmall[i]], W=[Bys[i]])
            for i in range(n):
                P.v(gmul_eng, "tensor_tensor", ys[i], ys[i], g_bc, ALU.mult, R=[Bys[i]] + list(Bvec), W=[Bys[i]])
            for i, t in enumerate(tiles):
                P.v("pool", "tensor_tensor", x_sb[:, t, :], ys[i], b_bc, ALU.add, R=[Bys[i]] + list(Bvec), W=[Bx[t]])

        def ln_tr(tiles, next_mod, banks):
            layer, vA, vB = next_mod
            nb = len(banks) // 2
            for i, t in enumerate(tiles):
                c = tile_cond(t)
                pair = banks[2 * (i % nb):2 * (i % nb) + 2]
                for half in range(2):
                    pb, Bp = pair[half]
                    for q in range(4):
                        kc = half * 4 + q
                        P.tr(pb[:, q * 128:(q + 1) * 128], x_sb[:, t, kc * 128:(kc + 1) * 128], ident_f[:],
                             reads=[Bx[t], Bconst], writes=[Bp])
                for half in range(2):
                    pb, Bp = pair[half]
                    for q in range(4):
                        kc = half * 4 + q
                        if half == 0:
                            P.v("dve", "tensor_scalar", hT[:, kc, t * 128:(t + 1) * 128], pb[:, q * 128:(q + 1) * 128],
                                alpha_col(layer, vA, kc, c), alpha_col(layer, vB, kc, c),
                                op0=ALU.mult, op1=ALU.add, R=[Bp, Bmod[layer]], W=[BhT[t][0]])
                        else:
                            P.v("act", "activation", hT[:, kc, t * 128:(t + 1) * 128], pb[:, q * 128:(q + 1) * 128],
                                AF.Identity, scale=alpha_col(layer, vA, kc, c), bias=alpha_col(layer, vB, kc, c),
                                R=[Bp, Bmod[layer]], W=[BhT[t][1]])

        PBK = [(psum[i], PB[i]) for i in range(8)]

        def store_x(label):
            for t in range(NT):
                P.dma("sp", y_d[t * 128:(t + 1) * 128, :], x_sb[:, t, :], reads=[Bx[t]])

        if stop == "ada":
            store_x("ada")
            P.end_phase()
            return nc

        with ExitStack() as ph:
            bc = [sbuf(ph, "bc%d" % i, [128, D]) for i in range(8)]
            Bbc = [Buf("bc%d" % i) for i in range(8)]
            psc = sbuf(ph, "psc", [128, D])
            Bpsc = Buf("psc")
            dgall = sbuf(ph, "dgall", [128, 8, 128])
            Bdg = Buf("dgall")
            h_tok = sbuf(ph, "h_tok", [128, NT, D])
            Bh = [Buf("h%d" % t) for t in range(NT)]
            ptab = sbuf(ph, "ptab", [128, 4, 5, 128])
            Bptab = Buf("ptab")
            pw = sbuf(ph, "pw", [128, 4, 2, 256], BF16)
            Bpw = Buf("pw")
            pTt = [sbuf(ph, "pTt%d" % i, [128, 8, 128], BF16) for i in range(2)]
            BpTt = [Buf("pTt0"), Buf("pTt1")]
            ybuf = [sbuf(ph, "ybuf%d" % i, [128, D]) for i in range(4)]
            By = [Buf("y%d" % i) for i in range(4)]
            wa1 = sbuf(ph, "wa1", [128, 8, 512], BF16)
            Bwa1 = Buf("wa1")

            P.dma("sp", ptab[:], ptab_d.rearrange("w k s t -> s w k t"), writes=[Bptab])
            P.dma("pool", pw[:], pool_w_d.rearrange("g (i c) d -> c g i d", c=128), writes=[Bpw])
            load_bc(psc[:], pool_scale_d[0:1, :], [Bpsc])
            load_bc(bc[6][:], ln1g_d[0:1, :], [Bbc[6]])
            load_bc(bc[7][:], ln1b_d[0:1, :], [Bbc[7]])
            for c in range(2):
                make_bc(bc[2 * c], 0, 1, c, dgall, Bdg, PBK[0:2], [Bbc[2 * c]])
                make_bc(bc[2 * c + 1], 0, 0, c, dgall, Bdg, PBK[2:4], [Bbc[2 * c + 1]])
                make_bc(bc[4 + c], 0, 2, c, dgall, Bdg, PBK[4:6], [Bbc[4 + c]])
                P.v("pool", "tensor_tensor", bc[4 + c][:], bc[4 + c][:], psc[:], ALU.mult,
                    R=[Bbc[4 + c], Bpsc], W=[Bbc[4 + c]])
            for t in range(NT):
                c = tile_cond(t)
                P.v("dve", "tensor_tensor", h_tok[:, t, :], x_sb[:, t, :], bc[2 * c][:], ALU.mult,
                    R=[Bx[t], Bbc[2 * c]], W=[Bh[t]])
                P.v("pool", "tensor_tensor", h_tok[:, t, :], h_tok[:, t, :], bc[2 * c + 1][:], ALU.add,
                    R=[Bh[t], Bbc[2 * c + 1]], W=[Bh[t]])
            seq_of = {}
            for (t0, ntl) in SEQS:
                for q in range(ntl):
                    seq_of[t0 + q] = (q, ntl)
            for grp in range(3):
                tiles = list(range(4 * grp, 4 * grp + 4))
                for i, t in enumerate(tiles):
                    q, ntl = seq_of[t]
                    c = tile_cond(t)
                    k = i % 2
                    (pA, BA), (pBk, BB) = PBK[2 * k], PBK[2 * k + 1]
                    kind = 0 if q == 0 else (2 if q == ntl - 1 else 1)
                    srcs = [(t, kind)]
                    if q > 0:
                        srcs.append((t - 1, 3))
                    if q < ntl - 1:
                        srcs.append((t + 1, 4))
                    for kc in range(8):
                        wi = kc // 2
                        bank, Bb = (pA, BA) if kc < 4 else (pBk, BB)
                        dst = bank[:, (kc % 4) * 128:(kc % 4 + 1) * 128]
                        for si, (ts, kd) in enumerate(srcs):
                            P.mm(dst, h_tok[:, ts, kc * 128:(kc + 1) * 128], ptab[:, wi, kd, :],
                                 si == 0, si == len(srcs) - 1, reads=[Bh[ts], Bptab], writes=[Bb])
                    P.v("act", "copy", pTt[k][:, 0:4, :], pA[:].rearrange("p (a b) -> p a b", b=128),
                        R=[BA], W=[BpTt[k]])
                    P.v("act", "copy", pTt[k][:, 4:8, :], pBk[:].rearrange("p (a b) -> p a b", b=128),
                        R=[BB], W=[BpTt[k]])
                    for half in range(2):
                        bank, Bb = (pA, BA) if half == 0 else (pBk, BB)
                        for gg in range(2):
                            g = half * 2 + gg
                            for i2 in range(2):
                                P.mm(bank[:, gg * 256:(gg + 1) * 256], pTt[k][:, 2 * g + i2, :], pw[:, g, i2, :],
                                     i2 == 0, i2 == 1, reads=[BpTt[k], Bpw], writes=[Bb])
                        P.v("dve", "tensor_tensor", ybuf[i][:, half * 512:(half + 1) * 512], bank[:],
                            bc[4 + c][:, half * 512:(half + 1) * 512], ALU.mult, R=[Bb, Bbc[4 + c]], W=[By[i]])
                    P.v("dve", "scalar_tensor_tensor", ybuf[i][:], x_sb[:, t, :], ALPHA, ybuf[i][:],
                        op0=ALU.mult, op1=ALU.add, R=[Bx[t], By[i]], W=[By[i]])
                    ada_block(1, t, wa1, Bwa1, PBK[6 + t % 2])
                ln_group(tiles, [ybuf[i][:] for i in range(4)], By, bc[6][:], bc[7][:], [Bbc[6], Bbc[7]],
                         (0, 4, 3), PBK[4:8], "pool")
            if stop == "l0mix":
                store_x("l0mix")
            P.end_phase()
        if stop == "l0mix":
            return nc


        def moe_phase(layer, next_mod, final):
            with ExitStack() as ph:
                bc = [sbuf(ph, "bc%d" % i, [128, D]) for i in range(4)]
                Bbc = [Buf("mbc%d" % i) for i in range(4)]
                dgall = sbuf(ph, "dgall", [128, 8, 128])
                Bdg = Buf("dgall")
                wg = [sbuf(ph, "wg%d" % i, [128, 8, 512], BF16) for i in range(2)]
                wu = [sbuf(ph, "wu%d" % i, [128, 8, 512], BF16) for i in range(2)]
                wd = [sbuf(ph, "wd%d" % i, [128, 4, D], BF16) for i in range(2)]
                Bwg = [Buf("wg0"), Buf("wg1")]
                Bwu = [Buf("wu0"), Buf("wu1")]
                Bwd = [Buf("wd0"), Buf("wd1")]
                heT = [sbuf(ph, "heT%d" % i, [128, 4, 512], BF16) for i in range(2)]
                Bhe = [Buf("he0"), Buf("he1")]
                sg = [sbuf(ph, "sg%d" % i, [128, 512], BF16) for i in range(2)]
                Bsg = [Buf("sg0"), Buf("sg1")]
                yacc = sbuf(ph, "yacc", [128, NT, D])
                Byacc = [Buf("yacc%d" % t) for t in range(NT)]
                wr_b = sbuf(ph, "wr_b", [128, 8, 16], BF16)
                brt = sbuf(ph, "brt", [128, 16])
                Bwr = Buf("wr")
                Bbrt = Buf("brt")
                rt = sbuf(ph, "rt", [128, 1472])
                Brt = Buf("rt")
                G = [psum[0], psum[1]]
                U = [psum[2], psum[3]]
                Dn = [psum[4], psum[5], psum[6], psum[7]]
                BG, BU, BD = [PB[0], PB[1]], [PB[2], PB[3]], [PB[4], PB[5], PB[6], PB[7]]

                def load_expert(e):
                    sl = e % 2
                    P.dma("pool", wg[sl][:], w_gate_d[layer, e].rearrange("(kc k) f -> k kc f", k=128), writes=[Bwg[sl]])
                    P.dma("pool", wu[sl][:], w_up_d[layer, e].rearrange("(kc k) f -> k kc f", k=128), writes=[Bwu[sl]])
                    P.dma("pool", wd[sl][:], w_down_d[layer, e].rearrange("(fc f) d -> f fc d", f=128), writes=[Bwd[sl]])

                P.dma("pool", wr_b[:], w_router_d.rearrange("(kc k) e -> k kc e", k=128), writes=[Bwr])
                load_expert(0)
                load_expert(1)
                P.dma("sp", brt[:], b_router_d[0:1, :].partition_broadcast(128), writes=[Bbrt])
                load_bc(bc[2][:], ln2g_d[layer:layer + 1, :], [Bbc[2]])
                load_bc(bc[3][:], ln2b_d[layer:layer + 1, :], [Bbc[3]])
                for c in range(2):
                    make_bc(bc[c], layer, 5, c, dgall, Bdg, PBK[6:8], [Bbc[c]])

                lg = psum[4]
                for t in range(NT):
                    for kc in range(8):
                        P.mm(lg[:, t * 16:(t + 1) * 16], hT[:, kc, t * 128:(t + 1) * 128], wr_b[:, kc, :],
                             kc == 0, kc == 7, reads=[BhT[t][0], BhT[t][1], Bwr], writes=[PB[4]])
                o = [0]

                def carve(nel):
                    v_ = rt[:, o[0]:o[0] + nel]
                    o[0] += nel
                    return v_
                s_sb, sel, t6, gs, gm = carve(192), carve(192), carve(288), carve(48), carve(12)
                gmask, msel, top8, emask, wsel, wsum = carve(48), carve(192), carve(96), carve(192), carve(192), carve(12)
                Rr, Wr = [Brt], [Brt]
                P.v("act", "activation", s_sb, lg[:, 0:192], AF.Sigmoid, R=[PB[4]], W=Wr)
                P.v("dve", "tensor_tensor", sel.rearrange("p (t e) -> p t e", e=16),
                    s_sb.rearrange("p (t e) -> p t e", e=16), brt[:].unsqueeze(1).to_broadcast([128, NT, 16]),
                    ALU.add, R=Rr + [Bbrt], W=Wr)
                sel4 = sel.rearrange("p (g e) -> p g e", e=4)
                t6v = t6.rearrange("p (g e) -> p g e", e=6)
                P.v("dve", "tensor_tensor", t6v[:, :, 0:3], sel4[:, :, 0:3], sel4[:, :, 1:4], ALU.add, R=Rr, W=Wr)
                P.v("dve", "tensor_tensor", t6v[:, :, 3:5], sel4[:, :, 0:2], sel4[:, :, 2:4], ALU.add, R=Rr, W=Wr)
                P.v("dve", "tensor_tensor", t6v[:, :, 5:6], sel4[:, :, 0:1], sel4[:, :, 3:4], ALU.add, R=Rr, W=Wr)
                P.v("dve", "tensor_reduce", gs, t6v, AX.X, ALU.max, R=Rr, W=Wr)
                gs3 = gs.rearrange("p (t g) -> p t g", g=4)
                P.v("dve", "tensor_reduce", gm, gs3, AX.X, ALU.max, R=Rr, W=Wr)
                P.v("dve", "tensor_tensor", gmask.rearrange("p (t g) -> p t g", g=4), gs3,
                    gm.unsqueeze(2).to_broadcast([128, NT, 4]), ALU.is_equal, R=Rr, W=Wr)
                P.v("dve", "scalar_tensor_tensor", msel.rearrange("p (g e) -> p g e", e=4), sel4, 2.0,
                    gmask.unsqueeze(2).to_broadcast([128, 48, 4]), op0=ALU.add, op1=ALU.mult, R=Rr, W=Wr)
                top83 = top8.rearrange("p (t e) -> p t e", e=8)
                for t in range(NT):
                    P.v("dve", "max", top83[:, t, :], msel[:, t * 16:(t + 1) * 16], R=Rr, W=Wr)
                P.v("dve", "tensor_tensor", emask.rearrange("p (t e) -> p t e", e=16),
                    msel.rearrange("p (t e) -> p t e", e=16), top83[:, :, 1:2].to_broadcast([128, NT, 16]),
                    ALU.is_ge, R=Rr, W=Wr)
                P.v("dve", "tensor_tensor", wsel, s_sb, emask, ALU.mult, R=Rr, W=Wr)
                wsel3 = wsel.rearrange("p (t e) -> p t e", e=16)
                P.v("dve", "tensor_reduce", wsum, wsel3, AX.X, ALU.add, R=Rr, W=Wr)
                P.v("dve", "reciprocal", wsum, wsum, R=Rr, W=Wr)
                P.v("dve", "tensor_tensor", cmb[:], wsel3, wsum.unsqueeze(2).to_broadcast([128, NT, 16]),
                    ALU.mult, R=Rr, W=Bcmb)

                blocks = [(e, b) for e in range(16) for b in range(3)]
                hT_bufs = [[x for t in range(4 * b, 4 * b + 4) for x in BhT[t]] for b in range(3)]

                def GU(k):
                    e, b = blocks[k]
                    sl = e % 2
                    for fc in range(4):
                        gi = (k * 4 + fc) % 2
                        for kc in range(8):
                            P.mm(G[gi][:], wg[sl][:, kc, fc * 128:(fc + 1) * 128], hT[:, kc, b * 512:(b + 1) * 512],
                                 kc == 0, kc == 7, reads=[Bwg[sl]] + hT_bufs[b], writes=[BG[gi]])
                        for kc in range(8):
                            P.mm(U[gi][:], wu[sl][:, kc, fc * 128:(fc + 1) * 128], hT[:, kc, b * 512:(b + 1) * 512],
                                 kc == 0, kc == 7, reads=[Bwu[sl]] + hT_bufs[b], writes=[BU[gi]])
                        P.v("act", "activation", sg[gi][:], G[gi][:], AF.Silu, R=[BG[gi]], W=[Bsg[gi]])
                        P.v("dve", "tensor_tensor", heT[k % 2][:, fc, :], U[gi][:], sg[gi][:], ALU.mult,
                            R=[BU[gi], Bsg[gi]], W=[Bhe[k % 2]])

                def DNp(k):
                    e, b = blocks[k]
                    sl = e % 2
                    for tt in range(4):
                        t = 4 * b + tt
                        for half in range(2):
                            di = (k * 8 + tt * 2 + half) % 4
                            for fc in range(4):
                                P.mm(Dn[di][:], heT[k % 2][:, fc, tt * 128:(tt + 1) * 128],
                                     wd[sl][:, fc, half * 512:(half + 1) * 512], fc == 0, fc == 3,
                                     reads=[Bhe[k % 2], Bwd[sl]], writes=[BD[di]])
                            dst = yacc[:, t, half * 512:(half + 1) * 512]
                            if e == 0:
                                P.v("dve", "tensor_scalar", dst, Dn[di][:], cmb[:, t, e:e + 1], None, op0=ALU.mult,
                                    R=[BD[di], Bcmb[t]], W=[Byacc[t]])
                            else:
                                P.v("dve", "scalar_tensor_tensor", dst, Dn[di][:], cmb[:, t, e:e + 1], dst,
                                    op0=ALU.mult, op1=ALU.add, R=[BD[di], Bcmb[t], Byacc[t]], W=[Byacc[t]])

                def pre(grp):
                    for t in range(4 * grp, 4 * grp + 4):
                        c = tile_cond(t)
                        ya = yacc[:, t, :]
                        P.v("pool", "tensor_tensor", ya, ya, bc[c][:], ALU.mult, R=[Byacc[t], Bbc[c]], W=[Byacc[t]])
                    for t in range(4 * grp, 4 * grp + 4):
                        ya = yacc[:, t, :]
                        P.v("dve", "scalar_tensor_tensor", ya, x_sb[:, t, :], ALPHA, ya, op0=ALU.mult, op1=ALU.add,
                            R=[Bx[t], Byacc[t]], W=[Byacc[t]])

                def epi(grp, part):
                    tiles = list(range(4 * grp, 4 * grp + 4))
                    ln_group(tiles, [yacc[:, t, :] for t in tiles], [Byacc[t] for t in tiles], bc[2][:], bc[3][:],
                             [Bbc[2], Bbc[3]], next_mod, PBK, "dve", part=part)

                GU(0)
                for k in range(len(blocks)):
                    if k + 1 < len(blocks):
                        GU(k + 1)
                    DNp(k)
                    e_, b_ = blocks[k]
                    if b_ == 2 and e_ + 2 < 16:
                        load_expert(e_ + 2)
                    if e_ == 15:
                        pre(b_)
                        epi(b_, "elem")
                for grp in range(3):
                    if next_mod is not None:
                        epi(grp, "tr")
                    if final:
                        for t in range(4 * grp, 4 * grp + 4):
                            P.dma("sp", y_d[t * 128:(t + 1) * 128, :], x_sb[:, t, :], reads=[Bx[t]])
                P.end_phase()

        moe_phase(0, (1, 1, 0), False)
        if stop == "l0moe":
            store_x("l0moe")
            P.end_phase()
            return nc


        with ExitStack() as pa:
            QT_s = sbuf(pa, "QT_s", [128, 8, 1024], BF16)
            KT_s = sbuf(pa, "KT_s", [128, 8, 1024], BF16)
            Vp_s = sbuf(pa, "Vp_s", [128, 8, 8, 192], BF16)
            O_CTXV, O_M01, O_RBB, O_E, O_ONES, RG_N = 4096, 10240, 12032, 13824, 19200, 19392
            Rg = sbuf(pa, "Rg", [128, RG_N], BF16)
            QT_p = Rg[:, 0:4096].rearrange("p (c t) -> p c t", c=8)
            KT_p = Rg[:, 4096:8192].rearrange("p (c t) -> p c t", c=8)
            Vp_p = Rg[:, 8192:14336].rearrange("p (m a e) -> p m a e", m=4, a=8)
            ctxKT = Rg[:, 0:4096].rearrange("p (c t) -> p c t", c=8)
            ctxVp = Rg[:, O_CTXV:O_M01].rearrange("p (m a e) -> p m a e", m=4, a=8)
            m01 = Rg[:, O_M01:O_RBB].rearrange("p (m q) -> p m q", q=128)
            rbb = Rg[:, O_RBB:O_E].rearrange("p (b d q) -> p b d q", b=2, d=7)
            Etab = Rg[:, O_E:O_ONES].rearrange("p (b d q) -> p b d q", b=3, d=14)
            onesp = Rg[:, O_ONES:RG_N]
            BQs, BKs, BVs = Buf("QTs"), Buf("KTs"), Buf("Vs")
            BQp, BKp, BVp = Buf("QTp"), Buf("KTp"), Buf("Vp")
            Bones = Buf("onesp")

            with ExitStack() as ph:
                wb = [sbuf(ph, "wqkv%d" % i, [128, 8, 512], BF16) for i in range(2)]
                Bwb = [Buf("wb0"), Buf("wb1")]
                stg = [sbuf(ph, "stg%d" % i, [128, 512]) for i in range(2)]
                Bstg = [Buf("stg0"), Buf("stg1")]
                P.v("pool", "memset", Vp_s[:, :, :, 64:128], 0.0, W=[BVs])
                P.v("pool", "memset", Vp_p[:, :, :, 64:128], 0.0, W=[BVp])
                P.v("pool", "memset", onesp[:, 0:64], 1.0, W=[Bones])
                P.v("pool", "memset", onesp[:, 64:128], 0.0, W=[Bones])
                P.v("pool", "memset", onesp[:, 128:192], 1.0, W=[Bones])
                nb = [0]
                ns = [0]

                def bank():
                    i = nb[0] % 8
                    nb[0] += 1
                    return psum[i], PB[i], ("dve" if i % 2 == 0 else "act")

                def evac(eng, dst, src, Bsrc, W, mul=None):
                    if eng == "dve":
                        if mul is None:
                            P.v("dve", "tensor_copy", dst, src, R=[Bsrc], W=W)
                        else:
                            P.v("dve", "tensor_scalar_mul", dst, src, mul, R=[Bsrc], W=W)
                    else:
                        if mul is None:
                            P.v("act", "copy", dst, src, R=[Bsrc], W=W)
                        else:
                            P.v("act", "mul", dst, src, mul, R=[Bsrc], W=W)

                hTb = lambda tiles: [x for t in tiles for x in BhT[t]]
                for cb in range(6):
                    sl = cb % 2
                    P.dma("pool", wb[sl][:], w_qkv_d.rearrange("(kc k) n -> k kc n", k=128)[:, :, cb * 512:(cb + 1) * 512],
                          writes=[Bwb[sl]])
                    if cb < 4:
                        isq = cb < 2
                        for i in range(4):
                            a = (cb % 2) * 4 + i
                            for tb in range(3):
                                pb, Bp, eng = bank()
                                for kc in range(8):
                                    P.mm(pb[:], wb[sl][:, kc, i * 128:(i + 1) * 128], hT[:, kc, tb * 512:(tb + 1) * 512],
                                         kc == 0, kc == 7, reads=[Bwb[sl]] + hTb(range(4 * tb, 4 * tb + 4)), writes=[Bp])
                                if tb == 0:
                                    dst = (QT_p if isq else KT_p)[:, a, :]
                                    W = [BQp if isq else BKp]
                                else:
                                    dst = (QT_s if isq else KT_s)[:, a, (tb - 1) * 512:tb * 512]
                                    W = [BQs if isq else BKs]
                                evac(eng, dst, pb[:], Bp, W, mul=(0.125 if isq else None))
                    if cb in (2, 3, 4, 5):
                        isv = cb >= 4
                        hh = cb % 2
                        for t in (range(NT) if isv else range(4)):
                            pb, Bp, eng = bank()
                            for kc in range(8):
                                P.mm(pb[:], hT[:, kc, t * 128:(t + 1) * 128], wb[sl][:, kc, :], kc == 0, kc == 7,
                                     reads=[Bwb[sl]] + hTb([t]), writes=[Bp])
                            if isv:
                                src4 = pb[:].rearrange("p (a two d) -> p a two d", two=2, d=64)
                                for par in range(2):
                                    if t < 4:
                                        evac(eng, Vp_p[:, t, hh * 4:(hh + 1) * 4, par * 128:par * 128 + 64],
                                             src4[:, :, par, :], Bp, [BVp])
                                    else:
                                        evac(eng, Vp_s[:, t - 4, hh * 4:(hh + 1) * 4, par * 128:par * 128 + 64],
                                             src4[:, :, par, :], Bp, [BVs])
                            if t < 4:
                                k = ns[0] % 2
                                ns[0] += 1
                                evac("dve" if eng == "act" else "act", stg[k][:], pb[:], Bp, [Bstg[k]])
                                dd = nv_d if isv else nk_d
                                s_, l0 = t // 2, (t % 2) * 128
                                P.dma("sp", dd[s_, hh * 8:(hh + 1) * 8, l0:l0 + 128, :].rearrange("h l d -> l h d"),
                                      stg[k][:].rearrange("p (h d) -> p h d", d=64), reads=[Bstg[k]])
                P.end_phase()

            with ExitStack() as pb_:
                OT_all = sbuf(pb_, "OT_all", [128, 8, 1024], BF16)
                BOT = [Buf("OT%d" % a) for a in range(8)]
                wo = sbuf(pb_, "wo", [128, 8, D], BF16)
                Bwo = Buf("wo")
                By = [Buf("ay%d" % i) for i in range(4)]
                ST = [PBK[i] for i in range(4)]
                ACC = [(PBK[4], PBK[5]), (PBK[6], PBK[7])]
                cn = {"st": 0, "pt": 0, "em": 0}

                def heads_scope(scope):
                    PT = [sbuf(scope, "PT%d" % i, [128, 512], BF16) for i in range(4)]
                    BPT = [Buf("PT%d" % i) for i in range(4)]
                    rS = sbuf(scope, "rS", [128, 512])
                    BrS = Buf("rS")
                    return PT, BPT, rS, BrS

                LOOKAHEAD = 2
                pend = []

                def drain(keep):
                    while len(pend) > keep:
                        pend.pop(0)()

                def segment(PT, BPT, Kst, Qmv, nq, Bk, Bq, esl, BE_, Vblk, BV_, ones_blk, acc, c0, hold=False):
                    st, Bst = ST[cn["st"] % 4]
                    cn["st"] += 1
                    k = cn["pt"] % 4
                    cn["pt"] += 1
                    P.mm(st[:, 0:nq], Kst, Qmv, True, True, reads=[Bk, Bq], writes=[Bst])
                    P.v("act", "activation", PT[k][:, 0:nq], st[:, 0:nq], AF.Exp, R=[Bst], W=[BPT[k]])
                    if esl is not None:
                        P.v("dve", "tensor_tensor", PT[k][:, 0:nq], PT[k][:, 0:nq], esl, ALU.mult,
                            R=[BPT[k], BE_], W=[BPT[k]])
                    (pO, BO_), (pS, BS_) = acc

                    def back():
                        P.mm(pO[:, c0:c0 + nq], Vblk, PT[k][:, 0:nq], False, False, reads=[BPT[k], BV_], writes=[BO_],
                             skip=True)
                        P.mm(pS[:, c0:c0 + nq], ones_blk, PT[k][:, 0:nq], False, False, reads=[BPT[k], Bones],
                             writes=[BS_], skip=True)
                    pend.append(back)
                    if not hold:
                        drain(LOOKAHEAD)

                def zero_acc(acc, ncols):
                    (pO, BO_), (pS, BS_) = acc
                    P.v("dve", "memset", pO[:, 0:ncols], 0.0, W=[BO_])
                    P.v("dve", "memset", pS[:, 0:ncols], 0.0, W=[BS_])

                def normalize(acc, ncols, rS, BrS, dst, Wd):
                    (pO, BO_), (pS, BS_) = acc

                    def fin():
                        P.v("dve", "reciprocal", rS[:, 0:ncols], pS[:, 0:ncols], R=[BS_], W=[BrS])
                        P.v("dve", "tensor_tensor", dst, pO[:, 0:ncols], rS[:, 0:ncols], ALU.mult, R=[BO_, BrS], W=Wd)
                    pend.append(fin)

                def proj_group(tiles, tcols, ybufs, deadW, bcv, Bbcv):
                    for i, t in enumerate(tiles):
                        tc = tcols[i]
                        for half in range(2):
                            pb, Bp = PBK[2 * (i % 2) + half]
                            for kc in range(8):
                                P.mm(pb[:], OT_all[:, kc, tc:tc + 128], wo[:, kc, half * 512:(half + 1) * 512],
                                     kc == 0, kc == 7, reads=[BOT[kc], Bwo], writes=[Bp])
                            P.v("dve", "tensor_tensor", ybufs[i][:, half * 512:(half + 1) * 512], pb[:],
                                bcv[0][:, half * 512:(half + 1) * 512], ALU.mult, R=[Bp, Bbcv[0]],
                                W=[By[i]] + list(deadW))
                        P.v("dve", "scalar_tensor_tensor", ybufs[i], x_sb[:, t, :], ALPHA, ybufs[i],
                            op0=ALU.mult, op1=ALU.add, R=[Bx[t], By[i]], W=[By[i]])
                    ln_group(tiles, ybufs, By, bcv[1], bcv[2], [Bbcv[1], Bbcv[2]], (1, 4, 3), PBK[4:8], "pool")

                def prep_vectors(cidx, bcv, Bbcv, dgall_v, Bdg_v, deadW):
                    load_bc(bcv[1], ln1g_d[1:2, :], [Bbcv[1]] + list(deadW))
                    load_bc(bcv[2], ln1b_d[1:2, :], [Bbcv[2]] + list(deadW))
                    make_bc(bcv[0], 1, 2, cidx, dgall_v, Bdg_v, PBK[6:8], [Bbcv[0]] + list(deadW))

                with ExitStack() as ph:
                    PT, BPT, rS, BrS = heads_scope(ph)
                    P.dma("pool", wo[:], w_o_d.rearrange("(kc k) n -> k kc n", k=128), writes=[Bwo])
                    for a in range(8):
                        for s_ in range(2):
                            acc = ACC[(2 * a + s_) % 2]
                            zero_acc(acc, 256)
                            for c in range(2):
                                for rho in range(2):
                                    pr = slice(64 * rho, 64 * rho + 64)
                                    vc = slice(64 * rho, 64 * rho + 128)
                                    k0 = s_ * 256 + c * 128
                                    segment(PT, BPT, KT_p[pr, a, k0:k0 + 128], QT_p[pr, a, s_ * 256:(s_ + 1) * 256], 256,
                                            BKp, BQp, None, None, Vp_p[:, 2 * s_ + c, a, vc], BVp, onesp[:, vc], acc, 0, hold=(rho == 0))
                            normalize(acc, 256, rS, BrS, OT_all[:, a, s_ * 256:(s_ + 1) * 256], [BOT[a]])
                    drain(0)
                    P.end_phase()

                yb_p = [Rg[:, 2048 * i:2048 * (i + 1)].bitcast(F32) for i in range(4)]
                bc_p = [Rg[:, 8192 + 2048 * i:8192 + 2048 * (i + 1)].bitcast(F32) for i in range(3)]
                dg_p = Rg[:, 14336:16384].bitcast(F32).rearrange("p (a b) -> p a b", b=128)
                Bbc_p = [Buf("pbc%d" % i) for i in range(3)]
                prep_vectors(0, bc_p, Bbc_p, dg_p, Buf("pdg"), [])
                proj_group([0, 1, 2, 3], [0, 128, 256, 384], yb_p, [], bc_p, Bbc_p)
                if stop == "l1mixp":
                    store_x("l1mixp")
                P.end_phase()
                if stop == "l1mixp":
                    return nc

                with ExitStack() as ph:
                    PT, BPT, rS, BrS = heads_scope(ph)
                    ck_tok = OT_all[:, 0:4, :].rearrange("p c (h d) -> p c h d", d=64)
                    Bck = Buf("ck_tok")
                    BctxK, BctxV, Bm01 = Buf("ctxK"), Buf("ctxV"), Buf("m01")
                    Brbb = [Buf("rbb0"), Buf("rbb1")]
                    BE = [Buf("E0"), Buf("E1"), Buf("E2")]
                    for c in range(4):
                        P.dma("pool", ck_tok[:, c, :, :], ck_d[:, c * 128:(c + 1) * 128, :].rearrange("h l d -> l h d"),
                              writes=[Bck])
                    P.dma("pool", m01[:], m01_d.rearrange("m k q -> k m q"), writes=[Bm01])
                    P.v("pool", "memset", ctxVp[:, :, :, 64:128], 0.0, W=[BctxV])
                    cv4 = cv_d.rearrange("(a two) l d -> two l a d", two=2)
                    for c in range(4):
                        for par in range(2):
                            P.dma("pool", ctxVp[:, c, :, par * 128:par * 128 + 64], cv4[par, c * 128:(c + 1) * 128, :, :],
                                  writes=[BctxV])

                    def build_E(h):
                        hb, eb = h % 2, h % 3
                        P.dma("pool", rbb[:, hb, :, :], rbt_d[h].rearrange("d k q -> k d q"), writes=[Brbb[hb]])
                        P.v("act", "activation", rbb[:, hb, :, :], rbb[:, hb, :, :], AF.Exp, R=[Brbb[hb]], W=[Brbb[hb]])
                        for s2 in range(2):
                            P.v("dve", "tensor_tensor", Etab[:, eb, s2 * 7:(s2 + 1) * 7, :], rbb[:, hb, :, :],
                                m01[:, s2 * 7:(s2 + 1) * 7, :], ALU.mult, R=[Brbb[hb], Bm01], W=[BE[eb]])

                    for c in range(4):
                        pTb = psum[c % 2][:].bitcast(BF16)
                        for a in range(8):
                            P.tr(pTb[:, a * 128:(a + 1) * 128], ck_tok[:, c, 2 * a:2 * a + 2, :].rearrange("p h d -> p (h d)"),
                                 ident_b[:], reads=[Bck, Bconst], writes=[PB[c % 2]])
                        P.v("dve" if c % 2 == 0 else "act", "tensor_copy" if c % 2 == 0 else "copy",
                            ctxKT[:, :, c * 128:(c + 1) * 128], pTb.rearrange("p (a t) -> p a t", a=8),
                            R=[PB[c % 2]], W=[BctxK])
                    build_E(0)
                    build_E(1)
                    build_E(2)
                    for a in range(8):
                        for qb in range(2):
                            acc = ACC[(2 * a + qb) % 2]
                            zero_acc(acc, 512)
                            for (m, jlo, cnt, e2) in _SEGS[qb]:
                                for rho in range(2):
                                    eb = (2 * a + rho) % 3
                                    pr = slice(64 * rho, 64 * rho + 64)
                                    vc = slice(64 * rho, 64 * rho + 128)
                                    nq, q0 = cnt * 128, jlo * 128
                                    i0 = e2 * 7 + 3 - m + jlo
                                    esl = Etab[:, eb, i0:i0 + cnt, :].rearrange("p a b -> p (a b)")
                                    segment(PT, BPT, KT_s[pr, a, m * 128:(m + 1) * 128], QT_s[pr, a, q0:q0 + nq], nq,
                                            BKs, BQs, esl, BE[eb], Vp_s[:, m, a, vc], BVs, onesp[:, vc], acc, q0 - 512 * qb, hold=(rho == 0))
                            for c in range(4):
                                for rho in range(2):
                                    pr = slice(64 * rho, 64 * rho + 64)
                                    vc = slice(64 * rho, 64 * rho + 128)
                                    segment(PT, BPT, ctxKT[pr, a, c * 128:(c + 1) * 128], QT_s[pr, a, qb * 512:(qb + 1) * 512],
                                            512, BctxK, BQs, None, None, ctxVp[:, c, a, vc], BctxV, onesp[:, vc], acc, 0, hold=(rho == 0))
                            normalize(acc, 512, rS, BrS, OT_all[:, a, qb * 512:(qb + 1) * 512], [BOT[a], Bck])
                        for h2 in (2 * a + 3, 2 * a + 4):
                            if h2 < 16:
                                build_E(h2)
                    drain(0)
                    P.end_phase()

                yb_s = [QT_s[:, 2 * i:2 * i + 2, :].rearrange("p a t -> p (a t)").bitcast(F32) for i in range(4)]
                bc_s = [KT_s[:, 2 * i:2 * i + 2, :].rearrange("p a t -> p (a t)").bitcast(F32) for i in range(3)]
                dg_s = KT_s[:, 6:8, :].rearrange("p a t -> p (a t)").bitcast(F32).rearrange("p (a b) -> p a b", b=128)
                Bbc_s = [Buf("sbc%d" % i) for i in range(3)]
                prep_vectors(1, bc_s, Bbc_s, dg_s, Buf("sdg"), [])
                proj_group([4, 5, 6, 7], [0, 128, 256, 384], yb_s, [], bc_s, Bbc_s)
                proj_group([8, 9, 10, 11], [512, 640, 768, 896], yb_s, [], bc_s, Bbc_s)
                if stop == "l1mix":
                    store_x("l1mix")
                P.end_phase()
        if stop == "l1mix":
            return nc

        moe_phase(1, None, True)
        return nc


_CACHE = {}


def make_in_maps(inputs):
    f = lambda a: np.ascontiguousarray(np.asarray(a, dtype=np.float32))
    x_prompt, x_sample = f(inputs["x_prompt"]), f(inputs["x_sample"])
    cache_k, cache_v = f(inputs["cache_k"]), f(inputs["cache_v"])
    c, c_ctx = f(inputs["c"]), f(inputs["c_ctx"])
    b_ada = f(inputs["b_ada"])
    shared = {
        "w_ada": f(inputs["w_ada"]),
        "b_adaT": np.ascontiguousarray(b_ada.reshape(2, 48, 128).transpose(0, 2, 1)),
        "ln1_g": f(inputs["ln1_g"]), "ln1_b": f(inputs["ln1_b"]),
        "ln2_g": f(inputs["ln2_g"]), "ln2_b": f(inputs["ln2_b"]),
        "pool_w": f(inputs["pool_w"])[0],
        "pool_scale": f(inputs["pool_scale"]).reshape(1, D),
        "w_qkv": f(inputs["w_qkv"])[0],
        "w_o": f(inputs["w_o"])[0],
        "rbt": _rel_bias_layout(f(inputs["rel_bias"])[0]),
        "w_router": f(inputs["w_router"]),
        "b_router": f(inputs["b_router"]).reshape(1, 16),
        "w_gate": f(inputs["w_gate"]), "w_up": f(inputs["w_up"]), "w_down": f(inputs["w_down"]),
        "ptab": _pool_tables(),
        "m01": _mask01_table(),
    }
    in_maps = []
    for i in range(N_CORES):
        m = dict(shared)
        m["xin"] = np.ascontiguousarray(np.concatenate(
            [x_prompt[2 * i].reshape(256, D), x_prompt[2 * i + 1].reshape(256, D), x_sample[i]], axis=0))
        m["ck"] = np.ascontiguousarray(cache_k[i, 0])
        m["cv"] = np.ascontiguousarray(cache_v[i, 0])
        cond = np.stack([c_ctx, c[i]], axis=-1)
        m["condT"] = np.ascontiguousarray(cond.reshape(8, 128, 2).transpose(1, 0, 2))
        in_maps.append(m)
    return in_maps


def kernel(**inputs):
    if "nc" not in _CACHE:
        _CACHE["nc"] = build_nc()
    nc = _CACHE["nc"]
    in_maps = make_in_maps(inputs)
    res = run_bass_kernel_spmd(nc, in_maps, core_ids=list(range(N_CORES)))
    ys = [r["y"] for r in res.results]
    y_prompt = np.stack([y[:512].reshape(2, 256, D) for y in ys]).reshape(16, 256, D)
    y_sample = np.stack([y[512:] for y in ys])
    nk = np.stack([r["nk"] for r in res.results]).reshape(16, 1, 16, 256, 64)
    nv = np.stack([r["nv"] for r in res.results]).reshape(16, 1, 16, 256, 64)
    return (y_prompt.astype(np.float32), y_sample.astype(np.float32),
            nk.astype(np.float32), nv.astype(np.float32))
```

```python
from contextlib import ExitStack

import numpy as np
import concourse.bass as bass
import concourse.mybir as mybir
from concourse.bass_utils import run_bass_kernel_spmd

F32 = mybir.dt.float32
BF16 = mybir.dt.bfloat16
AF = mybir.ActivationFunctionType
ALU = mybir.AluOpType
AX = mybir.AxisListType

N_CORES = 8
D = 1024
NT = 12
T = NT * 128
ALPHA = 4.0 ** 0.25
LN_EPS = 1e-5
NEG = -30000.0
POOL_SIZES = (2, 4, 8, 16)
SEQS = ((0, 2), (2, 2), (4, 8))


def tile_cond(t):
    return 0 if t < 4 else 1


class Buf:
    __slots__ = ("name", "writer", "readers", "excl")

    def __init__(self, name="", excl=False):
        self.name = name
        self.writer = None
        self.readers = []
        self.excl = excl


class Lane:
    __slots__ = ("sem", "n", "last")

    def __init__(self, sem):
        self.sem = sem
        self.n = 0
        self.last = None


class Ins:
    __slots__ = ("eng", "fn", "deps", "signal", "count", "lane", "lval", "idx")

    def __init__(self, eng, fn):
        self.eng = eng
        self.fn = fn
        self.deps = []
        self.signal = False
        self.count = 0
        self.lane = None
        self.lval = 0
        self.idx = 0


class Prog:
    ENG = ("pe", "act", "dve", "pool", "sp")
    NLANES = {"sp": 24, "pool": 12, "act": 4}

    def __init__(self, nc, es):
        self.nc = nc
        self.es = es
        self.eng = {"pe": nc.tensor, "act": nc.scalar, "dve": nc.vector,
                    "pool": nc.gpsimd, "sp": nc.sync}
        self.sem = {e: es.enter_context(nc.semaphore("s_" + e)) for e in self.ENG}
        self.cnt = {e: 0 for e in self.ENG}
        self.bar = es.enter_context(nc.semaphore("s_bar"))
        self.nbar = 0
        self.lanes = {q: [Lane(es.enter_context(nc.semaphore("l_%s%d" % (q, i)))) for i in range(n)]
                      for q, n in self.NLANES.items()}
        self.lane_rr = {q: 0 for q in self.NLANES}
        self.waited = {e: {} for e in self.ENG}
        self.streams = {e: [] for e in self.ENG}
        self.touched = []
        self.n_ins = 0

    def _add_dep(self, ins, p):
        if p is None or p is ins:
            return
        if p.lane is None and ins.lane is None and p.eng == ins.eng and p.eng == "pe":
            return
        deps = ins.deps
        if p.lane is None:
            for i, q in enumerate(deps):
                if q.lane is None and q.eng == p.eng:
                    if p.idx > q.idx:
                        deps[i] = p
                    return
        elif p in deps:
            return
        deps.append(p)

    def _deps(self, ins, reads, writes):
        if any(b.excl for b in reads):
            writes = list(writes) + [b for b in reads if b.excl and b not in writes]
            reads = [b for b in reads if not b.excl]
        same = lambda p: (p is not None and p.lane is None and ins.lane is None and p.eng == ins.eng)
        for b in reads:
            self._add_dep(ins, b.writer)
        for b in writes:
            if not same(b.writer):
                self._add_dep(ins, b.writer)
            for r in b.readers:
                if not same(r):
                    self._add_dep(ins, r)
        for b in reads:
            if not b.readers and b.writer is None:
                self.touched.append(b)
            b.readers.append(ins)
        for b in writes:
            if not b.readers and b.writer is None:
                self.touched.append(b)
            b.writer = ins
            b.readers = []

    def op(self, eng, fn, reads=(), writes=()):
        ins = Ins(eng, fn)
        ins.idx = len(self.streams[eng])
        self._deps(ins, reads, writes)
        self.streams[eng].append(ins)
        return ins

    def dma(self, eng, out, in_, reads=(), writes=(), **kw):
        E = self.eng[eng]
        ins = Ins(eng, lambda: E.dma_start(out=out, in_=in_, **kw))
        ins.idx = len(self.streams[eng])
        lanes = self.lanes[eng]
        lane = lanes[self.lane_rr[eng] % len(lanes)]
        self.lane_rr[eng] += 1
        ins.lane = lane
        lane.n += 1
        ins.lval = 16 * lane.n
        if lane.last is not None:
            ins.deps.append(lane.last)
        lane.last = ins
        self._deps(ins, reads, writes)
        self.streams[eng].append(ins)
        return ins

    def mm(self, out, lhsT, rhs, start, stop, reads=(), writes=(), skip=False):
        t = self.nc.tensor
        if skip:
            return self.op("pe", lambda: t.matmul(out, lhsT, rhs, start=start, stop=stop, skip_group_check=True),
                           reads, writes)
        return self.op("pe", lambda: t.matmul(out, lhsT, rhs, start=start, stop=stop), reads, writes)

    def tr(self, out, in_, ident, reads=(), writes=()):
        t = self.nc.tensor
        return self.op("pe", lambda: t.transpose(out, in_, ident), reads, writes)

    def v(self, eng, meth, *args, R=(), W=(), **kw):
        E = self.eng[eng]
        return self.op(eng, lambda: getattr(E, meth)(*args, **kw), R, W)

    def end_phase(self):
        ENG = self.ENG
        for e in ENG:
            for ins in self.streams[e]:
                for p in ins.deps:
                    if p.lane is None:
                        p.signal = True
            for ins in reversed(self.streams[e]):
                if ins.lane is None:
                    ins.signal = True
                    break
        for e in ENG:
            for ins in self.streams[e]:
                if ins.lane is None and ins.signal:
                    self.cnt[e] += 1
                    ins.count = self.cnt[e]
        for e in ENG:
            E = self.eng[e]
            wd = self.waited[e]
            for ins in self.streams[e]:
                for p in ins.deps:
                    if p.lane is not None:
                        sem, val = p.lane.sem, p.lval
                    else:
                        sem, val = self.sem[p.eng], p.count
                    if wd.get(sem.num, 0) >= val:
                        continue
                    wd[sem.num] = val
                    E.wait_ge(sem, val)
                bi = ins.fn()
                if ins.lane is not None:
                    bi.then_inc(ins.lane.sem, 16)
                elif ins.signal:
                    bi.then_inc(self.sem[e], 1)
                self.n_ins += 1
        sp = self.eng["sp"]
        wd = self.waited["sp"]
        all_lanes = [l for q in self.lanes.values() for l in q]
        for e in ENG:
            if e == "sp" or self.cnt[e] == 0:
                continue
            if wd.get(self.sem[e].num, 0) < self.cnt[e]:
                sp.wait_ge(self.sem[e], self.cnt[e])
        for l in all_lanes:
            if l.n and wd.get(l.sem.num, 0) < 16 * l.n:
                sp.wait_ge(l.sem, 16 * l.n)
        self.nbar += 1
        sp.sem_inc(self.bar, 1)
        for e in ENG:
            if e != "sp":
                self.eng[e].wait_ge(self.bar, self.nbar)
            for e2 in ENG:
                self.waited[e][self.sem[e2].num] = self.cnt[e2]
            for l in all_lanes:
                self.waited[e][l.sem.num] = 16 * l.n
        for l in all_lanes:
            l.last = None
        for b in self.touched:
            b.writer = None
            b.readers = []
        self.touched = []
        self.streams = {e: [] for e in ENG}


def _pool_tables():
    n = 1024
    out = np.zeros((4, 5, 128, 128), np.float32)
    t = np.arange(n)
    for wi, w in enumerate(POOL_SIZES):
        lo = np.clip(t - w // 2, 0, n)
        hi = np.clip(t - w // 2 + w, 0, n)
        cnt = (hi - lo).astype(np.float64)
        s = np.arange(n)[:, None]
        Pm = ((s >= lo[None, :]) & (s < hi[None, :])) / cnt[None, :] - np.eye(n)
        Pm = Pm.astype(np.float32)
        out[wi, 0] = Pm[0:128, 0:128]
        out[wi, 1] = Pm[128:256, 128:256]
        out[wi, 2] = Pm[896:1024, 896:1024]
        out[wi, 3] = Pm[0:128, 128:256]
        out[wi, 4] = Pm[256:384, 128:256]
    return out


def _chunks_for(j):
    if j < 2:
        return list(range(0, 4))
    if j > 5:
        return list(range(4, 8))
    return list(range(j - 2, j + 3))


def _mask_tables():
    tiles = []
    index = {}
    keymap = {}
    col = np.arange(64)
    cs = np.clip(col - 8, 0, 48)
    colok = (col[None, :] >= cs[:, None]) & (col[None, :] < cs[:, None] + 16)
    for j in range(8):
        for m in _chunks_for(j):
            mt = np.full((2, 64, 2, 64), NEG, np.float32)
            for rq in range(2):
                r = 2 * j + rq
                rs = min(max(r - 4, 0), 8)
                for rk in range(2):
                    kr = 2 * m + rk
                    if rs <= kr < rs + 8:
                        mt[rk, :, rq, :] = np.where(colok.T, 0.0, NEG)
            key = mt.tobytes()
            if key not in keymap:
                keymap[key] = len(tiles)
                tiles.append(mt.reshape(128, 128))
            index[(j, m)] = keymap[key]
    return np.stack(tiles), index


_MASKS, _MASK_IDX = _mask_tables()
_M_FULL, _M_A, _M_B = _MASK_IDX[(0, 0)], _MASK_IDX[(2, 0)], _MASK_IDX[(2, 4)]


def _mask01_table():
    full = (_MASKS[_M_FULL] == 0).astype(np.float32)
    t = np.stack([full] * 14)
    t[7 + (3 - 2)] = (_MASKS[_M_B] == 0)
    t[7 + (3 + 2)] = (_MASKS[_M_A] == 0)
    return t


def _segments():
    segs = {0: [], 1: []}
    for m in range(8):
        for qb in range(2):
            js = [j for j in range(8) if m in _chunks_for(j) and j // 4 == qb]
            if not js:
                continue
            assert js == list(range(js[0], js[-1] + 1))
            e2 = any(_MASK_IDX[(j, m)] != _M_FULL for j in js)
            for j in js:
                want = _M_FULL
                if e2 and m - j == -2:
                    want = _M_A
                if e2 and m - j == 2:
                    want = _M_B
                assert _MASK_IDX[(j, m)] == want, (j, m)
            segs[qb].append((m, js[0], len(js), int(e2)))
    return segs


_SEGS = _segments()


def _rel_bias_layout(rel_bias):
    rk = np.arange(2)[:, None, None, None]
    ck = np.arange(64)[None, :, None, None]
    rq = np.arange(2)[None, None, :, None]
    cq = np.arange(64)[None, None, None, :]
    out = np.empty((16, 7, 128, 128), np.float32)
    for i in range(7):
        dl = 3 - i
        dy = np.clip(2 * dl + rk - rq + 7, 0, 14)
        dx = np.clip(ck - cq + 15, 0, 30)
        dy, dx = np.broadcast_arrays(dy, dx)
        out[:, i] = rel_bias[:, dy, dx].reshape(16, 128, 128)
    return out


def build_nc(stop=None):
    nc = bass.Bass("TRN2", target_bir_lowering=False)

    def din(name, shape):
        return nc.dram_tensor(name, list(shape), F32, kind="ExternalInput").ap()

    def dout(name, shape):
        return nc.dram_tensor(name, list(shape), F32, kind="ExternalOutput").ap()

    xin = din("xin", [T, D])
    ck_d = din("ck", [16, 512, 64])
    cv_d = din("cv", [16, 512, 64])
    condT_d = din("condT", [128, 8, 2])
    w_ada_d = din("w_ada", [2, D, 6 * D])
    b_adaT_d = din("b_adaT", [2, 128, 48])
    ln1g_d = din("ln1_g", [2, D])
    ln1b_d = din("ln1_b", [2, D])
    ln2g_d = din("ln2_g", [2, D])
    ln2b_d = din("ln2_b", [2, D])
    pool_w_d = din("pool_w", [4, 256, 256])
    pool_scale_d = din("pool_scale", [1, D])
    w_qkv_d = din("w_qkv", [D, 3 * D])
    w_o_d = din("w_o", [D, D])
    rbt_d = din("rbt", [16, 7, 128, 128])
    w_router_d = din("w_router", [D, 16])
    b_router_d = din("b_router", [1, 16])
    w_gate_d = din("w_gate", [2, 16, D, 512])
    w_up_d = din("w_up", [2, 16, D, 512])
    w_down_d = din("w_down", [2, 16, 512, D])
    ptab_d = din("ptab", [4, 5, 128, 128])
    m01_d = din("m01", [14, 128, 128])

    y_d = dout("y", [T, D])
    nk_d = dout("nk", [2, 16, 256, 64])
    nv_d = dout("nv", [2, 16, 256, 64])
    dbg_d = dout("dbg", [128, 2, 48, 2]) if stop == "ada" else None

    with ExitStack() as es:
        P = Prog(nc, es)

        _uid = [0]

        def sbuf(scope, name, shape, dt=F32):
            _uid[0] += 1
            return scope.enter_context(nc.sbuf_tensor("sb%d_%s" % (_uid[0], name), list(shape), dt))

        x_sb = sbuf(es, "x_sb", [128, NT, D])
        hT = sbuf(es, "hT", [128, 8, T], BF16)
        ident_f = sbuf(es, "ident_f", [128, 128])
        ident_b = sbuf(es, "ident_b", [128, 128], BF16)
        ones_f = sbuf(es, "ones_f", [128, 128])
        modcol = sbuf(es, "modcol", [128, 2, 48, 2])
        cmb = sbuf(es, "cmb", [128, NT, 16])
        small = sbuf(es, "small", [128, 64])
        psum = [es.enter_context(nc.psum_tensor("ps%d" % i, [128, 512], F32)) for i in range(8)]
        PB = [Buf("ps%d" % i, excl=True) for i in range(8)]

        Bx = [Buf("x%d" % t) for t in range(NT)]
        BhT = [(Buf("hTd%d" % t), Buf("hTa%d" % t)) for t in range(NT)]
        Bconst = Buf("const")
        Bmod = [Buf("modcol0"), Buf("modcol1")]
        sTb = sbuf(es, "sTb", [128, 8, 2], BF16)
        b_col = sbuf(es, "b_col", [128, 2, 48])
        BsT = Buf("sT")
        Bbcol = Buf("bcol")
        Bcmb = [Buf("cmb%d" % t) for t in range(NT)]


        def alpha_col(layer, v, ch, c):
            return modcol[:, layer, v * 8 + ch, c:c + 1]

        PBK = [(psum[i], PB[i]) for i in range(8)]

        def ada_block(layer, nb, wa_t, Bwa_t, bank):
            pc, Bpc = bank
            src = w_ada_d[layer].rearrange("(kc k) n -> k kc n", k=128)[:, :, nb * 512:(nb + 1) * 512]
            P.dma("pool", wa_t[:], src, writes=[Bwa_t])
            for n4 in range(4):
                for kc in range(8):
                    P.mm(pc[:, n4 * 2:n4 * 2 + 2], wa_t[:, kc, n4 * 128:(n4 + 1) * 128], sTb[:, kc, :],
                         kc == 0, kc == 7, reads=[Bwa_t, BsT], writes=[Bpc])
            dst = modcol[:, layer, nb * 4:(nb + 1) * 4, :]
            P.v("dve", "tensor_tensor", dst, pc[:, 0:8].rearrange("p (c k) -> p c k", k=2),
                b_col[:, layer, nb * 4:(nb + 1) * 4].unsqueeze(2).to_broadcast([128, 4, 2]), ALU.add,
                R=[Bpc, Bbcol], W=[Bmod[layer]])
            if nb // 2 in (1, 4):
                P.v("dve", "tensor_scalar_add", dst, dst, 1.0, R=[Bmod[layer]], W=[Bmod[layer]])

        with ExitStack() as ph:
            condT = sbuf(ph, "condT", [128, 8, 2])
            sT = sbuf(ph, "sT", [128, 8, 2])
            wa = [sbuf(ph, "wa%d" % i, [128, 8, 512], BF16) for i in range(3)]
            Bwa = [Buf("wa%d" % i) for i in range(3)]

            P.v("pool", "memset", ident_f[:], 0.0, W=[Bconst])
            P.v("pool", "affine_select", ident_f[:], ident_f[:], [[-1, 128]], ALU.not_equal, 1.0,
                base=0, channel_multiplier=1, R=[Bconst], W=[Bconst])
            P.v("pool", "tensor_copy", ident_b[:], ident_f[:], R=[Bconst], W=[Bconst])
            P.v("pool", "memset", ones_f[:], 1.0, W=[Bconst])

            for t in range(NT):
                P.dma("sp", x_sb[:, t, :], xin[t * 128:(t + 1) * 128, :], writes=[Bx[t]])
            P.dma("sp", condT[:], condT_d[:, :, :], writes=[BsT])
            P.dma("sp", b_col[:], b_adaT_d.rearrange("l p c -> p l c"), writes=[Bbcol])
            P.v("act", "activation", sT[:], condT[:], AF.Silu, R=[BsT], W=[BsT])
            P.v("dve", "tensor_copy", sTb[:], sT[:], R=[BsT], W=[BsT])

            for nb in range(12):
                ada_block(0, nb, wa[nb % 3], Bwa[nb % 3], PBK[nb % 2])
            if stop == "ada":
                for nb in range(12):
                    ada_block(1, nb, wa[nb % 3], Bwa[nb % 3], PBK[nb % 2])
            if stop == "ada":
                P.dma("sp", dbg_d[:, :, :, :], modcol[:], reads=Bmod)
            P.end_phase()

        def load_bc(dst, src_row, W):
            P.dma("sp", dst, src_row.partition_broadcast(128), writes=W)

        def make_bc(dst, layer, v, c, dgall, Bdg, banks, W):
            P.v("dve", "tensor_tensor", dgall[:], ident_f[:].unsqueeze(1).to_broadcast([128, 8, 128]),
                modcol[:, layer, v * 8:(v + 1) * 8, c:c + 1].to_broadcast([128, 8, 128]), ALU.mult,
                R=[Bconst, Bmod[layer]], W=[Bdg])
            for half in range(2):
                pb, Bp = banks[half]
                P.mm(pb[:], ones_f[:], dgall[:, half * 4:(half + 1) * 4, :].rearrange("p a b -> p (a b)"),
                     True, True, reads=[Bconst, Bdg], writes=[Bp])
                P.v("act", "copy", dst[:, half * 512:(half + 1) * 512], pb[:], R=[Bp], W=W)

        Bsmall = [Buf("mv%d" % i) for i in range(4)]

        def ln_group(tiles, ys, Bys, g_bc, b_bc, Bvec, next_mod, banks, gmul_eng, part="all"):
            n = len(tiles)
            sts = [small[:, 16 * i:16 * i + 12].rearrange("p (a b) -> p a b", b=6) for i in range(n)]
            mvv = [small[:, 16 * i + 12:16 * i + 16] for i in range(n)]
            if part != "tr":
                ln_elem(tiles, ys, Bys, g_bc, b_bc, Bvec, gmul_eng, n, sts, mvv)
            if next_mod is None or part == "elem":
                return
            ln_tr(tiles, next_mod, banks)

        def ln_elem(tiles, ys, Bys, g_bc, b_bc, Bvec, gmul_eng, n, sts, mvv):
            for i in range(n):
                P.v("dve", "bn_stats", sts[i][:, 0, :], ys[i][:, 0:512], R=[Bys[i]], W=[Bsmall[i]])
                P.v("dve", "bn_stats", sts[i][:, 1, :], ys[i][:, 512:1024], R=[Bys[i]], W=[Bsmall[i]])
                P.v("dve", "bn_aggr", mvv[i][:, 0:2], sts[i], R=[Bsmall[i]], W=[Bsmall[i]])
            for i in range(n):
                P.v("act", "activation", mvv[i][:, 2:3], mvv[i][:, 1:2], AF.Sqrt, bias=LN_EPS,
                    R=[Bsmall[i]], W=[Bsmall[i]])
            for i in range(n):
                P.v("dve", "reciprocal", mvv[i][:, 2:3], mvv[i][:, 2:3], R=[Bsmall[i]], W=[Bsmall[i]])
                P.v("dve", "tensor_scalar", mvv[i][:, 3:4], mvv[i][:, 0:1], mvv[i][:, 2:3], -1.0,
                    op0=ALU.mult, op1=ALU.mult, R=[Bsmall[i]], W=[Bsmall[i]])
            for i in range(n):
                P.v("act", "activation", ys[i], ys[i], AF.Identity, scale=mvv[i][:, 2:3], bias=mvv[i][:, 3:4],
                    R=[Bys[i], Bsmall[i]], W=[Bys[i]])
            for i in range(n):
                P.v(gmul_eng, "tensor_tensor", ys[i], ys[i], g_bc, ALU.mult, R=[Bys[i]] + list(Bvec), W=[Bys[i]])
            for i, t in enumerate(tiles):
                P.v("pool", "tensor_tensor", x_sb[:, t, :], ys[i], b_bc, ALU.add, R=[Bys[i]] + list(Bvec), W=[Bx[t]])

        def ln_tr(tiles, next_mod, banks):
            layer, vA, vB = next_mod
            nb = len(banks) // 2
            for i, t in enumerate(tiles):
                c = tile_cond(t)
                pair = banks[2 * (i % nb):2 * (i % nb) + 2]
                for half in range(2):
                    pb, Bp = pair[half]
                    for q in range(4):
                        kc = half * 4 + q
                        P.tr(pb[:, q * 128:(q + 1) * 128], x_sb[:, t, kc * 128:(kc + 1) * 128], ident_f[:],
                             reads=[Bx[t], Bconst], writes=[Bp])
                for half in range(2):
                    pb, Bp = pair[half]
                    for q in range(4):
                        kc = half * 4 + q
                        if half == 0:
                            P.v("dve", "tensor_scalar", hT[:, kc, t * 128:(t + 1) * 128], pb[:, q * 128:(q + 1) * 128],
                                alpha_col(layer, vA, kc, c), alpha_col(layer, vB, kc, c),
                                op0=ALU.mult, op1=ALU.add, R=[Bp, Bmod[layer]], W=[BhT[t][0]])
                        else:
                            P.v("act", "activation", hT[:, kc, t * 128:(t + 1) * 128], pb[:, q * 128:(q + 1) * 128],
                                AF.Identity, scale=alpha_col(layer, vA, kc, c), bias=alpha_col(layer, vB, kc, c),
                                R=[Bp, Bmod[layer]], W=[BhT[t][1]])

        PBK = [(psum[i], PB[i]) for i in range(8)]

        def store_x(label):
            for t in range(NT):
                P.dma("sp", y_d[t * 128:(t + 1) * 128, :], x_sb[:, t, :], reads=[Bx[t]])

        if stop == "ada":
            store_x("ada")
            P.end_phase()
            return nc

        with ExitStack() as ph:
            bc = [sbuf(ph, "bc%d" % i, [128, D]) for i in range(8)]
            Bbc = [Buf("bc%d" % i) for i in range(8)]
            psc = sbuf(ph, "psc", [128, D])
            Bpsc = Buf("psc")
            dgall = sbuf(ph, "dgall", [128, 8, 128])
            Bdg = Buf("dgall")
            h_tok = sbuf(ph, "h_tok", [128, NT, D])
            Bh = [Buf("h%d" % t) for t in range(NT)]
            ptab = sbuf(ph, "ptab", [128, 4, 5, 128])
            Bptab = Buf("ptab")
            pw = sbuf(ph, "pw", [128, 4, 2, 256], BF16)
            Bpw = Buf("pw")
            pTt = [sbuf(ph, "pTt%d" % i, [128, 8, 128], BF16) for i in range(2)]
            BpTt = [Buf("pTt0"), Buf("pTt1")]
            ybuf = [sbuf(ph, "ybuf%d" % i, [128, D]) for i in range(4)]
            By = [Buf("y%d" % i) for i in range(4)]
            wa1 = sbuf(ph, "wa1", [128, 8, 512], BF16)
            Bwa1 = Buf("wa1")

            P.dma("sp", ptab[:], ptab_d.rearrange("w k s t -> s w k t"), writes=[Bptab])
            P.dma("pool", pw[:], pool_w_d.rearrange("g (i c) d -> c g i d", c=128), writes=[Bpw])
            load_bc(psc[:], pool_scale_d[0:1, :], [Bpsc])
            load_bc(bc[6][:], ln1g_d[0:1, :], [Bbc[6]])
            load_bc(bc[7][:], ln1b_d[0:1, :], [Bbc[7]])
            for c in range(2):
                make_bc(bc[2 * c], 0, 1, c, dgall, Bdg, PBK[0:2], [Bbc[2 * c]])
                make_bc(bc[2 * c + 1], 0, 0, c, dgall, Bdg, PBK[2:4], [Bbc[2 * c + 1]])
                make_bc(bc[4 + c], 0, 2, c, dgall, Bdg, PBK[4:6], [Bbc[4 + c]])
                P.v("pool", "tensor_tensor", bc[4 + c][:], bc[4 + c][:], psc[:], ALU.mult,
                    R=[Bbc[4 + c], Bpsc], W=[Bbc[4 + c]])
            for t in range(NT):
                c = tile_cond(t)
                P.v("dve", "tensor_tensor", h_tok[:, t, :], x_sb[:, t, :], bc[2 * c][:], ALU.mult,
                    R=[Bx[t], Bbc[2 * c]], W=[Bh[t]])
                P.v("pool", "tensor_tensor", h_tok[:, t, :], h_tok[:, t, :], bc[2 * c + 1][:], ALU.add,
                    R=[Bh[t], Bbc[2 * c + 1]], W=[Bh[t]])
            seq_of = {}
            for (t0, ntl) in SEQS:
                for q in range(ntl):
                    seq_of[t0 + q] = (q, ntl)
            for grp in range(3):
                tiles = list(range(4 * grp, 4 * grp + 4))
                for i, t in enumerate(tiles):
                    q, ntl = seq_of[t]
                    c = tile_cond(t)
                    k = i % 2
                    (pA, BA), (pBk, BB) = PBK[2 * k], PBK[2 * k + 1]
                    kind = 0 if q == 0 else (2 if q == ntl - 1 else 1)
                    srcs = [(t, kind)]
                    if q > 0:
                        srcs.append((t - 1, 3))
                    if q < ntl - 1:
                        srcs.append((t + 1, 4))
                    for kc in range(8):
                        wi = kc // 2
                        bank, Bb = (pA, BA) if kc < 4 else (pBk, BB)
                        dst = bank[:, (kc % 4) * 128:(kc % 4 + 1) * 128]
                        for si, (ts, kd) in enumerate(srcs):
                            P.mm(dst, h_tok[:, ts, kc * 128:(kc + 1) * 128], ptab[:, wi, kd, :],
                                 si == 0, si == len(srcs) - 1, reads=[Bh[ts], Bptab], writes=[Bb])
                    P.v("act", "copy", pTt[k][:, 0:4, :], pA[:].rearrange("p (a b) -> p a b", b=128),
                        R=[BA], W=[BpTt[k]])
                    P.v("act", "copy", pTt[k][:, 4:8, :], pBk[:].rearrange("p (a b) -> p a b", b=128),
                        R=[BB], W=[BpTt[k]])
                    for half in range(2):
                        bank, Bb = (pA, BA) if half == 0 else (pBk, BB)
                        for gg in range(2):
                            g = half * 2 + gg
                            for i2 in range(2):
                                P.mm(bank[:, gg * 256:(gg + 1) * 256], pTt[k][:, 2 * g + i2, :], pw[:, g, i2, :],
                                     i2 == 0, i2 == 1, reads=[BpTt[k], Bpw], writes=[Bb])
                        P.v("dve", "tensor_tensor", ybuf[i][:, half * 512:(half + 1) * 512], bank[:],
                            bc[4 + c][:, half * 512:(half + 1) * 512], ALU.mult, R=[Bb, Bbc[4 + c]], W=[By[i]])
                    P.v("dve", "scalar_tensor_tensor", ybuf[i][:], x_sb[:, t, :], ALPHA, ybuf[i][:],
                        op0=ALU.mult, op1=ALU.add, R=[Bx[t], By[i]], W=[By[i]])
                    ada_block(1, t, wa1, Bwa1, PBK[6 + t % 2])
                ln_group(tiles, [ybuf[i][:] for i in range(4)], By, bc[6][:], bc[7][:], [Bbc[6], Bbc[7]],
                         (0, 4, 3), PBK[4:8], "pool")
            if stop == "l0mix":
                store_x("l0mix")
            P.end_phase()
        if stop == "l0mix":
            return nc


        def moe_phase(layer, next_mod, final):
            with ExitStack() as ph:
                bc = [sbuf(ph, "bc%d" % i, [128, D]) for i in range(4)]
                Bbc = [Buf("mbc%d" % i) for i in range(4)]
                dgall = sbuf(ph, "dgall", [128, 8, 128])
                Bdg = Buf("dgall")
                wg = [sbuf(ph, "wg%d" % i, [128, 8, 512], BF16) for i in range(2)]
                wu = [sbuf(ph, "wu%d" % i, [128, 8, 512], BF16) for i in range(2)]
                wd = [sbuf(ph, "wd%d" % i, [128, 4, D], BF16) for i in range(2)]
                Bwg = [Buf("wg0"), Buf("wg1")]
                Bwu = [Buf("wu0"), Buf("wu1")]
                Bwd = [Buf("wd0"), Buf("wd1")]
                heT = [sbuf(ph, "heT%d" % i, [128, 4, 512], BF16) for i in range(2)]
                Bhe = [Buf("he0"), Buf("he1")]
                sg = [sbuf(ph, "sg%d" % i, [128, 512], BF16) for i in range(2)]
                Bsg = [Buf("sg0"), Buf("sg1")]
                yacc = sbuf(ph, "yacc", [128, NT, D])
                Byacc = [Buf("yacc%d" % t) for t in range(NT)]
                wr_b = sbuf(ph, "wr_b", [128, 8, 16], BF16)
                brt = sbuf(ph, "brt", [128, 16])
                Bwr = Buf("wr")
                Bbrt = Buf("brt")
                rt = sbuf(ph, "rt", [128, 1472])
                Brt = Buf("rt")
                G = [psum[0], psum[1]]
                U = [psum[2], psum[3]]
                Dn = [psum[4], psum[5], psum[6], psum[7]]
                BG, BU, BD = [PB[0], PB[1]], [PB[2], PB[3]], [PB[4], PB[5], PB[6], PB[7]]

                def load_expert(e):
                    sl = e % 2
                    P.dma("pool", wg[sl][:], w_gate_d[layer, e].rearrange("(kc k) f -> k kc f", k=128), writes=[Bwg[sl]])
                    P.dma("pool", wu[sl][:], w_up_d[layer, e].rearrange("(kc k) f -> k kc f", k=128), writes=[Bwu[sl]])
                    P.dma("pool", wd[sl][:], w_down_d[layer, e].rearrange("(fc f) d -> f fc d", f=128), writes=[Bwd[sl]])

                P.dma("pool", wr_b[:], w_router_d.rearrange("(kc k) e -> k kc e", k=128), writes=[Bwr])
                load_expert(0)
                load_expert(1)
                P.dma("sp", brt[:], b_router_d[0:1, :].partition_broadcast(128), writes=[Bbrt])
                load_bc(bc[2][:], ln2g_d[layer:layer + 1, :], [Bbc[2]])
                load_bc(bc[3][:], ln2b_d[layer:layer + 1, :], [Bbc[3]])
                for c in range(2):
                    make_bc(bc[c], layer, 5, c, dgall, Bdg, PBK[6:8], [Bbc[c]])

                lg = psum[4]
                for t in range(NT):
                    for kc in range(8):
                        P.mm(lg[:, t * 16:(t + 1) * 16], hT[:, kc, t * 128:(t + 1) * 128], wr_b[:, kc, :],
                             kc == 0, kc == 7, reads=[BhT[t][0], BhT[t][1], Bwr], writes=[PB[4]])
                o = [0]

                def carve(nel):
                    v_ = rt[:, o[0]:o[0] + nel]
                    o[0] += nel
                    return v_
                s_sb, sel, t6, gs, gm = carve(192), carve(192), carve(288), carve(48), carve(12)
                gmask, msel, top8, emask, wsel, wsum = carve(48), carve(192), carve(96), carve(192), carve(192), carve(12)
                Rr, Wr = [Brt], [Brt]
                P.v("act", "activation", s_sb, lg[:, 0:192], AF.Sigmoid, R=[PB[4]], W=Wr)
                P.v("dve", "tensor_tensor", sel.rearrange("p (t e) -> p t e", e=16),
                    s_sb.rearrange("p (t e) -> p t e", e=16), brt[:].unsqueeze(1).to_broadcast([128, NT, 16]),
                    ALU.add, R=Rr + [Bbrt], W=Wr)
                sel4 = sel.rearrange("p (g e) -> p g e", e=4)
                t6v = t6.rearrange("p (g e) -> p g e", e=6)
                P.v("dve", "tensor_tensor", t6v[:, :, 0:3], sel4[:, :, 0:3], sel4[:, :, 1:4], ALU.add, R=Rr, W=Wr)
                P.v("dve", "tensor_tensor", t6v[:, :, 3:5], sel4[:, :, 0:2], sel4[:, :, 2:4], ALU.add, R=Rr, W=Wr)
                P.v("dve", "tensor_tensor", t6v[:, :, 5:6], sel4[:, :, 0:1], sel4[:, :, 3:4], ALU.add, R=Rr, W=Wr)
                P.v("dve", "tensor_reduce", gs, t6v, AX.X, ALU.max, R=Rr, W=Wr)
                gs3 = gs.rearrange("p (t g) -> p t g", g=4)
                P.v("dve", "tensor_reduce", gm, gs3, AX.X, ALU.max, R=Rr, W=Wr)
                P.v("dve", "tensor_tensor", gmask.rearrange("p (t g) -> p t g", g=4), gs3,
                    gm.unsqueeze(2).to_broadcast([128, NT, 4]), ALU.is_equal, R=Rr, W=Wr)
                P.v("dve", "scalar_tensor_tensor", msel.rearrange("p (g e) -> p g e", e=4), sel4, 2.0,
                    gmask.unsqueeze(2).to_broadcast([128, 48, 4]), op0=ALU.add, op1=ALU.mult, R=Rr, W=Wr)
                top83 = top8.rearrange("p (t e) -> p t e", e=8)
                for t in range(NT):
                    P.v("dve", "max", top83[:, t, :], msel[:, t * 16:(t + 1) * 16], R=Rr, W=Wr)
                P.v("dve", "tensor_tensor", emask.rearrange("p (t e) -> p t e", e=16),
                    msel.rearrange("p (t e) -> p t e", e=16), top83[:, :, 1:2].to_broadcast([128, NT, 16]),
                    ALU.is_ge, R=Rr, W=Wr)
                P.v("dve", "tensor_tensor", wsel, s_sb, emask, ALU.mult, R=Rr, W=Wr)
                wsel3 = wsel.rearrange("p (t e) -> p t e", e=16)
                P.v("dve", "tensor_reduce", wsum, wsel3, AX.X, ALU.add, R=Rr, W=Wr)
                P.v("dve", "reciprocal", wsum, wsum, R=Rr, W=Wr)
                P.v("dve", "tensor_tensor", cmb[:], wsel3, wsum.unsqueeze(2).to_broadcast([128, NT, 16]),
                    ALU.mult, R=Rr, W=Bcmb)

                blocks = [(e, b) for e in range(16) for b in range(3)]
                hT_bufs = [[x for t in range(4 * b, 4 * b + 4) for x in BhT[t]] for b in range(3)]

                def GU(k):
                    e, b = blocks[k]
                    sl = e % 2
                    for fc in range(4):
                        gi = (k * 4 + fc) % 2
                        for kc in range(8):
                            P.mm(G[gi][:], wg[sl][:, kc, fc * 128:(fc + 1) * 128], hT[:, kc, b * 512:(b + 1) * 512],
                                 kc == 0, kc == 7, reads=[Bwg[sl]] + hT_bufs[b], writes=[BG[gi]])
                        for kc in range(8):
                            P.mm(U[gi][:], wu[sl][:, kc, fc * 128:(fc + 1) * 128], hT[:, kc, b * 512:(b + 1) * 512],
                                 kc == 0, kc == 7, reads=[Bwu[sl]] + hT_bufs[b], writes=[BU[gi]])
                        P.v("act", "activation", sg[gi][:], G[gi][:], AF.Silu, R=[BG[gi]], W=[Bsg[gi]])
                        P.v("dve", "tensor_tensor", heT[k % 2][:, fc, :], U[gi][:], sg[gi][:], ALU.mult,
                            R=[BU[gi], Bsg[gi]], W=[Bhe[k % 2]])

                def DNp(k):
                    e, b = blocks[k]
                    sl = e % 2
                    for tt in range(4):
                        t = 4 * b + tt
                        for half in range(2):
                            di = (k * 8 + tt * 2 + half) % 4
                            for fc in range(4):
                                P.mm(Dn[di][:], heT[k % 2][:, fc, tt * 128:(tt + 1) * 128],
                                     wd[sl][:, fc, half * 512:(half + 1) * 512], fc == 0, fc == 3,
                                     reads=[Bhe[k % 2], Bwd[sl]], writes=[BD[di]])
                            dst = yacc[:, t, half * 512:(half + 1) * 512]
                            if e == 0:
                                P.v("dve", "tensor_scalar", dst, Dn[di][:], cmb[:, t, e:e + 1], None, op0=ALU.mult,
                                    R=[BD[di], Bcmb[t]], W=[Byacc[t]])
                            else:
                                P.v("dve", "scalar_tensor_tensor", dst, Dn[di][:], cmb[:, t, e:e + 1], dst,
                                    op0=ALU.mult, op1=ALU.add, R=[BD[di], Bcmb[t], Byacc[t]], W=[Byacc[t]])

                def pre(grp):
                    for t in range(4 * grp, 4 * grp + 4):
                        c = tile_cond(t)
                        ya = yacc[:, t, :]
                        P.v("pool", "tensor_tensor", ya, ya, bc[c][:], ALU.mult, R=[Byacc[t], Bbc[c]], W=[Byacc[t]])
                    for t in range(4 * grp, 4 * grp + 4):
                        ya = yacc[:, t, :]
                        P.v("dve", "scalar_tensor_tensor", ya, x_sb[:, t, :], ALPHA, ya, op0=ALU.mult, op1=ALU.add,
                            R=[Bx[t], Byacc[t]], W=[Byacc[t]])

                def epi(grp, part):
                    tiles = list(range(4 * grp, 4 * grp + 4))
                    ln_group(tiles, [yacc[:, t, :] for t in tiles], [Byacc[t] for t in tiles], bc[2][:], bc[3][:],
                             [Bbc[2], Bbc[3]], next_mod, PBK, "dve", part=part)

                GU(0)
                for k in range(len(blocks)):
                    if k + 1 < len(blocks):
                        GU(k + 1)
                    DNp(k)
                    e_, b_ = blocks[k]
                    if b_ == 2 and e_ + 2 < 16:
                        load_expert(e_ + 2)
                    if e_ == 15:
                        pre(b_)
                        epi(b_, "elem")
                for grp in range(3):
                    if next_mod is not None:
                        epi(grp, "tr")
                    if final:
                        for t in range(4 * grp, 4 * grp + 4):
                            P.dma("sp", y_d[t * 128:(t + 1) * 128, :], x_sb[:, t, :], reads=[Bx[t]])
                P.end_phase()

        moe_phase(0, (1, 1, 0), False)
        if stop == "l0moe":
            store_x("l0moe")
            P.end_phase()
            return nc


        with ExitStack() as pa:
            QT_s = sbuf(pa, "QT_s", [128, 8, 1024], BF16)
            KT_s = sbuf(pa, "KT_s", [128, 8, 1024], BF16)
            Vp_s = sbuf(pa, "Vp_s", [128, 8, 8, 192], BF16)
            O_CTXV, O_M01, O_RBB, O_E, O_ONES, RG_N = 4096, 10240, 12032, 13824, 19200, 19392
            Rg = sbuf(pa, "Rg", [128, RG_N], BF16)
            QT_p = Rg[:, 0:4096].rearrange("p (c t) -> p c t", c=8)
            KT_p = Rg[:, 4096:8192].rearrange("p (c t) -> p c t", c=8)
            Vp_p = Rg[:, 8192:14336].rearrange("p (m a e) -> p m a e", m=4, a=8)
            ctxKT = Rg[:, 0:4096].rearrange("p (c t) -> p c t", c=8)
            ctxVp = Rg[:, O_CTXV:O_M01].rearrange("p (m a e) -> p m a e", m=4, a=8)
            m01 = Rg[:, O_M01:O_RBB].rearrange("p (m q) -> p m q", q=128)
            rbb = Rg[:, O_RBB:O_E].rearrange("p (b d q) -> p b d q", b=2, d=7)
            Etab = Rg[:, O_E:O_ONES].rearrange("p (b d q) -> p b d q", b=3, d=14)
            onesp = Rg[:, O_ONES:RG_N]
            BQs, BKs, BVs = Buf("QTs"), Buf("KTs"), Buf("Vs")
            BQp, BKp, BVp = Buf("QTp"), Buf("KTp"), Buf("Vp")
            Bones = Buf("onesp")

            with ExitStack() as ph:
                wb = [sbuf(ph, "wqkv%d" % i, [128, 8, 512], BF16) for i in range(2)]
                Bwb = [Buf("wb0"), Buf("wb1")]
                stg = [sbuf(ph, "stg%d" % i, [128, 512]) for i in range(2)]
                Bstg = [Buf("stg0"), Buf("stg1")]
                P.v("pool", "memset", Vp_s[:, :, :, 64:128], 0.0, W=[BVs])
                P.v("pool", "memset", Vp_p[:, :, :, 64:128], 0.0, W=[BVp])
                P.v("pool", "memset", onesp[:, 0:64], 1.0, W=[Bones])
                P.v("pool", "memset", onesp[:, 64:128], 0.0, W=[Bones])
                P.v("pool", "memset", onesp[:, 128:192], 1.0, W=[Bones])
                nb = [0]
                ns = [0]

                def bank():
                    i = nb[0] % 8
                    nb[0] += 1
                    return psum[i], PB[i], ("dve" if i % 2 == 0 else "act")

                def evac(eng, dst, src, Bsrc, W, mul=None):
                    if eng == "dve":
                        if mul is None:
                            P.v("dve", "tensor_copy", dst, src, R=[Bsrc], W=W)
                        else:
                            P.v("dve", "tensor_scalar_mul", dst, src, mul, R=[Bsrc], W=W)
                    else:
                        if mul is None:
                            P.v("act", "copy", dst, src, R=[Bsrc], W=W)
                        else:
                            P.v("act", "mul", dst, src, mul, R=[Bsrc], W=W)

                hTb = lambda tiles: [x for t in tiles for x in BhT[t]]
                for cb in range(6):
                    sl = cb % 2
                    P.dma("pool", wb[sl][:], w_qkv_d.rearrange("(kc k) n -> k kc n", k=128)[:, :, cb * 512:(cb + 1) * 512],
                          writes=[Bwb[sl]])
                    if cb < 4:
                        isq = cb < 2
                        for i in range(4):
                            a = (cb % 2) * 4 + i
                            for tb in range(3):
                                pb, Bp, eng = bank()
                                for kc in range(8):
                                    P.mm(pb[:], wb[sl][:, kc, i * 128:(i + 1) * 128], hT[:, kc, tb * 512:(tb + 1) * 512],
                                         kc == 0, kc == 7, reads=[Bwb[sl]] + hTb(range(4 * tb, 4 * tb + 4)), writes=[Bp])
                                if tb == 0:
                                    dst = (QT_p if isq else KT_p)[:, a, :]
                                    W = [BQp if isq else BKp]
                                else:
                                    dst = (QT_s if isq else KT_s)[:, a, (tb - 1) * 512:tb * 512]
                                    W = [BQs if isq else BKs]
                                evac(eng, dst, pb[:], Bp, W, mul=(0.125 if isq else None))
                    if cb in (2, 3, 4, 5):
                        isv = cb >= 4
                        hh = cb % 2
                        for t in (range(NT) if isv else range(4)):
                            pb, Bp, eng = bank()
                            for kc in range(8):
                                P.mm(pb[:], hT[:, kc, t * 128:(t + 1) * 128], wb[sl][:, kc, :], kc == 0, kc == 7,
                                     reads=[Bwb[sl]] + hTb([t]), writes=[Bp])
                            if isv:
                                src4 = pb[:].rearrange("p (a two d) -> p a two d", two=2, d=64)
                                for par in range(2):
                                    if t < 4:
                                        evac(eng, Vp_p[:, t, hh * 4:(hh + 1) * 4, par * 128:par * 128 + 64],
                                             src4[:, :, par, :], Bp, [BVp])
                                    else:
                                        evac(eng, Vp_s[:, t - 4, hh * 4:(hh + 1) * 4, par * 128:par * 128 + 64],
                                             src4[:, :, par, :], Bp, [BVs])
                            if t < 4:
                                k = ns[0] % 2
                                ns[0] += 1
                                evac("dve" if eng == "act" else "act", stg[k][:], pb[:], Bp, [Bstg[k]])
                                dd = nv_d if isv else nk_d
                                s_, l0 = t // 2, (t % 2) * 128
                                P.dma("sp", dd[s_, hh * 8:(hh + 1) * 8, l0:l0 + 128, :].rearrange("h l d -> l h d"),
                                      stg[k][:].rearrange("p (h d) -> p h d", d=64), reads=[Bstg[k]])
                P.end_phase()

            with ExitStack() as pb_:
                OT_all = sbuf(pb_, "OT_all", [128, 8, 1024], BF16)
                BOT = [Buf("OT%d" % a) for a in range(8)]
                wo = sbuf(pb_, "wo", [128, 8, D], BF16)
                Bwo = Buf("wo")
                By = [Buf("ay%d" % i) for i in range(4)]
                ST = [PBK[i] for i in range(4)]
                ACC = [(PBK[4], PBK[5]), (PBK[6], PBK[7])]
                cn = {"st": 0, "pt": 0, "em": 0}

                def heads_scope(scope):
                    PT = [sbuf(scope, "PT%d" % i, [128, 512], BF16)[:] for i in range(4)]
                    PT += [hT[:, kc, 512:1024] for kc in range(4)]
                    BPT = [Buf("PT%d" % i) for i in range(8)]
                    rS = sbuf(scope, "rS", [128, 512])
                    BrS = Buf("rS")
                    return PT, BPT, rS, BrS

                LOOKAHEAD = 6
                pend = []

                def drain(keep):
                    while len(pend) > keep:
                        pend.pop(0)()

                def segment(PT, BPT, Kst, Qmv, nq, Bk, Bq, esl, BE_, Vblk, BV_, ones_blk, acc, c0, hold=False):
                    st, Bst = ST[cn["st"] % 4]
                    cn["st"] += 1
                    k = cn["pt"] % 8
                    cn["pt"] += 1
                    P.mm(st[:, 0:nq], Kst, Qmv, True, True, reads=[Bk, Bq], writes=[Bst])
                    P.v("act", "activation", PT[k][:, 0:nq], st[:, 0:nq], AF.Exp, R=[Bst], W=[BPT[k]])
                    if esl is not None:
                        P.v("dve", "tensor_tensor", PT[k][:, 0:nq], PT[k][:, 0:nq], esl, ALU.mult,
                            R=[BPT[k], BE_], W=[BPT[k]])
                    (pO, BO_), (pS, BS_) = acc

                    def back():
                        P.mm(pO[:, c0:c0 + nq], Vblk, PT[k][:, 0:nq], False, False, reads=[BPT[k], BV_], writes=[BO_],
                             skip=True)
                        P.mm(pS[:, c0:c0 + nq], ones_blk, PT[k][:, 0:nq], False, False, reads=[BPT[k], Bones],
                             writes=[BS_], skip=True)
                    pend.append(back)
                    if not hold:
                        drain(LOOKAHEAD)

                def zero_acc(acc, ncols):
                    drain(5)
                    (pO, BO_), (pS, BS_) = acc
                    P.v("dve", "memset", pO[:, 0:ncols], 0.0, W=[BO_])
                    P.v("dve", "memset", pS[:, 0:ncols], 0.0, W=[BS_])

                def normalize(acc, ncols, rS, BrS, dst, Wd):
                    (pO, BO_), (pS, BS_) = acc

                    def fin():
                        P.v("dve", "reciprocal", rS[:, 0:ncols], pS[:, 0:ncols], R=[BS_], W=[BrS])
                        P.v("dve", "tensor_tensor", dst, pO[:, 0:ncols], rS[:, 0:ncols], ALU.mult, R=[BO_, BrS], W=Wd)
                    pend.append(fin)

                def proj_group(tiles, tcols, ybufs, deadW, bcv, Bbcv):
                    for i, t in enumerate(tiles):
                        tc = tcols[i]
                        for half in range(2):
                            pb, Bp = PBK[2 * (i % 2) + half]
                            for kc in range(8):
                                P.mm(pb[:], OT_all[:, kc, tc:tc + 128], wo[:, kc, half * 512:(half + 1) * 512],
                                     kc == 0, kc == 7, reads=[BOT[kc], Bwo], writes=[Bp])
                            P.v("dve", "tensor_tensor", ybufs[i][:, half * 512:(half + 1) * 512], pb[:],
                                bcv[0][:, half * 512:(half + 1) * 512], ALU.mult, R=[Bp, Bbcv[0]],
                                W=[By[i]] + list(deadW))
                        P.v("dve", "scalar_tensor_tensor", ybufs[i], x_sb[:, t, :], ALPHA, ybufs[i],
                            op0=ALU.mult, op1=ALU.add, R=[Bx[t], By[i]], W=[By[i]])
                    ln_group(tiles, ybufs, By, bcv[1], bcv[2], [Bbcv[1], Bbcv[2]], (1, 4, 3), PBK[4:8], "pool")

                def prep_vectors(cidx, bcv, Bbcv, dgall_v, Bdg_v, deadW):
                    load_bc(bcv[1], ln1g_d[1:2, :], [Bbcv[1]] + list(deadW))
                    load_bc(bcv[2], ln1b_d[1:2, :], [Bbcv[2]] + list(deadW))
                    make_bc(bcv[0], 1, 2, cidx, dgall_v, Bdg_v, PBK[6:8], [Bbcv[0]] + list(deadW))

                with ExitStack() as ph:
                    PT, BPT, rS, BrS = heads_scope(ph)
                    P.dma("pool", wo[:], w_o_d.rearrange("(kc k) n -> k kc n", k=128), writes=[Bwo])
                    for a in range(8):
                        for s_ in range(2):
                            acc = ACC[(2 * a + s_) % 2]
                            zero_acc(acc, 256)
                            for c in range(2):
                                for rho in range(2):
                                    pr = slice(64 * rho, 64 * rho + 64)
                                    vc = slice(64 * rho, 64 * rho + 128)
                                    k0 = s_ * 256 + c * 128
                                    segment(PT, BPT, KT_p[pr, a, k0:k0 + 128], QT_p[pr, a, s_ * 256:(s_ + 1) * 256], 256,
                                            BKp, BQp, None, None, Vp_p[:, 2 * s_ + c, a, vc], BVp, onesp[:, vc], acc, 0, hold=(rho == 0))
                            normalize(acc, 256, rS, BrS, OT_all[:, a, s_ * 256:(s_ + 1) * 256], [BOT[a]])
                    drain(0)
                    P.end_phase()

                yb_p = [Rg[:, 2048 * i:2048 * (i + 1)].bitcast(F32) for i in range(4)]
                bc_p = [Rg[:, 8192 + 2048 * i:8192 + 2048 * (i + 1)].bitcast(F32) for i in range(3)]
                dg_p = Rg[:, 14336:16384].bitcast(F32).rearrange("p (a b) -> p a b", b=128)
                Bbc_p = [Buf("pbc%d" % i) for i in range(3)]
                prep_vectors(0, bc_p, Bbc_p, dg_p, Buf("pdg"), [])
                proj_group([0, 1, 2, 3], [0, 128, 256, 384], yb_p, [], bc_p, Bbc_p)
                if stop == "l1mixp":
                    store_x("l1mixp")
                P.end_phase()
                if stop == "l1mixp":
                    return nc

                with ExitStack() as ph:
                    PT, BPT, rS, BrS = heads_scope(ph)
                    ck_tok = OT_all[:, 0:4, :].rearrange("p c (h d) -> p c h d", d=64)
                    Bck = Buf("ck_tok")
                    BctxK, BctxV, Bm01 = Buf("ctxK"), Buf("ctxV"), Buf("m01")
                    Brbb = [Buf("rbb0"), Buf("rbb1")]
                    BE = [Buf("E0"), Buf("E1"), Buf("E2")]
                    for c in range(4):
                        P.dma("pool", ck_tok[:, c, :, :], ck_d[:, c * 128:(c + 1) * 128, :].rearrange("h l d -> l h d"),
                              writes=[Bck])
                    P.dma("pool", m01[:], m01_d.rearrange("m k q -> k m q"), writes=[Bm01])
                    P.v("pool", "memset", ctxVp[:, :, :, 64:128], 0.0, W=[BctxV])
                    cv4 = cv_d.rearrange("(a two) l d -> two l a d", two=2)
                    for c in range(4):
                        for par in range(2):
                            P.dma("pool", ctxVp[:, c, :, par * 128:par * 128 + 64], cv4[par, c * 128:(c + 1) * 128, :, :],
                                  writes=[BctxV])

                    def build_E(h):
                        hb, eb = h % 2, h % 3
                        P.dma("pool", rbb[:, hb, :, :], rbt_d[h].rearrange("d k q -> k d q"), writes=[Brbb[hb]])
                        P.v("act", "activation", rbb[:, hb, :, :], rbb[:, hb, :, :], AF.Exp, R=[Brbb[hb]], W=[Brbb[hb]])
                        for s2 in range(2):
                            P.v("dve", "tensor_tensor", Etab[:, eb, s2 * 7:(s2 + 1) * 7, :], rbb[:, hb, :, :],
                                m01[:, s2 * 7:(s2 + 1) * 7, :], ALU.mult, R=[Brbb[hb], Bm01], W=[BE[eb]])

                    for c in range(4):
                        pTb = psum[c % 2][:].bitcast(BF16)
                        for a in range(8):
                            P.tr(pTb[:, a * 128:(a + 1) * 128], ck_tok[:, c, 2 * a:2 * a + 2, :].rearrange("p h d -> p (h d)"),
                                 ident_b[:], reads=[Bck, Bconst], writes=[PB[c % 2]])
                        P.v("dve" if c % 2 == 0 else "act", "tensor_copy" if c % 2 == 0 else "copy",
                            ctxKT[:, :, c * 128:(c + 1) * 128], pTb.rearrange("p (a t) -> p a t", a=8),
                            R=[PB[c % 2]], W=[BctxK])
                    build_E(0)
                    build_E(1)
                    build_E(2)
                    for a in range(8):
                        for qb in range(2):
                            acc = ACC[(2 * a + qb) % 2]
                            zero_acc(acc, 512)
                            for (m, jlo, cnt, e2) in _SEGS[qb]:
                                for rho in range(2):
                                    eb = (2 * a + rho) % 3
                                    pr = slice(64 * rho, 64 * rho + 64)
                                    vc = slice(64 * rho, 64 * rho + 128)
                                    nq, q0 = cnt * 128, jlo * 128
                                    i0 = e2 * 7 + 3 - m + jlo
                                    esl = Etab[:, eb, i0:i0 + cnt, :].rearrange("p a b -> p (a b)")
                                    segment(PT, BPT, KT_s[pr, a, m * 128:(m + 1) * 128], QT_s[pr, a, q0:q0 + nq], nq,
                                            BKs, BQs, esl, BE[eb], Vp_s[:, m, a, vc], BVs, onesp[:, vc], acc, q0 - 512 * qb, hold=(rho == 0))
                            for c in range(4):
                                for rho in range(2):
                                    pr = slice(64 * rho, 64 * rho + 64)
                                    vc = slice(64 * rho, 64 * rho + 128)
                                    segment(PT, BPT, ctxKT[pr, a, c * 128:(c + 1) * 128], QT_s[pr, a, qb * 512:(qb + 1) * 512],
                                            512, BctxK, BQs, None, None, ctxVp[:, c, a, vc], BctxV, onesp[:, vc], acc, 0, hold=(rho == 0))
                            normalize(acc, 512, rS, BrS, OT_all[:, a, qb * 512:(qb + 1) * 512], [BOT[a], Bck])
                        for h2 in (2 * a + 3, 2 * a + 4):
                            if h2 < 16:
                                build_E(h2)
                    drain(0)
                    P.end_phase()

                yb_s = [QT_s[:, 2 * i:2 * i + 2, :].rearrange("p a t -> p (a t)").bitcast(F32) for i in range(4)]
                bc_s = [KT_s[:, 2 * i:2 * i + 2, :].rearrange("p a t -> p (a t)").bitcast(F32) for i in range(3)]
                dg_s = KT_s[:, 6:8, :].rearrange("p a t -> p (a t)").bitcast(F32).rearrange("p (a b) -> p a b", b=128)
                Bbc_s = [Buf("sbc%d" % i) for i in range(3)]
                prep_vectors(1, bc_s, Bbc_s, dg_s, Buf("sdg"), [])
                proj_group([4, 5, 6, 7], [0, 128, 256, 384], yb_s, [], bc_s, Bbc_s)
                proj_group([8, 9, 10, 11], [512, 640, 768, 896], yb_s, [], bc_s, Bbc_s)
                if stop == "l1mix":
                    store_x("l1mix")
                P.end_phase()
        if stop == "l1mix":
            return nc

        moe_phase(1, None, True)
        return nc


_CACHE = {}


def make_in_maps(inputs):
    f = lambda a: np.ascontiguousarray(np.asarray(a, dtype=np.float32))
    x_prompt, x_sample = f(inputs["x_prompt"]), f(inputs["x_sample"])
    cache_k, cache_v = f(inputs["cache_k"]), f(inputs["cache_v"])
    c, c_ctx = f(inputs["c"]), f(inputs["c_ctx"])
    b_ada = f(inputs["b_ada"])
    shared = {
        "w_ada": f(inputs["w_ada"]),
        "b_adaT": np.ascontiguousarray(b_ada.reshape(2, 48, 128).transpose(0, 2, 1)),
        "ln1_g": f(inputs["ln1_g"]), "ln1_b": f(inputs["ln1_b"]),
        "ln2_g": f(inputs["ln2_g"]), "ln2_b": f(inputs["ln2_b"]),
        "pool_w": f(inputs["pool_w"])[0],
        "pool_scale": f(inputs["pool_scale"]).reshape(1, D),
        "w_qkv": f(inputs["w_qkv"])[0],
        "w_o": f(inputs["w_o"])[0],
        "rbt": _rel_bias_layout(f(inputs["rel_bias"])[0]),
        "w_router": f(inputs["w_router"]),
        "b_router": f(inputs["b_router"]).reshape(1, 16),
        "w_gate": f(inputs["w_gate"]), "w_up": f(inputs["w_up"]), "w_down": f(inputs["w_down"]),
        "ptab": _pool_tables(),
        "m01": _mask01_table(),
    }
    in_maps = []
    for i in range(N_CORES):
        m = dict(shared)
        m["xin"] = np.ascontiguousarray(np.concatenate(
            [x_prompt[2 * i].reshape(256, D), x_prompt[2 * i + 1].reshape(256, D), x_sample[i]], axis=0))
        m["ck"] = np.ascontiguousarray(cache_k[i, 0])
        m["cv"] = np.ascontiguousarray(cache_v[i, 0])
        cond = np.stack([c_ctx, c[i]], axis=-1)
        m["condT"] = np.ascontiguousarray(cond.reshape(8, 128, 2).transpose(1, 0, 2))
        in_maps.append(m)
    return in_maps


def kernel(**inputs):
    if "nc" not in _CACHE:
        _CACHE["nc"] = build_nc()
    nc = _CACHE["nc"]
    in_maps = make_in_maps(inputs)
    res = run_bass_kernel_spmd(nc, in_maps, core_ids=list(range(N_CORES)))
    ys = [r["y"] for r in res.results]
    y_prompt = np.stack([y[:512].reshape(2, 256, D) for y in ys]).reshape(16, 256, D)
    y_sample = np.stack([y[512:] for y in ys])
    nk = np.stack([r["nk"] for r in res.results]).reshape(16, 1, 16, 256, 64)
    nv = np.stack([r["nv"] for r in res.results]).reshape(16, 1, 16, 256, 64)
    return (y_prompt.astype(np.float32), y_sample.astype(np.float32),
            nk.astype(np.float32), nv.astype(np.float32))
```

```python
from contextlib import ExitStack

import numpy as np
import concourse.bass as bass
import concourse.mybir as mybir
from concourse.bass_utils import run_bass_kernel_spmd

F32 = mybir.dt.float32
BF16 = mybir.dt.bfloat16
AF = mybir.ActivationFunctionType
ALU = mybir.AluOpType
AX = mybir.AxisListType

N_CORES = 8
D = 1024
NT = 12
T = NT * 128
ALPHA = 4.0 ** 0.25
LN_EPS = 1e-5
NEG = -30000.0
POOL_SIZES = (2, 4, 8, 16)
SEQS = ((0, 2), (2, 2), (4, 8))


def tile_cond(t):
    return 0 if t < 4 else 1


class Buf:
    __slots__ = ("name", "writer", "readers", "excl")

    def __init__(self, name="", excl=False):
        self.name = name
        self.writer = None
        self.readers = []
        self.excl = excl


class Lane:
    __slots__ = ("sem", "n", "last")

    def __init__(self, sem):
        self.sem = sem
        self.n = 0
        self.last = None


class Ins:
    __slots__ = ("eng", "fn", "deps", "signal", "count", "lane", "lval", "idx")

    def __init__(self, eng, fn):
        self.eng = eng
        self.fn = fn
        self.deps = []
        self.signal = False
        self.count = 0
        self.lane = None
        self.lval = 0
        self.idx = 0


class Prog:
    ENG = ("pe", "act", "dve", "pool", "sp")
    NLANES = {"sp": 24, "pool": 12, "act": 4}

    def __init__(self, nc, es):
        self.nc = nc
        self.es = es
        self.eng = {"pe": nc.tensor, "act": nc.scalar, "dve": nc.vector,
                    "pool": nc.gpsimd, "sp": nc.sync}
        self.sem = {e: es.enter_context(nc.semaphore("s_" + e)) for e in self.ENG}
        self.cnt = {e: 0 for e in self.ENG}
        self.bar = es.enter_context(nc.semaphore("s_bar"))
        self.nbar = 0
        self.lanes = {q: [Lane(es.enter_context(nc.semaphore("l_%s%d" % (q, i)))) for i in range(n)]
                      for q, n in self.NLANES.items()}
        self.lane_rr = {q: 0 for q in self.NLANES}
        self.waited = {e: {} for e in self.ENG}
        self.streams = {e: [] for e in self.ENG}
        self.touched = []
        self.n_ins = 0

    def _add_dep(self, ins, p):
        if p is None or p is ins:
            return
        if p.lane is None and ins.lane is None and p.eng == ins.eng and p.eng == "pe":
            return
        deps = ins.deps
        if p.lane is None:
            for i, q in enumerate(deps):
                if q.lane is None and q.eng == p.eng:
                    if p.idx > q.idx:
                        deps[i] = p
                    return
        elif p in deps:
            return
        deps.append(p)

    def _deps(self, ins, reads, writes):
        if any(b.excl for b in reads):
            writes = list(writes) + [b for b in reads if b.excl and b not in writes]
            reads = [b for b in reads if not b.excl]
        same = lambda p: (p is not None and p.lane is None and ins.lane is None and p.eng == ins.eng)
        for b in reads:
            self._add_dep(ins, b.writer)
        for b in writes:
            if not same(b.writer):
                self._add_dep(ins, b.writer)
            for r in b.readers:
                if not same(r):
                    self._add_dep(ins, r)
        for b in reads:
            if not b.readers and b.writer is None:
                self.touched.append(b)
            b.readers.append(ins)
        for b in writes:
            if not b.readers and b.writer is None:
                self.touched.append(b)
            b.writer = ins
            b.readers = []

    def op(self, eng, fn, reads=(), writes=()):
        ins = Ins(eng, fn)
        ins.idx = len(self.streams[eng])
        self._deps(ins, reads, writes)
        self.streams[eng].append(ins)
        return ins

    def dma(self, eng, out, in_, reads=(), writes=(), **kw):
        E = self.eng[eng]
        ins = Ins(eng, lambda: E.dma_start(out=out, in_=in_, **kw))
        ins.idx = len(self.streams[eng])
        lanes = self.lanes[eng]
        lane = lanes[self.lane_rr[eng] % len(lanes)]
        self.lane_rr[eng] += 1
        ins.lane = lane
        lane.n += 1
        ins.lval = 16 * lane.n
        if lane.last is not None:
            ins.deps.append(lane.last)
        lane.last = ins
        self._deps(ins, reads, writes)
        self.streams[eng].append(ins)
        return ins

    def mm(self, out, lhsT, rhs, start, stop, reads=(), writes=(), skip=False):
        t = self.nc.tensor
        if skip:
            return self.op("pe", lambda: t.matmul(out, lhsT, rhs, start=start, stop=stop, skip_group_check=True),
                           reads, writes)
        return self.op("pe", lambda: t.matmul(out, lhsT, rhs, start=start, stop=stop), reads, writes)

    def tr(self, out, in_, ident, reads=(), writes=()):
        t = self.nc.tensor
        return self.op("pe", lambda: t.transpose(out, in_, ident), reads, writes)

    def v(self, eng, meth, *args, R=(), W=(), **kw):
        E = self.eng[eng]
        return self.op(eng, lambda: getattr(E, meth)(*args, **kw), R, W)

    def end_phase(self):
        ENG = self.ENG
        for e in ENG:
            for ins in self.streams[e]:
                for p in ins.deps:
                    if p.lane is None:
                        p.signal = True
            for ins in reversed(self.streams[e]):
                if ins.lane is None:
                    ins.signal = True
                    break
        for e in ENG:
            for ins in self.streams[e]:
                if ins.lane is None and ins.signal:
                    self.cnt[e] += 1
                    ins.count = self.cnt[e]
        for e in ENG:
            E = self.eng[e]
            wd = self.waited[e]
            for ins in self.streams[e]:
                for p in ins.deps:
                    if p.lane is not None:
                        sem, val = p.lane.sem, p.lval
                    else:
                        sem, val = self.sem[p.eng], p.count
                    if wd.get(sem.num, 0) >= val:
                        continue
                    wd[sem.num] = val
                    E.wait_ge(sem, val)
                bi = ins.fn()
                if ins.lane is not None:
                    bi.then_inc(ins.lane.sem, 16)
                elif ins.signal:
                    bi.then_inc(self.sem[e], 1)
                self.n_ins += 1
        sp = self.eng["sp"]
        wd = self.waited["sp"]
        all_lanes = [l for q in self.lanes.values() for l in q]
        for e in ENG:
            if e == "sp" or self.cnt[e] == 0:
                continue
            if wd.get(self.sem[e].num, 0) < self.cnt[e]:
                sp.wait_ge(self.sem[e], self.cnt[e])
        for l in all_lanes:
            if l.n and wd.get(l.sem.num, 0) < 16 * l.n:
                sp.wait_ge(l.sem, 16 * l.n)
        self.nbar += 1
        sp.sem_inc(self.bar, 1)
        for e in ENG:
            if e != "sp":
                self.eng[e].wait_ge(self.bar, self.nbar)
            for e2 in ENG:
                self.waited[e][self.sem[e2].num] = self.cnt[e2]
            for l in all_lanes:
                self.waited[e][l.sem.num] = 16 * l.n
        for l in all_lanes:
            l.last = None
        for b in self.touched:
            b.writer = None
            b.readers = []
        self.touched = []
        self.streams = {e: [] for e in ENG}


def _pool_tables():
    n = 1024
    out = np.zeros((4, 5, 128, 128), np.float32)
    t = np.arange(n)
    for wi, w in enumerate(POOL_SIZES):
        lo = np.clip(t - w // 2, 0, n)
        hi = np.clip(t - w // 2 + w, 0, n)
        cnt = (hi - lo).astype(np.float64)
        s = np.arange(n)[:, None]
        Pm = ((s >= lo[None, :]) & (s < hi[None, :])) / cnt[None, :] - np.eye(n)
        Pm = Pm.astype(np.float32)
        out[wi, 0] = Pm[0:128, 0:128]
        out[wi, 1] = Pm[128:256, 128:256]
        out[wi, 2] = Pm[896:1024, 896:1024]
        out[wi, 3] = Pm[0:128, 128:256]
        out[wi, 4] = Pm[256:384, 128:256]
    return out


def _chunks_for(j):
    if j < 2:
        return list(range(0, 4))
    if j > 5:
        return list(range(4, 8))
    return list(range(j - 2, j + 3))


def _mask_tables():
    tiles = []
    index = {}
    keymap = {}
    col = np.arange(64)
    cs = np.clip(col - 8, 0, 48)
    colok = (col[None, :] >= cs[:, None]) & (col[None, :] < cs[:, None] + 16)
    for j in range(8):
        for m in _chunks_for(j):
            mt = np.full((2, 64, 2, 64), NEG, np.float32)
            for rq in range(2):
                r = 2 * j + rq
                rs = min(max(r - 4, 0), 8)
                for rk in range(2):
                    kr = 2 * m + rk
                    if rs <= kr < rs + 8:
                        mt[rk, :, rq, :] = np.where(colok.T, 0.0, NEG)
            key = mt.tobytes()
            if key not in keymap:
                keymap[key] = len(tiles)
                tiles.append(mt.reshape(128, 128))
            index[(j, m)] = keymap[key]
    return np.stack(tiles), index


_MASKS, _MASK_IDX = _mask_tables()
_M_FULL, _M_A, _M_B = _MASK_IDX[(0, 0)], _MASK_IDX[(2, 0)], _MASK_IDX[(2, 4)]


def _mask01_table():
    full = (_MASKS[_M_FULL] == 0).astype(np.float32)
    t = np.stack([full] * 14)
    t[7 + (3 - 2)] = (_MASKS[_M_B] == 0)
    t[7 + (3 + 2)] = (_MASKS[_M_A] == 0)
    return t


def _segments():
    segs = {0: [], 1: []}
    for m in range(8):
        for qb in range(2):
            js = [j for j in range(8) if m in _chunks_for(j) and j // 4 == qb]
            if not js:
                continue
            assert js == list(range(js[0], js[-1] + 1))
            e2 = any(_MASK_IDX[(j, m)] != _M_FULL for j in js)
            for j in js:
                want = _M_FULL
                if e2 and m - j == -2:
                    want = _M_A
                if e2 and m - j == 2:
                    want = _M_B
                assert _MASK_IDX[(j, m)] == want, (j, m)
            segs[qb].append((m, js[0], len(js), int(e2)))
    return segs


_SEGS = _segments()


def _rel_bias_layout(rel_bias):
    rk = np.arange(2)[:, None, None, None]
    ck = np.arange(64)[None, :, None, None]
    rq = np.arange(2)[None, None, :, None]
    cq = np.arange(64)[None, None, None, :]
    out = np.empty((16, 7, 128, 128), np.float32)
    for i in range(7):
        dl = 3 - i
        dy = np.clip(2 * dl + rk - rq + 7, 0, 14)
        dx = np.clip(ck - cq + 15, 0, 30)
        dy, dx = np.broadcast_arrays(dy, dx)
        out[:, i] = rel_bias[:, dy, dx].reshape(16, 128, 128)
    return out


def build_nc(stop=None):
    nc = bass.Bass("TRN2", target_bir_lowering=False)

    def din(name, shape):
        return nc.dram_tensor(name, list(shape), F32, kind="ExternalInput").ap()

    def dout(name, shape):
        return nc.dram_tensor(name, list(shape), F32, kind="ExternalOutput").ap()

    xin = din("xin", [T, D])
    ck_d = din("ck", [16, 512, 64])
    cv_d = din("cv", [16, 512, 64])
    condT_d = din("condT", [128, 8, 2])
    w_ada_d = din("w_ada", [2, D, 6 * D])
    b_adaT_d = din("b_adaT", [2, 128, 48])
    ln1g_d = din("ln1_g", [2, D])
    ln1b_d = din("ln1_b", [2, D])
    ln2g_d = din("ln2_g", [2, D])
    ln2b_d = din("ln2_b", [2, D])
    pool_w_d = din("pool_w", [4, 256, 256])
    pool_scale_d = din("pool_scale", [1, D])
    w_qkv_d = din("w_qkv", [D, 3 * D])
    w_o_d = din("w_o", [D, D])
    rbt_d = din("rbt", [16, 7, 128, 128])
    w_router_d = din("w_router", [D, 16])
    b_router_d = din("b_router", [1, 16])
    w_gate_d = din("w_gate", [2, 16, D, 512])
    w_up_d = din("w_up", [2, 16, D, 512])
    w_down_d = din("w_down", [2, 16, 512, D])
    ptab_d = din("ptab", [4, 5, 128, 128])
    m01_d = din("m01", [14, 128, 128])

    y_d = dout("y", [T, D])
    nk_d = dout("nk", [2, 16, 256, 64])
    nv_d = dout("nv", [2, 16, 256, 64])
    dbg_d = dout("dbg", [128, 2, 48, 2]) if stop == "ada" else None

    with ExitStack() as es:
        P = Prog(nc, es)

        _uid = [0]

        def sbuf(scope, name, shape, dt=F32):
            _uid[0] += 1
            return scope.enter_context(nc.sbuf_tensor("sb%d_%s" % (_uid[0], name), list(shape), dt))

        x_sb = sbuf(es, "x_sb", [128, NT, D])
        hT = sbuf(es, "hT", [128, 8, T], BF16)
        ident_f = sbuf(es, "ident_f", [128, 128])
        ident_b = sbuf(es, "ident_b", [128, 128], BF16)
        ones_f = sbuf(es, "ones_f", [128, 128])
        modcol = sbuf(es, "modcol", [128, 2, 48, 2])
        cmb = sbuf(es, "cmb", [128, NT, 16])
        small = sbuf(es, "small", [128, 64])
        psum = [es.enter_context(nc.psum_tensor("ps%d" % i, [128, 512], F32)) for i in range(8)]
        PB = [Buf("ps%d" % i, excl=True) for i in range(8)]

        Bx = [Buf("x%d" % t) for t in range(NT)]
        BhT = [(Buf("hTd%d" % t), Buf("hTa%d" % t)) for t in range(NT)]
        Bconst = Buf("const")
        Bmod = [Buf("modcol0"), Buf("modcol1")]
        sTb = sbuf(es, "sTb", [128, 8, 2], BF16)
        b_col = sbuf(es, "b_col", [128, 2, 48])
        BsT = Buf("sT")
        Bbcol = Buf("bcol")
        Bcmb = [Buf("cmb%d" % t) for t in range(NT)]


        def alpha_col(layer, v, ch, c):
            return modcol[:, layer, v * 8 + ch, c:c + 1]

        PBK = [(psum[i], PB[i]) for i in range(8)]

        def ada_block(layer, nb, wa_t, Bwa_t, bank):
            pc, Bpc = bank
            src = w_ada_d[layer].rearrange("(kc k) n -> k kc n", k=128)[:, :, nb * 512:(nb + 1) * 512]
            P.dma("pool", wa_t[:], src, writes=[Bwa_t])
            for n4 in range(4):
                for kc in range(8):
                    P.mm(pc[:, n4 * 2:n4 * 2 + 2], wa_t[:, kc, n4 * 128:(n4 + 1) * 128], sTb[:, kc, :],
                         kc == 0, kc == 7, reads=[Bwa_t, BsT], writes=[Bpc])
            dst = modcol[:, layer, nb * 4:(nb + 1) * 4, :]
            P.v("dve", "tensor_tensor", dst, pc[:, 0:8].rearrange("p (c k) -> p c k", k=2),
                b_col[:, layer, nb * 4:(nb + 1) * 4].unsqueeze(2).to_broadcast([128, 4, 2]), ALU.add,
                R=[Bpc, Bbcol], W=[Bmod[layer]])
            if nb // 2 in (1, 4):
                P.v("dve", "tensor_scalar_add", dst, dst, 1.0, R=[Bmod[layer]], W=[Bmod[layer]])

        with ExitStack() as ph:
            condT = sbuf(ph, "condT", [128, 8, 2])
            sT = sbuf(ph, "sT", [128, 8, 2])
            wa = [sbuf(ph, "wa%d" % i, [128, 8, 512], BF16) for i in range(3)]
            Bwa = [Buf("wa%d" % i) for i in range(3)]

            P.v("pool", "memset", ident_f[:], 0.0, W=[Bconst])
            P.v("pool", "affine_select", ident_f[:], ident_f[:], [[-1, 128]], ALU.not_equal, 1.0,
                base=0, channel_multiplier=1, R=[Bconst], W=[Bconst])
            P.v("pool", "tensor_copy", ident_b[:], ident_f[:], R=[Bconst], W=[Bconst])
            P.v("pool", "memset", ones_f[:], 1.0, W=[Bconst])

            for t in range(NT):
                P.dma("sp", x_sb[:, t, :], xin[t * 128:(t + 1) * 128, :], writes=[Bx[t]])
            P.dma("sp", condT[:], condT_d[:, :, :], writes=[BsT])
            P.dma("sp", b_col[:], b_adaT_d.rearrange("l p c -> p l c"), writes=[Bbcol])
            P.v("act", "activation", sT[:], condT[:], AF.Silu, R=[BsT], W=[BsT])
            P.v("dve", "tensor_copy", sTb[:], sT[:], R=[BsT], W=[BsT])

            for nb in range(12):
                ada_block(0, nb, wa[nb % 3], Bwa[nb % 3], PBK[nb % 2])
            if stop == "ada":
                for nb in range(12):
                    ada_block(1, nb, wa[nb % 3], Bwa[nb % 3], PBK[nb % 2])
            if stop == "ada":
                P.dma("sp", dbg_d[:, :, :, :], modcol[:], reads=Bmod)
            P.end_phase()

        def load_bc(dst, src_row, W):
            P.dma("sp", dst, src_row.partition_broadcast(128), writes=W)

        def make_bc(dst, layer, v, c, dgall, Bdg, banks, W):
            P.v("dve", "tensor_tensor", dgall[:], ident_f[:].unsqueeze(1).to_broadcast([128, 8, 128]),
                modcol[:, layer, v * 8:(v + 1) * 8, c:c + 1].to_broadcast([128, 8, 128]), ALU.mult,
                R=[Bconst, Bmod[layer]], W=[Bdg])
            for half in range(2):
                pb, Bp = banks[half]
                P.mm(pb[:], ones_f[:], dgall[:, half * 4:(half + 1) * 4, :].rearrange("p a b -> p (a b)"),
                     True, True, reads=[Bconst, Bdg], writes=[Bp])
                P.v("act", "copy", dst[:, half * 512:(half + 1) * 512], pb[:], R=[Bp], W=W)

        Bsmall = [Buf("mv%d" % i) for i in range(4)]

        def ln_group(tiles, ys, Bys, g_bc, b_bc, Bvec, next_mod, banks, gmul_eng, part="all"):
            n = len(tiles)
            sts = [small[:, 16 * i:16 * i + 12].rearrange("p (a b) -> p a b", b=6) for i in range(n)]
            mvv = [small[:, 16 * i + 12:16 * i + 16] for i in range(n)]
            if part != "tr":
                ln_elem(tiles, ys, Bys, g_bc, b_bc, Bvec, gmul_eng, n, sts, mvv)
            if next_mod is None or part == "elem":
                return
            ln_tr(tiles, next_mod, banks)

        def ln_elem(tiles, ys, Bys, g_bc, b_bc, Bvec, gmul_eng, n, sts, mvv):
            for i in range(n):
                P.v("dve", "bn_stats", sts[i][:, 0, :], ys[i][:, 0:512], R=[Bys[i]], W=[Bsmall[i]])
                P.v("dve", "bn_stats", sts[i][:, 1, :], ys[i][:, 512:1024], R=[Bys[i]], W=[Bsmall[i]])
                P.v("dve", "bn_aggr", mvv[i][:, 0:2], sts[i], R=[Bsmall[i]], W=[Bsmall[i]])
            for i in range(n):
                P.v("act", "activation", mvv[i][:, 2:3], mvv[i][:, 1:2], AF.Sqrt, bias=LN_EPS,
                    R=[Bsmall[i]], W=[Bsmall[i]])
            for i in range(n):
                P.v("dve", "reciprocal", mvv[i][:, 2:3], mvv[i][:, 2:3], R=[Bsmall[i]], W=[Bsmall[i]])
                P.v("dve", "tensor_scalar", mvv[i][:, 3:4], mvv[i][:, 0:1], mvv[i][:, 2:3], -1.0,
                    op0=ALU.mult, op1=ALU.mult, R=[Bsmall[i]], W=[Bsmall[i]])
            for i in range(n):
                P.v("act", "activation", ys[i], ys[i], AF.Identity, scale=mvv[i][:, 2:3], bias=mvv[i][:, 3:4],
                    R=[Bys[i], Bsmall[i]], W=[Bys[i]])
            for i in range(n):
                P.v("dve", "tensor_tensor", ys[i], ys[i], g_bc, ALU.mult, R=[Bys[i]] + list(Bvec), W=[Bys[i]])
            for i, t in enumerate(tiles):
                P.v("dve", "tensor_tensor", x_sb[:, t, :], ys[i], b_bc, ALU.add, R=[Bys[i]] + list(Bvec), W=[Bx[t]])

        def ln_tr(tiles, next_mod, banks):
            layer, vA, vB = next_mod
            nb = len(banks) // 2
            for i, t in enumerate(tiles):
                c = tile_cond(t)
                pair = banks[2 * (i % nb):2 * (i % nb) + 2]
                for half in range(2):
                    pb, Bp = pair[half]
                    for q in range(4):
                        kc = half * 4 + q
                        P.tr(pb[:, q * 128:(q + 1) * 128], x_sb[:, t, kc * 128:(kc + 1) * 128], ident_f[:],
                             reads=[Bx[t], Bconst], writes=[Bp])
                for half in range(2):
                    pb, Bp = pair[half]
                    for q in range(4):
                        kc = half * 4 + q
                        if half == 0:
                            P.v("dve", "tensor_scalar", hT[:, kc, t * 128:(t + 1) * 128], pb[:, q * 128:(q + 1) * 128],
                                alpha_col(layer, vA, kc, c), alpha_col(layer, vB, kc, c),
                                op0=ALU.mult, op1=ALU.add, R=[Bp, Bmod[layer]], W=[BhT[t][0]])
                        else:
                            P.v("act", "activation", hT[:, kc, t * 128:(t + 1) * 128], pb[:, q * 128:(q + 1) * 128],
                                AF.Identity, scale=alpha_col(layer, vA, kc, c), bias=alpha_col(layer, vB, kc, c),
                                R=[Bp, Bmod[layer]], W=[BhT[t][1]])

        PBK = [(psum[i], PB[i]) for i in range(8)]

        def store_x(label):
            for t in range(NT):
                P.dma("sp", y_d[t * 128:(t + 1) * 128, :], x_sb[:, t, :], reads=[Bx[t]])

        if stop == "ada":
            store_x("ada")
            P.end_phase()
            return nc

        with ExitStack() as ph:
            bc = [sbuf(ph, "bc%d" % i, [128, D]) for i in range(8)]
            Bbc = [Buf("bc%d" % i) for i in range(8)]
            psc = sbuf(ph, "psc", [128, D])
            Bpsc = Buf("psc")
            dgall = sbuf(ph, "dgall", [128, 8, 128])
            Bdg = Buf("dgall")
            h_tok = sbuf(ph, "h_tok", [128, NT, D])
            Bh = [Buf("h%d" % t) for t in range(NT)]
            ptab = sbuf(ph, "ptab", [128, 4, 5, 128])
            Bptab = Buf("ptab")
            pw = sbuf(ph, "pw", [128, 4, 2, 256], BF16)
            Bpw = Buf("pw")
            pTt = [sbuf(ph, "pTt%d" % i, [128, 8, 128], BF16) for i in range(2)]
            BpTt = [Buf("pTt0"), Buf("pTt1")]
            ybuf = [sbuf(ph, "ybuf%d" % i, [128, D]) for i in range(4)]
            By = [Buf("y%d" % i) for i in range(4)]
            wa1 = sbuf(ph, "wa1", [128, 8, 512], BF16)
            Bwa1 = Buf("wa1")

            P.dma("sp", ptab[:], ptab_d.rearrange("w k s t -> s w k t"), writes=[Bptab])
            P.dma("pool", pw[:], pool_w_d.rearrange("g (i c) d -> c g i d", c=128), writes=[Bpw])
            load_bc(psc[:], pool_scale_d[0:1, :], [Bpsc])
            load_bc(bc[6][:], ln1g_d[0:1, :], [Bbc[6]])
            load_bc(bc[7][:], ln1b_d[0:1, :], [Bbc[7]])
            for c in range(2):
                make_bc(bc[2 * c], 0, 1, c, dgall, Bdg, PBK[0:2], [Bbc[2 * c]])
                make_bc(bc[2 * c + 1], 0, 0, c, dgall, Bdg, PBK[2:4], [Bbc[2 * c + 1]])
                make_bc(bc[4 + c], 0, 2, c, dgall, Bdg, PBK[4:6], [Bbc[4 + c]])
                P.v("pool", "tensor_tensor", bc[4 + c][:], bc[4 + c][:], psc[:], ALU.mult,
                    R=[Bbc[4 + c], Bpsc], W=[Bbc[4 + c]])
            for t in range(NT):
                c = tile_cond(t)
                P.v("dve", "tensor_tensor", h_tok[:, t, :], x_sb[:, t, :], bc[2 * c][:], ALU.mult,
                    R=[Bx[t], Bbc[2 * c]], W=[Bh[t]])
                P.v("dve", "tensor_tensor", h_tok[:, t, :], h_tok[:, t, :], bc[2 * c + 1][:], ALU.add,
                    R=[Bh[t], Bbc[2 * c + 1]], W=[Bh[t]])
            seq_of = {}
            for (t0, ntl) in SEQS:
                for q in range(ntl):
                    seq_of[t0 + q] = (q, ntl)
            for grp in range(3):
                tiles = list(range(4 * grp, 4 * grp + 4))
                for i, t in enumerate(tiles):
                    q, ntl = seq_of[t]
                    c = tile_cond(t)
                    k = i % 2
                    (pA, BA), (pBk, BB) = PBK[2 * k], PBK[2 * k + 1]
                    kind = 0 if q == 0 else (2 if q == ntl - 1 else 1)
                    srcs = [(t, kind)]
                    if q > 0:
                        srcs.append((t - 1, 3))
                    if q < ntl - 1:
                        srcs.append((t + 1, 4))
                    for kc in range(8):
                        wi = kc // 2
                        bank, Bb = (pA, BA) if kc < 4 else (pBk, BB)
                        dst = bank[:, (kc % 4) * 128:(kc % 4 + 1) * 128]
                        for si, (ts, kd) in enumerate(srcs):
                            P.mm(dst, h_tok[:, ts, kc * 128:(kc + 1) * 128], ptab[:, wi, kd, :],
                                 si == 0, si == len(srcs) - 1, reads=[Bh[ts], Bptab], writes=[Bb])
                    P.v("act", "copy", pTt[k][:, 0:4, :], pA[:].rearrange("p (a b) -> p a b", b=128),
                        R=[BA], W=[BpTt[k]])
                    P.v("act", "copy", pTt[k][:, 4:8, :], pBk[:].rearrange("p (a b) -> p a b", b=128),
                        R=[BB], W=[BpTt[k]])
                    for half in range(2):
                        bank, Bb = (pA, BA) if half == 0 else (pBk, BB)
                        for gg in range(2):
                            g = half * 2 + gg
                            for i2 in range(2):
                                P.mm(bank[:, gg * 256:(gg + 1) * 256], pTt[k][:, 2 * g + i2, :], pw[:, g, i2, :],
                                     i2 == 0, i2 == 1, reads=[BpTt[k], Bpw], writes=[Bb])
                        P.v("dve", "tensor_tensor", ybuf[i][:, half * 512:(half + 1) * 512], bank[:],
                            bc[4 + c][:, half * 512:(half + 1) * 512], ALU.mult, R=[Bb, Bbc[4 + c]], W=[By[i]])
                    P.v("dve", "scalar_tensor_tensor", ybuf[i][:], x_sb[:, t, :], ALPHA, ybuf[i][:],
                        op0=ALU.mult, op1=ALU.add, R=[Bx[t], By[i]], W=[By[i]])
                    ada_block(1, t, wa1, Bwa1, PBK[6 + t % 2])
                ln_group(tiles, [ybuf[i][:] for i in range(4)], By, bc[6][:], bc[7][:], [Bbc[6], Bbc[7]],
                         (0, 4, 3), PBK[4:8], "pool")
            if stop == "l0mix":
                store_x("l0mix")
            P.end_phase()
        if stop == "l0mix":
            return nc


        def moe_phase(layer, next_mod, final):
            with ExitStack() as ph:
                bc = [sbuf(ph, "bc%d" % i, [128, D]) for i in range(4)]
                Bbc = [Buf("mbc%d" % i) for i in range(4)]
                dgall = sbuf(ph, "dgall", [128, 8, 128])
                Bdg = Buf("dgall")
                wg = [sbuf(ph, "wg%d" % i, [128, 8, 512], BF16) for i in range(2)]
                wu = [sbuf(ph, "wu%d" % i, [128, 8, 512], BF16) for i in range(2)]
                wd = [sbuf(ph, "wd%d" % i, [128, 4, D], BF16) for i in range(2)]
                Bwg = [Buf("wg0"), Buf("wg1")]
                Bwu = [Buf("wu0"), Buf("wu1")]
                Bwd = [Buf("wd0"), Buf("wd1")]
                heT = [sbuf(ph, "heT%d" % i, [128, 4, 512], BF16) for i in range(2)]
                Bhe = [Buf("he0"), Buf("he1")]
                sg = [sbuf(ph, "sg%d" % i, [128, 512], BF16) for i in range(2)]
                Bsg = [Buf("sg0"), Buf("sg1")]
                yacc = sbuf(ph, "yacc", [128, NT, D])
                Byacc = [Buf("yacc%d" % t) for t in range(NT)]
                wr_b = sbuf(ph, "wr_b", [128, 8, 16], BF16)
                brt = sbuf(ph, "brt", [128, 16])
                Bwr = Buf("wr")
                Bbrt = Buf("brt")
                rt = sbuf(ph, "rt", [128, 1472])
                Brt = Buf("rt")
                G = [psum[0], psum[1]]
                U = [psum[2], psum[3]]
                Dn = [psum[4], psum[5], psum[6], psum[7]]
                BG, BU, BD = [PB[0], PB[1]], [PB[2], PB[3]], [PB[4], PB[5], PB[6], PB[7]]

                def load_expert(e):
                    sl = e % 2
                    P.dma("pool", wg[sl][:], w_gate_d[layer, e].rearrange("(kc k) f -> k kc f", k=128), writes=[Bwg[sl]])
                    P.dma("pool", wu[sl][:], w_up_d[layer, e].rearrange("(kc k) f -> k kc f", k=128), writes=[Bwu[sl]])
                    P.dma("pool", wd[sl][:], w_down_d[layer, e].rearrange("(fc f) d -> f fc d", f=128), writes=[Bwd[sl]])

                P.dma("pool", wr_b[:], w_router_d.rearrange("(kc k) e -> k kc e", k=128), writes=[Bwr])
                load_expert(0)
                load_expert(1)
                P.dma("sp", brt[:], b_router_d[0:1, :].partition_broadcast(128), writes=[Bbrt])
                load_bc(bc[2][:], ln2g_d[layer:layer + 1, :], [Bbc[2]])
                load_bc(bc[3][:], ln2b_d[layer:layer + 1, :], [Bbc[3]])
                for c in range(2):
                    make_bc(bc[c], layer, 5, c, dgall, Bdg, PBK[6:8], [Bbc[c]])

                lg = psum[4]
                for t in range(NT):
                    for kc in range(8):
                        P.mm(lg[:, t * 16:(t + 1) * 16], hT[:, kc, t * 128:(t + 1) * 128], wr_b[:, kc, :],
                             kc == 0, kc == 7, reads=[BhT[t][0], BhT[t][1], Bwr], writes=[PB[4]])
                o = [0]

                def carve(nel):
                    v_ = rt[:, o[0]:o[0] + nel]
                    o[0] += nel
                    return v_
                s_sb, sel, t6, gs, gm = carve(192), carve(192), carve(288), carve(48), carve(12)
                gmask, msel, top8, emask, wsel, wsum = carve(48), carve(192), carve(96), carve(192), carve(192), carve(12)
                Rr, Wr = [Brt], [Brt]
                P.v("act", "activation", s_sb, lg[:, 0:192], AF.Sigmoid, R=[PB[4]], W=Wr)
                P.v("dve", "tensor_tensor", sel.rearrange("p (t e) -> p t e", e=16),
                    s_sb.rearrange("p (t e) -> p t e", e=16), brt[:].unsqueeze(1).to_broadcast([128, NT, 16]),
                    ALU.add, R=Rr + [Bbrt], W=Wr)
                sel4 = sel.rearrange("p (g e) -> p g e", e=4)
                t6v = t6.rearrange("p (g e) -> p g e", e=6)
                P.v("dve", "tensor_tensor", t6v[:, :, 0:3], sel4[:, :, 0:3], sel4[:, :, 1:4], ALU.add, R=Rr, W=Wr)
                P.v("dve", "tensor_tensor", t6v[:, :, 3:5], sel4[:, :, 0:2], sel4[:, :, 2:4], ALU.add, R=Rr, W=Wr)
                P.v("dve", "tensor_tensor", t6v[:, :, 5:6], sel4[:, :, 0:1], sel4[:, :, 3:4], ALU.add, R=Rr, W=Wr)
                P.v("dve", "tensor_reduce", gs, t6v, AX.X, ALU.max, R=Rr, W=Wr)
                gs3 = gs.rearrange("p (t g) -> p t g", g=4)
                P.v("dve", "tensor_reduce", gm, gs3, AX.X, ALU.max, R=Rr, W=Wr)
                P.v("dve", "tensor_tensor", gmask.rearrange("p (t g) -> p t g", g=4), gs3,
                    gm.unsqueeze(2).to_broadcast([128, NT, 4]), ALU.is_equal, R=Rr, W=Wr)
                P.v("dve", "scalar_tensor_tensor", msel.rearrange("p (g e) -> p g e", e=4), sel4, 2.0,
                    gmask.unsqueeze(2).to_broadcast([128, 48, 4]), op0=ALU.add, op1=ALU.mult, R=Rr, W=Wr)
                top83 = top8.rearrange("p (t e) -> p t e", e=8)
                for t in range(NT):
                    P.v("dve", "max", top83[:, t, :], msel[:, t * 16:(t + 1) * 16], R=Rr, W=Wr)
                P.v("dve", "tensor_tensor", emask.rearrange("p (t e) -> p t e", e=16),
                    msel.rearrange("p (t e) -> p t e", e=16), top83[:, :, 1:2].to_broadcast([128, NT, 16]),
                    ALU.is_ge, R=Rr, W=Wr)
                P.v("dve", "tensor_tensor", wsel, s_sb, emask, ALU.mult, R=Rr, W=Wr)
                wsel3 = wsel.rearrange("p (t e) -> p t e", e=16)
                P.v("dve", "tensor_reduce", wsum, wsel3, AX.X, ALU.add, R=Rr, W=Wr)
                P.v("dve", "reciprocal", wsum, wsum, R=Rr, W=Wr)
                P.v("dve", "tensor_tensor", cmb[:], wsel3, wsum.unsqueeze(2).to_broadcast([128, NT, 16]),
                    ALU.mult, R=Rr, W=Bcmb)

                blocks = [(e, b) for e in range(16) for b in range(3)]
                hT_bufs = [[x for t in range(4 * b, 4 * b + 4) for x in BhT[t]] for b in range(3)]

                def GU(k):
                    e, b = blocks[k]
                    sl = e % 2
                    for fc in range(4):
                        gi = (k * 4 + fc) % 2
                        for kc in range(8):
                            P.mm(G[gi][:], wg[sl][:, kc, fc * 128:(fc + 1) * 128], hT[:, kc, b * 512:(b + 1) * 512],
                                 kc == 0, kc == 7, reads=[Bwg[sl]] + hT_bufs[b], writes=[BG[gi]])
                        for kc in range(8):
                            P.mm(U[gi][:], wu[sl][:, kc, fc * 128:(fc + 1) * 128], hT[:, kc, b * 512:(b + 1) * 512],
                                 kc == 0, kc == 7, reads=[Bwu[sl]] + hT_bufs[b], writes=[BU[gi]])
                        P.v("act", "activation", sg[gi][:], G[gi][:], AF.Silu, R=[BG[gi]], W=[Bsg[gi]])
                        P.v("dve", "tensor_tensor", heT[k % 2][:, fc, :], U[gi][:], sg[gi][:], ALU.mult,
                            R=[BU[gi], Bsg[gi]], W=[Bhe[k % 2]])

                def DNp(k):
                    e, b = blocks[k]
                    sl = e % 2
                    for tt in range(4):
                        t = 4 * b + tt
                        for half in range(2):
                            di = (k * 8 + tt * 2 + half) % 4
                            for fc in range(4):
                                P.mm(Dn[di][:], heT[k % 2][:, fc, tt * 128:(tt + 1) * 128],
                                     wd[sl][:, fc, half * 512:(half + 1) * 512], fc == 0, fc == 3,
                                     reads=[Bhe[k % 2], Bwd[sl]], writes=[BD[di]])
                            dst = yacc[:, t, half * 512:(half + 1) * 512]
                            if e == 0:
                                P.v("dve", "tensor_scalar", dst, Dn[di][:], cmb[:, t, e:e + 1], None, op0=ALU.mult,
                                    R=[BD[di], Bcmb[t]], W=[Byacc[t]])
                            else:
                                P.v("dve", "scalar_tensor_tensor", dst, Dn[di][:], cmb[:, t, e:e + 1], dst,
                                    op0=ALU.mult, op1=ALU.add, R=[BD[di], Bcmb[t], Byacc[t]], W=[Byacc[t]])

                def pre(grp):
                    for t in range(4 * grp, 4 * grp + 4):
                        c = tile_cond(t)
                        ya = yacc[:, t, :]
                        P.v("dve", "tensor_tensor", ya, ya, bc[c][:], ALU.mult, R=[Byacc[t], Bbc[c]], W=[Byacc[t]])
                    for t in range(4 * grp, 4 * grp + 4):
                        ya = yacc[:, t, :]
                        P.v("dve", "scalar_tensor_tensor", ya, x_sb[:, t, :], ALPHA, ya, op0=ALU.mult, op1=ALU.add,
                            R=[Bx[t], Byacc[t]], W=[Byacc[t]])

                def epi(grp, part):
                    tiles = list(range(4 * grp, 4 * grp + 4))
                    ln_group(tiles, [yacc[:, t, :] for t in tiles], [Byacc[t] for t in tiles], bc[2][:], bc[3][:],
                             [Bbc[2], Bbc[3]], next_mod, PBK, "dve", part=part)

                GU(0)
                for k in range(len(blocks)):
                    if k + 1 < len(blocks):
                        GU(k + 1)
                    DNp(k)
                    e_, b_ = blocks[k]
                    if b_ == 2 and e_ + 2 < 16:
                        load_expert(e_ + 2)
                    if e_ == 15:
                        pre(b_)
                        epi(b_, "elem")
                for grp in range(3):
                    if next_mod is not None:
                        epi(grp, "tr")
                    if final:
                        for t in range(4 * grp, 4 * grp + 4):
                            P.dma("sp", y_d[t * 128:(t + 1) * 128, :], x_sb[:, t, :], reads=[Bx[t]])
                P.end_phase()

        moe_phase(0, (1, 1, 0), False)
        if stop == "l0moe":
            store_x("l0moe")
            P.end_phase()
            return nc


        with ExitStack() as pa:
            QT_s = sbuf(pa, "QT_s", [128, 8, 1024], BF16)
            KT_s = sbuf(pa, "KT_s", [128, 8, 1024], BF16)
            Vp_s = sbuf(pa, "Vp_s", [128, 8, 8, 192], BF16)
            O_CTXV, O_M01, O_RBB, O_E, O_ONES, RG_N = 4096, 10240, 12032, 13824, 19200, 19392
            Rg = sbuf(pa, "Rg", [128, RG_N], BF16)
            QT_p = Rg[:, 0:4096].rearrange("p (c t) -> p c t", c=8)
            KT_p = Rg[:, 4096:8192].rearrange("p (c t) -> p c t", c=8)
            Vp_p = Rg[:, 8192:14336].rearrange("p (m a e) -> p m a e", m=4, a=8)
            ctxKT = Rg[:, 0:4096].rearrange("p (c t) -> p c t", c=8)
            ctxVp = Rg[:, O_CTXV:O_M01].rearrange("p (m a e) -> p m a e", m=4, a=8)
            m01 = Rg[:, O_M01:O_RBB].rearrange("p (m q) -> p m q", q=128)
            rbb = Rg[:, O_RBB:O_E].rearrange("p (b d q) -> p b d q", b=2, d=7)
            Etab = Rg[:, O_E:O_ONES].rearrange("p (b d q) -> p b d q", b=3, d=14)
            onesp = Rg[:, O_ONES:RG_N]
            BQs, BKs, BVs = Buf("QTs"), Buf("KTs"), Buf("Vs")
            BQp, BKp, BVp = Buf("QTp"), Buf("KTp"), Buf("Vp")
            Bones = Buf("onesp")

            with ExitStack() as ph:
                wb = [sbuf(ph, "wqkv%d" % i, [128, 8, 512], BF16) for i in range(2)]
                Bwb = [Buf("wb0"), Buf("wb1")]
                stg = [sbuf(ph, "stg%d" % i, [128, 512]) for i in range(2)]
                Bstg = [Buf("stg0"), Buf("stg1")]
                P.v("pool", "memset", Vp_s[:, :, :, 64:128], 0.0, W=[BVs])
                P.v("pool", "memset", Vp_p[:, :, :, 64:128], 0.0, W=[BVp])
                P.v("pool", "memset", onesp[:, 0:64], 1.0, W=[Bones])
                P.v("pool", "memset", onesp[:, 64:128], 0.0, W=[Bones])
                P.v("pool", "memset", onesp[:, 128:192], 1.0, W=[Bones])
                nb = [0]
                ns = [0]

                def bank():
                    i = nb[0] % 8
                    nb[0] += 1
                    return psum[i], PB[i], ("dve" if i % 2 == 0 else "act")

                def evac(eng, dst, src, Bsrc, W, mul=None):
                    if eng == "dve":
                        if mul is None:
                            P.v("dve", "tensor_copy", dst, src, R=[Bsrc], W=W)
                        else:
                            P.v("dve", "tensor_scalar_mul", dst, src, mul, R=[Bsrc], W=W)
                    else:
                        if mul is None:
                            P.v("act", "copy", dst, src, R=[Bsrc], W=W)
                        else:
                            P.v("act", "mul", dst, src, mul, R=[Bsrc], W=W)

                hTb = lambda tiles: [x for t in tiles for x in BhT[t]]
                for cb in range(6):
                    sl = cb % 2
                    P.dma("pool", wb[sl][:], w_qkv_d.rearrange("(kc k) n -> k kc n", k=128)[:, :, cb * 512:(cb + 1) * 512],
                          writes=[Bwb[sl]])
                    if cb < 4:
                        isq = cb < 2
                        for i in range(4):
                            a = (cb % 2) * 4 + i
                            for tb in range(3):
                                pb, Bp, eng = bank()
                                for kc in range(8):
                                    P.mm(pb[:], wb[sl][:, kc, i * 128:(i + 1) * 128], hT[:, kc, tb * 512:(tb + 1) * 512],
                                         kc == 0, kc == 7, reads=[Bwb[sl]] + hTb(range(4 * tb, 4 * tb + 4)), writes=[Bp])
                                if tb == 0:
                                    dst = (QT_p if isq else KT_p)[:, a, :]
                                    W = [BQp if isq else BKp]
                                else:
                                    dst = (QT_s if isq else KT_s)[:, a, (tb - 1) * 512:tb * 512]
                                    W = [BQs if isq else BKs]
                                evac(eng, dst, pb[:], Bp, W, mul=(0.125 if isq else None))
                    if cb in (2, 3, 4, 5):
                        isv = cb >= 4
                        hh = cb % 2
                        for t in (range(NT) if isv else range(4)):
                            pb, Bp, eng = bank()
                            for kc in range(8):
                                P.mm(pb[:], hT[:, kc, t * 128:(t + 1) * 128], wb[sl][:, kc, :], kc == 0, kc == 7,
                                     reads=[Bwb[sl]] + hTb([t]), writes=[Bp])
                            if isv:
                                src4 = pb[:].rearrange("p (a two d) -> p a two d", two=2, d=64)
                                for par in range(2):
                                    if t < 4:
                                        evac(eng, Vp_p[:, t, hh * 4:(hh + 1) * 4, par * 128:par * 128 + 64],
                                             src4[:, :, par, :], Bp, [BVp])
                                    else:
                                        evac(eng, Vp_s[:, t - 4, hh * 4:(hh + 1) * 4, par * 128:par * 128 + 64],
                                             src4[:, :, par, :], Bp, [BVs])
                            if t < 4:
                                k = ns[0] % 2
                                ns[0] += 1
                                evac("dve" if eng == "act" else "act", stg[k][:], pb[:], Bp, [Bstg[k]])
                                dd = nv_d if isv else nk_d
                                s_, l0 = t // 2, (t % 2) * 128
                                P.dma("sp", dd[s_, hh * 8:(hh + 1) * 8, l0:l0 + 128, :].rearrange("h l d -> l h d"),
                                      stg[k][:].rearrange("p (h d) -> p h d", d=64), reads=[Bstg[k]])
                P.end_phase()

            with ExitStack() as pb_:
                OT_all = sbuf(pb_, "OT_all", [128, 8, 1024], BF16)
                BOT = [Buf("OT%d" % a) for a in range(8)]
                wo = sbuf(pb_, "wo", [128, 8, D], BF16)
                Bwo = Buf("wo")
                By = [Buf("ay%d" % i) for i in range(4)]
                ST = [PBK[i] for i in range(4)]
                ACC = [(PBK[4], PBK[5]), (PBK[6], PBK[7])]
                cn = {"st": 0, "pt": 0, "em": 0}

                def heads_scope(scope):
                    PT = [sbuf(scope, "PT%d" % i, [128, 512], BF16)[:] for i in range(4)]
                    PT += [hT[:, kc, 512:1024] for kc in range(4)]
                    BPT = [Buf("PT%d" % i) for i in range(8)]
                    rS = sbuf(scope, "rS", [128, 512])
                    BrS = Buf("rS")
                    return PT, BPT, rS, BrS

                LOOKAHEAD = 6
                pend = []

                def drain(keep):
                    while len(pend) > keep:
                        pend.pop(0)()

                def segment(PT, BPT, Kst, Qmv, nq, Bk, Bq, esl, BE_, Vblk, BV_, ones_blk, acc, c0, hold=False):
                    st, Bst = ST[cn["st"] % 4]
                    cn["st"] += 1
                    k = cn["pt"] % 8
                    cn["pt"] += 1
                    P.mm(st[:, 0:nq], Kst, Qmv, True, True, reads=[Bk, Bq], writes=[Bst])
                    P.v("act", "activation", PT[k][:, 0:nq], st[:, 0:nq], AF.Exp, R=[Bst], W=[BPT[k]])
                    if esl is not None:
                        P.v("dve", "tensor_tensor", PT[k][:, 0:nq], PT[k][:, 0:nq], esl, ALU.mult,
                            R=[BPT[k], BE_], W=[BPT[k]])
                    (pO, BO_), (pS, BS_) = acc

                    def back():
                        P.mm(pO[:, c0:c0 + nq], Vblk, PT[k][:, 0:nq], False, False, reads=[BPT[k], BV_], writes=[BO_],
                             skip=True)
                        P.mm(pS[:, c0:c0 + nq], ones_blk, PT[k][:, 0:nq], False, False, reads=[BPT[k], Bones],
                             writes=[BS_], skip=True)
                    pend.append(back)
                    if not hold:
                        drain(LOOKAHEAD)

                def zero_acc(acc, ncols):
                    drain(5)
                    (pO, BO_), (pS, BS_) = acc
                    P.v("dve", "memset", pO[:, 0:ncols], 0.0, W=[BO_])
                    P.v("dve", "memset", pS[:, 0:ncols], 0.0, W=[BS_])

                def normalize(acc, ncols, rS, BrS, dst, Wd):
                    (pO, BO_), (pS, BS_) = acc

                    def fin():
                        P.v("dve", "reciprocal", rS[:, 0:ncols], pS[:, 0:ncols], R=[BS_], W=[BrS])
                        P.v("dve", "tensor_tensor", dst, pO[:, 0:ncols], rS[:, 0:ncols], ALU.mult, R=[BO_, BrS], W=Wd)
                    pend.append(fin)

                def proj_group(tiles, tcols, ybufs, deadW, bcv, Bbcv):
                    for i, t in enumerate(tiles):
                        tc = tcols[i]
                        for half in range(2):
                            pb, Bp = PBK[2 * (i % 2) + half]
                            for kc in range(8):
                                P.mm(pb[:], OT_all[:, kc, tc:tc + 128], wo[:, kc, half * 512:(half + 1) * 512],
                                     kc == 0, kc == 7, reads=[BOT[kc], Bwo], writes=[Bp])
                            P.v("dve", "tensor_tensor", ybufs[i][:, half * 512:(half + 1) * 512], pb[:],
                                bcv[0][:, half * 512:(half + 1) * 512], ALU.mult, R=[Bp, Bbcv[0]],
                                W=[By[i]] + list(deadW))
                        P.v("dve", "scalar_tensor_tensor", ybufs[i], x_sb[:, t, :], ALPHA, ybufs[i],
                            op0=ALU.mult, op1=ALU.add, R=[Bx[t], By[i]], W=[By[i]])
                    ln_group(tiles, ybufs, By, bcv[1], bcv[2], [Bbcv[1], Bbcv[2]], (1, 4, 3), PBK[4:8], "pool")

                def prep_vectors(cidx, bcv, Bbcv, dgall_v, Bdg_v, deadW):
                    load_bc(bcv[1], ln1g_d[1:2, :], [Bbcv[1]] + list(deadW))
                    load_bc(bcv[2], ln1b_d[1:2, :], [Bbcv[2]] + list(deadW))
                    make_bc(bcv[0], 1, 2, cidx, dgall_v, Bdg_v, PBK[6:8], [Bbcv[0]] + list(deadW))

                with ExitStack() as ph:
                    PT, BPT, rS, BrS = heads_scope(ph)
                    P.dma("pool", wo[:], w_o_d.rearrange("(kc k) n -> k kc n", k=128), writes=[Bwo])
                    for a in range(8):
                        for s_ in range(2):
                            acc = ACC[(2 * a + s_) % 2]
                            zero_acc(acc, 256)
                            for c in range(2):
                                for rho in range(2):
                                    pr = slice(64 * rho, 64 * rho + 64)
                                    vc = slice(64 * rho, 64 * rho + 128)
                                    k0 = s_ * 256 + c * 128
                                    segment(PT, BPT, KT_p[pr, a, k0:k0 + 128], QT_p[pr, a, s_ * 256:(s_ + 1) * 256], 256,
                                            BKp, BQp, None, None, Vp_p[:, 2 * s_ + c, a, vc], BVp, onesp[:, vc], acc, 0, hold=(rho == 0))
                            normalize(acc, 256, rS, BrS, OT_all[:, a, s_ * 256:(s_ + 1) * 256], [BOT[a]])
                    drain(0)
                    P.end_phase()

                yb_p = [Rg[:, 2048 * i:2048 * (i + 1)].bitcast(F32) for i in range(4)]
                bc_p = [Rg[:, 8192 + 2048 * i:8192 + 2048 * (i + 1)].bitcast(F32) for i in range(3)]
                dg_p = Rg[:, 14336:16384].bitcast(F32).rearrange("p (a b) -> p a b", b=128)
                Bbc_p = [Buf("pbc%d" % i) for i in range(3)]
                prep_vectors(0, bc_p, Bbc_p, dg_p, Buf("pdg"), [])
                proj_group([0, 1, 2, 3], [0, 128, 256, 384], yb_p, [], bc_p, Bbc_p)
                if stop == "l1mixp":
                    store_x("l1mixp")
                P.end_phase()
                if stop == "l1mixp":
                    return nc

                with ExitStack() as ph:
                    PT, BPT, rS, BrS = heads_scope(ph)
                    ck_tok = OT_all[:, 0:4, :].rearrange("p c (h d) -> p c h d", d=64)
                    Bck = Buf("ck_tok")
                    BctxK, BctxV, Bm01 = Buf("ctxK"), Buf("ctxV"), Buf("m01")
                    Brbb = [Buf("rbb0"), Buf("rbb1")]
                    BE = [Buf("E0"), Buf("E1"), Buf("E2")]
                    for c in range(4):
                        P.dma("pool", ck_tok[:, c, :, :], ck_d[:, c * 128:(c + 1) * 128, :].rearrange("h l d -> l h d"),
                              writes=[Bck])
                    P.dma("pool", m01[:], m01_d.rearrange("m k q -> k m q"), writes=[Bm01])
                    P.v("pool", "memset", ctxVp[:, :, :, 64:128], 0.0, W=[BctxV])
                    cv4 = cv_d.rearrange("(a two) l d -> two l a d", two=2)
                    for c in range(4):
                        for par in range(2):
                            P.dma("pool", ctxVp[:, c, :, par * 128:par * 128 + 64], cv4[par, c * 128:(c + 1) * 128, :, :],
                                  writes=[BctxV])

                    def build_E(h):
                        hb, eb = h % 2, h % 3
                        P.dma("pool", rbb[:, hb, :, :], rbt_d[h].rearrange("d k q -> k d q"), writes=[Brbb[hb]])
                        P.v("act", "activation", rbb[:, hb, :, :], rbb[:, hb, :, :], AF.Exp, R=[Brbb[hb]], W=[Brbb[hb]])
                        for s2 in range(2):
                            P.v("dve", "tensor_tensor", Etab[:, eb, s2 * 7:(s2 + 1) * 7, :], rbb[:, hb, :, :],
                                m01[:, s2 * 7:(s2 + 1) * 7, :], ALU.mult, R=[Brbb[hb], Bm01], W=[BE[eb]])

                    for c in range(4):
                        pTb = psum[c % 2][:].bitcast(BF16)
                        for a in range(8):
                            P.tr(pTb[:, a * 128:(a + 1) * 128], ck_tok[:, c, 2 * a:2 * a + 2, :].rearrange("p h d -> p (h d)"),
                                 ident_b[:], reads=[Bck, Bconst], writes=[PB[c % 2]])
                        P.v("dve" if c % 2 == 0 else "act", "tensor_copy" if c % 2 == 0 else "copy",
                            ctxKT[:, :, c * 128:(c + 1) * 128], pTb.rearrange("p (a t) -> p a t", a=8),
                            R=[PB[c % 2]], W=[BctxK])
                    build_E(0)
                    build_E(1)
                    build_E(2)
                    for a in range(8):
                        for qb in range(2):
                            acc = ACC[(2 * a + qb) % 2]
                            zero_acc(acc, 512)
                            for (m, jlo, cnt, e2) in _SEGS[qb]:
                                for rho in range(2):
                                    eb = (2 * a + rho) % 3
                                    pr = slice(64 * rho, 64 * rho + 64)
                                    vc = slice(64 * rho, 64 * rho + 128)
                                    nq, q0 = cnt * 128, jlo * 128
                                    i0 = e2 * 7 + 3 - m + jlo
                                    esl = Etab[:, eb, i0:i0 + cnt, :].rearrange("p a b -> p (a b)")
                                    segment(PT, BPT, KT_s[pr, a, m * 128:(m + 1) * 128], QT_s[pr, a, q0:q0 + nq], nq,
                                            BKs, BQs, esl, BE[eb], Vp_s[:, m, a, vc], BVs, onesp[:, vc], acc, q0 - 512 * qb, hold=(rho == 0))
                            for c in range(4):
                                for rho in range(2):
                                    pr = slice(64 * rho, 64 * rho + 64)
                                    vc = slice(64 * rho, 64 * rho + 128)
                                    segment(PT, BPT, ctxKT[pr, a, c * 128:(c + 1) * 128], QT_s[pr, a, qb * 512:(qb + 1) * 512],
                                            512, BctxK, BQs, None, None, ctxVp[:, c, a, vc], BctxV, onesp[:, vc], acc, 0, hold=(rho == 0))
                            normalize(acc, 512, rS, BrS, OT_all[:, a, qb * 512:(qb + 1) * 512], [BOT[a], Bck])
                        for h2 in (2 * a + 3, 2 * a + 4):
                            if h2 < 16:
                                build_E(h2)
                    drain(0)
                    P.end_phase()

                yb_s = [QT_s[:, 2 * i:2 * i + 2, :].rearrange("p a t -> p (a t)").bitcast(F32) for i in range(4)]
                bc_s = [KT_s[:, 2 * i:2 * i + 2, :].rearrange("p a t -> p (a t)").bitcast(F32) for i in range(3)]
                dg_s = KT_s[:, 6:8, :].rearrange("p a t -> p (a t)").bitcast(F32).rearrange("p (a b) -> p a b", b=128)
                Bbc_s = [Buf("sbc%d" % i) for i in range(3)]
                prep_vectors(1, bc_s, Bbc_s, dg_s, Buf("sdg"), [])
                proj_group([4, 5, 6, 7], [0, 128, 256, 384], yb_s, [], bc_s, Bbc_s)
                proj_group([8, 9, 10, 11], [512, 640, 768, 896], yb_s, [], bc_s, Bbc_s)
                if stop == "l1mix":
                    store_x("l1mix")
                P.end_phase()
        if stop == "l1mix":
            return nc

        moe_phase(1, None, True)
        return nc


_CACHE = {}


def make_in_maps(inputs):
    f = lambda a: np.ascontiguousarray(np.asarray(a, dtype=np.float32))
    x_prompt, x_sample = f(inputs["x_prompt"]), f(inputs["x_sample"])
    cache_k, cache_v = f(inputs["cache_k"]), f(inputs["cache_v"])
    c, c_ctx = f(inputs["c"]), f(inputs["c_ctx"])
    b_ada = f(inputs["b_ada"])
    shared = {
        "w_ada": f(inputs["w_ada"]),
        "b_adaT": np.ascontiguousarray(b_ada.reshape(2, 48, 128).transpose(0, 2, 1)),
        "ln1_g": f(inputs["ln1_g"]), "ln1_b": f(inputs["ln1_b"]),
        "ln2_g": f(inputs["ln2_g"]), "ln2_b": f(inputs["ln2_b"]),
        "pool_w": f(inputs["pool_w"])[0],
        "pool_scale": f(inputs["pool_scale"]).reshape(1, D),
        "w_qkv": f(inputs["w_qkv"])[0],
        "w_o": f(inputs["w_o"])[0],
        "rbt": _rel_bias_layout(f(inputs["rel_bias"])[0]),
        "w_router": f(inputs["w_router"]),
        "b_router": f(inputs["b_router"]).reshape(1, 16),
        "w_gate": f(inputs["w_gate"]), "w_up": f(inputs["w_up"]), "w_down": f(inputs["w_down"]),
        "ptab": _pool_tables(),
        "m01": _mask01_table(),
    }
    in_maps = []
    for i in range(N_CORES):
        m = dict(shared)
        m["xin"] = np.ascontiguousarray(np.concatenate(
            [x_prompt[2 * i].reshape(256, D), x_prompt[2 * i + 1].reshape(256, D), x_sample[i]], axis=0))
        m["ck"] = np.ascontiguousarray(cache_k[i, 0])
        m["cv"] = np.ascontiguousarray(cache_v[i, 0])
        cond = np.stack([c_ctx, c[i]], axis=-1)
        m["condT"] = np.ascontiguousarray(cond.reshape(8, 128, 2).transpose(1, 0, 2))
        in_maps.append(m)
    return in_maps


def kernel(**inputs):
    if "nc" not in _CACHE:
        _CACHE["nc"] = build_nc()
    nc = _CACHE["nc"]
    in_maps = make_in_maps(inputs)
    res = run_bass_kernel_spmd(nc, in_maps, core_ids=list(range(N_CORES)))
    ys = [r["y"] for r in res.results]
    y_prompt = np.stack([y[:512].reshape(2, 256, D) for y in ys]).reshape(16, 256, D)
    y_sample = np.stack([y[512:] for y in ys])
    nk = np.stack([r["nk"] for r in res.results]).reshape(16, 1, 16, 256, 64)
    nv = np.stack([r["nv"] for r in res.results]).reshape(16, 1, 16, 256, 64)
    return (y_prompt.astype(np.float32), y_sample.astype(np.float32),
            nk.astype(np.float32), nv.astype(np.float32))
```

```python
from contextlib import ExitStack

import numpy as np
import concourse.bass as bass
import concourse.mybir as mybir
from concourse.bass_utils import run_bass_kernel_spmd

F32 = mybir.dt.float32
BF16 = mybir.dt.bfloat16
AF = mybir.ActivationFunctionType
ALU = mybir.AluOpType
AX = mybir.AxisListType

N_CORES = 8
D = 1024
NT = 12
T = NT * 128
ALPHA = 4.0 ** 0.25
LN_EPS = 1e-5
NEG = -30000.0
POOL_SIZES = (2, 4, 8, 16)
SEQS = ((0, 2), (2, 2), (4, 8))


def tile_cond(t):
    return 0 if t < 4 else 1


class Buf:
    __slots__ = ("name", "writer", "readers", "excl")

    def __init__(self, name="", excl=False):
        self.name = name
        self.writer = None
        self.readers = []
        self.excl = excl


class Lane:
    __slots__ = ("sem", "n", "last")

    def __init__(self, sem):
        self.sem = sem
        self.n = 0
        self.last = None


class Ins:
    __slots__ = ("eng", "fn", "deps", "signal", "count", "lane", "lval", "idx")

    def __init__(self, eng, fn):
        self.eng = eng
        self.fn = fn
        self.deps = []
        self.signal = False
        self.count = 0
        self.lane = None
        self.lval = 0
        self.idx = 0


class Prog:
    ENG = ("pe", "act", "dve", "pool", "sp")
    NLANES = {"sp": 24, "pool": 12, "act": 4}

    def __init__(self, nc, es):
        self.nc = nc
        self.es = es
        self.eng = {"pe": nc.tensor, "act": nc.scalar, "dve": nc.vector,
                    "pool": nc.gpsimd, "sp": nc.sync}
        self.sem = {e: es.enter_context(nc.semaphore("s_" + e)) for e in self.ENG}
        self.cnt = {e: 0 for e in self.ENG}
        self.bar = es.enter_context(nc.semaphore("s_bar"))
        self.nbar = 0
        self.lanes = {q: [Lane(es.enter_context(nc.semaphore("l_%s%d" % (q, i)))) for i in range(n)]
                      for q, n in self.NLANES.items()}
        self.lane_rr = {q: 0 for q in self.NLANES}
        self.waited = {e: {} for e in self.ENG}
        self.streams = {e: [] for e in self.ENG}
        self.touched = []
        self.n_ins = 0

    def _add_dep(self, ins, p):
        if p is None or p is ins:
            return
        if p.lane is None and ins.lane is None and p.eng == ins.eng and p.eng == "pe":
            return
        deps = ins.deps
        if p.lane is None:
            for i, q in enumerate(deps):
                if q.lane is None and q.eng == p.eng:
                    if p.idx > q.idx:
                        deps[i] = p
                    return
        elif p in deps:
            return
        deps.append(p)

    def _deps(self, ins, reads, writes):
        if any(b.excl for b in reads):
            writes = list(writes) + [b for b in reads if b.excl and b not in writes]
            reads = [b for b in reads if not b.excl]
        same = lambda p: (p is not None and p.lane is None and ins.lane is None and p.eng == ins.eng)
        for b in reads:
            self._add_dep(ins, b.writer)
        for b in writes:
            if not same(b.writer):
                self._add_dep(ins, b.writer)
            for r in b.readers:
                if not same(r):
                    self._add_dep(ins, r)
        for b in reads:
            if not b.readers and b.writer is None:
                self.touched.append(b)
            b.readers.append(ins)
        for b in writes:
            if not b.readers and b.writer is None:
                self.touched.append(b)
            b.writer = ins
            b.readers = []

    def op(self, eng, fn, reads=(), writes=()):
        ins = Ins(eng, fn)
        ins.idx = len(self.streams[eng])
        self._deps(ins, reads, writes)
        self.streams[eng].append(ins)
        return ins

    def dma(self, eng, out, in_, reads=(), writes=(), **kw):
        E = self.eng[eng]
        ins = Ins(eng, lambda: E.dma_start(out=out, in_=in_, **kw))
        ins.idx = len(self.streams[eng])
        lanes = self.lanes[eng]
        lane = lanes[self.lane_rr[eng] % len(lanes)]
        self.lane_rr[eng] += 1
        ins.lane = lane
        lane.n += 1
        ins.lval = 16 * lane.n
        if lane.last is not None:
            ins.deps.append(lane.last)
        lane.last = ins
        self._deps(ins, reads, writes)
        self.streams[eng].append(ins)
        return ins

    def mm(self, out, lhsT, rhs, start, stop, reads=(), writes=(), skip=False):
        t = self.nc.tensor
        if skip:
            return self.op("pe", lambda: t.matmul(out, lhsT, rhs, start=start, stop=stop, skip_group_check=True),
                           reads, writes)
        return self.op("pe", lambda: t.matmul(out, lhsT, rhs, start=start, stop=stop), reads, writes)

    def tr(self, out, in_, ident, reads=(), writes=()):
        t = self.nc.tensor
        return self.op("pe", lambda: t.transpose(out, in_, ident), reads, writes)

    def v(self, eng, meth, *args, R=(), W=(), **kw):
        E = self.eng[eng]
        return self.op(eng, lambda: getattr(E, meth)(*args, **kw), R, W)

    def end_phase(self):
        ENG = self.ENG
        for e in ENG:
            for ins in self.streams[e]:
                for p in ins.deps:
                    if p.lane is None:
                        p.signal = True
            for ins in reversed(self.streams[e]):
                if ins.lane is None:
                    ins.signal = True
                    break
        for e in ENG:
            for ins in self.streams[e]:
                if ins.lane is None and ins.signal:
                    self.cnt[e] += 1
                    ins.count = self.cnt[e]
        for e in ENG:
            E = self.eng[e]
            wd = self.waited[e]
            for ins in self.streams[e]:
                for p in ins.deps:
                    if p.lane is not None:
                        sem, val = p.lane.sem, p.lval
                    else:
                        sem, val = self.sem[p.eng], p.count
                    if wd.get(sem.num, 0) >= val:
                        continue
                    wd[sem.num] = val
                    E.wait_ge(sem, val)
                bi = ins.fn()
                if ins.lane is not None:
                    bi.then_inc(ins.lane.sem, 16)
                elif ins.signal:
                    bi.then_inc(self.sem[e], 1)
                self.n_ins += 1
        sp = self.eng["sp"]
        wd = self.waited["sp"]
        all_lanes = [l for q in self.lanes.values() for l in q]
        for e in ENG:
            if e == "sp" or self.cnt[e] == 0:
                continue
            if wd.get(self.sem[e].num, 0) < self.cnt[e]:
                sp.wait_ge(self.sem[e], self.cnt[e])
        for l in all_lanes:
            if l.n and wd.get(l.sem.num, 0) < 16 * l.n:
                sp.wait_ge(l.sem, 16 * l.n)
        self.nbar += 1
        sp.sem_inc(self.bar, 1)
        for e in ENG:
            if e != "sp":
                self.eng[e].wait_ge(self.bar, self.nbar)
            for e2 in ENG:
                self.waited[e][self.sem[e2].num] = self.cnt[e2]
            for l in all_lanes:
                self.waited[e][l.sem.num] = 16 * l.n
        for l in all_lanes:
            l.last = None
        for b in self.touched:
            b.writer = None
            b.readers = []
        self.touched = []
        self.streams = {e: [] for e in ENG}


def _pool_tables():
    n = 1024
    out = np.zeros((4, 5, 128, 128), np.float32)
    t = np.arange(n)
    for wi, w in enumerate(POOL_SIZES):
        lo = np.clip(t - w // 2, 0, n)
        hi = np.clip(t - w // 2 + w, 0, n)
        cnt = (hi - lo).astype(np.float64)
        s = np.arange(n)[:, None]
        Pm = ((s >= lo[None, :]) & (s < hi[None, :])) / cnt[None, :] - np.eye(n)
        Pm = Pm.astype(np.float32)
        out[wi, 0] = Pm[0:128, 0:128]
        out[wi, 1] = Pm[128:256, 128:256]
        out[wi, 2] = Pm[896:1024, 896:1024]
        out[wi, 3] = Pm[0:128, 128:256]
        out[wi, 4] = Pm[256:384, 128:256]
    return out


def _chunks_for(j):
    if j < 2:
        return list(range(0, 4))
    if j > 5:
        return list(range(4, 8))
    return list(range(j - 2, j + 3))


def _mask_tables():
    tiles = []
    index = {}
    keymap = {}
    col = np.arange(64)
    cs = np.clip(col - 8, 0, 48)
    colok = (col[None, :] >= cs[:, None]) & (col[None, :] < cs[:, None] + 16)
    for j in range(8):
        for m in _chunks_for(j):
            mt = np.full((2, 64, 2, 64), NEG, np.float32)
            for rq in range(2):
                r = 2 * j + rq
                rs = min(max(r - 4, 0), 8)
                for rk in range(2):
                    kr = 2 * m + rk
                    if rs <= kr < rs + 8:
                        mt[rk, :, rq, :] = np.where(colok.T, 0.0, NEG)
            key = mt.tobytes()
            if key not in keymap:
                keymap[key] = len(tiles)
                tiles.append(mt.reshape(128, 128))
            index[(j, m)] = keymap[key]
    return np.stack(tiles), index


_MASKS, _MASK_IDX = _mask_tables()
_M_FULL, _M_A, _M_B = _MASK_IDX[(0, 0)], _MASK_IDX[(2, 0)], _MASK_IDX[(2, 4)]


def _mask01_table():
    full = (_MASKS[_M_FULL] == 0).astype(np.float32)
    t = np.stack([full] * 14)
    t[7 + (3 - 2)] = (_MASKS[_M_B] == 0)
    t[7 + (3 + 2)] = (_MASKS[_M_A] == 0)
    return t


def _segments():
    segs = {0: [], 1: []}
    for m in range(8):
        for qb in range(2):
            js = [j for j in range(8) if m in _chunks_for(j) and j // 4 == qb]
            if not js:
                continue
            assert js == list(range(js[0], js[-1] + 1))
            e2 = any(_MASK_IDX[(j, m)] != _M_FULL for j in js)
            for j in js:
                want = _M_FULL
                if e2 and m - j == -2:
                    want = _M_A
                if e2 and m - j == 2:
                    want = _M_B
                assert _MASK_IDX[(j, m)] == want, (j, m)
            segs[qb].append((m, js[0], len(js), int(e2)))
    return segs


_SEGS = _segments()


def _rel_bias_layout(rel_bias):
    rk = np.arange(2)[:, None, None, None]
    ck = np.arange(64)[None, :, None, None]
    rq = np.arange(2)[None, None, :, None]
    cq = np.arange(64)[None, None, None, :]
    out = np.empty((16, 7, 128, 128), np.float32)
    for i in range(7):
        dl = 3 - i
        dy = np.clip(2 * dl + rk - rq + 7, 0, 14)
        dx = np.clip(ck - cq + 15, 0, 30)
        dy, dx = np.broadcast_arrays(dy, dx)
        out[:, i] = rel_bias[:, dy, dx].reshape(16, 128, 128)
    return out


def build_nc(stop=None):
    nc = bass.Bass("TRN2", target_bir_lowering=False)

    def din(name, shape):
        return nc.dram_tensor(name, list(shape), F32, kind="ExternalInput").ap()

    def dout(name, shape):
        return nc.dram_tensor(name, list(shape), F32, kind="ExternalOutput").ap()

    xin = din("xin", [T, D])
    ck_d = din("ck", [16, 512, 64])
    cv_d = din("cv", [16, 512, 64])
    condT_d = din("condT", [128, 8, 2])
    w_ada_d = din("w_ada", [2, D, 6 * D])
    b_adaT_d = din("b_adaT", [2, 128, 48])
    ln1g_d = din("ln1_g", [2, D])
    ln1b_d = din("ln1_b", [2, D])
    ln2g_d = din("ln2_g", [2, D])
    ln2b_d = din("ln2_b", [2, D])
    pool_w_d = din("pool_w", [4, 256, 256])
    pool_scale_d = din("pool_scale", [1, D])
    w_qkv_d = din("w_qkv", [D, 3 * D])
    w_o_d = din("w_o", [D, D])
    rbt_d = din("rbt", [16, 7, 128, 128])
    w_router_d = din("w_router", [D, 16])
    b_router_d = din("b_router", [1, 16])
    w_gate_d = din("w_gate", [2, 16, D, 512])
    w_up_d = din("w_up", [2, 16, D, 512])
    w_down_d = din("w_down", [2, 16, 512, D])
    ptab_d = din("ptab", [4, 5, 128, 128])
    m01_d = din("m01", [14, 128, 128])

    y_d = dout("y", [T, D])
    nk_d = dout("nk", [2, 16, 256, 64])
    nv_d = dout("nv", [2, 16, 256, 64])
    dbg_d = dout("dbg", [128, 2, 48, 2]) if stop == "ada" else None

    with ExitStack() as es:
        P = Prog(nc, es)

        _uid = [0]

        def sbuf(scope, name, shape, dt=F32):
            _uid[0] += 1
            return scope.enter_context(nc.sbuf_tensor("sb%d_%s" % (_uid[0], name), list(shape), dt))

        x_sb = sbuf(es, "x_sb", [128, NT, D])
        hT = sbuf(es, "hT", [128, 8, T], BF16)
        ident_f = sbuf(es, "ident_f", [128, 128])
        ident_b = sbuf(es, "ident_b", [128, 128], BF16)
        ones_f = sbuf(es, "ones_f", [128, 128])
        modcol = sbuf(es, "modcol", [128, 2, 48, 2])
        cmb = sbuf(es, "cmb", [128, NT, 16])
        small = sbuf(es, "small", [128, 64])
        psum = [es.enter_context(nc.psum_tensor("ps%d" % i, [128, 512], F32)) for i in range(8)]
        PB = [Buf("ps%d" % i, excl=True) for i in range(8)]

        Bx = [Buf("x%d" % t) for t in range(NT)]
        BhT = [(Buf("hTd%d" % t), Buf("hTa%d" % t)) for t in range(NT)]
        Bconst = Buf("const")
        Bmod = [Buf("modcol0"), Buf("modcol1")]
        sTb = sbuf(es, "sTb", [128, 8, 2], BF16)
        b_col = sbuf(es, "b_col", [128, 2, 48])
        BsT = Buf("sT")
        Bbcol = Buf("bcol")
        Bcmb = [Buf("cmb%d" % t) for t in range(NT)]


        def alpha_col(layer, v, ch, c):
            return modcol[:, layer, v * 8 + ch, c:c + 1]

        PBK = [(psum[i], PB[i]) for i in range(8)]

        def ada_block(layer, nb, wa_t, Bwa_t, bank):
            pc, Bpc = bank
            src = w_ada_d[layer].rearrange("(kc k) n -> k kc n", k=128)[:, :, nb * 512:(nb + 1) * 512]
            P.dma("pool", wa_t[:], src, writes=[Bwa_t])
            for n4 in range(4):
                for kc in range(8):
                    P.mm(pc[:, n4 * 2:n4 * 2 + 2], wa_t[:, kc, n4 * 128:(n4 + 1) * 128], sTb[:, kc, :],
                         kc == 0, kc == 7, reads=[Bwa_t, BsT], writes=[Bpc])
            dst = modcol[:, layer, nb * 4:(nb + 1) * 4, :]
            P.v("dve", "tensor_tensor", dst, pc[:, 0:8].rearrange("p (c k) -> p c k", k=2),
                b_col[:, layer, nb * 4:(nb + 1) * 4].unsqueeze(2).to_broadcast([128, 4, 2]), ALU.add,
                R=[Bpc, Bbcol], W=[Bmod[layer]])
            if nb // 2 in (1, 4):
                P.v("dve", "tensor_scalar_add", dst, dst, 1.0, R=[Bmod[layer]], W=[Bmod[layer]])

        with ExitStack() as ph:
            condT = sbuf(ph, "condT", [128, 8, 2])
            sT = sbuf(ph, "sT", [128, 8, 2])
            wa = [sbuf(ph, "wa%d" % i, [128, 8, 512], BF16) for i in range(3)]
            Bwa = [Buf("wa%d" % i) for i in range(3)]

            P.v("pool", "memset", ident_f[:], 0.0, W=[Bconst])
            P.v("pool", "affine_select", ident_f[:], ident_f[:], [[-1, 128]], ALU.not_equal, 1.0,
                base=0, channel_multiplier=1, R=[Bconst], W=[Bconst])
            P.v("pool", "tensor_copy", ident_b[:], ident_f[:], R=[Bconst], W=[Bconst])
            P.v("pool", "memset", ones_f[:], 1.0, W=[Bconst])

            for t in range(NT):
                P.dma("sp", x_sb[:, t, :], xin[t * 128:(t + 1) * 128, :], writes=[Bx[t]])
            P.dma("sp", condT[:], condT_d[:, :, :], writes=[BsT])
            P.dma("sp", b_col[:], b_adaT_d.rearrange("l p c -> p l c"), writes=[Bbcol])
            P.v("act", "activation", sT[:], condT[:], AF.Silu, R=[BsT], W=[BsT])
            P.v("dve", "tensor_copy", sTb[:], sT[:], R=[BsT], W=[BsT])

            for nb in range(12):
                ada_block(0, nb, wa[nb % 3], Bwa[nb % 3], PBK[nb % 2])
            if stop == "ada":
                for nb in range(12):
                    ada_block(1, nb, wa[nb % 3], Bwa[nb % 3], PBK[nb % 2])
            if stop == "ada":
                P.dma("sp", dbg_d[:, :, :, :], modcol[:], reads=Bmod)
            P.end_phase()

        def load_bc(dst, src_row, W):
            P.dma("sp", dst, src_row.partition_broadcast(128), writes=W)

        def make_bc(dst, layer, v, c, dgall, Bdg, banks, W):
            P.v("dve", "tensor_tensor", dgall[:], ident_f[:].unsqueeze(1).to_broadcast([128, 8, 128]),
                modcol[:, layer, v * 8:(v + 1) * 8, c:c + 1].to_broadcast([128, 8, 128]), ALU.mult,
                R=[Bconst, Bmod[layer]], W=[Bdg])
            for half in range(2):
                pb, Bp = banks[half]
                P.mm(pb[:], ones_f[:], dgall[:, half * 4:(half + 1) * 4, :].rearrange("p a b -> p (a b)"),
                     True, True, reads=[Bconst, Bdg], writes=[Bp])
                P.v("act", "copy", dst[:, half * 512:(half + 1) * 512], pb[:], R=[Bp], W=W)

        Bsmall = [Buf("mv%d" % i) for i in range(4)]

        def ln_group(tiles, ys, Bys, g_bc, b_bc, Bvec, next_mod, banks, gmul_eng, part="all"):
            n = len(tiles)
            sts = [small[:, 16 * i:16 * i + 12].rearrange("p (a b) -> p a b", b=6) for i in range(n)]
            mvv = [small[:, 16 * i + 12:16 * i + 16] for i in range(n)]
            if part != "tr":
                ln_elem(tiles, ys, Bys, g_bc, b_bc, Bvec, gmul_eng, n, sts, mvv)
            if next_mod is None or part == "elem":
                return
            ln_tr(tiles, next_mod, banks)

        def ln_elem(tiles, ys, Bys, g_bc, b_bc, Bvec, gmul_eng, n, sts, mvv):
            for i in range(n):
                P.v("dve", "bn_stats", sts[i][:, 0, :], ys[i][:, 0:512], R=[Bys[i]], W=[Bsmall[i]])
                P.v("dve", "bn_stats", sts[i][:, 1, :], ys[i][:, 512:1024], R=[Bys[i]], W=[Bsmall[i]])
                P.v("dve", "bn_aggr", mvv[i][:, 0:2], sts[i], R=[Bsmall[i]], W=[Bsmall[i]])
            for i in range(n):
                P.v("act", "activation", mvv[i][:, 2:3], mvv[i][:, 1:2], AF.Sqrt, bias=LN_EPS,
                    R=[Bsmall[i]], W=[Bsmall[i]])
            for i in range(n):
                P.v("dve", "reciprocal", mvv[i][:, 2:3], mvv[i][:, 2:3], R=[Bsmall[i]], W=[Bsmall[i]])
                P.v("dve", "tensor_scalar", mvv[i][:, 3:4], mvv[i][:, 0:1], mvv[i][:, 2:3], -1.0,
                    op0=ALU.mult, op1=ALU.mult, R=[Bsmall[i]], W=[Bsmall[i]])
            for i in range(n):
                P.v("act", "activation", ys[i], ys[i], AF.Identity, scale=mvv[i][:, 2:3], bias=mvv[i][:, 3:4],
                    R=[Bys[i], Bsmall[i]], W=[Bys[i]])
            for i in range(n):
                P.v("dve", "tensor_tensor", ys[i], ys[i], g_bc, ALU.mult, R=[Bys[i]] + list(Bvec), W=[Bys[i]])
            for i, t in enumerate(tiles):
                P.v("dve", "tensor_tensor", x_sb[:, t, :], ys[i], b_bc, ALU.add, R=[Bys[i]] + list(Bvec), W=[Bx[t]])

        def ln_tr(tiles, next_mod, banks):
            layer, vA, vB = next_mod
            nb = len(banks) // 2
            for i, t in enumerate(tiles):
                c = tile_cond(t)
                pair = banks[2 * (i % nb):2 * (i % nb) + 2]
                for half in range(2):
                    pb, Bp = pair[half]
                    for q in range(4):
                        kc = half * 4 + q
                        P.tr(pb[:, q * 128:(q + 1) * 128], x_sb[:, t, kc * 128:(kc + 1) * 128], ident_f[:],
                             reads=[Bx[t], Bconst], writes=[Bp])
                for half in range(2):
                    pb, Bp = pair[half]
                    for q in range(4):
                        kc = half * 4 + q
                        if half == 0:
                            P.v("dve", "tensor_scalar", hT[:, kc, t * 128:(t + 1) * 128], pb[:, q * 128:(q + 1) * 128],
                                alpha_col(layer, vA, kc, c), alpha_col(layer, vB, kc, c),
                                op0=ALU.mult, op1=ALU.add, R=[Bp, Bmod[layer]], W=[BhT[t][0]])
                        else:
                            P.v("act", "activation", hT[:, kc, t * 128:(t + 1) * 128], pb[:, q * 128:(q + 1) * 128],
                                AF.Identity, scale=alpha_col(layer, vA, kc, c), bias=alpha_col(layer, vB, kc, c),
                                R=[Bp, Bmod[layer]], W=[BhT[t][1]])

        PBK = [(psum[i], PB[i]) for i in range(8)]

        def store_x(label):
            for t in range(NT):
                P.dma("sp", y_d[t * 128:(t + 1) * 128, :], x_sb[:, t, :], reads=[Bx[t]])

        if stop == "ada":
            store_x("ada")
            P.end_phase()
            return nc

        with ExitStack() as ph:
            bc = [sbuf(ph, "bc%d" % i, [128, D]) for i in range(8)]
            Bbc = [Buf("bc%d" % i) for i in range(8)]
            psc = sbuf(ph, "psc", [128, D])
            Bpsc = Buf("psc")
            dgall = sbuf(ph, "dgall", [128, 8, 128])
            Bdg = Buf("dgall")
            h_tok = sbuf(ph, "h_tok", [128, NT, D])
            Bh = [Buf("h%d" % t) for t in range(NT)]
            ptab = sbuf(ph, "ptab", [128, 4, 5, 128])
            Bptab = Buf("ptab")
            pw = sbuf(ph, "pw", [128, 4, 2, 256], BF16)
            Bpw = Buf("pw")
            pTt = [sbuf(ph, "pTt%d" % i, [128, 8, 128], BF16) for i in range(2)]
            BpTt = [Buf("pTt0"), Buf("pTt1")]
            ybuf = [sbuf(ph, "ybuf%d" % i, [128, D]) for i in range(4)]
            By = [Buf("y%d" % i) for i in range(4)]
            wa1 = sbuf(ph, "wa1", [128, 8, 512], BF16)
            Bwa1 = Buf("wa1")

            P.dma("sp", ptab[:], ptab_d.rearrange("w k s t -> s w k t"), writes=[Bptab])
            P.dma("pool", pw[:], pool_w_d.rearrange("g (i c) d -> c g i d", c=128), writes=[Bpw])
            load_bc(psc[:], pool_scale_d[0:1, :], [Bpsc])
            load_bc(bc[6][:], ln1g_d[0:1, :], [Bbc[6]])
            load_bc(bc[7][:], ln1b_d[0:1, :], [Bbc[7]])
            for c in range(2):
                make_bc(bc[2 * c], 0, 1, c, dgall, Bdg, PBK[0:2], [Bbc[2 * c]])
                make_bc(bc[2 * c + 1], 0, 0, c, dgall, Bdg, PBK[2:4], [Bbc[2 * c + 1]])
                make_bc(bc[4 + c], 0, 2, c, dgall, Bdg, PBK[4:6], [Bbc[4 + c]])
                P.v("pool", "tensor_tensor", bc[4 + c][:], bc[4 + c][:], psc[:], ALU.mult,
                    R=[Bbc[4 + c], Bpsc], W=[Bbc[4 + c]])
            for t in range(NT):
                c = tile_cond(t)
                P.v("dve", "tensor_tensor", h_tok[:, t, :], x_sb[:, t, :], bc[2 * c][:], ALU.mult,
                    R=[Bx[t], Bbc[2 * c]], W=[Bh[t]])
                P.v("dve", "tensor_tensor", h_tok[:, t, :], h_tok[:, t, :], bc[2 * c + 1][:], ALU.add,
                    R=[Bh[t], Bbc[2 * c + 1]], W=[Bh[t]])
            seq_of = {}
            for (t0, ntl) in SEQS:
                for q in range(ntl):
                    seq_of[t0 + q] = (q, ntl)
            def pool_group(grp):
                tiles = list(range(4 * grp, 4 * grp + 4))
                for i, t in enumerate(tiles):
                    q, ntl = seq_of[t]
                    c = tile_cond(t)
                    k = i % 2
                    (pA, BA), (pBk, BB) = PBK[2 * k], PBK[2 * k + 1]
                    kind = 0 if q == 0 else (2 if q == ntl - 1 else 1)
                    srcs = [(t, kind)]
                    if q > 0:
                        srcs.append((t - 1, 3))
                    if q < ntl - 1:
                        srcs.append((t + 1, 4))
                    for kc in range(8):
                        wi = kc // 2
                        bank, Bb = (pA, BA) if kc < 4 else (pBk, BB)
                        dst = bank[:, (kc % 4) * 128:(kc % 4 + 1) * 128]
                        for si, (ts, kd) in enumerate(srcs):
                            P.mm(dst, h_tok[:, ts, kc * 128:(kc + 1) * 128], ptab[:, wi, kd, :],
                                 si == 0, si == len(srcs) - 1, reads=[Bh[ts], Bptab], writes=[Bb])
                    P.v("act", "copy", pTt[k][:, 0:4, :], pA[:].rearrange("p (a b) -> p a b", b=128),
                        R=[BA], W=[BpTt[k]])
                    P.v("act", "copy", pTt[k][:, 4:8, :], pBk[:].rearrange("p (a b) -> p a b", b=128),
                        R=[BB], W=[BpTt[k]])
                    for half in range(2):
                        bank, Bb = (pA, BA) if half == 0 else (pBk, BB)
                        for gg in range(2):
                            g = half * 2 + gg
                            for i2 in range(2):
                                P.mm(bank[:, gg * 256:(gg + 1) * 256], pTt[k][:, 2 * g + i2, :], pw[:, g, i2, :],
                                     i2 == 0, i2 == 1, reads=[BpTt[k], Bpw], writes=[Bb])
                        P.v("dve", "tensor_tensor", ybuf[i][:, half * 512:(half + 1) * 512], bank[:],
                            bc[4 + c][:, half * 512:(half + 1) * 512], ALU.mult, R=[Bb, Bbc[4 + c]], W=[By[i]])
                    P.v("dve", "scalar_tensor_tensor", ybuf[i][:], x_sb[:, t, :], ALPHA, ybuf[i][:],
                        op0=ALU.mult, op1=ALU.add, R=[Bx[t], By[i]], W=[By[i]])
                    ada_block(1, t, wa1, Bwa1, PBK[6 + t % 2])

            def pool_ln(grp, part):
                tiles = list(range(4 * grp, 4 * grp + 4))
                ln_group(tiles, [ybuf[i][:] for i in range(4)], By, bc[6][:], bc[7][:], [Bbc[6], Bbc[7]],
                         (0, 4, 3), PBK[4:8], "pool", part=part)

            pool_group(0)
            for grp in range(3):
                pool_ln(grp, "elem")
                if grp + 1 < 3:
                    pool_group(grp + 1)
                pool_ln(grp, "tr")
            if stop == "l0mix":
                store_x("l0mix")
            P.end_phase()
        if stop == "l0mix":
            return nc


        def moe_phase(layer, next_mod, final):
            with ExitStack() as ph:
                bc = [sbuf(ph, "bc%d" % i, [128, D]) for i in range(4)]
                Bbc = [Buf("mbc%d" % i) for i in range(4)]
                dgall = sbuf(ph, "dgall", [128, 8, 128])
                Bdg = Buf("dgall")
                wg = [sbuf(ph, "wg%d" % i, [128, 8, 512], BF16) for i in range(2)]
                wu = [sbuf(ph, "wu%d" % i, [128, 8, 512], BF16) for i in range(2)]
                wd = [sbuf(ph, "wd%d" % i, [128, 4, D], BF16) for i in range(2)]
                Bwg = [Buf("wg0"), Buf("wg1")]
                Bwu = [Buf("wu0"), Buf("wu1")]
                Bwd = [Buf("wd0"), Buf("wd1")]
                heT = [sbuf(ph, "heT%d" % i, [128, 4, 512], BF16) for i in range(2)]
                Bhe = [Buf("he0"), Buf("he1")]
                sg = [sbuf(ph, "sg%d" % i, [128, 512], BF16) for i in range(2)]
                Bsg = [Buf("sg0"), Buf("sg1")]
                yacc = sbuf(ph, "yacc", [128, NT, D])
                Byacc = [Buf("yacc%d" % t) for t in range(NT)]
                wr_b = sbuf(ph, "wr_b", [128, 8, 16], BF16)
                brt = sbuf(ph, "brt", [128, 16])
                Bwr = Buf("wr")
                Bbrt = Buf("brt")
                rt = sbuf(ph, "rt", [128, 1472])
                Brt = Buf("rt")
                G = [psum[0], psum[1]]
                U = [psum[2], psum[3]]
                Dn = [psum[4], psum[5], psum[6], psum[7]]
                BG, BU, BD = [PB[0], PB[1]], [PB[2], PB[3]], [PB[4], PB[5], PB[6], PB[7]]

                def load_expert(e):
                    sl = e % 2
                    P.dma("pool", wg[sl][:], w_gate_d[layer, e].rearrange("(kc k) f -> k kc f", k=128), writes=[Bwg[sl]])
                    P.dma("pool", wu[sl][:], w_up_d[layer, e].rearrange("(kc k) f -> k kc f", k=128), writes=[Bwu[sl]])
                    P.dma("pool", wd[sl][:], w_down_d[layer, e].rearrange("(fc f) d -> f fc d", f=128), writes=[Bwd[sl]])

                P.dma("pool", wr_b[:], w_router_d.rearrange("(kc k) e -> k kc e", k=128), writes=[Bwr])
                load_expert(0)
                load_expert(1)
                P.dma("sp", brt[:], b_router_d[0:1, :].partition_broadcast(128), writes=[Bbrt])
                load_bc(bc[2][:], ln2g_d[layer:layer + 1, :], [Bbc[2]])
                load_bc(bc[3][:], ln2b_d[layer:layer + 1, :], [Bbc[3]])
                for c in range(2):
                    make_bc(bc[c], layer, 5, c, dgall, Bdg, PBK[6:8], [Bbc[c]])

                lg = psum[4]
                for t in range(NT):
                    for kc in range(8):
                        P.mm(lg[:, t * 16:(t + 1) * 16], hT[:, kc, t * 128:(t + 1) * 128], wr_b[:, kc, :],
                             kc == 0, kc == 7, reads=[BhT[t][0], BhT[t][1], Bwr], writes=[PB[4]])
                o = [0]

                def carve(nel):
                    v_ = rt[:, o[0]:o[0] + nel]
                    o[0] += nel
                    return v_
                s_sb, sel, t6, gs, gm = carve(192), carve(192), carve(288), carve(48), carve(12)
                gmask, msel, top8, emask, wsel, wsum = carve(48), carve(192), carve(96), carve(192), carve(192), carve(12)
                Rr, Wr = [Brt], [Brt]
                P.v("act", "activation", s_sb, lg[:, 0:192], AF.Sigmoid, R=[PB[4]], W=Wr)
                P.v("dve", "tensor_tensor", sel.rearrange("p (t e) -> p t e", e=16),
                    s_sb.rearrange("p (t e) -> p t e", e=16), brt[:].unsqueeze(1).to_broadcast([128, NT, 16]),
                    ALU.add, R=Rr + [Bbrt], W=Wr)
                sel4 = sel.rearrange("p (g e) -> p g e", e=4)
                t6v = t6.rearrange("p (g e) -> p g e", e=6)
                P.v("dve", "tensor_tensor", t6v[:, :, 0:3], sel4[:, :, 0:3], sel4[:, :, 1:4], ALU.add, R=Rr, W=Wr)
                P.v("dve", "tensor_tensor", t6v[:, :, 3:5], sel4[:, :, 0:2], sel4[:, :, 2:4], ALU.add, R=Rr, W=Wr)
                P.v("dve", "tensor_tensor", t6v[:, :, 5:6], sel4[:, :, 0:1], sel4[:, :, 3:4], ALU.add, R=Rr, W=Wr)
                P.v("dve", "tensor_reduce", gs, t6v, AX.X, ALU.max, R=Rr, W=Wr)
                gs3 = gs.rearrange("p (t g) -> p t g", g=4)
                P.v("dve", "tensor_reduce", gm, gs3, AX.X, ALU.max, R=Rr, W=Wr)
                P.v("dve", "tensor_tensor", gmask.rearrange("p (t g) -> p t g", g=4), gs3,
                    gm.unsqueeze(2).to_broadcast([128, NT, 4]), ALU.is_equal, R=Rr, W=Wr)
                P.v("dve", "scalar_tensor_tensor", msel.rearrange("p (g e) -> p g e", e=4), sel4, 2.0,
                    gmask.unsqueeze(2).to_broadcast([128, 48, 4]), op0=ALU.add, op1=ALU.mult, R=Rr, W=Wr)
                top83 = top8.rearrange("p (t e) -> p t e", e=8)
                for t in range(NT):
                    P.v("dve", "max", top83[:, t, :], msel[:, t * 16:(t + 1) * 16], R=Rr, W=Wr)
                P.v("dve", "tensor_tensor", emask.rearrange("p (t e) -> p t e", e=16),
                    msel.rearrange("p (t e) -> p t e", e=16), top83[:, :, 1:2].to_broadcast([128, NT, 16]),
                    ALU.is_ge, R=Rr, W=Wr)
                P.v("dve", "tensor_tensor", wsel, s_sb, emask, ALU.mult, R=Rr, W=Wr)
                wsel3 = wsel.rearrange("p (t e) -> p t e", e=16)
                P.v("dve", "tensor_reduce", wsum, wsel3, AX.X, ALU.add, R=Rr, W=Wr)
                P.v("dve", "reciprocal", wsum, wsum, R=Rr, W=Wr)
                P.v("dve", "tensor_tensor", cmb[:], wsel3, wsum.unsqueeze(2).to_broadcast([128, NT, 16]),
                    ALU.mult, R=Rr, W=Bcmb)

                blocks = [(e, b) for e in range(16) for b in range(3)]
                hT_bufs = [[x for t in range(4 * b, 4 * b + 4) for x in BhT[t]] for b in range(3)]

                def GU(k):
                    e, b = blocks[k]
                    sl = e % 2
                    for fc in range(4):
                        gi = (k * 4 + fc) % 2
                        for kc in range(8):
                            P.mm(G[gi][:], wg[sl][:, kc, fc * 128:(fc + 1) * 128], hT[:, kc, b * 512:(b + 1) * 512],
                                 kc == 0, kc == 7, reads=[Bwg[sl]] + hT_bufs[b], writes=[BG[gi]])
                        for kc in range(8):
                            P.mm(U[gi][:], wu[sl][:, kc, fc * 128:(fc + 1) * 128], hT[:, kc, b * 512:(b + 1) * 512],
                                 kc == 0, kc == 7, reads=[Bwu[sl]] + hT_bufs[b], writes=[BU[gi]])
                        P.v("act", "activation", sg[gi][:], G[gi][:], AF.Silu, R=[BG[gi]], W=[Bsg[gi]])
                        P.v("dve", "tensor_tensor", heT[k % 2][:, fc, :], U[gi][:], sg[gi][:], ALU.mult,
                            R=[BU[gi], Bsg[gi]], W=[Bhe[k % 2]])

                def DNp(k):
                    e, b = blocks[k]
                    sl = e % 2
                    for tt in range(4):
                        t = 4 * b + tt
                        for half in range(2):
                            di = (k * 8 + tt * 2 + half) % 4
                            for fc in range(4):
                                P.mm(Dn[di][:], heT[k % 2][:, fc, tt * 128:(tt + 1) * 128],
                                     wd[sl][:, fc, half * 512:(half + 1) * 512], fc == 0, fc == 3,
                                     reads=[Bhe[k % 2], Bwd[sl]], writes=[BD[di]])
                            dst = yacc[:, t, half * 512:(half + 1) * 512]
                            if e == 0:
                                P.v("dve", "tensor_scalar", dst, Dn[di][:], cmb[:, t, e:e + 1], None, op0=ALU.mult,
                                    R=[BD[di], Bcmb[t]], W=[Byacc[t]])
                            else:
                                P.v("dve", "scalar_tensor_tensor", dst, Dn[di][:], cmb[:, t, e:e + 1], dst,
                                    op0=ALU.mult, op1=ALU.add, R=[BD[di], Bcmb[t], Byacc[t]], W=[Byacc[t]])

                def pre(grp):
                    for t in range(4 * grp, 4 * grp + 4):
                        c = tile_cond(t)
                        ya = yacc[:, t, :]
                        P.v("dve", "tensor_tensor", ya, ya, bc[c][:], ALU.mult, R=[Byacc[t], Bbc[c]], W=[Byacc[t]])
                    for t in range(4 * grp, 4 * grp + 4):
                        ya = yacc[:, t, :]
                        P.v("dve", "scalar_tensor_tensor", ya, x_sb[:, t, :], ALPHA, ya, op0=ALU.mult, op1=ALU.add,
                            R=[Bx[t], Byacc[t]], W=[Byacc[t]])

                def epi(grp, part):
                    tiles = list(range(4 * grp, 4 * grp + 4))
                    ln_group(tiles, [yacc[:, t, :] for t in tiles], [Byacc[t] for t in tiles], bc[2][:], bc[3][:],
                             [Bbc[2], Bbc[3]], next_mod, PBK, "dve", part=part)

                GU(0)
                for k in range(len(blocks)):
                    if k + 1 < len(blocks):
                        GU(k + 1)
                    DNp(k)
                    e_, b_ = blocks[k]
                    if b_ == 2 and e_ + 2 < 16:
                        load_expert(e_ + 2)
                    if e_ == 15:
                        pre(b_)
                        epi(b_, "elem")
                for grp in range(3):
                    if next_mod is not None:
                        epi(grp, "tr")
                    if final:
                        for t in range(4 * grp, 4 * grp + 4):
                            P.dma("sp", y_d[t * 128:(t + 1) * 128, :], x_sb[:, t, :], reads=[Bx[t]])
                P.end_phase()

        moe_phase(0, (1, 1, 0), False)
        if stop == "l0moe":
            store_x("l0moe")
            P.end_phase()
            return nc


        with ExitStack() as pa:
            QT_s = sbuf(pa, "QT_s", [128, 8, 1024], BF16)
            KT_s = sbuf(pa, "KT_s", [128, 8, 1024], BF16)
            Vp_s = sbuf(pa, "Vp_s", [128, 8, 8, 192], BF16)
            O_CTXV, O_M01, O_RBB, O_E, O_ONES, RG_N = 4096, 10240, 12032, 13824, 19200, 19392
            Rg = sbuf(pa, "Rg", [128, RG_N], BF16)
            QT_p = Rg[:, 0:4096].rearrange("p (c t) -> p c t", c=8)
            KT_p = Rg[:, 4096:8192].rearrange("p (c t) -> p c t", c=8)
            Vp_p = Rg[:, 8192:14336].rearrange("p (m a e) -> p m a e", m=4, a=8)
            ctxKT = Rg[:, 0:4096].rearrange("p (c t) -> p c t", c=8)
            ctxVp = Rg[:, O_CTXV:O_M01].rearrange("p (m a e) -> p m a e", m=4, a=8)
            m01 = Rg[:, O_M01:O_RBB].rearrange("p (m q) -> p m q", q=128)
            rbb = Rg[:, O_RBB:O_E].rearrange("p (b d q) -> p b d q", b=2, d=7)
            Etab = Rg[:, O_E:O_ONES].rearrange("p (b d q) -> p b d q", b=3, d=14)
            onesp = Rg[:, O_ONES:RG_N]
            BQs, BKs, BVs = Buf("QTs"), Buf("KTs"), Buf("Vs")
            BQp, BKp, BVp = Buf("QTp"), Buf("KTp"), Buf("Vp")
            Bones = Buf("onesp")

            with ExitStack() as ph:
                wb = [sbuf(ph, "wqkv%d" % i, [128, 8, 512], BF16) for i in range(2)]
                Bwb = [Buf("wb0"), Buf("wb1")]
                stg = [sbuf(ph, "stg%d" % i, [128, 512]) for i in range(2)]
                Bstg = [Buf("stg0"), Buf("stg1")]
                P.v("pool", "memset", Vp_s[:, :, :, 64:128], 0.0, W=[BVs])
                P.v("pool", "memset", Vp_p[:, :, :, 64:128], 0.0, W=[BVp])
                P.v("pool", "memset", onesp[:, 0:64], 1.0, W=[Bones])
                P.v("pool", "memset", onesp[:, 64:128], 0.0, W=[Bones])
                P.v("pool", "memset", onesp[:, 128:192], 1.0, W=[Bones])
                nb = [0]
                ns = [0]

                def bank():
                    i = nb[0] % 8
                    nb[0] += 1
                    return psum[i], PB[i], ("dve" if i % 2 == 0 else "act")

                def evac(eng, dst, src, Bsrc, W, mul=None):
                    if eng == "dve":
                        if mul is None:
                            P.v("dve", "tensor_copy", dst, src, R=[Bsrc], W=W)
                        else:
                            P.v("dve", "tensor_scalar_mul", dst, src, mul, R=[Bsrc], W=W)
                    else:
                        if mul is None:
                            P.v("act", "copy", dst, src, R=[Bsrc], W=W)
                        else:
                            P.v("act", "mul", dst, src, mul, R=[Bsrc], W=W)

                hTb = lambda tiles: [x for t in tiles for x in BhT[t]]
                for cb in range(6):
                    sl = cb % 2
                    P.dma("pool", wb[sl][:], w_qkv_d.rearrange("(kc k) n -> k kc n", k=128)[:, :, cb * 512:(cb + 1) * 512],
                          writes=[Bwb[sl]])
                    if cb < 4:
                        isq = cb < 2
                        for i in range(4):
                            a = (cb % 2) * 4 + i
                            for tb in range(3):
                                pb, Bp, eng = bank()
                                for kc in range(8):
                                    P.mm(pb[:], wb[sl][:, kc, i * 128:(i + 1) * 128], hT[:, kc, tb * 512:(tb + 1) * 512],
                                         kc == 0, kc == 7, reads=[Bwb[sl]] + hTb(range(4 * tb, 4 * tb + 4)), writes=[Bp])
                                if tb == 0:
                                    dst = (QT_p if isq else KT_p)[:, a, :]
                                    W = [BQp if isq else BKp]
                                else:
                                    dst = (QT_s if isq else KT_s)[:, a, (tb - 1) * 512:tb * 512]
                                    W = [BQs if isq else BKs]
                                evac(eng, dst, pb[:], Bp, W, mul=(0.125 if isq else None))
                    if cb in (2, 3, 4, 5):
                        isv = cb >= 4
                        hh = cb % 2
                        for t in (range(NT) if isv else range(4)):
                            pb, Bp, eng = bank()
                            for kc in range(8):
                                P.mm(pb[:], hT[:, kc, t * 128:(t + 1) * 128], wb[sl][:, kc, :], kc == 0, kc == 7,
                                     reads=[Bwb[sl]] + hTb([t]), writes=[Bp])
                            if isv:
                                src4 = pb[:].rearrange("p (a two d) -> p a two d", two=2, d=64)
                                for par in range(2):
                                    if t < 4:
                                        evac(eng, Vp_p[:, t, hh * 4:(hh + 1) * 4, par * 128:par * 128 + 64],
                                             src4[:, :, par, :], Bp, [BVp])
                                    else:
                                        evac(eng, Vp_s[:, t - 4, hh * 4:(hh + 1) * 4, par * 128:par * 128 + 64],
                                             src4[:, :, par, :], Bp, [BVs])
                            if t < 4:
                                k = ns[0] % 2
                                ns[0] += 1
                                evac("dve" if eng == "act" else "act", stg[k][:], pb[:], Bp, [Bstg[k]])
                                dd = nv_d if isv else nk_d
                                s_, l0 = t // 2, (t % 2) * 128
                                P.dma("sp", dd[s_, hh * 8:(hh + 1) * 8, l0:l0 + 128, :].rearrange("h l d -> l h d"),
                                      stg[k][:].rearrange("p (h d) -> p h d", d=64), reads=[Bstg[k]])
                P.end_phase()

            with ExitStack() as pb_:
                OT_all = sbuf(pb_, "OT_all", [128, 8, 1024], BF16)
                BOT = [Buf("OT%d" % a) for a in range(8)]
                wo = sbuf(pb_, "wo", [128, 8, D], BF16)
                Bwo = Buf("wo")
                By = [Buf("ay%d" % i) for i in range(4)]
                ST = [PBK[i] for i in range(4)]
                ACC = [(PBK[4], PBK[5]), (PBK[6], PBK[7])]
                cn = {"st": 0, "pt": 0, "em": 0}

                def heads_scope(scope):
                    PT = [sbuf(scope, "PT%d" % i, [128, 512], BF16)[:] for i in range(4)]
                    PT += [hT[:, kc, 512:1024] for kc in range(4)]
                    BPT = [Buf("PT%d" % i) for i in range(8)]
                    rS = sbuf(scope, "rS", [128, 512])
                    BrS = Buf("rS")
                    return PT, BPT, rS, BrS

                LOOKAHEAD = 6
                pend = []

                def drain(keep):
                    while len(pend) > keep:
                        pend.pop(0)()

                def segment(PT, BPT, Kst, Qmv, nq, Bk, Bq, esl, BE_, Vblk, BV_, ones_blk, acc, c0, hold=False):
                    st, Bst = ST[cn["st"] % 4]
                    cn["st"] += 1
                    k = cn["pt"] % 8
                    cn["pt"] += 1
                    P.mm(st[:, 0:nq], Kst, Qmv, True, True, reads=[Bk, Bq], writes=[Bst])
                    P.v("act", "activation", PT[k][:, 0:nq], st[:, 0:nq], AF.Exp, R=[Bst], W=[BPT[k]])
                    if esl is not None:
                        P.v("dve", "tensor_tensor", PT[k][:, 0:nq], PT[k][:, 0:nq], esl, ALU.mult,
                            R=[BPT[k], BE_], W=[BPT[k]])
                    (pO, BO_), (pS, BS_) = acc

                    def back():
                        P.mm(pO[:, c0:c0 + nq], Vblk, PT[k][:, 0:nq], False, False, reads=[BPT[k], BV_], writes=[BO_],
                             skip=True)
                        P.mm(pS[:, c0:c0 + nq], ones_blk, PT[k][:, 0:nq], False, False, reads=[BPT[k], Bones],
                             writes=[BS_], skip=True)
                    pend.append(back)
                    if not hold:
                        drain(LOOKAHEAD)

                def zero_acc(acc, ncols):
                    drain(5)
                    (pO, BO_), (pS, BS_) = acc
                    P.v("dve", "memset", pO[:, 0:ncols], 0.0, W=[BO_])
                    P.v("dve", "memset", pS[:, 0:ncols], 0.0, W=[BS_])

                def normalize(acc, ncols, rS, BrS, dst, Wd):
                    (pO, BO_), (pS, BS_) = acc

                    def fin():
                        P.v("dve", "reciprocal", rS[:, 0:ncols], pS[:, 0:ncols], R=[BS_], W=[BrS])
                        P.v("dve", "tensor_tensor", dst, pO[:, 0:ncols], rS[:, 0:ncols], ALU.mult, R=[BO_, BrS], W=Wd)
                    pend.append(fin)

                def proj_group(tiles, tcols, ybufs, deadW, bcv, Bbcv):
                    for i, t in enumerate(tiles):
                        tc = tcols[i]
                        for half in range(2):
                            pb, Bp = PBK[2 * (i % 2) + half]
                            for kc in range(8):
                                P.mm(pb[:], OT_all[:, kc, tc:tc + 128], wo[:, kc, half * 512:(half + 1) * 512],
                                     kc == 0, kc == 7, reads=[BOT[kc], Bwo], writes=[Bp])
                            P.v("dve", "tensor_tensor", ybufs[i][:, half * 512:(half + 1) * 512], pb[:],
                                bcv[0][:, half * 512:(half + 1) * 512], ALU.mult, R=[Bp, Bbcv[0]],
                                W=[By[i]] + list(deadW))
                        P.v("dve", "scalar_tensor_tensor", ybufs[i], x_sb[:, t, :], ALPHA, ybufs[i],
                            op0=ALU.mult, op1=ALU.add, R=[Bx[t], By[i]], W=[By[i]])
                    ln_group(tiles, ybufs, By, bcv[1], bcv[2], [Bbcv[1], Bbcv[2]], (1, 4, 3), PBK[4:8], "pool")

                def prep_vectors(cidx, bcv, Bbcv, dgall_v, Bdg_v, deadW):
                    load_bc(bcv[1], ln1g_d[1:2, :], [Bbcv[1]] + list(deadW))
                    load_bc(bcv[2], ln1b_d[1:2, :], [Bbcv[2]] + list(deadW))
                    make_bc(bcv[0], 1, 2, cidx, dgall_v, Bdg_v, PBK[6:8], [Bbcv[0]] + list(deadW))

                with ExitStack() as ph:
                    PT, BPT, rS, BrS = heads_scope(ph)
                    P.dma("pool", wo[:], w_o_d.rearrange("(kc k) n -> k kc n", k=128), writes=[Bwo])
                    for a in range(8):
                        for s_ in range(2):
                            acc = ACC[(2 * a + s_) % 2]
                            zero_acc(acc, 256)
                            for c in range(2):
                                for rho in range(2):
                                    pr = slice(64 * rho, 64 * rho + 64)
                                    vc = slice(64 * rho, 64 * rho + 128)
                                    k0 = s_ * 256 + c * 128
                                    segment(PT, BPT, KT_p[pr, a, k0:k0 + 128], QT_p[pr, a, s_ * 256:(s_ + 1) * 256], 256,
                                            BKp, BQp, None, None, Vp_p[:, 2 * s_ + c, a, vc], BVp, onesp[:, vc], acc, 0, hold=(rho == 0))
                            normalize(acc, 256, rS, BrS, OT_all[:, a, s_ * 256:(s_ + 1) * 256], [BOT[a]])
                    drain(0)
                    P.end_phase()

                yb_p = [Rg[:, 2048 * i:2048 * (i + 1)].bitcast(F32) for i in range(4)]
                bc_p = [Rg[:, 8192 + 2048 * i:8192 + 2048 * (i + 1)].bitcast(F32) for i in range(3)]
                dg_p = Rg[:, 14336:16384].bitcast(F32).rearrange("p (a b) -> p a b", b=128)
                Bbc_p = [Buf("pbc%d" % i) for i in range(3)]
                prep_vectors(0, bc_p, Bbc_p, dg_p, Buf("pdg"), [])
                proj_group([0, 1, 2, 3], [0, 128, 256, 384], yb_p, [], bc_p, Bbc_p)
                if stop == "l1mixp":
                    store_x("l1mixp")
                P.end_phase()
                if stop == "l1mixp":
                    return nc

                with ExitStack() as ph:
                    PT, BPT, rS, BrS = heads_scope(ph)
                    ck_tok = OT_all[:, 0:4, :].rearrange("p c (h d) -> p c h d", d=64)
                    Bck = Buf("ck_tok")
                    BctxK, BctxV, Bm01 = Buf("ctxK"), Buf("ctxV"), Buf("m01")
                    Brbb = [Buf("rbb0"), Buf("rbb1")]
                    BE = [Buf("E0"), Buf("E1"), Buf("E2")]
                    for c in range(4):
                        P.dma("pool", ck_tok[:, c, :, :], ck_d[:, c * 128:(c + 1) * 128, :].rearrange("h l d -> l h d"),
                              writes=[Bck])
                    P.dma("pool", m01[:], m01_d.rearrange("m k q -> k m q"), writes=[Bm01])
                    P.v("pool", "memset", ctxVp[:, :, :, 64:128], 0.0, W=[BctxV])
                    cv4 = cv_d.rearrange("(a two) l d -> two l a d", two=2)
                    for c in range(4):
                        for par in range(2):
                            P.dma("pool", ctxVp[:, c, :, par * 128:par * 128 + 64], cv4[par, c * 128:(c + 1) * 128, :, :],
                                  writes=[BctxV])

                    def build_E(h):
                        hb, eb = h % 2, h % 3
                        P.dma("pool", rbb[:, hb, :, :], rbt_d[h].rearrange("d k q -> k d q"), writes=[Brbb[hb]])
                        P.v("act", "activation", rbb[:, hb, :, :], rbb[:, hb, :, :], AF.Exp, R=[Brbb[hb]], W=[Brbb[hb]])
                        for s2 in range(2):
                            P.v("dve", "tensor_tensor", Etab[:, eb, s2 * 7:(s2 + 1) * 7, :], rbb[:, hb, :, :],
                                m01[:, s2 * 7:(s2 + 1) * 7, :], ALU.mult, R=[Brbb[hb], Bm01], W=[BE[eb]])

                    for c in range(4):
                        pTb = psum[c % 2][:].bitcast(BF16)
                        for a in range(8):
                            P.tr(pTb[:, a * 128:(a + 1) * 128], ck_tok[:, c, 2 * a:2 * a + 2, :].rearrange("p h d -> p (h d)"),
                                 ident_b[:], reads=[Bck, Bconst], writes=[PB[c % 2]])
                        P.v("dve" if c % 2 == 0 else "act", "tensor_copy" if c % 2 == 0 else "copy",
                            ctxKT[:, :, c * 128:(c + 1) * 128], pTb.rearrange("p (a t) -> p a t", a=8),
                            R=[PB[c % 2]], W=[BctxK])
                    build_E(0)
                    build_E(1)
                    build_E(2)
                    for a in range(8):
                        for qb in range(2):
                            acc = ACC[(2 * a + qb) % 2]
                            zero_acc(acc, 512)
                            for (m, jlo, cnt, e2) in _SEGS[qb]:
                                for rho in range(2):
                                    eb = (2 * a + rho) % 3
                                    pr = slice(64 * rho, 64 * rho + 64)
                                    vc = slice(64 * rho, 64 * rho + 128)
                                    nq, q0 = cnt * 128, jlo * 128
                                    i0 = e2 * 7 + 3 - m + jlo
                                    esl = Etab[:, eb, i0:i0 + cnt, :].rearrange("p a b -> p (a b)")
                                    segment(PT, BPT, KT_s[pr, a, m * 128:(m + 1) * 128], QT_s[pr, a, q0:q0 + nq], nq,
                                            BKs, BQs, esl, BE[eb], Vp_s[:, m, a, vc], BVs, onesp[:, vc], acc, q0 - 512 * qb, hold=(rho == 0))
                            for c in range(4):
                                for rho in range(2):
                                    pr = slice(64 * rho, 64 * rho + 64)
                                    vc = slice(64 * rho, 64 * rho + 128)
                                    segment(PT, BPT, ctxKT[pr, a, c * 128:(c + 1) * 128], QT_s[pr, a, qb * 512:(qb + 1) * 512],
                                            512, BctxK, BQs, None, None, ctxVp[:, c, a, vc], BctxV, onesp[:, vc], acc, 0, hold=(rho == 0))
                            normalize(acc, 512, rS, BrS, OT_all[:, a, qb * 512:(qb + 1) * 512], [BOT[a], Bck])
                        for h2 in (2 * a + 3, 2 * a + 4):
                            if h2 < 16:
                                build_E(h2)
                    drain(0)
                    P.end_phase()

                yb_s = [QT_s[:, 2 * i:2 * i + 2, :].rearrange("p a t -> p (a t)").bitcast(F32) for i in range(4)]
                bc_s = [KT_s[:, 2 * i:2 * i + 2, :].rearrange("p a t -> p (a t)").bitcast(F32) for i in range(3)]
                dg_s = KT_s[:, 6:8, :].rearrange("p a t -> p (a t)").bitcast(F32).rearrange("p (a b) -> p a b", b=128)
                Bbc_s = [Buf("sbc%d" % i) for i in range(3)]
                prep_vectors(1, bc_s, Bbc_s, dg_s, Buf("sdg"), [])
                proj_group([4, 5, 6, 7], [0, 128, 256, 384], yb_s, [], bc_s, Bbc_s)
                proj_group([8, 9, 10, 11], [512, 640, 768, 896], yb_s, [], bc_s, Bbc_s)
                if stop == "l1mix":
                    store_x("l1mix")
                P.end_phase()
        if stop == "l1mix":
            return nc

        moe_phase(1, None, True)
        return nc


_CACHE = {}


def make_in_maps(inputs):
    f = lambda a: np.ascontiguousarray(np.asarray(a, dtype=np.float32))
    x_prompt, x_sample = f(inputs["x_prompt"]), f(inputs["x_sample"])
    cache_k, cache_v = f(inputs["cache_k"]), f(inputs["cache_v"])
    c, c_ctx = f(inputs["c"]), f(inputs["c_ctx"])
    b_ada = f(inputs["b_ada"])
    shared = {
        "w_ada": f(inputs["w_ada"]),
        "b_adaT": np.ascontiguousarray(b_ada.reshape(2, 48, 128).transpose(0, 2, 1)),
        "ln1_g": f(inputs["ln1_g"]), "ln1_b": f(inputs["ln1_b"]),
        "ln2_g": f(inputs["ln2_g"]), "ln2_b": f(inputs["ln2_b"]),
        "pool_w": f(inputs["pool_w"])[0],
        "pool_scale": f(inputs["pool_scale"]).reshape(1, D),
        "w_qkv": f(inputs["w_qkv"])[0],
        "w_o": f(inputs["w_o"])[0],
        "rbt": _rel_bias_layout(f(inputs["rel_bias"])[0]),
        "w_router": f(inputs["w_router"]),
        "b_router": f(inputs["b_router"]).reshape(1, 16),
        "w_gate": f(inputs["w_gate"]), "w_up": f(inputs["w_up"]), "w_down": f(inputs["w_down"]),
        "ptab": _pool_tables(),
        "m01": _mask01_table(),
    }
    in_maps = []
    for i in range(N_CORES):
        m = dict(shared)
        m["xin"] = np.ascontiguousarray(np.concatenate(
            [x_prompt[2 * i].reshape(256, D), x_prompt[2 * i + 1].reshape(256, D), x_sample[i]], axis=0))
        m["ck"] = np.ascontiguousarray(cache_k[i, 0])
        m["cv"] = np.ascontiguousarray(cache_v[i, 0])
        cond = np.stack([c_ctx, c[i]], axis=-1)
        m["condT"] = np.ascontiguousarray(cond.reshape(8, 128, 2).transpose(1, 0, 2))
        in_maps.append(m)
    return in_maps


def kernel(**inputs):
    if "nc" not in _CACHE:
        _CACHE["nc"] = build_nc()
    nc = _CACHE["nc"]
    in_maps = make_in_maps(inputs)
    res = run_bass_kernel_spmd(nc, in_maps, core_ids=list(range(N_CORES)))
    ys = [r["y"] for r in res.results]
    y_prompt = np.stack([y[:512].reshape(2, 256, D) for y in ys]).reshape(16, 256, D)
    y_sample = np.stack([y[512:] for y in ys])
    nk = np.stack([r["nk"] for r in res.results]).reshape(16, 1, 16, 256, 64)
    nv = np.stack([r["nv"] for r in res.results]).reshape(16, 1, 16, 256, 64)
    return (y_prompt.astype(np.float32), y_sample.astype(np.float32),
            nk.astype(np.float32), nv.astype(np.float32))
```
